# Optimizing a Trainium2 kernel written in Bass

```python
import jax, jax.numpy as jnp
from jax import lax
import numpy as np

D_MODEL = 4096
BATCH = 2
SEQ = 8192
DEPTH = 2

CTX_LEN = 256
GRID_W = 64

N_HEADS = 16
N_KV_HEADS = 4
HEAD_DIM = 128
ATTN_W = N_HEADS * HEAD_DIM
KV_W = N_KV_HEADS * HEAD_DIM
WINDOW = 128
BLOCK = 128
ROPE_BASE = 10000.0

FOURIER_GROUPS = 4
FOURIER_GROUP_W = 256
FOURIER_W = FOURIER_GROUPS * FOURIER_GROUP_W

CONV_W = 1024
CONV_K = 31

N_BRANCH = 3
EPS = 1e-6
NEG_INF = -1e30

Q_OFF = 0
K_OFF = Q_OFF + ATTN_W
V_OFF = K_OFF + KV_W
AG_OFF = V_OFF + KV_W
F_OFF = AG_OFF + ATTN_W
FG_OFF = F_OFF + FOURIER_W
CA_OFF = FG_OFF + FOURIER_W
CB_OFF = CA_OFF + CONV_W
CG_OFF = CB_OFF + CONV_W
MG_OFF = CG_OFF + CONV_W
IN_W = MG_OFF + N_BRANCH * D_MODEL

kernel_name = 'hybrid_fnet_swa_conformer_dit_block'


def rms_norm(x, g):
    xf = x.astype(jnp.float32)
    y = xf * lax.rsqrt(jnp.mean(jnp.square(xf), axis=-1, keepdims=True) + EPS)
    return (y * g.astype(jnp.float32)).astype(x.dtype)


def layer_norm(x, g, b):
    xf = x.astype(jnp.float32)
    mu = jnp.mean(xf, axis=-1, keepdims=True)
    var = jnp.mean(jnp.square(xf - mu), axis=-1, keepdims=True)
    y = (xf - mu) * lax.rsqrt(var + EPS)
    return (y * g.astype(jnp.float32) + b.astype(jnp.float32)).astype(x.dtype)


def adaln_modulate(x, g, shift, scale):
    return rms_norm(x, g) * (1 + scale) + shift


def axial_rope_tables(n_tok):
    rows = n_tok // GRID_W
    row = jnp.broadcast_to(jnp.arange(rows, dtype=jnp.float32)[:, None], (rows, GRID_W)).reshape(-1)
    col = jnp.broadcast_to(jnp.arange(GRID_W, dtype=jnp.float32)[None, :], (rows, GRID_W)).reshape(-1)
    axis_dim = HEAD_DIM // 2
    inv_freq = ROPE_BASE ** (-jnp.arange(0, axis_dim, 2, dtype=jnp.float32) / axis_dim)
    ang_r = row[:, None] * inv_freq[None, :]
    ang_c = col[:, None] * inv_freq[None, :]
    ang = jnp.concatenate([ang_r, ang_r, ang_c, ang_c], axis=-1)
    return jnp.cos(ang), jnp.sin(ang)


def _rotate_half(u):
    u1, u2 = jnp.split(u, 2, axis=-1)
    return jnp.concatenate([-u2, u1], axis=-1)


def apply_axial_rope(x, cos, sin):
    xf = x.astype(jnp.float32)
    xr, xc = jnp.split(xf, 2, axis=-1)
    rot = jnp.concatenate([_rotate_half(xr), _rotate_half(xc)], axis=-1)
    return (xf * cos[:, None, :] + rot * sin[:, None, :]).astype(x.dtype)


def kv_heads(pkv, k_g):
    B, L = pkv.shape[:2]
    k = rms_norm(pkv[..., :KV_W].reshape(B, L, N_KV_HEADS, HEAD_DIM), k_g)
    v = pkv[..., KV_W:].reshape(B, L, N_KV_HEADS, HEAD_DIM)
    return k, v


def attn_heads(p, q_g, k_g):
    B, L = p.shape[:2]
    q = rms_norm(p[..., Q_OFF:K_OFF].reshape(B, L, N_HEADS, HEAD_DIM), q_g)
    k, v = kv_heads(p[..., K_OFF:AG_OFF], k_g)
    return q, k, v


def context_attention(q, k, v, sink):
    B, C = q.shape[:2]
    G = N_HEADS // N_KV_HEADS
    qg = q.reshape(B, C, N_KV_HEADS, G, HEAD_DIM)
    s = jnp.einsum('bqhgd,bkhd->bhgqk', qg, k, preferred_element_type=jnp.float32) * (HEAD_DIM ** -0.5)
    sink_col = jnp.broadcast_to(sink.astype(jnp.float32).reshape(1, N_KV_HEADS, G, 1, 1), s.shape[:-1] + (1,))
    pr = jax.nn.softmax(jnp.concatenate([s, sink_col], axis=-1), axis=-1)[..., :-1]
    o = jnp.einsum('bhgqk,bkhd->bqhgd', pr.astype(v.dtype), v)
    return o.reshape(B, C, ATTN_W)


def windowed_attention_with_context(q, k, v, k_ctx, v_ctx, sink):
    B, L = q.shape[:2]
    nb = L // BLOCK
    G = N_HEADS // N_KV_HEADS
    scale = HEAD_DIM ** -0.5
    qb = q.reshape(B, nb, BLOCK, N_KV_HEADS, G, HEAD_DIM)
    pad = ((0, 0), (BLOCK, BLOCK), (0, 0), (0, 0))
    kb = jnp.pad(k, pad).reshape(B, nb + 2, BLOCK, N_KV_HEADS, HEAD_DIM)
    vb = jnp.pad(v, pad).reshape(B, nb + 2, BLOCK, N_KV_HEADS, HEAD_DIM)
    k_band = jnp.concatenate([kb[:, :-2], kb[:, 1:-1], kb[:, 2:]], axis=2)
    v_band = jnp.concatenate([vb[:, :-2], vb[:, 1:-1], vb[:, 2:]], axis=2)
    s_loc = jnp.einsum('bnqhgd,bnkhd->bnhgqk', qb, k_band, preferred_element_type=jnp.float32) * scale
    s_ctx = jnp.einsum('bnqhgd,bchd->bnhgqc', qb, k_ctx, preferred_element_type=jnp.float32) * scale
    qi = jnp.arange(BLOCK)[:, None]
    kj = jnp.arange(3 * BLOCK)[None, :]
    rel = kj - qi
    band = (rel >= BLOCK - WINDOW) & (rel <= BLOCK + WINDOW)
    kpos = jnp.arange(nb)[:, None] * BLOCK - BLOCK + jnp.arange(3 * BLOCK)[None, :]
    in_range = (kpos >= 0) & (kpos < L)
    mask = band[None, :, :] & in_range[:, None, :]
    s_loc = jnp.where(mask[None, :, None, None, :, :], s_loc, NEG_INF)
    sink_b = sink.astype(jnp.float32).reshape(1, 1, N_KV_HEADS, G, 1, 1)
    m = jnp.maximum(jnp.maximum(jnp.max(s_loc, axis=-1, keepdims=True),
                                jnp.max(s_ctx, axis=-1, keepdims=True)), sink_b)
    p_loc = jnp.exp(s_loc - m)
    p_ctx = jnp.exp(s_ctx - m)
    denom = jnp.sum(p_loc, axis=-1, keepdims=True) + jnp.sum(p_ctx, axis=-1, keepdims=True) + jnp.exp(sink_b - m)
    inv = 1.0 / denom
    o = (jnp.einsum('bnhgqk,bnkhd->bnqhgd', (p_loc * inv).astype(v.dtype), v_band)
         + jnp.einsum('bnhgqc,bchd->bnqhgd', (p_ctx * inv).astype(v.dtype), v_ctx))
    return o.reshape(B, L, ATTN_W)


def fourier_mix(u, w_mix):
    B, L, _ = u.shape
    ug = u.astype(jnp.float32).reshape(B, L, FOURIER_GROUPS, FOURIER_GROUP_W)
    f = jnp.fft.fft2(ug, axes=(1, 3), norm='ortho').real
    y = jnp.einsum('blgc,gcd->blgd', f, w_mix.astype(jnp.float32))
    return y.reshape(B, L, FOURIER_W).astype(u.dtype)


def conformer_conv(a, b, dw_w, dw_b, ln_g, ln_b, w_pw):
    u = a * jax.nn.sigmoid(b)
    y = lax.conv_general_dilated(u, dw_w[:, None, :], window_strides=(1,),
                                 padding=[(CONV_K // 2, CONV_K // 2)],
                                 dimension_numbers=('NWC', 'WIO', 'NWC'),
                                 feature_group_count=CONV_W) + dw_b
    y = jax.nn.silu(layer_norm(y, ln_g, ln_b))
    return y @ w_pw


def merge_branches(p, attn_o, w_attn_up, w_fmix, w_four_up, dw_w, dw_b, cln_g, cln_b, w_pw, w_conv_up, w_out):
    B, L = p.shape[:2]
    y_attn = (attn_o * jax.nn.silu(p[..., AG_OFF:F_OFF])) @ w_attn_up
    y_four = (fourier_mix(p[..., F_OFF:FG_OFF], w_fmix) * jax.nn.silu(p[..., FG_OFF:CA_OFF])) @ w_four_up
    y_conv = (conformer_conv(p[..., CA_OFF:CB_OFF], p[..., CB_OFF:CG_OFF], dw_w, dw_b, cln_g, cln_b, w_pw)
              * jax.nn.silu(p[..., CG_OFF:MG_OFF])) @ w_conv_up
    g = jax.nn.sigmoid(p[..., MG_OFF:]).reshape(B, L, N_BRANCH, D_MODEL)
    merged = g[..., 0, :] * y_attn + g[..., 1, :] * y_four + g[..., 2, :] * y_conv
    return merged @ w_out


def setup_inputs(seed: int = 0) -> dict:
    key = jax.random.key(seed)
    ks = jax.random.split(key, 24)

    def nrm(k, shape, s):
        return jax.random.normal(k, shape, jnp.float32) * s

    return {
        'x': nrm(ks[0], (BATCH, SEQ, D_MODEL), 1.0),
        'c': nrm(ks[1], (BATCH, D_MODEL), 1.0),
        'ctx': nrm(ks[2], (BATCH, CTX_LEN, D_MODEL), 1.0),
        'c_ctx': nrm(ks[3], (D_MODEL,), 1.0),
        'norm_g': 1.0 + nrm(ks[4], (DEPTH, D_MODEL), 0.02),
        'w_mod': nrm(ks[5], (DEPTH, D_MODEL, 3 * D_MODEL), 0.5 * D_MODEL ** -0.5),
        'b_mod': nrm(ks[6], (DEPTH, 3 * D_MODEL), 0.02),
        'w_in': nrm(ks[7], (DEPTH, D_MODEL, IN_W), D_MODEL ** -0.5),
        'q_norm_g': 1.0 + nrm(ks[8], (DEPTH, HEAD_DIM), 0.02),
        'k_norm_g': 1.0 + nrm(ks[9], (DEPTH, HEAD_DIM), 0.02),
        'attn_sink': nrm(ks[10], (DEPTH, N_HEADS), 0.5),
        'w_attn_up': nrm(ks[11], (DEPTH, ATTN_W, D_MODEL), ATTN_W ** -0.5),
        'w_fourier_mix': nrm(ks[12], (DEPTH, FOURIER_GROUPS, FOURIER_GROUP_W, FOURIER_GROUP_W), FOURIER_GROUP_W ** -0.5),
        'w_fourier_up': nrm(ks[13], (DEPTH, FOURIER_W, D_MODEL), FOURIER_W ** -0.5),
        'conv_dw_w': nrm(ks[14], (DEPTH, CONV_K, CONV_W), CONV_K ** -0.5),
        'conv_dw_b': nrm(ks[15], (DEPTH, CONV_W), 0.02),
        'conv_ln_g': 1.0 + nrm(ks[16], (DEPTH, CONV_W), 0.02),
        'conv_ln_b': nrm(ks[17], (DEPTH, CONV_W), 0.02),
        'w_conv_pw': nrm(ks[18], (DEPTH, CONV_W, CONV_W), CONV_W ** -0.5),
        'w_conv_up': nrm(ks[19], (DEPTH, CONV_W, D_MODEL), CONV_W ** -0.5),
        'w_out': nrm(ks[20], (DEPTH, D_MODEL, D_MODEL), D_MODEL ** -0.5),
    }


def reference(x, c, ctx, c_ctx, norm_g, w_mod, b_mod, w_in, q_norm_g, k_norm_g, attn_sink,
              w_attn_up, w_fourier_mix, w_fourier_up, conv_dw_w, conv_dw_b, conv_ln_g, conv_ln_b,
              w_conv_pw, w_conv_up, w_out):
    n_lat = x.shape[1]
    cos, sin = axial_rope_tables(n_lat)
    for l in range(DEPTH):
        mod_lat = jax.nn.silu(c) @ w_mod[l] + b_mod[l]
        sh, sc, gt = jnp.split(mod_lat[:, None, :], 3, axis=-1)
        mod_ctx = jax.nn.silu(c_ctx) @ w_mod[l] + b_mod[l]
        sh_c, sc_c, gt_c = jnp.split(mod_ctx, 3)
        branch_w = (w_attn_up[l], w_fourier_mix[l], w_fourier_up[l], conv_dw_w[l], conv_dw_b[l],
                    conv_ln_g[l], conv_ln_b[l], w_conv_pw[l], w_conv_up[l], w_out[l])

        h_ctx = adaln_modulate(ctx, norm_g[l], sh_c, sc_c)
        if l < DEPTH - 1:
            p_ctx = h_ctx @ w_in[l]
            q_c, k_c, v_c = attn_heads(p_ctx, q_norm_g[l], k_norm_g[l])
            o_c = context_attention(q_c, k_c, v_c, attn_sink[l])
            ctx_next = ctx + gt_c * merge_branches(p_ctx, o_c, *branch_w)
        else:
            k_c, v_c = kv_heads(h_ctx @ w_in[l][:, K_OFF:AG_OFF], k_norm_g[l])
            ctx_next = ctx

        h = adaln_modulate(x, norm_g[l], sh, sc)
        p = h @ w_in[l]
        q, k, v = attn_heads(p, q_norm_g[l], k_norm_g[l])
        q = apply_axial_rope(q, cos, sin)
        k = apply_axial_rope(k, cos, sin)
        o = windowed_attention_with_context(q, k, v, k_c, v_c, attn_sink[l])
        x = x + gt * merge_branches(p, o, *branch_w)
        ctx = ctx_next
    return x
```

```python
import contextlib
import numpy as np
import concourse.bass as bass
import concourse.mybir as mybir
from concourse.bass_utils import run_bass_kernel_spmd

F32 = mybir.dt.float32
BF16 = mybir.dt.bfloat16
AF = mybir.ActivationFunctionType
ALU = mybir.AluOpType
AX = mybir.AxisListType

ENGS = ("pe", "act", "dve", "pool", "sp")
DMAQ = ("sp", "act", "pool")
RING = 8
EPS = 1e-6


class Buf:
    __slots__ = ("writers", "readers")

    def __init__(self):
        self.writers = {}
        self.readers = {}


class Op:
    __slots__ = ("eng", "fn", "deps", "signal", "val", "dma", "slot", "key", "inc", "epoch")

    def __init__(self, eng, fn, dma=False):
        self.eng = eng
        self.fn = fn
        self.deps = []
        self.signal = False
        self.val = 0
        self.dma = dma
        self.slot = -1
        self.key = eng
        self.inc = 1
        self.epoch = 0


class Prog:
    def __init__(self, nc):
        self.nc = nc
        self.top = contextlib.ExitStack()
        self.scope = self.top
        st = self.top
        self.esem = {e: st.enter_context(nc.semaphore("s_" + e)) for e in ENGS}
        self.ring = {q: [st.enter_context(nc.semaphore("r_%s%d" % (q, i))) for i in range(RING)] for q in DMAQ}
        self.ccsem = st.enter_context(nc.semaphore("s_cc"))
        self.cnt = {e: 0 for e in ENGS}
        self.ndma = {q: 0 for q in DMAQ}
        self.ncc = 0
        self.epoch = 0
        self.ops = {e: [] for e in ENGS}
        self.waited = {e: {} for e in ENGS}
        self.nops = 0
        self.mk = self.sbuf("mk", [128, 8], F32)

    def sbuf(self, name, shape, dt):
        self.nalloc = getattr(self, "nalloc", 0) + 1
        return self.scope.enter_context(self.nc.sbuf_tensor("%s_%d" % (name, self.nalloc), list(shape), dt))

    def psum(self, name, shape, dt):
        return self.scope.enter_context(self.nc.psum_tensor(name, list(shape), dt))

    @contextlib.contextmanager
    def phase(self):
        old = self.scope
        with contextlib.ExitStack() as st:
            self.scope = st
            yield
            self.nphase = getattr(self, "nphase", 0) + 1
            import os as _os
            if self.nphase <= int(_os.environ.get("KSTOP", "999")):
                self.flush()
            else:
                self.ops = {e: [] for e in ENGS}
        self.scope = old

    def _add(self, op, reads, writes, partial):
        op.epoch = self.epoch
        deps = {}
        for b in reads:
            for w in b.writers.values():
                deps[id(w)] = w
        for b in writes:
            for w in b.readers.values():
                deps[id(w)] = w
            for w in b.writers.values():
                deps[id(w)] = w
        for d in deps.values():
            if d is op or d.epoch != self.epoch:
                continue
            if (not d.dma) and (not op.dma) and d.eng == op.eng and op.eng == "pe":
                continue
            d.signal = True
            op.deps.append(d)
        for b in reads:
            b.readers[op.key] = op
        for b in writes:
            if not partial:
                b.writers = {}
                b.readers = {}
            b.writers[op.key] = op
        self.ops[op.eng].append(op)
        self.nops += 1
        return op

    def op(self, eng, fn, reads=(), writes=(), partial=False):
        return self._add(Op(eng, fn), reads, writes, partial)

    def dma(self, q, fn, reads=(), writes=(), partial=False):
        o = Op(q, fn, dma=True)
        i = self.ndma[q]
        self.ndma[q] = i + 1
        o.slot = i % RING
        o.val = 16 * (i // RING + 1)
        o.inc = 16
        o.key = (q, o.slot)
        o.signal = True
        return self._add(o, reads, writes, partial)

    def cc(self, fn, reads=(), writes=()):
        o = Op("pool", fn, dma=True)
        self.ncc += 1
        o.slot = -2
        o.val = self.ncc
        o.inc = 1
        o.key = ("cc", 0)
        o.signal = True
        return self._add(o, reads, writes, False)

    def flush(self):
        nc = self.nc
        ops_snap = self.ops
        mval = {}
        for e in ENGS:
            c = self.cnt[e]
            for o in self.ops[e]:
                if not o.dma and o.signal:
                    c += 1
                    o.val = c
            mval[e] = c + 1
            self.cnt[e] = c + (0 if e == "pe" else 1)
        mk = self.mk

        def semof(o):
            if o.slot == -2:
                return self.ccsem
            if o.dma:
                return self.ring[o.eng][o.slot]
            return self.esem[o.eng]

        def run(e, eng):
            waited = self.waited[e]

            def wait(s, v):
                if waited.get(id(s), 0) < v:
                    eng.wait_ge(s, v)
                    waited[id(s)] = v

            last = {}
            for o in ops_snap[e]:
                for d in o.deps:
                    wait(semof(d), d.val)
                if o.slot == -2:
                    o.fn(eng).then_inc(self.ccsem, 1)
                    wait(self.ccsem, o.val)
                elif o.dma:
                    s = self.ring[e][o.slot]
                    if o.val > 16:
                        wait(s, o.val - 16)
                    o.fn(eng).then_inc(s, 16)
                    last[o.slot] = o
                else:
                    ins = o.fn(eng)
                    if o.signal:
                        ins.then_inc(self.esem[e], 1)
            for sl, o in last.items():
                wait(self.ring[e][sl], o.val)
            if e == "dve":
                m = eng.memset(mk[:, 0:1], 0.0)
            elif e == "pool":
                m = eng.memset(mk[:, 1:2], 0.0)
            elif e == "act":
                m = eng.memzero(mk[:, 2:3])
            elif e == "sp":
                m = eng.nop()
            else:
                m = None
            if m is not None:
                m.then_inc(self.esem[e], 1)
            for e2 in ENGS:
                if e2 != "pe":
                    wait(self.esem[e2], mval[e2])

        self.pending = getattr(self, 'pending', [])
        self.pending.append(run)

        self.epoch += 1
        self.ops = {e: [] for e in ENGS}


def finish(P):
    nc = P.nc
    with nc.Block() as block:
        @block.tensor
        def _(eng):
            for r in P.pending:
                r("pe", eng)

        @block.scalar
        def _(eng):
            for r in P.pending:
                r("act", eng)

        @block.vector
        def _(eng):
            for r in P.pending:
                r("dve", eng)

        @block.gpsimd
        def _(eng):
            for r in P.pending:
                r("pool", eng)

        @block.sync
        def _(eng):
            for r in P.pending:
                r("sp", eng)


class Tl:
    def __init__(self, P, name, shape, dt, psum=False):
        self.t = (P.psum if psum else P.sbuf)(name, shape, dt)
        self.b = Buf()


class Rot:
    def __init__(self, P, name, n, shape, dt):
        self.ts = [Tl(P, "%s%d" % (name, i), shape, dt) for i in range(n)]
        self.i = 0

    def next(self):
        t = self.ts[self.i % len(self.ts)]
        self.i += 1
        return t


class Cfg:
    def __init__(self, D=4096, SEQ=8192):
        self.D = D
        self.SEQ = SEQ
        self.KD = D // 128
        self.LT = SEQ // 4
        self.CT = 256
        self.NTOK = self.CT + self.LT
        self.MG = 10240
        self.INW = 10240 + 3 * D
        self.NA = SEQ // 128
        self.MC = 3 * D // 4
        self.TT = min(512, self.LT)


Q_OFF, K_OFF, V_OFF, AG_OFF, F_OFF, FG_OFF, CA_OFF, CB_OFF, CG_OFF = 0, 2048, 2560, 3072, 5120, 6144, 7168, 8192, 9216
VBLK = V_OFF // 512


def wchunk(K, N):
    n = K // 4
    for rc in range(n, 0, -1):
        if n % rc == 0 and rc * N * 2 <= (1 << 20):
            return rc


def gshape(nm, K, N):
    if nm == "wpw":
        return K, N
    return N // 4, 4 * K


def fam_func(blk):
    c = blk * 512
    if AG_OFF <= c < F_OFF or FG_OFF <= c < CA_OFF or CG_OFF <= c < 10240:
        return AF.Silu
    if CB_OFF <= c < CG_OFF or c >= 10240:
        return AF.Sigmoid
    return AF.Copy


def build(cfg, debug=False):
    D, KD, LT, CT, NTOK, INW, NA, MC, SEQ = cfg.D, cfg.KD, cfg.LT, cfg.CT, cfg.NTOK, cfg.INW, cfg.NA, cfg.MC, cfg.SEQ
    MG = cfg.MG
    NB = LT // 128
    nc = bass.Bass("TRN2", target_bir_lowering=False)

    def din(name, shape, dt=F32):
        return nc.dram_tensor(name, list(shape), dt, kind="ExternalInput").ap()

    def dscr(name, shape, dt=BF16, dbg=False):
        kind = "ExternalOutput" if (dbg and debug and dt == F32) else "Internal"
        return nc.dram_tensor(name, list(shape), dt, kind=kind).ap()

    x_in = din("x", [LT, D])
    ctx_in = din("ctx", [CT, D])
    cst_in = din("cst", [128, 1040])
    rope_in = din("rope", [2, 128, LT])
    dftB_in = din("dftB", [2, 128, LT])
    dftA_in = din("dftA", [2, NA, LT])
    dftctx_in = din("dftctx", [2, 2, 128, 256])
    dftc_in = din("dftc", [128, 2, 2, 256])
    cvT_in = din("cvT", [128, KD, 2])
    ngT_in = din("ngT", [2, 128, KD])
    qkg_in = din("qkg", [2, 128, 2])
    qkrow_in = din("qkrow", [2, 1, 256])
    sink_in = din("sink", [2, 1, 16])
    wfm_in = din("wfm", [2, 4, 256, 256])
    dww_in = din("dww", [2, 128, 8, 31])
    cv3_in = din("cv3", [2, 128, 8, 3])
    w_in_s = din("w_in", [2, INW // 16, 4 * D])
    w_au_s = din("w_au", [2, D // 16, 4 * 2048])
    w_fu_s = din("w_fu", [2, D // 16, 4 * 1024])
    w_cu_s = din("w_cu", [2, D // 16, 4 * 1024])
    w_o_s = din("w_o", [2, D // 16, 4 * D])
    w_pw_s = din("w_pw", [2, 256, 1024])
    w_mod_s = din("w_mod", [2, D, MC])
    b_mod_s = din("b_mod", [2, 2, MC])
    out = nc.dram_tensor("out", [LT, D], F32, kind="ExternalOutput").ap()

    wspec = [("win", D, INW, w_in_s), ("wau", 2048, D, w_au_s), ("wfu", 1024, D, w_fu_s),
             ("wcu", 1024, D, w_cu_s), ("wo", D, D, w_o_s), ("wpw", 1024, 1024, w_pw_s)]
    Wg = {}
    Wgin = {}
    for nm, K, N, _ in wspec:
        R_g, C_g = gshape(nm, K, N)
        for l in range(2):
            Wgin[(nm, l)] = dscr("gi_%s%d" % (nm, l), [R_g // 4, C_g])
            Wg[(nm, l)] = dscr("g_%s%d" % (nm, l), [R_g, C_g])
    pT = dscr("pT", [INW, NTOK], dbg=True)
    Vtm = dscr("Vtm", [NTOK, 512], dbg=True)
    kTn = dscr("kTn", [512, NTOK], dbg=True)
    gK_in = dscr("gK_in", [512, 256]); gK_out = dscr("gK_out", [4 * 512, 256])
    gV_in = dscr("gV_in", [256, 512]); gV_out = dscr("gV_out", [4 * 256, 512])
    gU_in = dscr("gU_in", [1024, 32]); gU_out = dscr("gU_out", [4 * 1024, 32])
    gF_in = dscr("gF_in", [LT, 2048]); gF_out = dscr("gF_out", [SEQ, 2048])
    gFc = dscr("gFc", [CT, 2048])
    tabC = dscr("tabC", [NA, 128, LT]); tabS = dscr("tabS", [NA, 128, LT])
    aT = dscr("aT", [2048, NTOK], dbg=True)
    fT = dscr("fT", [1024, NTOK], dbg=True)
    cT = dscr("cT", [1024, NTOK], dbg=True)
    mTd = dscr("mTd", [D, NTOK], dbg=True)
    x1 = dscr("x1", [NTOK, D], F32, dbg=True)
    gmod_in = dscr("gmod_in", [4, MC], F32); gmod_out = dscr("gmod_out", [16, MC], F32)
    gtb_d = dscr("gtb_d", [4, 128, D], F32)
    RG = [[0, 1, 2, 3], [4, 5, 6, 7]]
    RG8 = [list(range(8))]

    import os as _os2
    KSUB = int(_os2.environ.get('KSUB', '99'))
    P = Prog(nc)
    op, dma = P.op, P.dma

    def MM(o, lt, rh, start, stop, R, W):
        op("pe", lambda e: e.matmul(o, lhsT=lt, rhs=rh, start=start, stop=stop), reads=R, writes=W, partial=not start)

    def ACT(o, i, func, R, W, bias=None, scale=None, accum=None, partial=False):
        kw = {}
        if bias is not None:
            kw["bias"] = bias
        if scale is not None:
            kw["scale"] = scale
        if accum is not None:
            kw["accum_out"] = accum
        op("act", lambda e: e.activation(out=o, in_=i, func=func, **kw), reads=R, writes=W, partial=partial)

    def TT(eng, o, a, b, aop, R, W, partial=False):
        op(eng, lambda e: e.tensor_tensor(out=o, in0=a, in1=b, op=aop), reads=R, writes=W, partial=partial)

    def TS(eng, o, a, s1, s2, op0, op1, R, W, partial=False):
        if s2 is None:
            op(eng, lambda e: e.tensor_scalar(out=o, in0=a, scalar1=s1, scalar2=None, op0=op0), reads=R, writes=W, partial=partial)
        else:
            op(eng, lambda e: e.tensor_scalar(out=o, in0=a, scalar1=s1, scalar2=s2, op0=op0, op1=op1), reads=R, writes=W, partial=partial)

    def STT(eng, o, a, s, b, op0, op1, R, W, partial=False):
        op(eng, lambda e: e.scalar_tensor_tensor(out=o, in0=a, scalar=s, in1=b, op0=op0, op1=op1), reads=R, writes=W, partial=partial)

    def CP(eng, o, i, R, W, partial=False):
        op(eng, lambda e: e.tensor_copy(out=o, in_=i), reads=R, writes=W, partial=partial)

    def RECIP(o, i, R, W):
        op("dve", lambda e: e.reciprocal(out=o, in_=i), reads=R, writes=W)

    def DMA(o, i, R=(), W=(), q="sp", partial=False):
        if q == "sp" and str(o.space).endswith("DRAM") and not str(i.space).endswith("DRAM"):
            q = "act"
        dma(q, lambda e: e.dma_start(out=o, in_=i), reads=R, writes=W, partial=partial)

    def CC(i, o, groups, R, W):
        P.cc(lambda e: e.collective_compute("AllGather", ALU.bypass, replica_groups=groups, ins=[i], outs=[o]), reads=R, writes=W)

    cst = Tl(P, "cst", [128, 1040], F32)
    ident = cst.t[:, 0:128]
    rotT = cst.t[:, 128:256]
    selc = lambda i: cst.t[:, 512 + i:513 + i]
    selm = cst.t[0:4, 528:1040]
    ones32 = Tl(P, "ones32", [128, 128], F32)
    onesbf = Tl(P, "onesbf", [128, 128], BF16)
    identbf = Tl(P, "identbf", [128, 128], BF16)
    mskbf = Tl(P, "mskbf", [128, 4, 128], BF16)
    gsT = Tl(P, "gsT", [128, 4, KD], F32)
    shT = Tl(P, "shT", [128, 4, KD], F32)
    qkg = Tl(P, "qkg", [128, 2, 2], F32)
    negB = Tl(P, "negB", [128, 2], F32)
    sinkrow = Tl(P, "sinkrow", [1, 2, 2048], BF16)
    dftc = Tl(P, "dftc", [128, 2, 2, 256], F32)
    PS = [Tl(P, "ps%d" % i, [128, 512], F32, psum=True) for i in range(8)]
    psi = [0]

    def psn(lo=0, hi=8):
        t = PS[lo + psi[0] % (hi - lo)]
        psi[0] += 1
        return t

    with P.phase():
        DMA(cst.t[:], cst_in, W=[cst.b])
        DMA(qkg.t[:], qkg_in.rearrange("l p k -> p l k"), W=[qkg.b])
        DMA(dftc.t[:], dftc_in, W=[dftc.b])
        op("dve", lambda e: e.memset(ones32.t[:], 1.0), writes=[ones32.b])
        op("pool", lambda e: e.memset(onesbf.t[:], 1.0), writes=[onesbf.b])
        CP("dve", identbf.t[:], ident, [cst.b], [identbf.b])
        CP("dve", mskbf.t[:, 0, :], cst.t[:, 256:384], [cst.b], [mskbf.b], partial=True)
        CP("dve", mskbf.t[:, 1, :], cst.t[:, 384:512], [cst.b], [mskbf.b], partial=True)
        TS("dve", mskbf.t[:, 2, :], cst.t[:, 256:384], selc(8), None, ALU.mult, None, [cst.b], [mskbf.b], partial=True)
        TS("dve", mskbf.t[:, 3, :], cst.t[:, 384:512], selc(9), None, ALU.mult, None, [cst.b], [mskbf.b], partial=True)
        qkrow = Tl(P, "qkrow", [1, 2, 256], F32)
        sk = Tl(P, "sk", [1, 2, 16], F32)
        mx = Tl(P, "mx", [1, 8], F32)
        DMA(qkrow.t[:], qkrow_in.rearrange("l o k -> o l k"), W=[qkrow.b])
        DMA(sk.t[:], sink_in.rearrange("l o k -> o l k"), W=[sk.b])
        for l in range(2):
            for j in range(2):
                op("dve", lambda e, l=l, j=j: e.reduce_max(out=mx.t[0:1, 2 * l + j:2 * l + j + 1], in_=qkrow.t[0:1, l, j * 128:(j + 1) * 128],
                                                         axis=AX.X, apply_absolute_value=True), reads=[qkrow.b], writes=[mx.b], partial=True)
            TT("dve", mx.t[0:1, 4 + l:5 + l], mx.t[0:1, 2 * l:2 * l + 1], mx.t[0:1, 2 * l + 1:2 * l + 2], ALU.mult, [mx.b], [mx.b], partial=True)
            pb = psn()
            MM(pb.t[:, 0:1], ones32.t[0:1, :], mx.t[0:1, 4 + l:5 + l], True, True, [ones32.b, mx.b], [pb.b])
            ACT(negB.t[:, l:l + 1], pb.t[:, 0:1], AF.Copy, [pb.b], [negB.b], scale=-(128.0 ** 0.5), partial=True)
            es = Tl(P, "es%d" % l, [1, 16], F32)
            ACT(es.t[:], sk.t[0:1, l, :], AF.Exp, [sk.b, negB.b], [es.b], bias=negB.t[0:1, l:l + 1])
            for h in range(16):
                TS("dve", sinkrow.t[0:1, l, h * 128:(h + 1) * 128], ones32.t[0:1, :], es.t[0:1, h:h + 1], None, ALU.mult, None,
                   [ones32.b, es.b], [sinkrow.b], partial=True)

    with P.phase():
        gb = Buf()
        for nm, K, N, src in wspec:
            for l in range(2):
                DMA(Wgin[(nm, l)], src[l], W=[gb], q="pool", partial=True)
    chunkbuf = {}

    def gather_list():
        order = [("win", 0)] + [(nm, 0) for nm in ("wpw", "wau", "wfu", "wcu", "wo")] + [(nm, 1) for nm in ("win", "wpw", "wau", "wfu", "wcu", "wo")]
        dims = {nm: gshape(nm, K, N) for nm, K, N, _ in wspec}
        out_ = []
        for nm, l in order:
            R_g, C_g = dims[nm]
            rc = wchunk(R_g, C_g)
            for c in range((R_g // 4) // rc):
                out_.append((nm, l, c, rc))
        return out_

    glist = gather_list()
    g_l0 = [g for g in glist if g[1] == 0]
    g_l1 = [g for g in glist if g[1] == 1]
    n1 = len([g for g in g_l1 if g[0] == "win"]) * 10 // 11

    def record_gathers(items):
        for nm, l, c, rc in items:
            b = Buf()
            chunkbuf[(nm, l, c)] = b
            CC(Wgin[(nm, l)][c * rc:(c + 1) * rc, :], Wg[(nm, l)][c * 4 * rc:(c + 1) * 4 * rc, :], RG, [], [b])

    def wbufs(nm, l, r0, r1):
        K_, N_ = [(K, N) for n_, K, N, _ in wspec if n_ == nm][0]
        R_g, C_g = gshape(nm, K_, N_)
        rc4 = 4 * wchunk(R_g, C_g)
        return [chunkbuf[(nm, l, c)] for c in range(r0 // rc4, (r1 - 1) // rc4 + 1)]

    with P.phase():
        cvT = Tl(P, "cvT", [128, KD, 2], F32)
        scT = Tl(P, "scT", [128, KD, 2], F32)
        bm = Tl(P, "bm", [2, 2, MC], F32)
        mrow = Tl(P, "mrow", [2, 2, MC], F32)
        DMA(cvT.t[:], cvT_in, W=[cvT.b])
        DMA(bm.t[:], b_mod_s.rearrange("l q m -> q l m"), W=[bm.b])
        ACT(scT.t[:], cvT.t[:], AF.Silu, [cvT.b], [scT.b])
        wmr = Rot(P, "wm", 3, [128, MC], F32)
        nch = [(o, min(512, MC - o)) for o in range(0, MC, 512)]
        gmb = Buf()
        for l in range(2):
            pss = [PS[i] for i in range(len(nch))]
            for kc in range(KD):
                wm = wmr.next()
                DMA(wm.t[:], w_mod_s[l, kc * 128:(kc + 1) * 128, :], W=[wm.b])
                for i, (o, n) in enumerate(nch):
                    MM(pss[i].t[0:2, 0:n], scT.t[:, kc, :], wm.t[:, o:o + n], kc == 0, kc == KD - 1, [scT.b, wm.b], [pss[i].b])
            for i, (o, n) in enumerate(nch):
                TT("dve", mrow.t[0:2, l, o:o + n], pss[i].t[0:2, 0:n], bm.t[0:2, l, o:o + n], ALU.add, [pss[i].b, bm.b], [mrow.b], partial=True)
            DMA(gmod_in[2 * l:2 * l + 2, :], mrow.t[0:2, l, :], R=[mrow.b], W=[gmb], partial=True)
        gob = Buf()
        if KSUB >= 1:
            CC(gmod_in, gmod_out, RG, [gmb], [gob])
        R_ = Tl(P, "Rr", [4, 3 * D], F32)
        if KSUB >= 2:
          DMA(R_.t[:].rearrange("q (r j) -> q r j", r=4), gmod_out.rearrange("(r q) j -> q r j", q=4), R=[gob], W=[R_.b])
        modT = Tl(P, "modT", [128, 2 * KD, 4], F32)
        for c0 in (range(0, 2 * KD, 64) if KSUB >= 3 else []):
            pm = psn()
            n = min(64, 2 * KD - c0)
            for c in range(n):
                op("pe", lambda e, c=c, c0=c0, pm=pm: e.transpose(pm.t[:, c * 4:c * 4 + 4], R_.t[0:4, (c0 + c) * 128:(c0 + c + 1) * 128], cst.t[0:4, 0:4]),
                   reads=[R_.b, cst.b], writes=[pm.b], partial=(c > 0))
            CP("dve", modT.t[:, c0:c0 + n, :], pm.t[:, 0:4 * n].rearrange("p (c q) -> p c q", q=4), [pm.b], [modT.b], partial=True)
        ngT = Tl(P, "ngT", [128, 2, KD], F32)
        DMA(ngT.t[:], ngT_in.rearrange("l p k -> p l k"), W=[ngT.b])
        for q in (range(4) if KSUB >= 4 else []):
            STT("dve", gsT.t[:, q, :], modT.t[:, KD:2 * KD, q], 1.0, ngT.t[:, q // 2, :], ALU.add, ALU.mult, [modT.b, ngT.b], [gsT.b], partial=True)
            CP("dve", shT.t[:, q, :], modT.t[:, 0:KD, q], [modT.b], [shT.b], partial=True)
        gst = Rot(P, "gst", 2, [128, 512], F32)
        for q in (range(4) if KSUB >= 5 else []):
            for nb in range(D // 512):
                pg = psn()
                MM(pg.t[:, 0:512], selm[:, q * 128:(q + 1) * 128], R_.t[0:4, 2 * D + nb * 512:2 * D + (nb + 1) * 512], True, True, [cst.b, R_.b], [pg.b])
                g = gst.next()
                ACT(g.t[:], pg.t[:, 0:512], AF.Copy, [pg.b], [g.b])
                DMA(gtb_d[q, :, nb * 512:(nb + 1) * 512], g.t[:], R=[g.b])

    with P.phase():
        CB = Tl(P, "CB", [128, LT], F32)
        SB = Tl(P, "SB", [128, LT], F32)
        DMA(CB.t[:], dftB_in[0], W=[CB.b])
        DMA(SB.t[:], dftB_in[1], W=[SB.b])
        car = Rot(P, "ca", 2, [128, LT], F32)
        sar = Rot(P, "sa", 2, [128, LT], F32)
        t1r = Rot(P, "t1", 2, [128, LT], F32)
        t2r = Rot(P, "t2", 2, [128, LT], F32)
        ocr = Rot(P, "oc", 2, [128, LT], BF16)
        osr = Rot(P, "os", 2, [128, LT], BF16)
        for a in range(NA):
            ca, sa, t1, t2, oc, os_ = car.next(), sar.next(), t1r.next(), t2r.next(), ocr.next(), osr.next()
            DMA(ca.t[:], dftA_in[0, a:a + 1, :].partition_broadcast(128), W=[ca.b])
            DMA(sa.t[:], dftA_in[1, a:a + 1, :].partition_broadcast(128), W=[sa.b])
            TT("dve", t1.t[:], ca.t[:], CB.t[:], ALU.mult, [ca.b, CB.b], [t1.b])
            TT("pool", t2.t[:], sa.t[:], SB.t[:], ALU.mult, [sa.b, SB.b], [t2.b])
            TT("dve", oc.t[:], t1.t[:], t2.t[:], ALU.subtract, [t1.b, t2.b], [oc.b])
            DMA(tabC[a], oc.t[:], R=[oc.b])
            TT("pool", t2.t[:], sa.t[:], CB.t[:], ALU.mult, [sa.b, CB.b], [t2.b])
            TT("dve", t1.t[:], ca.t[:], SB.t[:], ALU.mult, [ca.b, SB.b], [t1.b])
            STT("dve", os_.t[:], t2.t[:], -1.0, t1.t[:], ALU.mult, ALU.subtract, [t1.b, t2.b], [os_.b])
            DMA(tabS[a], os_.t[:], R=[os_.b])

    def src_rows(l, tok0, n):
        if l == 1:
            return x1[tok0:tok0 + n, :]
        if tok0 < CT:
            return ctx_in[tok0:tok0 + n, :]
        return x_in[tok0 - CT:tok0 - CT + n, :]

    def tiles():
        ts = [(0, CT, 1)]
        for t0 in range(0, LT, cfg.TT):
            ts.append((CT + t0, cfg.TT, 0))
        return ts

    def normrope(l, which, src, srcb, T, cos, sin, ropeb, outap, outb, tmp):
        sq, rs, kn, t1, t2 = [r_.next() for r_ in tmp]
        ACT(sq.t[:, 0:T], src, AF.Square, [srcb], [sq.b])
        p1 = psn(0, 4)
        MM(p1.t[:, 0:T], ones32.t[:], sq.t[:, 0:T], True, True, [ones32.b, sq.b], [p1.b])
        ACT(rs.t[:, 0:T], p1.t[:, 0:T], AF.Sqrt, [p1.b], [rs.b], bias=EPS, scale=1.0 / 128)
        RECIP(rs.t[:, 0:T], rs.t[:, 0:T], [rs.b], [rs.b])
        STT("dve", kn.t[:, 0:T], src, qkg.t[:, l, which:which + 1], rs.t[:, 0:T], ALU.mult, ALU.mult, [srcb, qkg.b, rs.b], [kn.b])
        if cos is None:
            CP("pool", outap, kn.t[:, 0:T], [kn.b], [outb], partial=True)
            return
        p2 = psn(0, 4)
        MM(p2.t[:, 0:T], rotT, kn.t[:, 0:T], True, True, [cst.b, kn.b], [p2.b])
        TT("pool", t1.t[:, 0:T], kn.t[:, 0:T], cos, ALU.mult, [kn.b, ropeb], [t1.b])
        TT("dve", t2.t[:, 0:T], p2.t[:, 0:T], sin, ALU.mult, [p2.b, ropeb], [t2.b])
        TT("pool", outap, t1.t[:, 0:T], t2.t[:, 0:T], ALU.add, [t1.b, t2.b], [outb], partial=True)

    for l in range(2):
        W = lambda nm: Wg[(nm, l)]
        with P.phase():
            xr = Rot(P, "xt", 2, [128, D], F32)
            junk = Tl(P, "junk", [128, D], BF16)
            ssr = Rot(P, "ss", 2, [128, 2], F32)
            hT = Tl(P, "hT", [128, KD, 512], BF16)
            wr = Rot(P, "wblk", 2, [128, KD, 512], BF16)
            stg = Rot(P, "stg", 3, [128, 512], BF16)
            win = W("win")
            if l == 0:
                record_gathers(g_l0)
            for (tok0, T, isctx) in tiles():
                q = 2 * l + isctx
                for s in range(T // 128):
                    xt = xr.next()
                    ss = ssr.next()
                    DMA(xt.t[:], src_rows(l, tok0 + s * 128, 128), W=[xt.b])
                    op("dve", lambda e, ss=ss: e.memset(ss.t[:], 0.0), writes=[ss.b])
                    ACT(junk.t[:], xt.t[:], AF.Square, [xt.b, ss.b], [junk.b, ss.b], accum=ss.t[:, 0:1])
                    ACT(ss.t[:, 1:2], ss.t[:, 0:1], AF.Sqrt, [ss.b], [ss.b], bias=EPS, scale=1.0 / D)
                    RECIP(ss.t[:, 1:2], ss.t[:, 1:2], [ss.b], [ss.b])
                    TS("dve", xt.t[:], xt.t[:], ss.t[:, 1:2], None, ALU.mult, None, [xt.b, ss.b], [xt.b])
                    for k0 in range(0, KD, 4):
                        pt = psn()
                        for j in range(4):
                            kc = k0 + j
                            op("pe", lambda e, pt=pt, j=j, kc=kc, xt=xt: e.transpose(pt.t[:, j * 128:(j + 1) * 128], xt.t[:, kc * 128:(kc + 1) * 128], ident),
                               reads=[xt.b, cst.b], writes=[pt.b], partial=(j > 0))
                        for j in range(4):
                            kc = k0 + j
                            if j % 2 == 0:
                                TS("dve", hT.t[:, kc, s * 128:(s + 1) * 128], pt.t[:, j * 128:(j + 1) * 128], gsT.t[:, q, kc:kc + 1], shT.t[:, q, kc:kc + 1],
                                   ALU.mult, ALU.add, [pt.b, gsT.b, shT.b], [hT.b], partial=True)
                            else:
                                ACT(hT.t[:, kc, s * 128:(s + 1) * 128], pt.t[:, j * 128:(j + 1) * 128], AF.Identity, [pt.b, gsT.b, shT.b], [hT.b],
                                    bias=shT.t[:, q, kc:kc + 1], scale=gsT.t[:, q, kc:kc + 1], partial=True)
                blks = list(range(INW // 512))
                if l == 1 and isctx:
                    blks = [K_OFF // 512, VBLK]
                for blk in blks:
                    wt = wr.next()
                    DMA(wt.t[:].rearrange("p k n -> p (k n)"), win[blk * 128:(blk + 1) * 128, :], R=wbufs("win", l, blk * 128, (blk + 1) * 128), W=[wt.b])
                    if blk == VBLK:
                        for s in range(T // 128):
                            pv = psn()
                            for kc in range(KD):
                                MM(pv.t[:, 0:512], hT.t[:, kc, s * 128:(s + 1) * 128], wt.t[:, kc, :], kc == 0, kc == KD - 1, [hT.b, wt.b], [pv.b])
                            sg = stg.next()
                            ACT(sg.t[:], pv.t[:, 0:512], AF.Copy, [pv.b], [sg.b])
                            DMA(Vtm[tok0 + s * 128:tok0 + (s + 1) * 128, :], sg.t[:], R=[sg.b])
                    else:
                        fn = fam_func(blk)
                        for c in range(4):
                            pv = psn()
                            for kc in range(KD):
                                MM(pv.t[:, 0:T], wt.t[:, kc, c * 128:(c + 1) * 128], hT.t[:, kc, 0:T], kc == 0, kc == KD - 1, [hT.b, wt.b], [pv.b])
                            sg = stg.next()
                            ACT(sg.t[:, 0:T], pv.t[:, 0:T], fn, [pv.b], [sg.b])
                            r0 = (blk * 4 + c) * 128
                            DMA(pT[r0:r0 + 128, tok0:tok0 + T], sg.t[:, 0:T], R=[sg.b])

        with P.phase():
            rope = Tl(P, "rope", [128, 2, LT], F32)
            DMA(rope.t[:], rope_in.rearrange("c p t -> p c t"), W=[rope.b])
            kr = Rot(P, "kraw", 2, [128, 512], BF16)
            ko = Rot(P, "kout", 2, [128, 512], BF16)
            tmp = [Rot(P, "nr%d_" % i, 2, [128, 512], F32) for i in range(5)]
            gkb = Buf()
            for (tok0, T, isctx) in tiles():
                for g in range(4):
                    k = kr.next()
                    o = ko.next()
                    r0 = K_OFF + g * 128
                    DMA(k.t[:, 0:T], pT[r0:r0 + 128, tok0:tok0 + T], W=[k.b])
                    if isctx:
                        normrope(l, 1, k.t[:, 0:T], k.b, T, None, None, None, o.t[:, 0:T], o.b, tmp)
                    else:
                        t0 = tok0 - CT
                        normrope(l, 1, k.t[:, 0:T], k.b, T, rope.t[:, 0, t0:t0 + T], rope.t[:, 1, t0:t0 + T], rope.b, o.t[:, 0:T], o.b, tmp)
                    DMA(kTn[g * 128:(g + 1) * 128, tok0:tok0 + T], o.t[:, 0:T], R=[o.b])
                    if tok0 == CT:
                        DMA(gK_in[g * 128:(g + 1) * 128, 0:128], o.t[:, 0:128], R=[o.b], W=[gkb], partial=True)
                    if tok0 + T == NTOK:
                        DMA(gK_in[g * 128:(g + 1) * 128, 128:256], o.t[:, T - 128:T], R=[o.b], W=[gkb], partial=True)
            DMA(gV_in[0:128, :], Vtm[CT:CT + 128, :], W=[gkb], partial=True)
            DMA(gV_in[128:256, :], Vtm[NTOK - 128:NTOK, :], W=[gkb], partial=True)
            CC(gK_in, gK_out, RG, [gkb], [])
            CC(gV_in, gV_out, RG, [gkb], [])

        with P.phase():
            rope = Tl(P, "rope", [128, 2, LT], F32)
            DMA(rope.t[:], rope_in.rearrange("c p t -> p c t"), W=[rope.b])
            kTa = Tl(P, "kTa", [128, 4, LT + 256], BF16)
            kTc = Tl(P, "kTc", [128, 4, CT], BF16)
            Va = Tl(P, "Va", [128, NB + 4, 512], BF16)
            ek = Tl(P, "ek", [128, 4, 4, 256], BF16)
            ev = Tl(P, "ev", [128, 4, 2, 512], BF16)
            DMA(kTa.t[:, :, 128:128 + LT], kTn[:, CT:NTOK].rearrange("(g p) t -> p g t", p=128), W=[kTa.b])
            DMA(kTc.t[:], kTn[:, 0:CT].rearrange("(g p) t -> p g t", p=128), W=[kTc.b])
            DMA(Va.t[:, 1:NB + 1, :], Vtm[CT:NTOK, :].rearrange("(b p) n -> p b n", p=128), W=[Va.b])
            DMA(Va.t[:, NB + 2:NB + 4, :], Vtm[0:CT, :].rearrange("(b p) n -> p b n", p=128), W=[Va.b], partial=True)
            DMA(ek.t[:], gK_out.rearrange("(r g p) c -> p r g c", r=4, g=4), W=[ek.b])
            DMA(ev.t[:], gV_out.rearrange("(r e p) n -> p r e n", r=4, e=2), W=[ev.b])
            for side, dst_k, dst_v, ecol, erow, s0 in ((0, kTa.t[:, :, 0:128], Va.t[:, 0, :], slice(128, 256), 1, 0),
                                                       (1, kTa.t[:, :, 128 + LT:256 + LT], Va.t[:, NB + 1, :], slice(0, 128), 0, 4)):
                TS("dve", dst_k, ek.t[:, 0, :, ecol], selc(s0), None, ALU.mult, None, [ek.b, cst.b], [kTa.b], partial=True)
                TS("pool", dst_v, ev.t[:, 0, erow, :], selc(s0), None, ALU.mult, None, [ev.b, cst.b], [Va.b], partial=True)
                for r in range(1, 4):
                    STT("dve", dst_k, ek.t[:, r, :, ecol], selc(s0 + r), dst_k, ALU.mult, ALU.add, [ek.b, cst.b, kTa.b], [kTa.b], partial=True)
                    STT("dve", dst_v, ev.t[:, r, erow, :], selc(s0 + r), dst_v, ALU.mult, ALU.add, [ev.b, cst.b, Va.b], [Va.b], partial=True)
            qraw = Tl(P, "qraw", [128, 16, 512], BF16)
            qTn = qraw
            agT = Tl(P, "agT", [128, 16, 512], BF16)
            ogT = Tl(P, "ogT", [128, 16, 512], BF16)
            tmp = [Rot(P, "nr%d_" % i, 2, [128, 512], F32) for i in range(5)]
            ptr = Rot(P, "PT", 3, [128, 512], BF16)
            rdr = Rot(P, "rden", 2, [128, 512], F32)
            onr = Rot(P, "on", 2, [128, 512], F32)
            scale = 128.0 ** -0.5
            for (tok0, T, isctx) in tiles():
                if isctx and l == 1:
                    continue
                DMA(qraw.t[:, :, 0:T], pT[0:2048, tok0:tok0 + T].rearrange("(h p) t -> p h t", p=128), W=[qraw.b])
                DMA(agT.t[:, :, 0:T], pT[AG_OFF:AG_OFF + 2048, tok0:tok0 + T].rearrange("(h p) t -> p h t", p=128), W=[agT.b])
                for h in range(16):
                    if isctx:
                        normrope(l, 0, qraw.t[:, h, 0:T], qraw.b, T, None, None, None, qTn.t[:, h, 0:T], qTn.b, tmp)
                    else:
                        t0 = tok0 - CT
                        normrope(l, 0, qraw.t[:, h, 0:T], qraw.b, T, rope.t[:, 0, t0:t0 + T], rope.t[:, 1, t0:t0 + T], rope.b, qTn.t[:, h, 0:T], qTn.b, tmp)
                for blk in range(T // 128):
                    qs = slice(blk * 128, (blk + 1) * 128)
                    for g in range(4):
                        chunks = []
                        if not isctx:
                            nb = (tok0 - CT) // 128 + blk
                            for d_, mi in ((0, 0), (1, None), (2, 1)):
                                m = mi
                                if d_ == 0 and nb == 0:
                                    m = 2
                                if d_ == 2 and nb == NB - 1:
                                    m = 3
                                cb = nb + d_
                                chunks.append((kTa.t[:, g, cb * 128:(cb + 1) * 128], kTa.b, Va.t[:, cb, g * 128:(g + 1) * 128], m))
                        for cc_ in range(2):
                            chunks.append((kTc.t[:, g, cc_ * 128:(cc_ + 1) * 128], kTc.b, Va.t[:, NB + 2 + cc_, g * 128:(g + 1) * 128], None))
                        pO = PS[4 + (psi[0] % 2)]
                        pD = PS[6 + (psi[0] % 2)]
                        psi[0] += 1
                        def fin(ci, pS, vap, m, last):
                            pt_ = ptr.next()
                            ACT(pt_.t[:], pS.t[:, 0:512], AF.Exp, [pS.b, negB.b], [pt_.b], bias=negB.t[:, l:l + 1], scale=scale)
                            if m is not None:
                                p3 = pt_.t[:].rearrange("p (h q) -> p h q", h=4)
                                TT("dve", p3, p3, mskbf.t[:, m, :].unsqueeze(1).to_broadcast([128, 4, 128]), ALU.mult, [pt_.b, mskbf.b], [pt_.b])
                            MM(pO.t[:, 0:512], vap, pt_.t[:], ci == 0, last, [Va.b, pt_.b], [pO.b])
                            MM(pD.t[:, 0:512], onesbf.t[:], pt_.t[:], ci == 0, False, [onesbf.b, pt_.b], [pD.b])

                        pend = None
                        for ci, (kap, kb_, vap, m) in enumerate(chunks):
                            pS = psn(0, 4)
                            MM(pS.t[:, 0:512].rearrange("p (h q) -> p h q", h=4), kap, qTn.t[:, 4 * g:4 * g + 4, qs], True, True, [kb_, qTn.b], [pS.b])
                            if pend is not None:
                                fin(*pend)
                            pend = (ci, pS, vap, m, ci == len(chunks) - 1)
                        fin(*pend)
                        MM(pD.t[:, 0:512], onesbf.t[0:1, :], sinkrow.t[0:1, l, g * 512:(g + 1) * 512], False, True, [onesbf.b, sinkrow.b], [pD.b])
                        rd = rdr.next()
                        on = onr.next()
                        RECIP(rd.t[:], pD.t[:, 0:512], [pD.b], [rd.b])
                        TT("dve", on.t[:], pO.t[:, 0:512], rd.t[:], ALU.mult, [pO.b, rd.b], [on.b])
                        TT("pool", ogT.t[:, 4 * g:4 * g + 4, qs], on.t[:].rearrange("p (h q) -> p h q", h=4), agT.t[:, 4 * g:4 * g + 4, qs], ALU.mult,
                           [on.b, agT.b], [ogT.b], partial=True)
                DMA(aT[:, tok0:tok0 + T].rearrange("(h p) t -> p h t", p=128), ogT.t[:, :, 0:T], R=[ogT.b])

        with P.phase():
            segs = [(CT, LT, 0)] + ([(0, CT, 1)] if l == 0 else [])
            wpw = Tl(P, "wpw", [128, 8, 1024], BF16)
            DMA(wpw.t[:], W("wpw").rearrange("(k p) n -> p k n", p=128), W=[wpw.b])
            dww = Tl(P, "dww", [128, 8, 31], F32)
            cv3 = Tl(P, "cv3", [128, 8, 3], F32)
            DMA(dww.t[:], dww_in[l], W=[dww.b])
            DMA(cv3.t[:], cv3_in[l], W=[cv3.b])
            ar = Rot(P, "cva", 1, [128, 8, 512], BF16)
            br = Rot(P, "cvb", 1, [128, 8, 512], BF16)
            cgr = Rot(P, "cvg", 1, [128, 8, 512], BF16)
            dgc = Rot(P, "dgc", 1, [128, 31, 128], BF16)
            ybuf = Tl(P, "ybuf", [128, 8, 512], F32)
            ysq = Rot(P, "ysq", 2, [128, 512], F32)
            st4 = [Tl(P, "st%d" % i, [128, 512], F32) for i in range(4)]
            tdr = Rot(P, "td", 2, [128, 512], F32)
            zT = Tl(P, "zT", [128, 8, 512], BF16)
            cst_ = Rot(P, "cstg", 1, [128, 8, 512], BF16)
            for (s0, SL, isctx) in segs:
                uT = Tl(P, "uT%d" % isctx, [128, 8, SL + 32], BF16)
                TTs = min(512, SL)
                op("pool", lambda e, uT=uT: e.memset(uT.t[:], 0.0), writes=[uT.b])
                for t0 in range(0, SL, TTs):
                    a_, b_ = ar.next(), br.next()
                    DMA(a_.t[:, :, 0:TTs], pT[CA_OFF:CA_OFF + 1024, s0 + t0:s0 + t0 + TTs].rearrange("(c p) t -> p c t", p=128), W=[a_.b])
                    DMA(b_.t[:, :, 0:TTs], pT[CB_OFF:CB_OFF + 1024, s0 + t0:s0 + t0 + TTs].rearrange("(c p) t -> p c t", p=128), W=[b_.b])
                    TT("dve", uT.t[:, :, 16 + t0:16 + t0 + TTs], a_.t[:, :, 0:TTs], b_.t[:, :, 0:TTs], ALU.mult, [a_.b, b_.b], [uT.b], partial=True)
                if not isctx:
                    gub = Buf()
                    gob2 = Buf()
                    DMA(gU_in[:, 0:16].rearrange("(c p) e -> p c e", p=128), uT.t[:, :, 16:32], R=[uT.b], W=[gub], partial=True)
                    DMA(gU_in[:, 16:32].rearrange("(c p) e -> p c e", p=128), uT.t[:, :, SL:SL + 16], R=[uT.b], W=[gub], partial=True)
                    CC(gU_in, gU_out, RG, [gub], [gob2])
                    eu = Tl(P, "eu", [128, 4, 8, 32], BF16)
                    DMA(eu.t[:], gU_out.rearrange("(r c p) e -> p r c e", r=4, c=8), R=[gob2], W=[eu.b])
                    for dst, ecol, sb in ((uT.t[:, :, 0:16], slice(16, 32), 0), (uT.t[:, :, 16 + SL:32 + SL], slice(0, 16), 4)):
                        TS("dve", dst, eu.t[:, 0, :, ecol], selc(sb), None, ALU.mult, None, [eu.b, cst.b], [uT.b], partial=True)
                        for r in range(1, 4):
                            STT("dve", dst, eu.t[:, r, :, ecol], selc(sb + r), dst, ALU.mult, ALU.add, [eu.b, cst.b, uT.b], [uT.b], partial=True)
                for t0 in range(0, SL, TTs):
                    T = TTs
                    cg = cgr.next()
                    DMA(cg.t[:, :, 0:T], pT[CG_OFF:CG_OFF + 1024, s0 + t0:s0 + t0 + T].rearrange("(c p) t -> p c t", p=128), W=[cg.b])
                    p1, p2 = PS[0], PS[1]
                    for c in range(8):
                        dg = dgc.next()
                        TT("pool", dg.t[:], identbf.t[:].unsqueeze(1).to_broadcast([128, 31, 128]),
                           dww.t[:, c, :].unsqueeze(2).to_broadcast([128, 31, 128]), ALU.mult, [identbf.b, dww.b], [dg.b])
                        pc = psn(2, 6)
                        for j in range(31):
                            MM(pc.t[:, 0:T], dg.t[:, j, :], uT.t[:, c, t0 + j + 1:t0 + j + 1 + T], j == 0, j == 30, [dg.b, uT.b], [pc.b])
                        ACT(ybuf.t[:, c, 0:T], pc.t[:, 0:T], AF.Identity, [pc.b, cv3.b], [ybuf.b], bias=cv3.t[:, c, 0:1], partial=True)
                        yq = ysq.next()
                        ACT(yq.t[:, 0:T], ybuf.t[:, c, 0:T], AF.Square, [ybuf.b], [yq.b])
                        MM(p1.t[:, 0:T], ones32.t[:], ybuf.t[:, c, 0:T], c == 0, c == 7, [ones32.b, ybuf.b], [p1.b])
                        MM(p2.t[:, 0:T], ones32.t[:], yq.t[:, 0:T], c == 0, c == 7, [ones32.b, yq.b], [p2.b])
                    mean, ex2, var, rin = st4
                    ACT(mean.t[:, 0:T], p1.t[:, 0:T], AF.Copy, [p1.b], [mean.b], scale=1.0 / 1024)
                    ACT(ex2.t[:, 0:T], p2.t[:, 0:T], AF.Copy, [p2.b], [ex2.b], scale=1.0 / 1024)
                    TT("dve", var.t[:, 0:T], mean.t[:, 0:T], mean.t[:, 0:T], ALU.mult, [mean.b], [var.b])
                    TT("dve", var.t[:, 0:T], ex2.t[:, 0:T], var.t[:, 0:T], ALU.subtract, [ex2.b, var.b], [var.b])
                    ACT(rin.t[:, 0:T], var.t[:, 0:T], AF.Sqrt, [var.b], [rin.b], bias=EPS)
                    RECIP(rin.t[:, 0:T], rin.t[:, 0:T], [rin.b], [rin.b])
                    for c in range(8):
                        td = tdr.next()
                        TT("dve", td.t[:, 0:T], ybuf.t[:, c, 0:T], mean.t[:, 0:T], ALU.subtract, [ybuf.b, mean.b], [td.b])
                        TT("pool", td.t[:, 0:T], td.t[:, 0:T], rin.t[:, 0:T], ALU.mult, [td.b, rin.b], [td.b])
                        ACT(zT.t[:, c, 0:T], td.t[:, 0:T], AF.Silu, [td.b, cv3.b], [zT.b], bias=cv3.t[:, c, 2:3], scale=cv3.t[:, c, 1:2], partial=True)
                    cs_ = cst_.next()
                    for co in range(8):
                        pw_ = psn(6, 8)
                        for ci in range(8):
                            MM(pw_.t[:, 0:T], wpw.t[:, ci, co * 128:(co + 1) * 128], zT.t[:, ci, 0:T], ci == 0, ci == 7, [wpw.b, zT.b], [pw_.b])
                        TT("dve", cs_.t[:, co, 0:T], pw_.t[:, 0:T], cg.t[:, co, 0:T], ALU.mult, [pw_.b, cg.b], [cs_.b], partial=True)
                    DMA(cT[:, s0 + t0:s0 + t0 + T].rearrange("(c p) t -> p c t", p=128), cs_.t[:, :, 0:T], R=[cs_.b])

        with P.phase():
            wfm = Tl(P, "wfm", [128, 8, 256], F32)
            DMA(wfm.t[:], wfm_in[l].rearrange("g (k p) d -> p (g k) d", p=128), W=[wfm.b])
            CW = Tl(P, "CW", [128, 4, 2, 512], BF16)
            for g in range(4):
                for cs in range(2):
                    for mch in range(2):
                        pc = psn()
                        for k in range(2):
                            MM(pc.t[:, 0:256], dftc.t[:, cs, k, mch * 128:(mch + 1) * 128], wfm.t[:, g * 2 + k, :], k == 0, k == 1, [dftc.b, wfm.b], [pc.b])
                        ACT(CW.t[:, g, mch, cs * 256:(cs + 1) * 256], pc.t[:, 0:256], AF.Copy, [pc.b], [CW.b], partial=True)
            ufr = Rot(P, "uF", 2, [128, 8, 512], BF16)
            abr = Rot(P, "AB", 2, [128, 4, 512], BF16)
            gfb = Buf()
            for (tok0, T, isctx) in tiles():
                if isctx and l == 1:
                    continue
                uF = ufr.next()
                DMA(uF.t[:, :, 0:T], pT[F_OFF:F_OFF + 1024, tok0:tok0 + T].rearrange("(c p) t -> p c t", p=128), W=[uF.b])
                for s in range(T // 128):
                    ab = abr.next()
                    for g in range(4):
                        pa = psn()
                        for mch in range(2):
                            MM(pa.t[:, 0:512], uF.t[:, g * 2 + mch, s * 128:(s + 1) * 128], CW.t[:, g, mch, :], mch == 0, mch == 1, [uF.b, CW.b], [pa.b])
                        if g % 2 == 0:
                            ACT(ab.t[:, g, :], pa.t[:, 0:512], AF.Copy, [pa.b], [ab.b], partial=True)
                        else:
                            CP("dve", ab.t[:, g, :], pa.t[:, 0:512], [pa.b], [ab.b], partial=True)
                    if isctx:
                        DMA(gFc[s * 128:(s + 1) * 128, :], ab.t[:].rearrange("p g n -> p (g n)"), R=[ab.b])
                    else:
                        r0 = tok0 - CT + s * 128
                        DMA(gF_in[r0:r0 + 128, :], ab.t[:].rearrange("p g n -> p (g n)"), R=[ab.b], W=[gfb], partial=True)
            frc = min(LT, 256)
            for c in range(LT // frc):
                CC(gF_in[c * frc:(c + 1) * frc, :], gF_out[c * 4 * frc:(c + 1) * 4 * frc, :], RG, [gfb], [])

        with P.phase():
            gar = Rot(P, "ga", 3, [128, 2048], BF16)
            tcr = Rot(P, "tc", 3, [128, 512], BF16)
            tsr = Rot(P, "tsn", 3, [128, 512], BF16)
            fgr = Rot(P, "fg", 2, [128, 8, 512], BF16)
            fst = Rot(P, "fst", 2, [128, 8, 512], BF16)
            tcx = Tl(P, "tcx", [128, 2, 2, 256], F32)
            tcb = Tl(P, "tcb", [128, 2, 2, 256], BF16)
            DMA(tcx.t[:], dftctx_in.rearrange("c a p k -> p c a k"), W=[tcx.b])
            CP("dve", tcb.t[:], tcx.t[:], [tcx.b], [tcb.b])
            for (tok0, T, isctx) in tiles():
                if isctx and l == 1:
                    continue
                na = 2 if isctx else NA
                fg = fgr.next()
                DMA(fg.t[:, :, 0:T], pT[FG_OFF:FG_OFF + 1024, tok0:tok0 + T].rearrange("(c p) t -> p c t", p=128), W=[fg.b])
                for a in range(na):
                    ga = gar.next()
                    if isctx:
                        DMA(ga.t[:], gFc[a * 128:(a + 1) * 128, :], W=[ga.b])
                        tcap, tsap, tb1, tb2 = tcb.t[:, 0, a, :], tcb.t[:, 1, a, :], tcb.b, tcb.b
                    else:
                        t0 = tok0 - CT
                        DMA(ga.t[:], gF_out[a * 128:(a + 1) * 128, :], W=[ga.b])
                        tc_, ts_ = tcr.next(), tsr.next()
                        DMA(tc_.t[:, 0:T], tabC[a, :, t0:t0 + T], W=[tc_.b])
                        DMA(ts_.t[:, 0:T], tabS[a, :, t0:t0 + T], W=[ts_.b])
                        tcap, tsap, tb1, tb2 = tc_.t[:, 0:T], ts_.t[:, 0:T], tc_.b, ts_.b
                    for fc in range(8):
                        g, half = fc // 2, fc % 2
                        c0 = g * 512 + half * 128
                        MM(PS[fc].t[:, 0:T], ga.t[:, c0:c0 + 128], tcap, a == 0, False, [ga.b, tb1], [PS[fc].b])
                        MM(PS[fc].t[:, 0:T], ga.t[:, c0 + 256:c0 + 384], tsap, False, a == na - 1, [ga.b, tb2], [PS[fc].b])
                fs = fst.next()
                for fc in range(8):
                    TT("dve", fs.t[:, fc, 0:T], PS[fc].t[:, 0:T], fg.t[:, fc, 0:T], ALU.mult, [PS[fc].b, fg.b], [fs.b], partial=True)
                DMA(fT[:, tok0:tok0 + T].rearrange("(c p) t -> p c t", p=128), fs.t[:, :, 0:T], R=[fs.b])

        with P.phase():
            aTr = Rot(P, "aTt", 1, [128, 16, 512], BF16)
            fTr = Rot(P, "fTt", 1, [128, 8, 512], BF16)
            cTr = Rot(P, "cTt", 1, [128, 8, 512], BF16)
            war = Rot(P, "wa", 2, [128, 16, 512], BF16)
            wfr = Rot(P, "wf", 2, [128, 8, 512], BF16)
            wcr = Rot(P, "wc", 2, [128, 8, 512], BF16)
            mgr = Rot(P, "mg", 2, [128, 3, 4, 512], BF16)
            e1t = [Rot(P, "e1t%d" % i, 2, [128, 512], F32) for i in range(3)]
            mTt = Rot(P, "mTt", 2, [128, 4, 512], BF16)
            wau, wfu, wcu = W("wau"), W("wfu"), W("wcu")
            epool = "dve" if l == 0 else "pool"
            if l == 0:
                record_gathers(g_l1[:n1])
            for (tok0, T, isctx) in tiles():
                if isctx and l == 1:
                    continue
                at, ft, ct = aTr.next(), fTr.next(), cTr.next()
                DMA(at.t[:, :, 0:T], aT[:, tok0:tok0 + T].rearrange("(c p) t -> p c t", p=128), W=[at.b])
                DMA(ft.t[:, :, 0:T], fT[:, tok0:tok0 + T].rearrange("(c p) t -> p c t", p=128), W=[ft.b])
                DMA(ct.t[:, :, 0:T], cT[:, tok0:tok0 + T].rearrange("(c p) t -> p c t", p=128), W=[ct.b])
                for cb in range(D // 512):
                    cs = slice(cb * 512, (cb + 1) * 512)
                    wa, wf, wc, mg = war.next(), wfr.next(), wcr.next(), mgr.next()
                    DMA(wa.t[:].rearrange("p k n -> p (k n)"), wau[cb * 128:(cb + 1) * 128, :], W=[wa.b])
                    DMA(wf.t[:].rearrange("p k n -> p (k n)"), wfu[cb * 128:(cb + 1) * 128, :], W=[wf.b])
                    DMA(wc.t[:].rearrange("p k n -> p (k n)"), wcu[cb * 128:(cb + 1) * 128, :], W=[wc.b])
                    for br_ in range(3):
                        r0 = MG + br_ * D + cb * 512
                        DMA(mg.t[:, br_, :, 0:T], pT[r0:r0 + 512, tok0:tok0 + T].rearrange("(c p) t -> p c t", p=128), W=[mg.b], partial=(br_ > 0))
                    mt = mTt.next()
                    for dcl in range(4):
                        ds_ = slice(dcl * 128, (dcl + 1) * 128)
                        pa, pf, pc = psn(0, 3), psn(3, 6), psn(6, 8)
                        for k in range(16):
                            MM(pa.t[:, 0:T], wa.t[:, k, ds_], at.t[:, k, 0:T], k == 0, k == 15, [wa.b, at.b], [pa.b])
                        for k in range(8):
                            MM(pf.t[:, 0:T], wf.t[:, k, ds_], ft.t[:, k, 0:T], k == 0, k == 7, [wf.b, ft.b], [pf.b])
                        for k in range(8):
                            MM(pc.t[:, 0:T], wc.t[:, k, ds_], ct.t[:, k, 0:T], k == 0, k == 7, [wc.b, ct.b], [pc.b])
                        t0_, t1_, t2_ = e1t[0].next(), e1t[1].next(), e1t[2].next()
                        TT("dve", t0_.t[:, 0:T], pa.t[:, 0:T], mg.t[:, 0, dcl, 0:T], ALU.mult, [pa.b, mg.b], [t0_.b])
                        TT("dve", t1_.t[:, 0:T], pf.t[:, 0:T], mg.t[:, 1, dcl, 0:T], ALU.mult, [pf.b, mg.b], [t1_.b])
                        TT("dve", t2_.t[:, 0:T], pc.t[:, 0:T], mg.t[:, 2, dcl, 0:T], ALU.mult, [pc.b, mg.b], [t2_.b])
                        TT(epool, t0_.t[:, 0:T], t0_.t[:, 0:T], t1_.t[:, 0:T], ALU.add, [t0_.b, t1_.b], [t0_.b])
                        TT(epool, mt.t[:, dcl, 0:T], t0_.t[:, 0:T], t2_.t[:, 0:T], ALU.add, [t0_.b, t2_.b], [mt.b], partial=True)
                    DMA(mTd[cb * 512:(cb + 1) * 512, tok0:tok0 + T].rearrange("(c p) t -> p c t", p=128), mt.t[:, :, 0:T], R=[mt.b])

        with P.phase():
            mTr = Rot(P, "mT", 2, [128, KD, 512], BF16)
            wor = Rot(P, "wo", 2, [128, KD, 512], BF16)
            gtr = Rot(P, "gtb", 2, [128, 512], F32)
            xsr = Rot(P, "xs", 3, [128, 512], F32)
            e2t = Rot(P, "e2t", 2, [128, 512], F32)
            xor_ = Rot(P, "xo", 3, [128, 512], F32)
            wo = W("wo")
            epool = "dve" if l == 0 else "pool"
            if l == 0:
                record_gathers(g_l1[n1:])
            for (tok0, T, isctx) in tiles():
                if isctx and l == 1:
                    continue
                q = 2 * l + isctx
                mt = mTr.next()
                for k0 in range(0, KD, 8):
                    k1 = min(KD, k0 + 8)
                    DMA(mt.t[:, k0:k1, 0:T], mTd[k0 * 128:k1 * 128, tok0:tok0 + T].rearrange("(c p) t -> p c t", p=128), W=[mt.b], partial=(k0 > 0))
                for nb in range(D // 512):
                    cs = slice(nb * 512, (nb + 1) * 512)
                    wt = wor.next()
                    DMA(wt.t[:].rearrange("p k n -> p (k n)"), wo[nb * 128:(nb + 1) * 128, :], W=[wt.b])
                    gt = gtr.next()
                    DMA(gt.t[:], gtb_d[q, :, cs], W=[gt.b])
                    for s in range(T // 128):
                        xs = xsr.next()
                        DMA(xs.t[:], src_rows(l, tok0 + s * 128, 128)[:, cs], W=[xs.b])
                        po = psn()
                        for kc in range(KD):
                            MM(po.t[:, 0:512], mt.t[:, kc, s * 128:(s + 1) * 128], wt.t[:, kc, :], kc == 0, kc == KD - 1, [mt.b, wt.b], [po.b])
                        tt_ = e2t.next()
                        xo = xor_.next()
                        TT("dve", tt_.t[:], po.t[:, 0:512], gt.t[:], ALU.mult, [po.b, gt.b], [tt_.b])
                        TT(epool, xo.t[:], tt_.t[:], xs.t[:], ALU.add, [tt_.b, xs.b], [xo.b])
                        if l == 0:
                            DMA(x1[tok0 + s * 128:tok0 + (s + 1) * 128, cs], xo.t[:], R=[xo.b])
                        else:
                            r0 = tok0 - CT + s * 128
                            DMA(out[r0:r0 + 128, cs], xo.t[:], R=[xo.b])

    finish(P)
    P.top.close()
    return nc, P


def host_consts(cfg, r):
    LT, SEQ, NA = cfg.LT, cfg.SEQ, cfg.NA
    cst = np.zeros((128, 1040), np.float32)
    cst[:, 0:128] = np.eye(128, dtype=np.float32)
    R = np.zeros((128, 128), np.float32)
    for base in (0, 64):
        for i in range(32):
            R[base + i, base + i + 32] = -1.0
            R[base + i + 32, base + i] = 1.0
    cst[:, 128:256] = R.T
    jj = np.arange(128)[:, None]
    ii = np.arange(128)[None, :]
    cst[:, 256:384] = (ii <= jj).astype(np.float32)
    cst[:, 384:512] = (jj <= ii).astype(np.float32)
    if r > 0:
        cst[:, 512 + r - 1] = 1.0
        cst[:, 520] = 1.0
    if r < 3:
        cst[:, 516 + r + 1] = 1.0
        cst[:, 521] = 1.0
    for q in range(4):
        cst[q, 528 + q * 128:528 + (q + 1) * 128] = 1.0
    pos = np.arange(r * LT, (r + 1) * LT)
    row = (pos // 64).astype(np.float64)
    col = (pos % 64).astype(np.float64)
    inv = 10000.0 ** (-np.arange(0, 64, 2, dtype=np.float64) / 64)
    ar_, ac_ = row[:, None] * inv[None, :], col[:, None] * inv[None, :]
    ang = np.concatenate([ar_, ar_, ac_, ac_], -1).astype(np.float32)
    rope = np.stack([np.cos(ang).T, np.sin(ang).T]).astype(np.float32)
    k = pos.astype(np.float64)[None, :]
    p = np.arange(128, dtype=np.float64)[:, None]
    a = np.arange(NA, dtype=np.float64)[:, None]
    sc = 1.0 / np.sqrt(SEQ * 256.0)
    angB = 2 * np.pi * ((p * k) % SEQ) / SEQ
    frc = min(LT, 256)
    ai = np.arange(NA)
    m0 = ai * 128
    cc_, rr_, ii_ = m0 // (4 * frc), (m0 % (4 * frc)) // frc, m0 % frc
    l0 = rr_ * LT + cc_ * frc + ii_
    a = (l0 // 128).astype(np.float64)[:, None]
    angA = 2 * np.pi * ((128 * a * k) % SEQ) / SEQ
    dftB = np.stack([np.cos(angB), np.sin(angB)]).astype(np.float32)
    dftA = (np.stack([np.cos(angA), np.sin(angA)]) * sc).astype(np.float32)
    l_ = np.arange(256, dtype=np.float64)
    a256 = 2 * np.pi * np.outer(l_, l_) / 256
    scc = 1.0 / 256.0
    dftctx = np.stack([np.cos(a256) * scc, -np.sin(a256) * scc]).reshape(2, 2, 128, 256).astype(np.float32)
    dc = np.stack([np.cos(a256), np.sin(a256)])
    dftc = dc.reshape(2, 2, 128, 256).transpose(2, 0, 1, 3).astype(np.float32)
    return dict(cst=cst, rope=rope, dftB=dftB, dftA=dftA, dftctx=np.ascontiguousarray(dftctx), dftc=np.ascontiguousarray(dftc))


def make_in_maps(cfg, inp):
    D, LT, KD, MC = cfg.D, cfg.LT, cfg.KD, cfg.MC
    f = lambda a: np.ascontiguousarray(np.asarray(a, dtype=np.float32))
    x, c, ctx, c_ctx = f(inp["x"]), f(inp["c"]), f(inp["ctx"]), f(inp["c_ctx"])
    maps = []
    ngT = f(f(inp["norm_g"]).reshape(2, KD, 128).transpose(0, 2, 1))
    qg, kg = f(inp["q_norm_g"]), f(inp["k_norm_g"])
    qkg = f(np.stack([qg, kg], -1))
    qkrow = f(np.concatenate([qg, kg], -1).reshape(2, 1, 256))
    sink = f(f(inp["attn_sink"]).reshape(2, 1, 16))
    dww = f(f(inp["conv_dw_w"]).reshape(2, 31, 8, 128).transpose(0, 3, 2, 1))
    cv3 = f(np.stack([f(inp["conv_dw_b"]), f(inp["conv_ln_g"]), f(inp["conv_ln_b"])], -1).reshape(2, 8, 128, 3).transpose(0, 2, 1, 3))
    bmod = f(inp["b_mod"])
    wl = {"w_in": f(inp["w_in"]), "w_au": f(inp["w_attn_up"]), "w_fu": f(inp["w_fourier_up"]), "w_cu": f(inp["w_conv_up"]),
          "w_o": f(inp["w_out"]), "w_pw": f(inp["w_conv_pw"])}
    wl_g = {}
    for k_, w in wl.items():
        K_, N_ = w.shape[1], w.shape[2]
        if k_ == "w_pw":
            wl_g[k_] = w
        else:
            wl_g[k_] = np.ascontiguousarray(w.reshape(2, K_ // 128, 128, N_ // 512, 512).transpose(0, 3, 2, 1, 4)).reshape(2, N_ // 4, 4 * K_)
    wmod = f(inp["w_mod"])
    wfm = f(inp["w_fourier_mix"])
    for core in range(8):
        b, r = core // 4, core % 4
        m = host_consts(cfg, r)
        m["x"] = f(x[b, r * LT:(r + 1) * LT])
        m["ctx"] = f(ctx[b])
        m["cvT"] = f(np.stack([c[b], c_ctx], -1).reshape(KD, 128, 2).transpose(1, 0, 2))
        m["ngT"], m["qkg"], m["qkrow"], m["sink"], m["wfm"], m["dww"], m["cv3"] = ngT, qkg, qkrow, sink, wfm, dww, cv3
        for k_, w in wl.items():
            g = wl_g[k_]
            R_g, C_g = g.shape[1], g.shape[2]
            rc = wchunk(R_g, C_g)
            m[k_] = f(g.reshape(2, R_g // (4 * rc), 4, rc, C_g)[:, :, r].reshape(2, R_g // 4, C_g))
        m["w_mod"] = f(wmod[:, :, r * MC:(r + 1) * MC])
        m["b_mod"] = f(np.repeat(bmod[:, None, r * MC:(r + 1) * MC], 2, axis=1))
        maps.append(m)
    return maps


def kernel(**inputs):
    cfg = Cfg()
    nc, _ = build(cfg)
    maps = make_in_maps(cfg, inputs)
    res = run_bass_kernel_spmd(nc, maps, core_ids=list(range(8)))
    outp = np.empty((2, cfg.SEQ, cfg.D), np.float32)
    for core in range(8):
        b, r = core // 4, core % 4
        outp[b, r * cfg.LT:(r + 1) * cfg.LT] = res.results[core]["out"]
    return outp
```

```python
import contextlib
import numpy as np
import concourse.bass as bass
import concourse.mybir as mybir
from concourse.bass_utils import run_bass_kernel_spmd

F32 = mybir.dt.float32
BF16 = mybir.dt.bfloat16
AF = mybir.ActivationFunctionType
ALU = mybir.AluOpType
AX = mybir.AxisListType

ENGS = ("pe", "act", "dve", "pool", "sp")
DMAQ = ("sp", "act", "pool")
RING = 8
EPS = 1e-6


class Buf:
    __slots__ = ("writers", "readers")

    def __init__(self):
        self.writers = {}
        self.readers = {}


class Op:
    __slots__ = ("eng", "fn", "deps", "signal", "val", "dma", "slot", "key", "inc", "epoch")

    def __init__(self, eng, fn, dma=False):
        self.eng = eng
        self.fn = fn
        self.deps = []
        self.signal = False
        self.val = 0
        self.dma = dma
        self.slot = -1
        self.key = eng
        self.inc = 1
        self.epoch = 0


class Prog:
    def __init__(self, nc):
        self.nc = nc
        self.top = contextlib.ExitStack()
        self.scope = self.top
        st = self.top
        self.esem = {e: st.enter_context(nc.semaphore("s_" + e)) for e in ENGS}
        self.ring = {q: [st.enter_context(nc.semaphore("r_%s%d" % (q, i))) for i in range(RING)] for q in DMAQ}
        self.ccsem = st.enter_context(nc.semaphore("s_cc"))
        self.cnt = {e: 0 for e in ENGS}
        self.ndma = {q: 0 for q in DMAQ}
        self.ncc = 0
        self.epoch = 0
        self.ops = {e: [] for e in ENGS}
        self.waited = {e: {} for e in ENGS}
        self.nops = 0
        self.mk = self.sbuf("mk", [128, 8], F32)

    def sbuf(self, name, shape, dt):
        self.nalloc = getattr(self, "nalloc", 0) + 1
        return self.scope.enter_context(self.nc.sbuf_tensor("%s_%d" % (name, self.nalloc), list(shape), dt))

    def psum(self, name, shape, dt):
        return self.scope.enter_context(self.nc.psum_tensor(name, list(shape), dt))

    @contextlib.contextmanager
    def phase(self):
        old = self.scope
        with contextlib.ExitStack() as st:
            self.scope = st
            yield
            self.nphase = getattr(self, "nphase", 0) + 1
            import os as _os
            if self.nphase <= int(_os.environ.get("KSTOP", "999")):
                self.flush()
            else:
                self.ops = {e: [] for e in ENGS}
        self.scope = old

    def _add(self, op, reads, writes, partial):
        op.epoch = self.epoch
        deps = {}
        for b in reads:
            for w in b.writers.values():
                deps[id(w)] = w
        for b in writes:
            for w in b.readers.values():
                deps[id(w)] = w
            for w in b.writers.values():
                deps[id(w)] = w
        for d in deps.values():
            if d is op or d.epoch != self.epoch:
                continue
            if (not d.dma) and (not op.dma) and d.eng == op.eng and op.eng == "pe":
                continue
            d.signal = True
            op.deps.append(d)
        for b in reads:
            b.readers[op.key] = op
        for b in writes:
            if not partial:
                b.writers = {}
                b.readers = {}
            b.writers[op.key] = op
        self.ops[op.eng].append(op)
        self.nops += 1
        return op

    def op(self, eng, fn, reads=(), writes=(), partial=False):
        return self._add(Op(eng, fn), reads, writes, partial)

    def dma(self, q, fn, reads=(), writes=(), partial=False):
        o = Op(q, fn, dma=True)
        i = self.ndma[q]
        self.ndma[q] = i + 1
        o.slot = i % RING
        o.val = 16 * (i // RING + 1)
        o.inc = 16
        o.key = (q, o.slot)
        o.signal = True
        return self._add(o, reads, writes, partial)

    def cc(self, fn, reads=(), writes=()):
        o = Op("pool", fn, dma=True)
        self.ncc += 1
        o.slot = -2
        o.val = self.ncc
        o.inc = 1
        o.key = ("cc", 0)
        o.signal = True
        return self._add(o, reads, writes, False)

    def flush(self):
        nc = self.nc
        ops_snap = self.ops
        mval = {}
        for e in ENGS:
            c = self.cnt[e]
            for o in self.ops[e]:
                if not o.dma and o.signal:
                    c += 1
                    o.val = c
            mval[e] = c + 1
            self.cnt[e] = c + (0 if e == "pe" else 1)
        mk = self.mk

        def semof(o):
            if o.slot == -2:
                return self.ccsem
            if o.dma:
                return self.ring[o.eng][o.slot]
            return self.esem[o.eng]

        def run(e, eng):
            waited = self.waited[e]

            def wait(s, v):
                if waited.get(id(s), 0) < v:
                    eng.wait_ge(s, v)
                    waited[id(s)] = v

            last = {}
            for o in ops_snap[e]:
                for d in o.deps:
                    wait(semof(d), d.val)
                if o.slot == -2:
                    o.fn(eng).then_inc(self.ccsem, 1)
                    wait(self.ccsem, o.val)
                elif o.dma:
                    s = self.ring[e][o.slot]
                    if o.val > 16:
                        wait(s, o.val - 16)
                    o.fn(eng).then_inc(s, 16)
                    last[o.slot] = o
                else:
                    ins = o.fn(eng)
                    if o.signal:
                        ins.then_inc(self.esem[e], 1)
            for sl, o in last.items():
                wait(self.ring[e][sl], o.val)
            if e == "dve":
                m = eng.memset(mk[:, 0:1], 0.0)
            elif e == "pool":
                m = eng.memset(mk[:, 1:2], 0.0)
            elif e == "act":
                m = eng.memzero(mk[:, 2:3])
            elif e == "sp":
                m = eng.nop()
            else:
                m = None
            if m is not None:
                m.then_inc(self.esem[e], 1)
            for e2 in ENGS:
                if e2 != "pe":
                    wait(self.esem[e2], mval[e2])

        self.pending = getattr(self, 'pending', [])
        self.pending.append(run)

        self.epoch += 1
        self.ops = {e: [] for e in ENGS}


def finish(P):
    nc = P.nc
    with nc.Block() as block:
        @block.tensor
        def _(eng):
            for r in P.pending:
                r("pe", eng)

        @block.scalar
        def _(eng):
            for r in P.pending:
                r("act", eng)

        @block.vector
        def _(eng):
            for r in P.pending:
                r("dve", eng)

        @block.gpsimd
        def _(eng):
            for r in P.pending:
                r("pool", eng)

        @block.sync
        def _(eng):
            for r in P.pending:
                r("sp", eng)


class Tl:
    def __init__(self, P, name, shape, dt, psum=False):
        self.t = (P.psum if psum else P.sbuf)(name, shape, dt)
        self.b = Buf()


class Rot:
    def __init__(self, P, name, n, shape, dt):
        self.ts = [Tl(P, "%s%d" % (name, i), shape, dt) for i in range(n)]
        self.i = 0

    def next(self):
        t = self.ts[self.i % len(self.ts)]
        self.i += 1
        return t


class Cfg:
    def __init__(self, D=4096, SEQ=8192):
        self.D = D
        self.SEQ = SEQ
        self.KD = D // 128
        self.LT = SEQ // 4
        self.CT = 256
        self.NTOK = self.CT + self.LT
        self.MG = 10240
        self.INW = 10240 + 3 * D
        self.NA = SEQ // 128
        self.MC = 3 * D // 4
        self.TT = min(512, self.LT)


Q_OFF, K_OFF, V_OFF, AG_OFF, F_OFF, FG_OFF, CA_OFF, CB_OFF, CG_OFF = 0, 2048, 2560, 3072, 5120, 6144, 7168, 8192, 9216
VBLK = V_OFF // 512


def wchunk(K, N):
    n = K // 4
    for rc in range(n, 0, -1):
        if n % rc == 0 and rc * N * 2 <= (1 << 20):
            return rc


def gshape(nm, K, N):
    if nm == "wpw":
        return K, N
    return N // 4, 4 * K


def fam_func(blk):
    c = blk * 512
    if AG_OFF <= c < F_OFF or FG_OFF <= c < CA_OFF or CG_OFF <= c < 10240:
        return AF.Silu
    if CB_OFF <= c < CG_OFF or c >= 10240:
        return AF.Sigmoid
    return AF.Copy


def build(cfg, debug=False):
    D, KD, LT, CT, NTOK, INW, NA, MC, SEQ = cfg.D, cfg.KD, cfg.LT, cfg.CT, cfg.NTOK, cfg.INW, cfg.NA, cfg.MC, cfg.SEQ
    MG = cfg.MG
    NB = LT // 128
    nc = bass.Bass("TRN2", target_bir_lowering=False)

    def din(name, shape, dt=F32):
        return nc.dram_tensor(name, list(shape), dt, kind="ExternalInput").ap()

    def dscr(name, shape, dt=BF16, dbg=False):
        kind = "ExternalOutput" if (dbg and debug and dt == F32) else "Internal"
        return nc.dram_tensor(name, list(shape), dt, kind=kind).ap()

    x_in = din("x", [LT, D])
    ctx_in = din("ctx", [CT, D])
    cst_in = din("cst", [128, 1040])
    rope_in = din("rope", [2, 128, LT])
    dftB_in = din("dftB", [2, 128, LT])
    dftA_in = din("dftA", [2, NA, LT])
    dftctx_in = din("dftctx", [2, 2, 128, 256])
    dftc_in = din("dftc", [128, 2, 2, 256])
    cvT_in = din("cvT", [128, KD, 2])
    ngT_in = din("ngT", [2, 128, KD])
    qkg_in = din("qkg", [2, 128, 2])
    qkrow_in = din("qkrow", [2, 1, 256])
    sink_in = din("sink", [2, 1, 16])
    wfm_in = din("wfm", [2, 4, 256, 256])
    dww_in = din("dww", [2, 128, 8, 31])
    cv3_in = din("cv3", [2, 128, 8, 3])
    w_in_s = din("w_in", [2, INW // 16, 4 * D])
    w_au_s = din("w_au", [2, D // 16, 4 * 2048])
    w_fu_s = din("w_fu", [2, D // 16, 4 * 1024])
    w_cu_s = din("w_cu", [2, D // 16, 4 * 1024])
    w_o_s = din("w_o", [2, D // 16, 4 * D])
    w_pw_s = din("w_pw", [2, 256, 1024])
    w_mod_s = din("w_mod", [2, D, MC])
    b_mod_s = din("b_mod", [2, 2, MC])
    out = nc.dram_tensor("out", [LT, D], F32, kind="ExternalOutput").ap()

    wspec = [("win", D, INW, w_in_s), ("wau", 2048, D, w_au_s), ("wfu", 1024, D, w_fu_s),
             ("wcu", 1024, D, w_cu_s), ("wo", D, D, w_o_s), ("wpw", 1024, 1024, w_pw_s)]
    Wg = {}
    Wgin = {}
    for nm, K, N, _ in wspec:
        R_g, C_g = gshape(nm, K, N)
        for l in range(2):
            Wgin[(nm, l)] = dscr("gi_%s%d" % (nm, l), [R_g // 4, C_g])
            Wg[(nm, l)] = dscr("g_%s%d" % (nm, l), [R_g, C_g])
    pT = dscr("pT", [INW, NTOK], dbg=True)
    Vtm = dscr("Vtm", [NTOK, 512], dbg=True)
    kTn = dscr("kTn", [512, NTOK], dbg=True)
    gK_in = dscr("gK_in", [512, 256]); gK_out = dscr("gK_out", [4 * 512, 256])
    gV_in = dscr("gV_in", [256, 512]); gV_out = dscr("gV_out", [4 * 256, 512])
    gU_in = dscr("gU_in", [1024, 32]); gU_out = dscr("gU_out", [4 * 1024, 32])
    gF_in = dscr("gF_in", [LT, 2048]); gF_out = dscr("gF_out", [SEQ, 2048])
    gFc = dscr("gFc", [CT, 2048])
    tabC = dscr("tabC", [NA, 128, LT]); tabS = dscr("tabS", [NA, 128, LT])
    aT = dscr("aT", [2048, NTOK], dbg=True)
    fT = dscr("fT", [1024, NTOK], dbg=True)
    cT = dscr("cT", [1024, NTOK], dbg=True)
    mTd = dscr("mTd", [D, NTOK], dbg=True)
    x1 = dscr("x1", [NTOK, D], F32, dbg=True)
    gmod_in = dscr("gmod_in", [4, MC], F32); gmod_out = dscr("gmod_out", [16, MC], F32)
    gtb_d = dscr("gtb_d", [4, 128, D], F32)
    RG = [[0, 1, 2, 3], [4, 5, 6, 7]]
    RG8 = [list(range(8))]

    import os as _os2
    KSUB = int(_os2.environ.get('KSUB', '99'))
    P = Prog(nc)
    op, dma = P.op, P.dma

    def MM(o, lt, rh, start, stop, R, W):
        op("pe", lambda e: e.matmul(o, lhsT=lt, rhs=rh, start=start, stop=stop), reads=R, writes=W, partial=not start)

    def ACT(o, i, func, R, W, bias=None, scale=None, accum=None, partial=False):
        kw = {}
        if bias is not None:
            kw["bias"] = bias
        if scale is not None:
            kw["scale"] = scale
        if accum is not None:
            kw["accum_out"] = accum
        op("act", lambda e: e.activation(out=o, in_=i, func=func, **kw), reads=R, writes=W, partial=partial)

    def TT(eng, o, a, b, aop, R, W, partial=False):
        op(eng, lambda e: e.tensor_tensor(out=o, in0=a, in1=b, op=aop), reads=R, writes=W, partial=partial)

    def TS(eng, o, a, s1, s2, op0, op1, R, W, partial=False):
        if s2 is None:
            op(eng, lambda e: e.tensor_scalar(out=o, in0=a, scalar1=s1, scalar2=None, op0=op0), reads=R, writes=W, partial=partial)
        else:
            op(eng, lambda e: e.tensor_scalar(out=o, in0=a, scalar1=s1, scalar2=s2, op0=op0, op1=op1), reads=R, writes=W, partial=partial)

    def STT(eng, o, a, s, b, op0, op1, R, W, partial=False):
        op(eng, lambda e: e.scalar_tensor_tensor(out=o, in0=a, scalar=s, in1=b, op0=op0, op1=op1), reads=R, writes=W, partial=partial)

    def CP(eng, o, i, R, W, partial=False):
        op(eng, lambda e: e.tensor_copy(out=o, in_=i), reads=R, writes=W, partial=partial)

    def RECIP(o, i, R, W):
        op("dve", lambda e: e.reciprocal(out=o, in_=i), reads=R, writes=W)

    def DMA(o, i, R=(), W=(), q="sp", partial=False):
        if q == "sp" and str(o.space).endswith("DRAM") and not str(i.space).endswith("DRAM"):
            q = "act"
        dma(q, lambda e: e.dma_start(out=o, in_=i), reads=R, writes=W, partial=partial)

    def CC(i, o, groups, R, W):
        P.cc(lambda e: e.collective_compute("AllGather", ALU.bypass, replica_groups=groups, ins=[i], outs=[o]), reads=R, writes=W)

    cst = Tl(P, "cst", [128, 1040], F32)
    ident = cst.t[:, 0:128]
    rotT = cst.t[:, 128:256]
    selc = lambda i: cst.t[:, 512 + i:513 + i]
    selm = cst.t[0:4, 528:1040]
    ones32 = Tl(P, "ones32", [128, 128], F32)
    onesbf = Tl(P, "onesbf", [128, 128], BF16)
    identbf = Tl(P, "identbf", [128, 128], BF16)
    mskbf = Tl(P, "mskbf", [128, 4, 128], BF16)
    gsT = Tl(P, "gsT", [128, 4, KD], F32)
    shT = Tl(P, "shT", [128, 4, KD], F32)
    qkg = Tl(P, "qkg", [128, 2, 2], F32)
    negB = Tl(P, "negB", [128, 2], F32)
    sinkrow = Tl(P, "sinkrow", [1, 2, 2048], BF16)
    dftc = Tl(P, "dftc", [128, 2, 2, 256], F32)
    PS = [Tl(P, "ps%d" % i, [128, 512], F32, psum=True) for i in range(8)]
    psi = [0]

    def psn(lo=0, hi=8):
        t = PS[lo + psi[0] % (hi - lo)]
        psi[0] += 1
        return t

    with P.phase():
        DMA(cst.t[:], cst_in, W=[cst.b])
        DMA(qkg.t[:], qkg_in.rearrange("l p k -> p l k"), W=[qkg.b])
        DMA(dftc.t[:], dftc_in, W=[dftc.b])
        op("dve", lambda e: e.memset(ones32.t[:], 1.0), writes=[ones32.b])
        op("pool", lambda e: e.memset(onesbf.t[:], 1.0), writes=[onesbf.b])
        CP("dve", identbf.t[:], ident, [cst.b], [identbf.b])
        CP("dve", mskbf.t[:, 0, :], cst.t[:, 256:384], [cst.b], [mskbf.b], partial=True)
        CP("dve", mskbf.t[:, 1, :], cst.t[:, 384:512], [cst.b], [mskbf.b], partial=True)
        TS("dve", mskbf.t[:, 2, :], cst.t[:, 256:384], selc(8), None, ALU.mult, None, [cst.b], [mskbf.b], partial=True)
        TS("dve", mskbf.t[:, 3, :], cst.t[:, 384:512], selc(9), None, ALU.mult, None, [cst.b], [mskbf.b], partial=True)
        qkrow = Tl(P, "qkrow", [1, 2, 256], F32)
        sk = Tl(P, "sk", [1, 2, 16], F32)
        mx = Tl(P, "mx", [1, 8], F32)
        DMA(qkrow.t[:], qkrow_in.rearrange("l o k -> o l k"), W=[qkrow.b])
        DMA(sk.t[:], sink_in.rearrange("l o k -> o l k"), W=[sk.b])
        for l in range(2):
            for j in range(2):
                op("dve", lambda e, l=l, j=j: e.reduce_max(out=mx.t[0:1, 2 * l + j:2 * l + j + 1], in_=qkrow.t[0:1, l, j * 128:(j + 1) * 128],
                                                         axis=AX.X, apply_absolute_value=True), reads=[qkrow.b], writes=[mx.b], partial=True)
            TT("dve", mx.t[0:1, 4 + l:5 + l], mx.t[0:1, 2 * l:2 * l + 1], mx.t[0:1, 2 * l + 1:2 * l + 2], ALU.mult, [mx.b], [mx.b], partial=True)
            pb = psn()
            MM(pb.t[:, 0:1], ones32.t[0:1, :], mx.t[0:1, 4 + l:5 + l], True, True, [ones32.b, mx.b], [pb.b])
            ACT(negB.t[:, l:l + 1], pb.t[:, 0:1], AF.Copy, [pb.b], [negB.b], scale=-(128.0 ** 0.5), partial=True)
            es = Tl(P, "es%d" % l, [1, 16], F32)
            ACT(es.t[:], sk.t[0:1, l, :], AF.Exp, [sk.b, negB.b], [es.b], bias=negB.t[0:1, l:l + 1])
            for h in range(16):
                TS("dve", sinkrow.t[0:1, l, h * 128:(h + 1) * 128], ones32.t[0:1, :], es.t[0:1, h:h + 1], None, ALU.mult, None,
                   [ones32.b, es.b], [sinkrow.b], partial=True)

    with P.phase():
        gb = Buf()
        for nm, K, N, src in wspec:
            for l in range(2):
                DMA(Wgin[(nm, l)], src[l], W=[gb], q="pool", partial=True)
    chunkbuf = {}

    def gather_list():
        order = [("win", 0)] + [(nm, 0) for nm in ("wpw", "wau", "wfu", "wcu", "wo")] + [(nm, 1) for nm in ("win", "wpw", "wau", "wfu", "wcu", "wo")]
        dims = {nm: gshape(nm, K, N) for nm, K, N, _ in wspec}
        out_ = []
        for nm, l in order:
            R_g, C_g = dims[nm]
            rc = wchunk(R_g, C_g)
            for c in range((R_g // 4) // rc):
                out_.append((nm, l, c, rc))
        return out_

    glist = gather_list()
    g_l0 = [g for g in glist if g[1] == 0]
    g_l1 = [g for g in glist if g[1] == 1]
    n1 = len([g for g in g_l1 if g[0] == "win"]) * 10 // 11

    def record_gathers(items):
        for nm, l, c, rc in items:
            b = Buf()
            chunkbuf[(nm, l, c)] = b
            CC(Wgin[(nm, l)][c * rc:(c + 1) * rc, :], Wg[(nm, l)][c * 4 * rc:(c + 1) * 4 * rc, :], RG, [], [b])

    def wbufs(nm, l, r0, r1):
        K_, N_ = [(K, N) for n_, K, N, _ in wspec if n_ == nm][0]
        R_g, C_g = gshape(nm, K_, N_)
        rc4 = 4 * wchunk(R_g, C_g)
        return [chunkbuf[(nm, l, c)] for c in range(r0 // rc4, (r1 - 1) // rc4 + 1)]

    with P.phase():
        cvT = Tl(P, "cvT", [128, KD, 2], F32)
        scT = Tl(P, "scT", [128, KD, 2], F32)
        bm = Tl(P, "bm", [2, 2, MC], F32)
        mrow = Tl(P, "mrow", [2, 2, MC], F32)
        DMA(cvT.t[:], cvT_in, W=[cvT.b])
        DMA(bm.t[:], b_mod_s.rearrange("l q m -> q l m"), W=[bm.b])
        ACT(scT.t[:], cvT.t[:], AF.Silu, [cvT.b], [scT.b])
        wmr = Rot(P, "wm", 3, [128, MC], F32)
        nch = [(o, min(512, MC - o)) for o in range(0, MC, 512)]
        gmb = Buf()
        for l in range(2):
            pss = [PS[i] for i in range(len(nch))]
            for kc in range(KD):
                wm = wmr.next()
                DMA(wm.t[:], w_mod_s[l, kc * 128:(kc + 1) * 128, :], W=[wm.b])
                for i, (o, n) in enumerate(nch):
                    MM(pss[i].t[0:2, 0:n], scT.t[:, kc, :], wm.t[:, o:o + n], kc == 0, kc == KD - 1, [scT.b, wm.b], [pss[i].b])
            for i, (o, n) in enumerate(nch):
                TT("dve", mrow.t[0:2, l, o:o + n], pss[i].t[0:2, 0:n], bm.t[0:2, l, o:o + n], ALU.add, [pss[i].b, bm.b], [mrow.b], partial=True)
            DMA(gmod_in[2 * l:2 * l + 2, :], mrow.t[0:2, l, :], R=[mrow.b], W=[gmb], partial=True)
        gob = Buf()
        if KSUB >= 1:
            CC(gmod_in, gmod_out, RG, [gmb], [gob])
        R_ = Tl(P, "Rr", [4, 3 * D], F32)
        if KSUB >= 2:
          DMA(R_.t[:].rearrange("q (r j) -> q r j", r=4), gmod_out.rearrange("(r q) j -> q r j", q=4), R=[gob], W=[R_.b])
        modT = Tl(P, "modT", [128, 2 * KD, 4], F32)
        for c0 in (range(0, 2 * KD, 64) if KSUB >= 3 else []):
            pm = psn()
            n = min(64, 2 * KD - c0)
            for c in range(n):
                op("pe", lambda e, c=c, c0=c0, pm=pm: e.transpose(pm.t[:, c * 4:c * 4 + 4], R_.t[0:4, (c0 + c) * 128:(c0 + c + 1) * 128], cst.t[0:4, 0:4]),
                   reads=[R_.b, cst.b], writes=[pm.b], partial=(c > 0))
            CP("dve", modT.t[:, c0:c0 + n, :], pm.t[:, 0:4 * n].rearrange("p (c q) -> p c q", q=4), [pm.b], [modT.b], partial=True)
        ngT = Tl(P, "ngT", [128, 2, KD], F32)
        DMA(ngT.t[:], ngT_in.rearrange("l p k -> p l k"), W=[ngT.b])
        for q in (range(4) if KSUB >= 4 else []):
            STT("dve", gsT.t[:, q, :], modT.t[:, KD:2 * KD, q], 1.0, ngT.t[:, q // 2, :], ALU.add, ALU.mult, [modT.b, ngT.b], [gsT.b], partial=True)
            CP("dve", shT.t[:, q, :], modT.t[:, 0:KD, q], [modT.b], [shT.b], partial=True)
        gst = Rot(P, "gst", 2, [128, 512], F32)
        for q in (range(4) if KSUB >= 5 else []):
            for nb in range(D // 512):
                pg = psn()
                MM(pg.t[:, 0:512], selm[:, q * 128:(q + 1) * 128], R_.t[0:4, 2 * D + nb * 512:2 * D + (nb + 1) * 512], True, True, [cst.b, R_.b], [pg.b])
                g = gst.next()
                ACT(g.t[:], pg.t[:, 0:512], AF.Copy, [pg.b], [g.b])
                DMA(gtb_d[q, :, nb * 512:(nb + 1) * 512], g.t[:], R=[g.b])

    with P.phase():
        CB = Tl(P, "CB", [128, LT], F32)
        SB = Tl(P, "SB", [128, LT], F32)
        DMA(CB.t[:], dftB_in[0], W=[CB.b])
        DMA(SB.t[:], dftB_in[1], W=[SB.b])
        car = Rot(P, "ca", 2, [128, LT], F32)
        sar = Rot(P, "sa", 2, [128, LT], F32)
        t1r = Rot(P, "t1", 2, [128, LT], F32)
        t2r = Rot(P, "t2", 2, [128, LT], F32)
        ocr = Rot(P, "oc", 2, [128, LT], BF16)
        osr = Rot(P, "os", 2, [128, LT], BF16)
        g_w0 = [g for g in g_l0 if g[0] == "win"]
        record_gathers(g_w0)
        for a in range(NA):
            ca, sa, t1, t2, oc, os_ = car.next(), sar.next(), t1r.next(), t2r.next(), ocr.next(), osr.next()
            DMA(ca.t[:], dftA_in[0, a:a + 1, :].partition_broadcast(128), W=[ca.b])
            DMA(sa.t[:], dftA_in[1, a:a + 1, :].partition_broadcast(128), W=[sa.b])
            TT("dve", t1.t[:], ca.t[:], CB.t[:], ALU.mult, [ca.b, CB.b], [t1.b])
            TT("dve", t2.t[:], sa.t[:], SB.t[:], ALU.mult, [sa.b, SB.b], [t2.b])
            TT("dve", oc.t[:], t1.t[:], t2.t[:], ALU.subtract, [t1.b, t2.b], [oc.b])
            DMA(tabC[a], oc.t[:], R=[oc.b])
            TT("dve", t2.t[:], sa.t[:], CB.t[:], ALU.mult, [sa.b, CB.b], [t2.b])
            TT("dve", t1.t[:], ca.t[:], SB.t[:], ALU.mult, [ca.b, SB.b], [t1.b])
            STT("dve", os_.t[:], t2.t[:], -1.0, t1.t[:], ALU.mult, ALU.subtract, [t1.b, t2.b], [os_.b])
            DMA(tabS[a], os_.t[:], R=[os_.b])

    def src_rows(l, tok0, n):
        if l == 1:
            return x1[tok0:tok0 + n, :]
        if tok0 < CT:
            return ctx_in[tok0:tok0 + n, :]
        return x_in[tok0 - CT:tok0 - CT + n, :]

    def tiles():
        ts = [(0, CT, 1)]
        for t0 in range(0, LT, cfg.TT):
            ts.append((CT + t0, cfg.TT, 0))
        return ts

    def normrope(l, which, src, srcb, T, cos, sin, ropeb, outap, outb, tmp, pe_="pool"):
        sq, rs, kn, t1, t2 = [r_.next() for r_ in tmp]
        ACT(sq.t[:, 0:T], src, AF.Square, [srcb], [sq.b])
        p1 = psn(0, 4)
        MM(p1.t[:, 0:T], ones32.t[:], sq.t[:, 0:T], True, True, [ones32.b, sq.b], [p1.b])
        ACT(rs.t[:, 0:T], p1.t[:, 0:T], AF.Sqrt, [p1.b], [rs.b], bias=EPS, scale=1.0 / 128)
        RECIP(rs.t[:, 0:T], rs.t[:, 0:T], [rs.b], [rs.b])
        STT("dve", kn.t[:, 0:T], src, qkg.t[:, l, which:which + 1], rs.t[:, 0:T], ALU.mult, ALU.mult, [srcb, qkg.b, rs.b], [kn.b])
        if cos is None:
            CP(pe_, outap, kn.t[:, 0:T], [kn.b], [outb], partial=True)
            return
        p2 = psn(0, 4)
        MM(p2.t[:, 0:T], rotT, kn.t[:, 0:T], True, True, [cst.b, kn.b], [p2.b])
        TT(pe_, t1.t[:, 0:T], kn.t[:, 0:T], cos, ALU.mult, [kn.b, ropeb], [t1.b])
        TT("dve", t2.t[:, 0:T], p2.t[:, 0:T], sin, ALU.mult, [p2.b, ropeb], [t2.b])
        TT(pe_, outap, t1.t[:, 0:T], t2.t[:, 0:T], ALU.add, [t1.b, t2.b], [outb], partial=True)

    for l in range(2):
        W = lambda nm: Wg[(nm, l)]
        with P.phase():
            xr = Rot(P, "xt", 2, [128, D], F32)
            junk = Tl(P, "junk", [128, D], BF16)
            ssr = Rot(P, "ss", 2, [128, 2], F32)
            hT = Tl(P, "hT", [128, KD, 512], BF16)
            wr = Rot(P, "wblk", 2, [128, KD, 512], BF16)
            stg = Rot(P, "stg", 3, [128, 512], BF16)
            win = W("win")
            if l == 0:
                record_gathers([g for g in g_l0 if g[0] != "win"])
            for (tok0, T, isctx) in tiles():
                q = 2 * l + isctx
                for s in range(T // 128):
                    xt = xr.next()
                    ss = ssr.next()
                    DMA(xt.t[:], src_rows(l, tok0 + s * 128, 128), W=[xt.b])
                    op("dve", lambda e, ss=ss: e.memset(ss.t[:], 0.0), writes=[ss.b])
                    ACT(junk.t[:], xt.t[:], AF.Square, [xt.b, ss.b], [junk.b, ss.b], accum=ss.t[:, 0:1])
                    ACT(ss.t[:, 1:2], ss.t[:, 0:1], AF.Sqrt, [ss.b], [ss.b], bias=EPS, scale=1.0 / D)
                    RECIP(ss.t[:, 1:2], ss.t[:, 1:2], [ss.b], [ss.b])
                    TS("dve", xt.t[:], xt.t[:], ss.t[:, 1:2], None, ALU.mult, None, [xt.b, ss.b], [xt.b])
                    for k0 in range(0, KD, 4):
                        pt = psn()
                        for j in range(4):
                            kc = k0 + j
                            op("pe", lambda e, pt=pt, j=j, kc=kc, xt=xt: e.transpose(pt.t[:, j * 128:(j + 1) * 128], xt.t[:, kc * 128:(kc + 1) * 128], ident),
                               reads=[xt.b, cst.b], writes=[pt.b], partial=(j > 0))
                        for j in range(4):
                            kc = k0 + j
                            if j % 2 == 0:
                                TS("dve", hT.t[:, kc, s * 128:(s + 1) * 128], pt.t[:, j * 128:(j + 1) * 128], gsT.t[:, q, kc:kc + 1], shT.t[:, q, kc:kc + 1],
                                   ALU.mult, ALU.add, [pt.b, gsT.b, shT.b], [hT.b], partial=True)
                            else:
                                ACT(hT.t[:, kc, s * 128:(s + 1) * 128], pt.t[:, j * 128:(j + 1) * 128], AF.Identity, [pt.b, gsT.b, shT.b], [hT.b],
                                    bias=shT.t[:, q, kc:kc + 1], scale=gsT.t[:, q, kc:kc + 1], partial=True)
                blks = list(range(INW // 512))
                if l == 1 and isctx:
                    blks = [K_OFF // 512, VBLK]
                for blk in blks:
                    wt = wr.next()
                    DMA(wt.t[:].rearrange("p k n -> p (k n)"), win[blk * 128:(blk + 1) * 128, :], R=wbufs("win", l, blk * 128, (blk + 1) * 128), W=[wt.b])
                    if blk == VBLK:
                        for s in range(T // 128):
                            pv = psn()
                            for kc in range(KD):
                                MM(pv.t[:, 0:512], hT.t[:, kc, s * 128:(s + 1) * 128], wt.t[:, kc, :], kc == 0, kc == KD - 1, [hT.b, wt.b], [pv.b])
                            sg = stg.next()
                            ACT(sg.t[:], pv.t[:, 0:512], AF.Copy, [pv.b], [sg.b])
                            DMA(Vtm[tok0 + s * 128:tok0 + (s + 1) * 128, :], sg.t[:], R=[sg.b])
                    else:
                        fn = fam_func(blk)
                        for c in range(4):
                            pv = psn()
                            for kc in range(KD):
                                MM(pv.t[:, 0:T], wt.t[:, kc, c * 128:(c + 1) * 128], hT.t[:, kc, 0:T], kc == 0, kc == KD - 1, [hT.b, wt.b], [pv.b])
                            sg = stg.next()
                            ACT(sg.t[:, 0:T], pv.t[:, 0:T], fn, [pv.b], [sg.b])
                            r0 = (blk * 4 + c) * 128
                            DMA(pT[r0:r0 + 128, tok0:tok0 + T], sg.t[:, 0:T], R=[sg.b])

        with P.phase():
            rope = Tl(P, "rope", [128, 2, LT], F32)
            DMA(rope.t[:], rope_in.rearrange("c p t -> p c t"), W=[rope.b])
            kr = Rot(P, "kraw", 2, [128, 512], BF16)
            ko = Rot(P, "kout", 2, [128, 512], BF16)
            tmp = [Rot(P, "nr%d_" % i, 2, [128, 512], F32) for i in range(5)]
            gkb = Buf()
            for (tok0, T, isctx) in tiles():
                for g in range(4):
                    k = kr.next()
                    o = ko.next()
                    r0 = K_OFF + g * 128
                    DMA(k.t[:, 0:T], pT[r0:r0 + 128, tok0:tok0 + T], W=[k.b])
                    if isctx:
                        normrope(l, 1, k.t[:, 0:T], k.b, T, None, None, None, o.t[:, 0:T], o.b, tmp)
                    else:
                        t0 = tok0 - CT
                        normrope(l, 1, k.t[:, 0:T], k.b, T, rope.t[:, 0, t0:t0 + T], rope.t[:, 1, t0:t0 + T], rope.b, o.t[:, 0:T], o.b, tmp)
                    DMA(kTn[g * 128:(g + 1) * 128, tok0:tok0 + T], o.t[:, 0:T], R=[o.b])
                    if tok0 == CT:
                        DMA(gK_in[g * 128:(g + 1) * 128, 0:128], o.t[:, 0:128], R=[o.b], W=[gkb], partial=True)
                    if tok0 + T == NTOK:
                        DMA(gK_in[g * 128:(g + 1) * 128, 128:256], o.t[:, T - 128:T], R=[o.b], W=[gkb], partial=True)
            DMA(gV_in[0:128, :], Vtm[CT:CT + 128, :], W=[gkb], partial=True)
            DMA(gV_in[128:256, :], Vtm[NTOK - 128:NTOK, :], W=[gkb], partial=True)
            CC(gK_in, gK_out, RG, [gkb], [])
            CC(gV_in, gV_out, RG, [gkb], [])

        with P.phase():
            rope = Tl(P, "rope", [128, 2, LT], F32)
            DMA(rope.t[:], rope_in.rearrange("c p t -> p c t"), W=[rope.b])
            cpool = "dve" if l == 0 else "pool"
            if l == 0:
                record_gathers(g_l1[:n1])
            kTa = Tl(P, "kTa", [128, 4, LT + 256], BF16)
            kTc = Tl(P, "kTc", [128, 4, CT], BF16)
            Va = Tl(P, "Va", [128, NB + 4, 512], BF16)
            ek = Tl(P, "ek", [128, 4, 4, 256], BF16)
            ev = Tl(P, "ev", [128, 4, 2, 512], BF16)
            DMA(kTa.t[:, :, 128:128 + LT], kTn[:, CT:NTOK].rearrange("(g p) t -> p g t", p=128), W=[kTa.b])
            DMA(kTc.t[:], kTn[:, 0:CT].rearrange("(g p) t -> p g t", p=128), W=[kTc.b])
            DMA(Va.t[:, 1:NB + 1, :], Vtm[CT:NTOK, :].rearrange("(b p) n -> p b n", p=128), W=[Va.b])
            DMA(Va.t[:, NB + 2:NB + 4, :], Vtm[0:CT, :].rearrange("(b p) n -> p b n", p=128), W=[Va.b], partial=True)
            DMA(ek.t[:], gK_out.rearrange("(r g p) c -> p r g c", r=4, g=4), W=[ek.b])
            DMA(ev.t[:], gV_out.rearrange("(r e p) n -> p r e n", r=4, e=2), W=[ev.b])
            for side, dst_k, dst_v, ecol, erow, s0 in ((0, kTa.t[:, :, 0:128], Va.t[:, 0, :], slice(128, 256), 1, 0),
                                                       (1, kTa.t[:, :, 128 + LT:256 + LT], Va.t[:, NB + 1, :], slice(0, 128), 0, 4)):
                TS("dve", dst_k, ek.t[:, 0, :, ecol], selc(s0), None, ALU.mult, None, [ek.b, cst.b], [kTa.b], partial=True)
                TS(cpool, dst_v, ev.t[:, 0, erow, :], selc(s0), None, ALU.mult, None, [ev.b, cst.b], [Va.b], partial=True)
                for r in range(1, 4):
                    STT("dve", dst_k, ek.t[:, r, :, ecol], selc(s0 + r), dst_k, ALU.mult, ALU.add, [ek.b, cst.b, kTa.b], [kTa.b], partial=True)
                    STT("dve", dst_v, ev.t[:, r, erow, :], selc(s0 + r), dst_v, ALU.mult, ALU.add, [ev.b, cst.b, Va.b], [Va.b], partial=True)
            qraw = Tl(P, "qraw", [128, 16, 512], BF16)
            qTn = qraw
            agT = Tl(P, "agT", [128, 16, 512], BF16)
            ogT = Tl(P, "ogT", [128, 16, 512], BF16)
            tmp = [Rot(P, "nr%d_" % i, 2, [128, 512], F32) for i in range(5)]
            ptr = Rot(P, "PT", 3, [128, 512], BF16)
            rdr = Rot(P, "rden", 2, [128, 512], F32)
            onr = Rot(P, "on", 2, [128, 512], F32)
            scale = 128.0 ** -0.5
            for (tok0, T, isctx) in tiles():
                if isctx and l == 1:
                    continue
                DMA(qraw.t[:, :, 0:T], pT[0:2048, tok0:tok0 + T].rearrange("(h p) t -> p h t", p=128), W=[qraw.b])
                DMA(agT.t[:, :, 0:T], pT[AG_OFF:AG_OFF + 2048, tok0:tok0 + T].rearrange("(h p) t -> p h t", p=128), W=[agT.b])
                for h in range(16):
                    if isctx:
                        normrope(l, 0, qraw.t[:, h, 0:T], qraw.b, T, None, None, None, qTn.t[:, h, 0:T], qTn.b, tmp, cpool)
                    else:
                        t0 = tok0 - CT
                        normrope(l, 0, qraw.t[:, h, 0:T], qraw.b, T, rope.t[:, 0, t0:t0 + T], rope.t[:, 1, t0:t0 + T], rope.b, qTn.t[:, h, 0:T], qTn.b, tmp, cpool)
                for blk in range(T // 128):
                    qs = slice(blk * 128, (blk + 1) * 128)
                    for g in range(4):
                        chunks = []
                        if not isctx:
                            nb = (tok0 - CT) // 128 + blk
                            for d_, mi in ((0, 0), (1, None), (2, 1)):
                                m = mi
                                if d_ == 0 and nb == 0:
                                    m = 2
                                if d_ == 2 and nb == NB - 1:
                                    m = 3
                                cb = nb + d_
                                chunks.append((kTa.t[:, g, cb * 128:(cb + 1) * 128], kTa.b, Va.t[:, cb, g * 128:(g + 1) * 128], m))
                        for cc_ in range(2):
                            chunks.append((kTc.t[:, g, cc_ * 128:(cc_ + 1) * 128], kTc.b, Va.t[:, NB + 2 + cc_, g * 128:(g + 1) * 128], None))
                        pO = PS[4 + (psi[0] % 2)]
                        pD = PS[6 + (psi[0] % 2)]
                        psi[0] += 1
                        def fin(ci, pS, vap, m, last):
                            pt_ = ptr.next()
                            ACT(pt_.t[:], pS.t[:, 0:512], AF.Exp, [pS.b, negB.b], [pt_.b], bias=negB.t[:, l:l + 1], scale=scale)
                            if m is not None:
                                p3 = pt_.t[:].rearrange("p (h q) -> p h q", h=4)
                                TT("dve", p3, p3, mskbf.t[:, m, :].unsqueeze(1).to_broadcast([128, 4, 128]), ALU.mult, [pt_.b, mskbf.b], [pt_.b])
                            MM(pO.t[:, 0:512], vap, pt_.t[:], ci == 0, last, [Va.b, pt_.b], [pO.b])
                            MM(pD.t[:, 0:512], onesbf.t[:], pt_.t[:], ci == 0, False, [onesbf.b, pt_.b], [pD.b])

                        pend = None
                        for ci, (kap, kb_, vap, m) in enumerate(chunks):
                            pS = psn(0, 4)
                            MM(pS.t[:, 0:512].rearrange("p (h q) -> p h q", h=4), kap, qTn.t[:, 4 * g:4 * g + 4, qs], True, True, [kb_, qTn.b], [pS.b])
                            if pend is not None:
                                fin(*pend)
                            pend = (ci, pS, vap, m, ci == len(chunks) - 1)
                        fin(*pend)
                        MM(pD.t[:, 0:512], onesbf.t[0:1, :], sinkrow.t[0:1, l, g * 512:(g + 1) * 512], False, True, [onesbf.b, sinkrow.b], [pD.b])
                        rd = rdr.next()
                        on = onr.next()
                        RECIP(rd.t[:], pD.t[:, 0:512], [pD.b], [rd.b])
                        TT("dve", on.t[:], pO.t[:, 0:512], rd.t[:], ALU.mult, [pO.b, rd.b], [on.b])
                        TT(cpool, ogT.t[:, 4 * g:4 * g + 4, qs], on.t[:].rearrange("p (h q) -> p h q", h=4), agT.t[:, 4 * g:4 * g + 4, qs], ALU.mult,
                           [on.b, agT.b], [ogT.b], partial=True)
                DMA(aT[:, tok0:tok0 + T].rearrange("(h p) t -> p h t", p=128), ogT.t[:, :, 0:T], R=[ogT.b])

        with P.phase():
            segs = [(CT, LT, 0)] + ([(0, CT, 1)] if l == 0 else [])
            wpw = Tl(P, "wpw", [128, 8, 1024], BF16)
            DMA(wpw.t[:], W("wpw").rearrange("(k p) n -> p k n", p=128), W=[wpw.b])
            dww = Tl(P, "dww", [128, 8, 31], F32)
            cv3 = Tl(P, "cv3", [128, 8, 3], F32)
            DMA(dww.t[:], dww_in[l], W=[dww.b])
            DMA(cv3.t[:], cv3_in[l], W=[cv3.b])
            ar = Rot(P, "cva", 1, [128, 8, 512], BF16)
            br = Rot(P, "cvb", 1, [128, 8, 512], BF16)
            cgr = Rot(P, "cvg", 1, [128, 8, 512], BF16)
            dgc = Rot(P, "dgc", 1, [128, 31, 128], BF16)
            ybuf = Tl(P, "ybuf", [128, 8, 512], F32)
            ysq = Rot(P, "ysq", 2, [128, 512], F32)
            st4 = [Tl(P, "st%d" % i, [128, 512], F32) for i in range(4)]
            tdr = Rot(P, "td", 2, [128, 512], F32)
            zT = Tl(P, "zT", [128, 8, 512], BF16)
            cst_ = Rot(P, "cstg", 1, [128, 8, 512], BF16)
            for (s0, SL, isctx) in segs:
                uT = Tl(P, "uT%d" % isctx, [128, 8, SL + 32], BF16)
                TTs = min(512, SL)
                op("pool", lambda e, uT=uT: e.memset(uT.t[:], 0.0), writes=[uT.b])
                for t0 in range(0, SL, TTs):
                    a_, b_ = ar.next(), br.next()
                    DMA(a_.t[:, :, 0:TTs], pT[CA_OFF:CA_OFF + 1024, s0 + t0:s0 + t0 + TTs].rearrange("(c p) t -> p c t", p=128), W=[a_.b])
                    DMA(b_.t[:, :, 0:TTs], pT[CB_OFF:CB_OFF + 1024, s0 + t0:s0 + t0 + TTs].rearrange("(c p) t -> p c t", p=128), W=[b_.b])
                    TT("dve", uT.t[:, :, 16 + t0:16 + t0 + TTs], a_.t[:, :, 0:TTs], b_.t[:, :, 0:TTs], ALU.mult, [a_.b, b_.b], [uT.b], partial=True)
                if not isctx:
                    gub = Buf()
                    gob2 = Buf()
                    DMA(gU_in[:, 0:16].rearrange("(c p) e -> p c e", p=128), uT.t[:, :, 16:32], R=[uT.b], W=[gub], partial=True)
                    DMA(gU_in[:, 16:32].rearrange("(c p) e -> p c e", p=128), uT.t[:, :, SL:SL + 16], R=[uT.b], W=[gub], partial=True)
                    CC(gU_in, gU_out, RG, [gub], [gob2])
                    eu = Tl(P, "eu", [128, 4, 8, 32], BF16)
                    DMA(eu.t[:], gU_out.rearrange("(r c p) e -> p r c e", r=4, c=8), R=[gob2], W=[eu.b])
                    for dst, ecol, sb in ((uT.t[:, :, 0:16], slice(16, 32), 0), (uT.t[:, :, 16 + SL:32 + SL], slice(0, 16), 4)):
                        TS("dve", dst, eu.t[:, 0, :, ecol], selc(sb), None, ALU.mult, None, [eu.b, cst.b], [uT.b], partial=True)
                        for r in range(1, 4):
                            STT("dve", dst, eu.t[:, r, :, ecol], selc(sb + r), dst, ALU.mult, ALU.add, [eu.b, cst.b, uT.b], [uT.b], partial=True)
                for t0 in range(0, SL, TTs):
                    T = TTs
                    cg = cgr.next()
                    DMA(cg.t[:, :, 0:T], pT[CG_OFF:CG_OFF + 1024, s0 + t0:s0 + t0 + T].rearrange("(c p) t -> p c t", p=128), W=[cg.b])
                    p1, p2 = PS[0], PS[1]
                    for c in range(8):
                        dg = dgc.next()
                        TT("pool", dg.t[:], identbf.t[:].unsqueeze(1).to_broadcast([128, 31, 128]),
                           dww.t[:, c, :].unsqueeze(2).to_broadcast([128, 31, 128]), ALU.mult, [identbf.b, dww.b], [dg.b])
                        pc = psn(2, 6)
                        for j in range(31):
                            MM(pc.t[:, 0:T], dg.t[:, j, :], uT.t[:, c, t0 + j + 1:t0 + j + 1 + T], j == 0, j == 30, [dg.b, uT.b], [pc.b])
                        ACT(ybuf.t[:, c, 0:T], pc.t[:, 0:T], AF.Identity, [pc.b, cv3.b], [ybuf.b], bias=cv3.t[:, c, 0:1], partial=True)
                        yq = ysq.next()
                        ACT(yq.t[:, 0:T], ybuf.t[:, c, 0:T], AF.Square, [ybuf.b], [yq.b])
                        MM(p1.t[:, 0:T], ones32.t[:], ybuf.t[:, c, 0:T], c == 0, c == 7, [ones32.b, ybuf.b], [p1.b])
                        MM(p2.t[:, 0:T], ones32.t[:], yq.t[:, 0:T], c == 0, c == 7, [ones32.b, yq.b], [p2.b])
                    mean, ex2, var, rin = st4
                    ACT(mean.t[:, 0:T], p1.t[:, 0:T], AF.Copy, [p1.b], [mean.b], scale=1.0 / 1024)
                    ACT(ex2.t[:, 0:T], p2.t[:, 0:T], AF.Copy, [p2.b], [ex2.b], scale=1.0 / 1024)
                    TT("dve", var.t[:, 0:T], mean.t[:, 0:T], mean.t[:, 0:T], ALU.mult, [mean.b], [var.b])
                    TT("dve", var.t[:, 0:T], ex2.t[:, 0:T], var.t[:, 0:T], ALU.subtract, [ex2.b, var.b], [var.b])
                    ACT(rin.t[:, 0:T], var.t[:, 0:T], AF.Sqrt, [var.b], [rin.b], bias=EPS)
                    RECIP(rin.t[:, 0:T], rin.t[:, 0:T], [rin.b], [rin.b])
                    for c in range(8):
                        td = tdr.next()
                        TT("dve", td.t[:, 0:T], ybuf.t[:, c, 0:T], mean.t[:, 0:T], ALU.subtract, [ybuf.b, mean.b], [td.b])
                        TT("pool", td.t[:, 0:T], td.t[:, 0:T], rin.t[:, 0:T], ALU.mult, [td.b, rin.b], [td.b])
                        ACT(zT.t[:, c, 0:T], td.t[:, 0:T], AF.Silu, [td.b, cv3.b], [zT.b], bias=cv3.t[:, c, 2:3], scale=cv3.t[:, c, 1:2], partial=True)
                    cs_ = cst_.next()
                    for co in range(8):
                        pw_ = psn(6, 8)
                        for ci in range(8):
                            MM(pw_.t[:, 0:T], wpw.t[:, ci, co * 128:(co + 1) * 128], zT.t[:, ci, 0:T], ci == 0, ci == 7, [wpw.b, zT.b], [pw_.b])
                        TT("dve", cs_.t[:, co, 0:T], pw_.t[:, 0:T], cg.t[:, co, 0:T], ALU.mult, [pw_.b, cg.b], [cs_.b], partial=True)
                    DMA(cT[:, s0 + t0:s0 + t0 + T].rearrange("(c p) t -> p c t", p=128), cs_.t[:, :, 0:T], R=[cs_.b])

        with P.phase():
            wfm = Tl(P, "wfm", [128, 8, 256], F32)
            DMA(wfm.t[:], wfm_in[l].rearrange("g (k p) d -> p (g k) d", p=128), W=[wfm.b])
            CW = Tl(P, "CW", [128, 4, 2, 512], BF16)
            for g in range(4):
                for cs in range(2):
                    for mch in range(2):
                        pc = psn()
                        for k in range(2):
                            MM(pc.t[:, 0:256], dftc.t[:, cs, k, mch * 128:(mch + 1) * 128], wfm.t[:, g * 2 + k, :], k == 0, k == 1, [dftc.b, wfm.b], [pc.b])
                        ACT(CW.t[:, g, mch, cs * 256:(cs + 1) * 256], pc.t[:, 0:256], AF.Copy, [pc.b], [CW.b], partial=True)
            ufr = Rot(P, "uF", 2, [128, 8, 512], BF16)
            abr = Rot(P, "AB", 2, [128, 4, 512], BF16)
            gfb = Buf()
            for (tok0, T, isctx) in tiles():
                if isctx and l == 1:
                    continue
                uF = ufr.next()
                DMA(uF.t[:, :, 0:T], pT[F_OFF:F_OFF + 1024, tok0:tok0 + T].rearrange("(c p) t -> p c t", p=128), W=[uF.b])
                for s in range(T // 128):
                    ab = abr.next()
                    for g in range(4):
                        pa = psn()
                        for mch in range(2):
                            MM(pa.t[:, 0:512], uF.t[:, g * 2 + mch, s * 128:(s + 1) * 128], CW.t[:, g, mch, :], mch == 0, mch == 1, [uF.b, CW.b], [pa.b])
                        if g % 2 == 0:
                            ACT(ab.t[:, g, :], pa.t[:, 0:512], AF.Copy, [pa.b], [ab.b], partial=True)
                        else:
                            CP("dve", ab.t[:, g, :], pa.t[:, 0:512], [pa.b], [ab.b], partial=True)
                    if isctx:
                        DMA(gFc[s * 128:(s + 1) * 128, :], ab.t[:].rearrange("p g n -> p (g n)"), R=[ab.b])
                    else:
                        r0 = tok0 - CT + s * 128
                        DMA(gF_in[r0:r0 + 128, :], ab.t[:].rearrange("p g n -> p (g n)"), R=[ab.b], W=[gfb], partial=True)
            frc = min(LT, 256)
            for c in range(LT // frc):
                CC(gF_in[c * frc:(c + 1) * frc, :], gF_out[c * 4 * frc:(c + 1) * 4 * frc, :], RG, [gfb], [])

        with P.phase():
            gar = Rot(P, "ga", 3, [128, 2048], BF16)
            tcr = Rot(P, "tc", 3, [128, 512], BF16)
            tsr = Rot(P, "tsn", 3, [128, 512], BF16)
            fgr = Rot(P, "fg", 2, [128, 8, 512], BF16)
            fst = Rot(P, "fst", 2, [128, 8, 512], BF16)
            tcx = Tl(P, "tcx", [128, 2, 2, 256], F32)
            tcb = Tl(P, "tcb", [128, 2, 2, 256], BF16)
            DMA(tcx.t[:], dftctx_in.rearrange("c a p k -> p c a k"), W=[tcx.b])
            CP("dve", tcb.t[:], tcx.t[:], [tcx.b], [tcb.b])
            for (tok0, T, isctx) in tiles():
                if isctx and l == 1:
                    continue
                na = 2 if isctx else NA
                fg = fgr.next()
                DMA(fg.t[:, :, 0:T], pT[FG_OFF:FG_OFF + 1024, tok0:tok0 + T].rearrange("(c p) t -> p c t", p=128), W=[fg.b])
                for a in range(na):
                    ga = gar.next()
                    if isctx:
                        DMA(ga.t[:], gFc[a * 128:(a + 1) * 128, :], W=[ga.b])
                        tcap, tsap, tb1, tb2 = tcb.t[:, 0, a, :], tcb.t[:, 1, a, :], tcb.b, tcb.b
                    else:
                        t0 = tok0 - CT
                        DMA(ga.t[:], gF_out[a * 128:(a + 1) * 128, :], W=[ga.b])
                        tc_, ts_ = tcr.next(), tsr.next()
                        DMA(tc_.t[:, 0:T], tabC[a, :, t0:t0 + T], W=[tc_.b])
                        DMA(ts_.t[:, 0:T], tabS[a, :, t0:t0 + T], W=[ts_.b])
                        tcap, tsap, tb1, tb2 = tc_.t[:, 0:T], ts_.t[:, 0:T], tc_.b, ts_.b
                    for fc in range(8):
                        g, half = fc // 2, fc % 2
                        c0 = g * 512 + half * 128
                        MM(PS[fc].t[:, 0:T], ga.t[:, c0:c0 + 128], tcap, a == 0, False, [ga.b, tb1], [PS[fc].b])
                        MM(PS[fc].t[:, 0:T], ga.t[:, c0 + 256:c0 + 384], tsap, False, a == na - 1, [ga.b, tb2], [PS[fc].b])
                fs = fst.next()
                for fc in range(8):
                    TT("dve", fs.t[:, fc, 0:T], PS[fc].t[:, 0:T], fg.t[:, fc, 0:T], ALU.mult, [PS[fc].b, fg.b], [fs.b], partial=True)
                DMA(fT[:, tok0:tok0 + T].rearrange("(c p) t -> p c t", p=128), fs.t[:, :, 0:T], R=[fs.b])

        with P.phase():
            aTr = Rot(P, "aTt", 1, [128, 16, 512], BF16)
            fTr = Rot(P, "fTt", 1, [128, 8, 512], BF16)
            cTr = Rot(P, "cTt", 1, [128, 8, 512], BF16)
            war = Rot(P, "wa", 2, [128, 16, 512], BF16)
            wfr = Rot(P, "wf", 2, [128, 8, 512], BF16)
            wcr = Rot(P, "wc", 2, [128, 8, 512], BF16)
            mgr = Rot(P, "mg", 2, [128, 3, 4, 512], BF16)
            e1t = [Rot(P, "e1t%d" % i, 2, [128, 512], F32) for i in range(3)]
            mTt = Rot(P, "mTt", 2, [128, 4, 512], BF16)
            wau, wfu, wcu = W("wau"), W("wfu"), W("wcu")
            epool = "dve" if l == 0 else "pool"
            if l == 0:
                record_gathers(g_l1[n1:])
            for (tok0, T, isctx) in tiles():
                if isctx and l == 1:
                    continue
                at, ft, ct = aTr.next(), fTr.next(), cTr.next()
                DMA(at.t[:, :, 0:T], aT[:, tok0:tok0 + T].rearrange("(c p) t -> p c t", p=128), W=[at.b])
                DMA(ft.t[:, :, 0:T], fT[:, tok0:tok0 + T].rearrange("(c p) t -> p c t", p=128), W=[ft.b])
                DMA(ct.t[:, :, 0:T], cT[:, tok0:tok0 + T].rearrange("(c p) t -> p c t", p=128), W=[ct.b])
                for cb in range(D // 512):
                    cs = slice(cb * 512, (cb + 1) * 512)
                    wa, wf, wc, mg = war.next(), wfr.next(), wcr.next(), mgr.next()
                    DMA(wa.t[:].rearrange("p k n -> p (k n)"), wau[cb * 128:(cb + 1) * 128, :], W=[wa.b])
                    DMA(wf.t[:].rearrange("p k n -> p (k n)"), wfu[cb * 128:(cb + 1) * 128, :], W=[wf.b])
                    DMA(wc.t[:].rearrange("p k n -> p (k n)"), wcu[cb * 128:(cb + 1) * 128, :], W=[wc.b])
                    for br_ in range(3):
                        r0 = MG + br_ * D + cb * 512
                        DMA(mg.t[:, br_, :, 0:T], pT[r0:r0 + 512, tok0:tok0 + T].rearrange("(c p) t -> p c t", p=128), W=[mg.b], partial=(br_ > 0))
                    mt = mTt.next()
                    for dcl in range(4):
                        ds_ = slice(dcl * 128, (dcl + 1) * 128)
                        pa, pf, pc = psn(0, 3), psn(3, 6), psn(6, 8)
                        for k in range(16):
                            MM(pa.t[:, 0:T], wa.t[:, k, ds_], at.t[:, k, 0:T], k == 0, k == 15, [wa.b, at.b], [pa.b])
                        for k in range(8):
                            MM(pf.t[:, 0:T], wf.t[:, k, ds_], ft.t[:, k, 0:T], k == 0, k == 7, [wf.b, ft.b], [pf.b])
                        for k in range(8):
                            MM(pc.t[:, 0:T], wc.t[:, k, ds_], ct.t[:, k, 0:T], k == 0, k == 7, [wc.b, ct.b], [pc.b])
                        t0_, t1_, t2_ = e1t[0].next(), e1t[1].next(), e1t[2].next()
                        TT("dve", t0_.t[:, 0:T], pa.t[:, 0:T], mg.t[:, 0, dcl, 0:T], ALU.mult, [pa.b, mg.b], [t0_.b])
                        TT("dve", t1_.t[:, 0:T], pf.t[:, 0:T], mg.t[:, 1, dcl, 0:T], ALU.mult, [pf.b, mg.b], [t1_.b])
                        TT("dve", t2_.t[:, 0:T], pc.t[:, 0:T], mg.t[:, 2, dcl, 0:T], ALU.mult, [pc.b, mg.b], [t2_.b])
                        TT(epool, t0_.t[:, 0:T], t0_.t[:, 0:T], t1_.t[:, 0:T], ALU.add, [t0_.b, t1_.b], [t0_.b])
                        TT(epool, mt.t[:, dcl, 0:T], t0_.t[:, 0:T], t2_.t[:, 0:T], ALU.add, [t0_.b, t2_.b], [mt.b], partial=True)
                    DMA(mTd[cb * 512:(cb + 1) * 512, tok0:tok0 + T].rearrange("(c p) t -> p c t", p=128), mt.t[:, :, 0:T], R=[mt.b])

        with P.phase():
            mTr = Rot(P, "mT", 2, [128, KD, 512], BF16)
            wor = Rot(P, "wo", 2, [128, KD, 512], BF16)
            gtr = Rot(P, "gtb", 2, [128, 512], F32)
            xsr = Rot(P, "xs", 3, [128, 512], F32)
            e2t = Rot(P, "e2t", 2, [128, 512], F32)
            xor_ = Rot(P, "xo", 3, [128, 512], F32)
            wo = W("wo")
            epool = "pool"
            if l == 0:
                pass
            for (tok0, T, isctx) in tiles():
                if isctx and l == 1:
                    continue
                q = 2 * l + isctx
                mt = mTr.next()
                for k0 in range(0, KD, 8):
                    k1 = min(KD, k0 + 8)
                    DMA(mt.t[:, k0:k1, 0:T], mTd[k0 * 128:k1 * 128, tok0:tok0 + T].rearrange("(c p) t -> p c t", p=128), W=[mt.b], partial=(k0 > 0))
                for nb in range(D // 512):
                    cs = slice(nb * 512, (nb + 1) * 512)
                    wt = wor.next()
                    DMA(wt.t[:].rearrange("p k n -> p (k n)"), wo[nb * 128:(nb + 1) * 128, :], W=[wt.b])
                    gt = gtr.next()
                    DMA(gt.t[:], gtb_d[q, :, cs], W=[gt.b])
                    for s in range(T // 128):
                        xs = xsr.next()
                        DMA(xs.t[:], src_rows(l, tok0 + s * 128, 128)[:, cs], W=[xs.b])
                        po = psn()
                        for kc in range(KD):
                            MM(po.t[:, 0:512], mt.t[:, kc, s * 128:(s + 1) * 128], wt.t[:, kc, :], kc == 0, kc == KD - 1, [mt.b, wt.b], [po.b])
                        tt_ = e2t.next()
                        xo = xor_.next()
                        TT("dve", tt_.t[:], po.t[:, 0:512], gt.t[:], ALU.mult, [po.b, gt.b], [tt_.b])
                        TT(epool, xo.t[:], tt_.t[:], xs.t[:], ALU.add, [tt_.b, xs.b], [xo.b])
                        if l == 0:
                            DMA(x1[tok0 + s * 128:tok0 + (s + 1) * 128, cs], xo.t[:], R=[xo.b])
                        else:
                            r0 = tok0 - CT + s * 128
                            DMA(out[r0:r0 + 128, cs], xo.t[:], R=[xo.b])

    finish(P)
    P.top.close()
    return nc, P


def host_consts(cfg, r):
    LT, SEQ, NA = cfg.LT, cfg.SEQ, cfg.NA
    cst = np.zeros((128, 1040), np.float32)
    cst[:, 0:128] = np.eye(128, dtype=np.float32)
    R = np.zeros((128, 128), np.float32)
    for base in (0, 64):
        for i in range(32):
            R[base + i, base + i + 32] = -1.0
            R[base + i + 32, base + i] = 1.0
    cst[:, 128:256] = R.T
    jj = np.arange(128)[:, None]
    ii = np.arange(128)[None, :]
    cst[:, 256:384] = (ii <= jj).astype(np.float32)
    cst[:, 384:512] = (jj <= ii).astype(np.float32)
    if r > 0:
        cst[:, 512 + r - 1] = 1.0
        cst[:, 520] = 1.0
    if r < 3:
        cst[:, 516 + r + 1] = 1.0
        cst[:, 521] = 1.0
    for q in range(4):
        cst[q, 528 + q * 128:528 + (q + 1) * 128] = 1.0
    pos = np.arange(r * LT, (r + 1) * LT)
    row = (pos // 64).astype(np.float64)
    col = (pos % 64).astype(np.float64)
    inv = 10000.0 ** (-np.arange(0, 64, 2, dtype=np.float64) / 64)
    ar_, ac_ = row[:, None] * inv[None, :], col[:, None] * inv[None, :]
    ang = np.concatenate([ar_, ar_, ac_, ac_], -1).astype(np.float32)
    rope = np.stack([np.cos(ang).T, np.sin(ang).T]).astype(np.float32)
    k = pos.astype(np.float64)[None, :]
    p = np.arange(128, dtype=np.float64)[:, None]
    a = np.arange(NA, dtype=np.float64)[:, None]
    sc = 1.0 / np.sqrt(SEQ * 256.0)
    angB = 2 * np.pi * ((p * k) % SEQ) / SEQ
    frc = min(LT, 256)
    ai = np.arange(NA)
    m0 = ai * 128
    cc_, rr_, ii_ = m0 // (4 * frc), (m0 % (4 * frc)) // frc, m0 % frc
    l0 = rr_ * LT + cc_ * frc + ii_
    a = (l0 // 128).astype(np.float64)[:, None]
    angA = 2 * np.pi * ((128 * a * k) % SEQ) / SEQ
    dftB = np.stack([np.cos(angB), np.sin(angB)]).astype(np.float32)
    dftA = (np.stack([np.cos(angA), np.sin(angA)]) * sc).astype(np.float32)
    l_ = np.arange(256, dtype=np.float64)
    a256 = 2 * np.pi * np.outer(l_, l_) / 256
    scc = 1.0 / 256.0
    dftctx = np.stack([np.cos(a256) * scc, -np.sin(a256) * scc]).reshape(2, 2, 128, 256).astype(np.float32)
    dc = np.stack([np.cos(a256), np.sin(a256)])
    dftc = dc.reshape(2, 2, 128, 256).transpose(2, 0, 1, 3).astype(np.float32)
    return dict(cst=cst, rope=rope, dftB=dftB, dftA=dftA, dftctx=np.ascontiguousarray(dftctx), dftc=np.ascontiguousarray(dftc))


def make_in_maps(cfg, inp):
    D, LT, KD, MC = cfg.D, cfg.LT, cfg.KD, cfg.MC
    f = lambda a: np.ascontiguousarray(np.asarray(a, dtype=np.float32))
    x, c, ctx, c_ctx = f(inp["x"]), f(inp["c"]), f(inp["ctx"]), f(inp["c_ctx"])
    maps = []
    ngT = f(f(inp["norm_g"]).reshape(2, KD, 128).transpose(0, 2, 1))
    qg, kg = f(inp["q_norm_g"]), f(inp["k_norm_g"])
    qkg = f(np.stack([qg, kg], -1))
    qkrow = f(np.concatenate([qg, kg], -1).reshape(2, 1, 256))
    sink = f(f(inp["attn_sink"]).reshape(2, 1, 16))
    dww = f(f(inp["conv_dw_w"]).reshape(2, 31, 8, 128).transpose(0, 3, 2, 1))
    cv3 = f(np.stack([f(inp["conv_dw_b"]), f(inp["conv_ln_g"]), f(inp["conv_ln_b"])], -1).reshape(2, 8, 128, 3).transpose(0, 2, 1, 3))
    bmod = f(inp["b_mod"])
    wl = {"w_in": f(inp["w_in"]), "w_au": f(inp["w_attn_up"]), "w_fu": f(inp["w_fourier_up"]), "w_cu": f(inp["w_conv_up"]),
          "w_o": f(inp["w_out"]), "w_pw": f(inp["w_conv_pw"])}
    wl_g = {}
    for k_, w in wl.items():
        K_, N_ = w.shape[1], w.shape[2]
        if k_ == "w_pw":
            wl_g[k_] = w
        else:
            wl_g[k_] = np.ascontiguousarray(w.reshape(2, K_ // 128, 128, N_ // 512, 512).transpose(0, 3, 2, 1, 4)).reshape(2, N_ // 4, 4 * K_)
    wmod = f(inp["w_mod"])
    wfm = f(inp["w_fourier_mix"])
    for core in range(8):
        b, r = core // 4, core % 4
        m = host_consts(cfg, r)
        m["x"] = f(x[b, r * LT:(r + 1) * LT])
        m["ctx"] = f(ctx[b])
        m["cvT"] = f(np.stack([c[b], c_ctx], -1).reshape(KD, 128, 2).transpose(1, 0, 2))
        m["ngT"], m["qkg"], m["qkrow"], m["sink"], m["wfm"], m["dww"], m["cv3"] = ngT, qkg, qkrow, sink, wfm, dww, cv3
        for k_, w in wl.items():
            g = wl_g[k_]
            R_g, C_g = g.shape[1], g.shape[2]
            rc = wchunk(R_g, C_g)
            m[k_] = f(g.reshape(2, R_g // (4 * rc), 4, rc, C_g)[:, :, r].reshape(2, R_g // 4, C_g))
        m["w_mod"] = f(wmod[:, :, r * MC:(r + 1) * MC])
        m["b_mod"] = f(np.repeat(bmod[:, None, r * MC:(r + 1) * MC], 2, axis=1))
        maps.append(m)
    return maps


def kernel(**inputs):
    cfg = Cfg()
    nc, _ = build(cfg)
    maps = make_in_maps(cfg, inputs)
    res = run_bass_kernel_spmd(nc, maps, core_ids=list(range(8)))
    outp = np.empty((2, cfg.SEQ, cfg.D), np.float32)
    for core in range(8):
        b, r = core // 4, core % 4
        outp[b, r * cfg.LT:(r + 1) * cfg.LT] = res.results[core]["out"]
    return outp
```

```python
import contextlib
import numpy as np
import concourse.bass as bass
import concourse.mybir as mybir
from concourse.bass_utils import run_bass_kernel_spmd

F32 = mybir.dt.float32
BF16 = mybir.dt.bfloat16
AF = mybir.ActivationFunctionType
ALU = mybir.AluOpType
AX = mybir.AxisListType

ENGS = ("pe", "act", "dve", "pool", "sp")
DMAQ = ("sp", "act", "pool")
RING = 8
EPS = 1e-6


class Buf:
    __slots__ = ("writers", "readers")

    def __init__(self):
        self.writers = {}
        self.readers = {}


class Op:
    __slots__ = ("eng", "fn", "deps", "signal", "val", "dma", "slot", "key", "inc", "epoch")

    def __init__(self, eng, fn, dma=False):
        self.eng = eng
        self.fn = fn
        self.deps = []
        self.signal = False
        self.val = 0
        self.dma = dma
        self.slot = -1
        self.key = eng
        self.inc = 1
        self.epoch = 0


class Prog:
    def __init__(self, nc):
        self.nc = nc
        self.top = contextlib.ExitStack()
        self.scope = self.top
        st = self.top
        self.esem = {e: st.enter_context(nc.semaphore("s_" + e)) for e in ENGS}
        self.ring = {q: [st.enter_context(nc.semaphore("r_%s%d" % (q, i))) for i in range(RING)] for q in DMAQ}
        self.ccsem = st.enter_context(nc.semaphore("s_cc"))
        self.cnt = {e: 0 for e in ENGS}
        self.ndma = {q: 0 for q in DMAQ}
        self.ncc = 0
        self.epoch = 0
        self.ops = {e: [] for e in ENGS}
        self.waited = {e: {} for e in ENGS}
        self.nops = 0
        self.mk = self.sbuf("mk", [128, 8], F32)

    def sbuf(self, name, shape, dt):
        self.nalloc = getattr(self, "nalloc", 0) + 1
        return self.scope.enter_context(self.nc.sbuf_tensor("%s_%d" % (name, self.nalloc), list(shape), dt))

    def psum(self, name, shape, dt):
        return self.scope.enter_context(self.nc.psum_tensor(name, list(shape), dt))

    @contextlib.contextmanager
    def phase(self):
        old = self.scope
        with contextlib.ExitStack() as st:
            self.scope = st
            yield
            self.nphase = getattr(self, "nphase", 0) + 1
            import os as _os
            if self.nphase <= int(_os.environ.get("KSTOP", "999")):
                self.flush()
            else:
                self.ops = {e: [] for e in ENGS}
        self.scope = old

    def _add(self, op, reads, writes, partial):
        op.epoch = self.epoch
        deps = {}
        for b in reads:
            for w in b.writers.values():
                deps[id(w)] = w
        for b in writes:
            for w in b.readers.values():
                deps[id(w)] = w
            for w in b.writers.values():
                deps[id(w)] = w
        for d in deps.values():
            if d is op or d.epoch != self.epoch:
                continue
            if (not d.dma) and (not op.dma) and d.eng == op.eng and op.eng == "pe":
                continue
            d.signal = True
            op.deps.append(d)
        for b in reads:
            b.readers[op.key] = op
        for b in writes:
            if not partial:
                b.writers = {}
                b.readers = {}
            b.writers[op.key] = op
        self.ops[op.eng].append(op)
        self.nops += 1
        return op

    def op(self, eng, fn, reads=(), writes=(), partial=False):
        return self._add(Op(eng, fn), reads, writes, partial)

    def dma(self, q, fn, reads=(), writes=(), partial=False):
        o = Op(q, fn, dma=True)
        i = self.ndma[q]
        self.ndma[q] = i + 1
        o.slot = i % RING
        o.val = 16 * (i // RING + 1)
        o.inc = 16
        o.key = (q, o.slot)
        o.signal = True
        return self._add(o, reads, writes, partial)

    def cc(self, fn, reads=(), writes=()):
        o = Op("pool", fn, dma=True)
        self.ncc += 1
        o.slot = -2
        o.val = self.ncc
        o.inc = 1
        o.key = ("cc", 0)
        o.signal = True
        return self._add(o, reads, writes, False)

    def flush(self):
        nc = self.nc
        ops_snap = self.ops
        mval = {}
        for e in ENGS:
            c = self.cnt[e]
            for o in self.ops[e]:
                if not o.dma and o.signal:
                    c += 1
                    o.val = c
            mval[e] = c + 1
            self.cnt[e] = c + (0 if e == "pe" else 1)
        mk = self.mk

        def semof(o):
            if o.slot == -2:
                return self.ccsem
            if o.dma:
                return self.ring[o.eng][o.slot]
            return self.esem[o.eng]

        def run(e, eng):
            waited = self.waited[e]

            def wait(s, v):
                if waited.get(id(s), 0) < v:
                    eng.wait_ge(s, v)
                    waited[id(s)] = v

            last = {}
            for o in ops_snap[e]:
                for d in o.deps:
                    wait(semof(d), d.val)
                if o.slot == -2:
                    o.fn(eng).then_inc(self.ccsem, 1)
                    wait(self.ccsem, o.val)
                elif o.dma:
                    s = self.ring[e][o.slot]
                    if o.val > 16:
                        wait(s, o.val - 16)
                    o.fn(eng).then_inc(s, 16)
                    last[o.slot] = o
                else:
                    ins = o.fn(eng)
                    if o.signal:
                        ins.then_inc(self.esem[e], 1)
            for sl, o in last.items():
                wait(self.ring[e][sl], o.val)
            if e == "dve":
                m = eng.memset(mk[:, 0:1], 0.0)
            elif e == "pool":
                m = eng.memset(mk[:, 1:2], 0.0)
            elif e == "act":
                m = eng.memzero(mk[:, 2:3])
            elif e == "sp":
                m = eng.nop()
            else:
                m = None
            if m is not None:
                m.then_inc(self.esem[e], 1)
            for e2 in ENGS:
                if e2 != "pe":
                    wait(self.esem[e2], mval[e2])

        self.pending = getattr(self, 'pending', [])
        self.pending.append(run)

        self.epoch += 1
        self.ops = {e: [] for e in ENGS}


def finish(P):
    nc = P.nc
    with nc.Block() as block:
        @block.tensor
        def _(eng):
            for r in P.pending:
                r("pe", eng)

        @block.scalar
        def _(eng):
            for r in P.pending:
                r("act", eng)

        @block.vector
        def _(eng):
            for r in P.pending:
                r("dve", eng)

        @block.gpsimd
        def _(eng):
            for r in P.pending:
                r("pool", eng)

        @block.sync
        def _(eng):
            for r in P.pending:
                r("sp", eng)


class Tl:
    def __init__(self, P, name, shape, dt, psum=False):
        self.t = (P.psum if psum else P.sbuf)(name, shape, dt)
        self.b = Buf()


class Rot:
    def __init__(self, P, name, n, shape, dt):
        self.ts = [Tl(P, "%s%d" % (name, i), shape, dt) for i in range(n)]
        self.i = 0

    def next(self):
        t = self.ts[self.i % len(self.ts)]
        self.i += 1
        return t


class Cfg:
    def __init__(self, D=4096, SEQ=8192):
        self.D = D
        self.SEQ = SEQ
        self.KD = D // 128
        self.LT = SEQ // 4
        self.CT = 256
        self.NTOK = self.CT + self.LT
        self.MG = 10240
        self.INW = 10240 + 3 * D
        self.NA = SEQ // 128
        self.MC = 3 * D // 4
        self.TT = min(512, self.LT)


Q_OFF, K_OFF, V_OFF, AG_OFF, F_OFF, FG_OFF, CA_OFF, CB_OFF, CG_OFF = 0, 2048, 2560, 3072, 5120, 6144, 7168, 8192, 9216
VBLK = V_OFF // 512


def wchunk(K, N):
    n = K // 4
    for rc in range(n, 0, -1):
        if n % rc == 0 and rc * N * 2 <= (1 << 20):
            return rc


def gshape(nm, K, N):
    if nm == "wpw":
        return K, N
    return N // 4, 4 * K


def fam_func(blk):
    c = blk * 512
    if AG_OFF <= c < F_OFF or FG_OFF <= c < CA_OFF or CG_OFF <= c < 10240:
        return AF.Silu
    if CB_OFF <= c < CG_OFF or c >= 10240:
        return AF.Sigmoid
    return AF.Copy


def build(cfg, debug=False):
    D, KD, LT, CT, NTOK, INW, NA, MC, SEQ = cfg.D, cfg.KD, cfg.LT, cfg.CT, cfg.NTOK, cfg.INW, cfg.NA, cfg.MC, cfg.SEQ
    MG = cfg.MG
    NB = LT // 128
    nc = bass.Bass("TRN2", target_bir_lowering=False)

    def din(name, shape, dt=F32):
        return nc.dram_tensor(name, list(shape), dt, kind="ExternalInput").ap()

    def dscr(name, shape, dt=BF16, dbg=False):
        kind = "ExternalOutput" if (dbg and debug and dt == F32) else "Internal"
        return nc.dram_tensor(name, list(shape), dt, kind=kind).ap()

    x_in = din("x", [LT, D])
    ctx_in = din("ctx", [CT, D])
    cst_in = din("cst", [128, 1040])
    rope_in = din("rope", [2, 128, LT])
    dftB_in = din("dftB", [2, 128, LT])
    dftA_in = din("dftA", [2, NA, LT])
    dftctx_in = din("dftctx", [2, 2, 128, 256])
    dftc_in = din("dftc", [128, 2, 2, 256])
    cvT_in = din("cvT", [128, KD, 2])
    ngT_in = din("ngT", [2, 128, KD])
    qkg_in = din("qkg", [2, 128, 2])
    qkrow_in = din("qkrow", [2, 1, 256])
    sink_in = din("sink", [2, 1, 16])
    wfm_in = din("wfm", [2, 4, 256, 256])
    dww_in = din("dww", [2, 128, 8, 31])
    cv3_in = din("cv3", [2, 128, 8, 3])
    w_in_s = din("w_in", [2, INW // 16, 4 * D])
    w_au_s = din("w_au", [2, D // 16, 4 * 2048])
    w_fu_s = din("w_fu", [2, D // 16, 4 * 1024])
    w_cu_s = din("w_cu", [2, D // 16, 4 * 1024])
    w_o_s = din("w_o", [2, D // 16, 4 * D])
    w_pw_s = din("w_pw", [2, 256, 1024])
    w_mod_s = din("w_mod", [2, D, MC])
    b_mod_s = din("b_mod", [2, 2, MC])
    out = nc.dram_tensor("out", [LT, D], F32, kind="ExternalOutput").ap()

    wspec = [("win", D, INW, w_in_s), ("wau", 2048, D, w_au_s), ("wfu", 1024, D, w_fu_s),
             ("wcu", 1024, D, w_cu_s), ("wo", D, D, w_o_s), ("wpw", 1024, 1024, w_pw_s)]
    Wg = {}
    Wgin = {}
    for nm, K, N, _ in wspec:
        R_g, C_g = gshape(nm, K, N)
        for l in range(2):
            Wgin[(nm, l)] = dscr("gi_%s%d" % (nm, l), [R_g // 4, C_g])
            Wg[(nm, l)] = dscr("g_%s%d" % (nm, l), [R_g, C_g])
    pT = dscr("pT", [INW, NTOK], dbg=True)
    Vtm = dscr("Vtm", [NTOK, 512], dbg=True)
    kTn = dscr("kTn", [512, NTOK], dbg=True)
    gK_in = dscr("gK_in", [512, 256]); gK_out = dscr("gK_out", [4 * 512, 256])
    gV_in = dscr("gV_in", [256, 512]); gV_out = dscr("gV_out", [4 * 256, 512])
    gU_in = dscr("gU_in", [1024, 32]); gU_out = dscr("gU_out", [4 * 1024, 32])
    gF_in = dscr("gF_in", [LT, 2048]); gF_out = dscr("gF_out", [SEQ, 2048])
    gFc = dscr("gFc", [CT, 2048])
    tabC = dscr("tabC", [NA, 128, LT]); tabS = dscr("tabS", [NA, 128, LT])
    aT = dscr("aT", [2048, NTOK], dbg=True)
    fT = dscr("fT", [1024, NTOK], dbg=True)
    cT = dscr("cT", [1024, NTOK], dbg=True)
    mTd = dscr("mTd", [D, NTOK], dbg=True)
    x1 = dscr("x1", [NTOK, D], F32, dbg=True)
    gmod_in = dscr("gmod_in", [4, MC], F32); gmod_out = dscr("gmod_out", [16, MC], F32)
    gtb_d = dscr("gtb_d", [4, 128, D], F32)
    RG = [[0, 1, 2, 3], [4, 5, 6, 7]]
    RG8 = [list(range(8))]

    import os as _os2
    KSUB = int(_os2.environ.get('KSUB', '99'))
    P = Prog(nc)
    op, dma = P.op, P.dma

    def MM(o, lt, rh, start, stop, R, W):
        op("pe", lambda e: e.matmul(o, lhsT=lt, rhs=rh, start=start, stop=stop), reads=R, writes=W, partial=not start)

    def ACT(o, i, func, R, W, bias=None, scale=None, accum=None, partial=False):
        kw = {}
        if bias is not None:
            kw["bias"] = bias
        if scale is not None:
            kw["scale"] = scale
        if accum is not None:
            kw["accum_out"] = accum
        op("act", lambda e: e.activation(out=o, in_=i, func=func, **kw), reads=R, writes=W, partial=partial)

    def TT(eng, o, a, b, aop, R, W, partial=False):
        op(eng, lambda e: e.tensor_tensor(out=o, in0=a, in1=b, op=aop), reads=R, writes=W, partial=partial)

    def TS(eng, o, a, s1, s2, op0, op1, R, W, partial=False):
        if s2 is None:
            op(eng, lambda e: e.tensor_scalar(out=o, in0=a, scalar1=s1, scalar2=None, op0=op0), reads=R, writes=W, partial=partial)
        else:
            op(eng, lambda e: e.tensor_scalar(out=o, in0=a, scalar1=s1, scalar2=s2, op0=op0, op1=op1), reads=R, writes=W, partial=partial)

    def STT(eng, o, a, s, b, op0, op1, R, W, partial=False):
        op(eng, lambda e: e.scalar_tensor_tensor(out=o, in0=a, scalar=s, in1=b, op0=op0, op1=op1), reads=R, writes=W, partial=partial)

    def CP(eng, o, i, R, W, partial=False):
        op(eng, lambda e: e.tensor_copy(out=o, in_=i), reads=R, writes=W, partial=partial)

    def RECIP(o, i, R, W):
        op("dve", lambda e: e.reciprocal(out=o, in_=i), reads=R, writes=W)

    def DMA(o, i, R=(), W=(), q="sp", partial=False):
        if q == "sp" and str(o.space).endswith("DRAM") and not str(i.space).endswith("DRAM"):
            q = "act"
        dma(q, lambda e: e.dma_start(out=o, in_=i), reads=R, writes=W, partial=partial)

    def CC(i, o, groups, R, W):
        P.cc(lambda e: e.collective_compute("AllGather", ALU.bypass, replica_groups=groups, ins=[i], outs=[o]), reads=R, writes=W)

    cst = Tl(P, "cst", [128, 1040], F32)
    ident = cst.t[:, 0:128]
    rotT = cst.t[:, 128:256]
    selc = lambda i: cst.t[:, 512 + i:513 + i]
    selm = cst.t[0:4, 528:1040]
    ones32 = Tl(P, "ones32", [128, 128], F32)
    onesbf = Tl(P, "onesbf", [128, 128], BF16)
    identbf = Tl(P, "identbf", [128, 128], BF16)
    mskbf = Tl(P, "mskbf", [128, 4, 128], BF16)
    gsT = Tl(P, "gsT", [128, 4, KD], F32)
    shT = Tl(P, "shT", [128, 4, KD], F32)
    qkg = Tl(P, "qkg", [128, 2, 2], F32)
    negB = Tl(P, "negB", [128, 2], F32)
    sinkrow = Tl(P, "sinkrow", [1, 2, 2048], BF16)
    dftc = Tl(P, "dftc", [128, 2, 2, 256], F32)
    PS = [Tl(P, "ps%d" % i, [128, 512], F32, psum=True) for i in range(8)]
    psi = [0]

    def psn(lo=0, hi=8):
        t = PS[lo + psi[0] % (hi - lo)]
        psi[0] += 1
        return t

    with P.phase():
        DMA(cst.t[:], cst_in, W=[cst.b])
        DMA(qkg.t[:], qkg_in.rearrange("l p k -> p l k"), W=[qkg.b])
        DMA(dftc.t[:], dftc_in, W=[dftc.b])
        op("dve", lambda e: e.memset(ones32.t[:], 1.0), writes=[ones32.b])
        op("pool", lambda e: e.memset(onesbf.t[:], 1.0), writes=[onesbf.b])
        CP("dve", identbf.t[:], ident, [cst.b], [identbf.b])
        CP("dve", mskbf.t[:, 0, :], cst.t[:, 256:384], [cst.b], [mskbf.b], partial=True)
        CP("dve", mskbf.t[:, 1, :], cst.t[:, 384:512], [cst.b], [mskbf.b], partial=True)
        TS("dve", mskbf.t[:, 2, :], cst.t[:, 256:384], selc(8), None, ALU.mult, None, [cst.b], [mskbf.b], partial=True)
        TS("dve", mskbf.t[:, 3, :], cst.t[:, 384:512], selc(9), None, ALU.mult, None, [cst.b], [mskbf.b], partial=True)
        qkrow = Tl(P, "qkrow", [1, 2, 256], F32)
        sk = Tl(P, "sk", [1, 2, 16], F32)
        mx = Tl(P, "mx", [1, 8], F32)
        DMA(qkrow.t[:], qkrow_in.rearrange("l o k -> o l k"), W=[qkrow.b])
        DMA(sk.t[:], sink_in.rearrange("l o k -> o l k"), W=[sk.b])
        for l in range(2):
            for j in range(2):
                op("dve", lambda e, l=l, j=j: e.reduce_max(out=mx.t[0:1, 2 * l + j:2 * l + j + 1], in_=qkrow.t[0:1, l, j * 128:(j + 1) * 128],
                                                         axis=AX.X, apply_absolute_value=True), reads=[qkrow.b], writes=[mx.b], partial=True)
            TT("dve", mx.t[0:1, 4 + l:5 + l], mx.t[0:1, 2 * l:2 * l + 1], mx.t[0:1, 2 * l + 1:2 * l + 2], ALU.mult, [mx.b], [mx.b], partial=True)
            pb = psn()
            MM(pb.t[:, 0:1], ones32.t[0:1, :], mx.t[0:1, 4 + l:5 + l], True, True, [ones32.b, mx.b], [pb.b])
            ACT(negB.t[:, l:l + 1], pb.t[:, 0:1], AF.Copy, [pb.b], [negB.b], scale=-(128.0 ** 0.5), partial=True)
            es = Tl(P, "es%d" % l, [1, 16], F32)
            ACT(es.t[:], sk.t[0:1, l, :], AF.Exp, [sk.b, negB.b], [es.b], bias=negB.t[0:1, l:l + 1])
            for h in range(16):
                TS("dve", sinkrow.t[0:1, l, h * 128:(h + 1) * 128], ones32.t[0:1, :], es.t[0:1, h:h + 1], None, ALU.mult, None,
                   [ones32.b, es.b], [sinkrow.b], partial=True)

    with P.phase():
        gb = Buf()
        for nm, K, N, src in wspec:
            for l in range(2):
                DMA(Wgin[(nm, l)], src[l], W=[gb], q="pool", partial=True)
    chunkbuf = {}

    def gather_list():
        order = [("win", 0)] + [(nm, 0) for nm in ("wpw", "wau", "wfu", "wcu", "wo")] + [(nm, 1) for nm in ("win", "wpw", "wau", "wfu", "wcu", "wo")]
        dims = {nm: gshape(nm, K, N) for nm, K, N, _ in wspec}
        out_ = []
        for nm, l in order:
            R_g, C_g = dims[nm]
            rc = wchunk(R_g, C_g)
            for c in range((R_g // 4) // rc):
                out_.append((nm, l, c, rc))
        return out_

    glist = gather_list()
    g_l0 = [g for g in glist if g[1] == 0]
    g_l1 = [g for g in glist if g[1] == 1]
    n1 = len([g for g in g_l1 if g[0] == "win"]) * 10 // 11

    def record_gathers(items):
        for nm, l, c, rc in items:
            b = Buf()
            chunkbuf[(nm, l, c)] = b
            CC(Wgin[(nm, l)][c * rc:(c + 1) * rc, :], Wg[(nm, l)][c * 4 * rc:(c + 1) * 4 * rc, :], RG, [], [b])

    def wbufs(nm, l, r0, r1):
        K_, N_ = [(K, N) for n_, K, N, _ in wspec if n_ == nm][0]
        R_g, C_g = gshape(nm, K_, N_)
        rc4 = 4 * wchunk(R_g, C_g)
        return [chunkbuf[(nm, l, c)] for c in range(r0 // rc4, (r1 - 1) // rc4 + 1)]

    with P.phase():
        cvT = Tl(P, "cvT", [128, KD, 2], F32)
        scT = Tl(P, "scT", [128, KD, 2], F32)
        bm = Tl(P, "bm", [2, 2, MC], F32)
        mrow = Tl(P, "mrow", [2, 2, MC], F32)
        DMA(cvT.t[:], cvT_in, W=[cvT.b])
        DMA(bm.t[:], b_mod_s.rearrange("l q m -> q l m"), W=[bm.b])
        ACT(scT.t[:], cvT.t[:], AF.Silu, [cvT.b], [scT.b])
        wmr = Rot(P, "wm", 3, [128, MC], F32)
        nch = [(o, min(512, MC - o)) for o in range(0, MC, 512)]
        gmb = Buf()
        for l in range(2):
            pss = [PS[i] for i in range(len(nch))]
            for kc in range(KD):
                wm = wmr.next()
                DMA(wm.t[:], w_mod_s[l, kc * 128:(kc + 1) * 128, :], W=[wm.b])
                for i, (o, n) in enumerate(nch):
                    MM(pss[i].t[0:2, 0:n], scT.t[:, kc, :], wm.t[:, o:o + n], kc == 0, kc == KD - 1, [scT.b, wm.b], [pss[i].b])
            for i, (o, n) in enumerate(nch):
                TT("dve", mrow.t[0:2, l, o:o + n], pss[i].t[0:2, 0:n], bm.t[0:2, l, o:o + n], ALU.add, [pss[i].b, bm.b], [mrow.b], partial=True)
            DMA(gmod_in[2 * l:2 * l + 2, :], mrow.t[0:2, l, :], R=[mrow.b], W=[gmb], partial=True)
        gob = Buf()
        if KSUB >= 1:
            CC(gmod_in, gmod_out, RG, [gmb], [gob])
        R_ = Tl(P, "Rr", [4, 3 * D], F32)
        if KSUB >= 2:
          DMA(R_.t[:].rearrange("q (r j) -> q r j", r=4), gmod_out.rearrange("(r q) j -> q r j", q=4), R=[gob], W=[R_.b])
        modT = Tl(P, "modT", [128, 2 * KD, 4], F32)
        for c0 in (range(0, 2 * KD, 64) if KSUB >= 3 else []):
            pm = psn()
            n = min(64, 2 * KD - c0)
            for c in range(n):
                op("pe", lambda e, c=c, c0=c0, pm=pm: e.transpose(pm.t[:, c * 4:c * 4 + 4], R_.t[0:4, (c0 + c) * 128:(c0 + c + 1) * 128], cst.t[0:4, 0:4]),
                   reads=[R_.b, cst.b], writes=[pm.b], partial=(c > 0))
            CP("dve", modT.t[:, c0:c0 + n, :], pm.t[:, 0:4 * n].rearrange("p (c q) -> p c q", q=4), [pm.b], [modT.b], partial=True)
        ngT = Tl(P, "ngT", [128, 2, KD], F32)
        DMA(ngT.t[:], ngT_in.rearrange("l p k -> p l k"), W=[ngT.b])
        for q in (range(4) if KSUB >= 4 else []):
            STT("dve", gsT.t[:, q, :], modT.t[:, KD:2 * KD, q], 1.0, ngT.t[:, q // 2, :], ALU.add, ALU.mult, [modT.b, ngT.b], [gsT.b], partial=True)
            CP("dve", shT.t[:, q, :], modT.t[:, 0:KD, q], [modT.b], [shT.b], partial=True)
        gst = Rot(P, "gst", 2, [128, 512], F32)
        for q in (range(4) if KSUB >= 5 else []):
            for nb in range(D // 512):
                pg = psn()
                MM(pg.t[:, 0:512], selm[:, q * 128:(q + 1) * 128], R_.t[0:4, 2 * D + nb * 512:2 * D + (nb + 1) * 512], True, True, [cst.b, R_.b], [pg.b])
                g = gst.next()
                ACT(g.t[:], pg.t[:, 0:512], AF.Copy, [pg.b], [g.b])
                DMA(gtb_d[q, :, nb * 512:(nb + 1) * 512], g.t[:], R=[g.b])

    with P.phase():
        CB = Tl(P, "CB", [128, LT], F32)
        SB = Tl(P, "SB", [128, LT], F32)
        DMA(CB.t[:], dftB_in[0], W=[CB.b])
        DMA(SB.t[:], dftB_in[1], W=[SB.b])
        car = Rot(P, "ca", 2, [128, LT], F32)
        sar = Rot(P, "sa", 2, [128, LT], F32)
        t1r = Rot(P, "t1", 2, [128, LT], F32)
        t2r = Rot(P, "t2", 2, [128, LT], F32)
        ocr = Rot(P, "oc", 2, [128, LT], BF16)
        osr = Rot(P, "os", 2, [128, LT], BF16)
        g_w0 = [g for g in g_l0 if g[0] == "win"]
        record_gathers(g_w0)
        for a in range(NA):
            ca, sa, t1, t2, oc, os_ = car.next(), sar.next(), t1r.next(), t2r.next(), ocr.next(), osr.next()
            DMA(ca.t[:], dftA_in[0, a:a + 1, :].partition_broadcast(128), W=[ca.b])
            DMA(sa.t[:], dftA_in[1, a:a + 1, :].partition_broadcast(128), W=[sa.b])
            TT("dve", t1.t[:], ca.t[:], CB.t[:], ALU.mult, [ca.b, CB.b], [t1.b])
            TT("dve", t2.t[:], sa.t[:], SB.t[:], ALU.mult, [sa.b, SB.b], [t2.b])
            TT("dve", oc.t[:], t1.t[:], t2.t[:], ALU.subtract, [t1.b, t2.b], [oc.b])
            DMA(tabC[a], oc.t[:], R=[oc.b])
            TT("dve", t2.t[:], sa.t[:], CB.t[:], ALU.mult, [sa.b, CB.b], [t2.b])
            TT("dve", t1.t[:], ca.t[:], SB.t[:], ALU.mult, [ca.b, SB.b], [t1.b])
            STT("dve", os_.t[:], t2.t[:], -1.0, t1.t[:], ALU.mult, ALU.subtract, [t1.b, t2.b], [os_.b])
            DMA(tabS[a], os_.t[:], R=[os_.b])

    def src_rows(l, tok0, n):
        if l == 1:
            return x1[tok0:tok0 + n, :]
        if tok0 < CT:
            return ctx_in[tok0:tok0 + n, :]
        return x_in[tok0 - CT:tok0 - CT + n, :]

    def tiles():
        ts = [(0, CT, 1)]
        for t0 in range(0, LT, cfg.TT):
            ts.append((CT + t0, cfg.TT, 0))
        return ts

    def normrope(l, which, src, srcb, T, cos, sin, ropeb, outap, outb, tmp, pe_="pool"):
        sq, rs, kn, t1, t2 = [r_.next() for r_ in tmp]
        ACT(sq.t[:, 0:T], src, AF.Square, [srcb], [sq.b])
        p1 = psn(0, 4)
        MM(p1.t[:, 0:T], onesbf.t[:], sq.t[:, 0:T], True, True, [onesbf.b, sq.b], [p1.b])
        ACT(rs.t[:, 0:T], p1.t[:, 0:T], AF.Ln, [p1.b], [rs.b], bias=EPS, scale=1.0 / 128)
        ACT(rs.t[:, 0:T], rs.t[:, 0:T], AF.Exp, [rs.b], [rs.b], scale=-0.5)
        STT("dve", kn.t[:, 0:T], src, qkg.t[:, l, which:which + 1], rs.t[:, 0:T], ALU.mult, ALU.mult, [srcb, qkg.b, rs.b], [kn.b])
        if cos is None:
            CP(pe_, outap, kn.t[:, 0:T], [kn.b], [outb], partial=True)
            return
        p2 = psn(0, 4)
        MM(p2.t[:, 0:T], rotT, kn.t[:, 0:T], True, True, [cst.b, kn.b], [p2.b])
        TT(pe_, t1.t[:, 0:T], kn.t[:, 0:T], cos, ALU.mult, [kn.b, ropeb], [t1.b])
        TT("dve", t2.t[:, 0:T], p2.t[:, 0:T], sin, ALU.mult, [p2.b, ropeb], [t2.b])
        TT("dve", outap, t1.t[:, 0:T], t2.t[:, 0:T], ALU.add, [t1.b, t2.b], [outb], partial=True)

    for l in range(2):
        W = lambda nm: Wg[(nm, l)]
        with P.phase():
            xr = Rot(P, "xt", 2, [128, D], F32)
            junk = Tl(P, "junk", [128, D], BF16)
            ssr = Rot(P, "ss", 2, [128, 2], F32)
            hT = Tl(P, "hT", [128, KD, 512], BF16)
            wr = Rot(P, "wblk", 2, [128, KD, 512], BF16)
            stg = Rot(P, "stg", 3, [128, 512], BF16)
            win = W("win")
            if l == 0:
                record_gathers([g for g in g_l0 if g[0] != "win"])
            for (tok0, T, isctx) in tiles():
                q = 2 * l + isctx
                for s in range(T // 128):
                    xt = xr.next()
                    ss = ssr.next()
                    DMA(xt.t[:], src_rows(l, tok0 + s * 128, 128), W=[xt.b])
                    op("dve", lambda e, ss=ss: e.memset(ss.t[:], 0.0), writes=[ss.b])
                    ACT(junk.t[:], xt.t[:], AF.Square, [xt.b, ss.b], [junk.b, ss.b], accum=ss.t[:, 0:1])
                    ACT(ss.t[:, 1:2], ss.t[:, 0:1], AF.Sqrt, [ss.b], [ss.b], bias=EPS, scale=1.0 / D)
                    RECIP(ss.t[:, 1:2], ss.t[:, 1:2], [ss.b], [ss.b])
                    TS("dve", xt.t[:], xt.t[:], ss.t[:, 1:2], None, ALU.mult, None, [xt.b, ss.b], [xt.b])
                    for k0 in range(0, KD, 4):
                        pt = psn()
                        for j in range(4):
                            kc = k0 + j
                            op("pe", lambda e, pt=pt, j=j, kc=kc, xt=xt: e.transpose(pt.t[:, j * 128:(j + 1) * 128], xt.t[:, kc * 128:(kc + 1) * 128], ident),
                               reads=[xt.b, cst.b], writes=[pt.b], partial=(j > 0))
                        for j in range(4):
                            kc = k0 + j
                            if j % 2 == 0:
                                TS("dve", hT.t[:, kc, s * 128:(s + 1) * 128], pt.t[:, j * 128:(j + 1) * 128], gsT.t[:, q, kc:kc + 1], shT.t[:, q, kc:kc + 1],
                                   ALU.mult, ALU.add, [pt.b, gsT.b, shT.b], [hT.b], partial=True)
                            else:
                                ACT(hT.t[:, kc, s * 128:(s + 1) * 128], pt.t[:, j * 128:(j + 1) * 128], AF.Identity, [pt.b, gsT.b, shT.b], [hT.b],
                                    bias=shT.t[:, q, kc:kc + 1], scale=gsT.t[:, q, kc:kc + 1], partial=True)
                blks = list(range(INW // 512))
                if l == 1 and isctx:
                    blks = [K_OFF // 512, VBLK]
                for blk in blks:
                    wt = wr.next()
                    DMA(wt.t[:].rearrange("p k n -> p (k n)"), win[blk * 128:(blk + 1) * 128, :], R=wbufs("win", l, blk * 128, (blk + 1) * 128), W=[wt.b])
                    if blk == VBLK:
                        for s in range(T // 128):
                            pv = psn()
                            for kc in range(KD):
                                MM(pv.t[:, 0:512], hT.t[:, kc, s * 128:(s + 1) * 128], wt.t[:, kc, :], kc == 0, kc == KD - 1, [hT.b, wt.b], [pv.b])
                            sg = stg.next()
                            ACT(sg.t[:], pv.t[:, 0:512], AF.Copy, [pv.b], [sg.b])
                            DMA(Vtm[tok0 + s * 128:tok0 + (s + 1) * 128, :], sg.t[:], R=[sg.b])
                    else:
                        fn = fam_func(blk)
                        for c in range(4):
                            pv = psn()
                            for kc in range(KD):
                                MM(pv.t[:, 0:T], wt.t[:, kc, c * 128:(c + 1) * 128], hT.t[:, kc, 0:T], kc == 0, kc == KD - 1, [hT.b, wt.b], [pv.b])
                            sg = stg.next()
                            ACT(sg.t[:, 0:T], pv.t[:, 0:T], fn, [pv.b], [sg.b])
                            r0 = (blk * 4 + c) * 128
                            DMA(pT[r0:r0 + 128, tok0:tok0 + T], sg.t[:, 0:T], R=[sg.b])

        with P.phase():
            rope = Tl(P, "rope", [128, 2, LT], F32)
            DMA(rope.t[:], rope_in.rearrange("c p t -> p c t"), W=[rope.b])
            kr = Rot(P, "kraw", 2, [128, 512], BF16)
            ko = Rot(P, "kout", 2, [128, 512], BF16)
            tmp = [Rot(P, "nr0_", 2, [128, 512], BF16)] + [Rot(P, "nr%d_" % i, 2, [128, 512], F32) for i in range(1, 5)]
            gkb = Buf()
            for (tok0, T, isctx) in tiles():
                for g in range(4):
                    k = kr.next()
                    o = ko.next()
                    r0 = K_OFF + g * 128
                    DMA(k.t[:, 0:T], pT[r0:r0 + 128, tok0:tok0 + T], W=[k.b])
                    if isctx:
                        normrope(l, 1, k.t[:, 0:T], k.b, T, None, None, None, o.t[:, 0:T], o.b, tmp)
                    else:
                        t0 = tok0 - CT
                        normrope(l, 1, k.t[:, 0:T], k.b, T, rope.t[:, 0, t0:t0 + T], rope.t[:, 1, t0:t0 + T], rope.b, o.t[:, 0:T], o.b, tmp)
                    DMA(kTn[g * 128:(g + 1) * 128, tok0:tok0 + T], o.t[:, 0:T], R=[o.b])
                    if tok0 == CT:
                        DMA(gK_in[g * 128:(g + 1) * 128, 0:128], o.t[:, 0:128], R=[o.b], W=[gkb], partial=True)
                    if tok0 + T == NTOK:
                        DMA(gK_in[g * 128:(g + 1) * 128, 128:256], o.t[:, T - 128:T], R=[o.b], W=[gkb], partial=True)
            DMA(gV_in[0:128, :], Vtm[CT:CT + 128, :], W=[gkb], partial=True)
            DMA(gV_in[128:256, :], Vtm[NTOK - 128:NTOK, :], W=[gkb], partial=True)
            CC(gK_in, gK_out, RG, [gkb], [])
            CC(gV_in, gV_out, RG, [gkb], [])

        with P.phase():
            rope = Tl(P, "rope", [128, 2, LT], F32)
            DMA(rope.t[:], rope_in.rearrange("c p t -> p c t"), W=[rope.b])
            cpool = "dve" if l == 0 else "pool"
            if l == 0:
                record_gathers(g_l1[:n1])
            kTa = Tl(P, "kTa", [128, 4, LT + 256], BF16)
            kTc = Tl(P, "kTc", [128, 4, CT], BF16)
            Va = Tl(P, "Va", [128, NB + 4, 512], BF16)
            ek = Tl(P, "ek", [128, 4, 4, 256], BF16)
            ev = Tl(P, "ev", [128, 4, 2, 512], BF16)
            DMA(kTa.t[:, :, 128:128 + LT], kTn[:, CT:NTOK].rearrange("(g p) t -> p g t", p=128), W=[kTa.b])
            DMA(kTc.t[:], kTn[:, 0:CT].rearrange("(g p) t -> p g t", p=128), W=[kTc.b])
            DMA(Va.t[:, 1:NB + 1, :], Vtm[CT:NTOK, :].rearrange("(b p) n -> p b n", p=128), W=[Va.b])
            DMA(Va.t[:, NB + 2:NB + 4, :], Vtm[0:CT, :].rearrange("(b p) n -> p b n", p=128), W=[Va.b], partial=True)
            DMA(ek.t[:], gK_out.rearrange("(r g p) c -> p r g c", r=4, g=4), W=[ek.b])
            DMA(ev.t[:], gV_out.rearrange("(r e p) n -> p r e n", r=4, e=2), W=[ev.b])
            for side, dst_k, dst_v, ecol, erow, s0 in ((0, kTa.t[:, :, 0:128], Va.t[:, 0, :], slice(128, 256), 1, 0),
                                                       (1, kTa.t[:, :, 128 + LT:256 + LT], Va.t[:, NB + 1, :], slice(0, 128), 0, 4)):
                TS("dve", dst_k, ek.t[:, 0, :, ecol], selc(s0), None, ALU.mult, None, [ek.b, cst.b], [kTa.b], partial=True)
                TS(cpool, dst_v, ev.t[:, 0, erow, :], selc(s0), None, ALU.mult, None, [ev.b, cst.b], [Va.b], partial=True)
                for r in range(1, 4):
                    STT("dve", dst_k, ek.t[:, r, :, ecol], selc(s0 + r), dst_k, ALU.mult, ALU.add, [ek.b, cst.b, kTa.b], [kTa.b], partial=True)
                    STT("dve", dst_v, ev.t[:, r, erow, :], selc(s0 + r), dst_v, ALU.mult, ALU.add, [ev.b, cst.b, Va.b], [Va.b], partial=True)
            qraw = Tl(P, "qraw", [128, 16, 512], BF16)
            qb = [Buf() for _ in range(16)]
            qTn = qraw
            agT = Tl(P, "agT", [128, 16, 512], BF16)
            ogT = Tl(P, "ogT", [128, 16, 512], BF16)
            tmp = [Rot(P, "nr0_", 2, [128, 512], BF16)] + [Rot(P, "nr%d_" % i, 2, [128, 512], F32) for i in range(1, 5)]
            ptr = Rot(P, "PT", 3, [128, 512], BF16)
            rdr = Rot(P, "rden", 2, [128, 512], F32)
            onr = Rot(P, "on", 2, [128, 512], F32)
            scale = 128.0 ** -0.5
            for (tok0, T, isctx) in tiles():
                if isctx and l == 1:
                    continue
                DMA(qraw.t[:, :, 0:T], pT[0:2048, tok0:tok0 + T].rearrange("(h p) t -> p h t", p=128), W=qb)
                DMA(agT.t[:, :, 0:T], pT[AG_OFF:AG_OFF + 2048, tok0:tok0 + T].rearrange("(h p) t -> p h t", p=128), W=[agT.b])
                for h in range(16):
                    if isctx:
                        normrope(l, 0, qraw.t[:, h, 0:T], qb[h], T, None, None, None, qTn.t[:, h, 0:T], qb[h], tmp, cpool)
                    else:
                        t0 = tok0 - CT
                        normrope(l, 0, qraw.t[:, h, 0:T], qb[h], T, rope.t[:, 0, t0:t0 + T], rope.t[:, 1, t0:t0 + T], rope.b, qTn.t[:, h, 0:T], qb[h], tmp, cpool)
                for blk in range(T // 128):
                    qs = slice(blk * 128, (blk + 1) * 128)
                    for g in range(4):
                        chunks = []
                        if not isctx:
                            nb = (tok0 - CT) // 128 + blk
                            for d_, mi in ((0, 0), (1, None), (2, 1)):
                                m = mi
                                if d_ == 0 and nb == 0:
                                    m = 2
                                if d_ == 2 and nb == NB - 1:
                                    m = 3
                                cb = nb + d_
                                chunks.append((kTa.t[:, g, cb * 128:(cb + 1) * 128], kTa.b, Va.t[:, cb, g * 128:(g + 1) * 128], m))
                        for cc_ in range(2):
                            chunks.append((kTc.t[:, g, cc_ * 128:(cc_ + 1) * 128], kTc.b, Va.t[:, NB + 2 + cc_, g * 128:(g + 1) * 128], None))
                        pO = PS[4 + (psi[0] % 2)]
                        pD = PS[6 + (psi[0] % 2)]
                        psi[0] += 1
                        def fin(ci, pS, vap, m, last):
                            pt_ = ptr.next()
                            ACT(pt_.t[:], pS.t[:, 0:512], AF.Exp, [pS.b, negB.b], [pt_.b], bias=negB.t[:, l:l + 1], scale=scale)
                            if m is not None:
                                p3 = pt_.t[:].rearrange("p (h q) -> p h q", h=4)
                                TT("dve", p3, p3, mskbf.t[:, m, :].unsqueeze(1).to_broadcast([128, 4, 128]), ALU.mult, [pt_.b, mskbf.b], [pt_.b])
                            MM(pO.t[:, 0:512], vap, pt_.t[:], ci == 0, last, [Va.b, pt_.b], [pO.b])
                            MM(pD.t[:, 0:512], onesbf.t[:], pt_.t[:], ci == 0, False, [onesbf.b, pt_.b], [pD.b])

                        pend = None
                        for ci, (kap, kb_, vap, m) in enumerate(chunks):
                            pS = psn(0, 4)
                            MM(pS.t[:, 0:512].rearrange("p (h q) -> p h q", h=4), kap, qTn.t[:, 4 * g:4 * g + 4, qs], True, True, [kb_] + qb[4 * g:4 * g + 4], [pS.b])
                            if pend is not None:
                                fin(*pend)
                            pend = (ci, pS, vap, m, ci == len(chunks) - 1)
                        fin(*pend)
                        MM(pD.t[:, 0:512], onesbf.t[0:1, :], sinkrow.t[0:1, l, g * 512:(g + 1) * 512], False, True, [onesbf.b, sinkrow.b], [pD.b])
                        rd = rdr.next()
                        on = onr.next()
                        RECIP(rd.t[:], pD.t[:, 0:512], [pD.b], [rd.b])
                        TT("dve", on.t[:], pO.t[:, 0:512], rd.t[:], ALU.mult, [pO.b, rd.b], [on.b])
                        TT(cpool, ogT.t[:, 4 * g:4 * g + 4, qs], on.t[:].rearrange("p (h q) -> p h q", h=4), agT.t[:, 4 * g:4 * g + 4, qs], ALU.mult,
                           [on.b, agT.b], [ogT.b], partial=True)
                DMA(aT[:, tok0:tok0 + T].rearrange("(h p) t -> p h t", p=128), ogT.t[:, :, 0:T], R=[ogT.b])

        with P.phase():
            segs = [(CT, LT, 0)] + ([(0, CT, 1)] if l == 0 else [])
            wpw = Tl(P, "wpw", [128, 8, 1024], BF16)
            DMA(wpw.t[:], W("wpw").rearrange("(k p) n -> p k n", p=128), W=[wpw.b])
            dww = Tl(P, "dww", [128, 8, 31], F32)
            cv3 = Tl(P, "cv3", [128, 8, 3], F32)
            DMA(dww.t[:], dww_in[l], W=[dww.b])
            DMA(cv3.t[:], cv3_in[l], W=[cv3.b])
            ar = Rot(P, "cva", 1, [128, 8, 512], BF16)
            br = Rot(P, "cvb", 1, [128, 8, 512], BF16)
            cgr = Rot(P, "cvg", 1, [128, 8, 512], BF16)
            dgc = Rot(P, "dgc", 1, [128, 31, 128], BF16)
            ybuf = Tl(P, "ybuf", [128, 8, 512], F32)
            ysq = Rot(P, "ysq", 2, [128, 512], F32)
            st4 = [Tl(P, "st%d" % i, [128, 512], F32) for i in range(4)]
            tdr = Rot(P, "td", 2, [128, 512], F32)
            zT = Tl(P, "zT", [128, 8, 512], BF16)
            cst_ = Rot(P, "cstg", 1, [128, 8, 512], BF16)
            for (s0, SL, isctx) in segs:
                uT = Tl(P, "uT%d" % isctx, [128, 8, SL + 32], BF16)
                TTs = min(512, SL)
                op("pool", lambda e, uT=uT: e.memset(uT.t[:], 0.0), writes=[uT.b])
                for t0 in range(0, SL, TTs):
                    a_, b_ = ar.next(), br.next()
                    DMA(a_.t[:, :, 0:TTs], pT[CA_OFF:CA_OFF + 1024, s0 + t0:s0 + t0 + TTs].rearrange("(c p) t -> p c t", p=128), W=[a_.b])
                    DMA(b_.t[:, :, 0:TTs], pT[CB_OFF:CB_OFF + 1024, s0 + t0:s0 + t0 + TTs].rearrange("(c p) t -> p c t", p=128), W=[b_.b])
                    TT("dve", uT.t[:, :, 16 + t0:16 + t0 + TTs], a_.t[:, :, 0:TTs], b_.t[:, :, 0:TTs], ALU.mult, [a_.b, b_.b], [uT.b], partial=True)
                if not isctx:
                    gub = Buf()
                    gob2 = Buf()
                    DMA(gU_in[:, 0:16].rearrange("(c p) e -> p c e", p=128), uT.t[:, :, 16:32], R=[uT.b], W=[gub], partial=True)
                    DMA(gU_in[:, 16:32].rearrange("(c p) e -> p c e", p=128), uT.t[:, :, SL:SL + 16], R=[uT.b], W=[gub], partial=True)
                    CC(gU_in, gU_out, RG, [gub], [gob2])
                    eu = Tl(P, "eu", [128, 4, 8, 32], BF16)
                    DMA(eu.t[:], gU_out.rearrange("(r c p) e -> p r c e", r=4, c=8), R=[gob2], W=[eu.b])
                    for dst, ecol, sb in ((uT.t[:, :, 0:16], slice(16, 32), 0), (uT.t[:, :, 16 + SL:32 + SL], slice(0, 16), 4)):
                        TS("dve", dst, eu.t[:, 0, :, ecol], selc(sb), None, ALU.mult, None, [eu.b, cst.b], [uT.b], partial=True)
                        for r in range(1, 4):
                            STT("dve", dst, eu.t[:, r, :, ecol], selc(sb + r), dst, ALU.mult, ALU.add, [eu.b, cst.b, uT.b], [uT.b], partial=True)
                for t0 in range(0, SL, TTs):
                    T = TTs
                    cg = cgr.next()
                    DMA(cg.t[:, :, 0:T], pT[CG_OFF:CG_OFF + 1024, s0 + t0:s0 + t0 + T].rearrange("(c p) t -> p c t", p=128), W=[cg.b])
                    p1, p2 = PS[0], PS[1]
                    for c in range(8):
                        dg = dgc.next()
                        TT("pool", dg.t[:], identbf.t[:].unsqueeze(1).to_broadcast([128, 31, 128]),
                           dww.t[:, c, :].unsqueeze(2).to_broadcast([128, 31, 128]), ALU.mult, [identbf.b, dww.b], [dg.b])
                        pc = psn(2, 6)
                        for j in range(31):
                            MM(pc.t[:, 0:T], dg.t[:, j, :], uT.t[:, c, t0 + j + 1:t0 + j + 1 + T], j == 0, j == 30, [dg.b, uT.b], [pc.b])
                        ACT(ybuf.t[:, c, 0:T], pc.t[:, 0:T], AF.Identity, [pc.b, cv3.b], [ybuf.b], bias=cv3.t[:, c, 0:1], partial=True)
                        yq = ysq.next()
                        ACT(yq.t[:, 0:T], ybuf.t[:, c, 0:T], AF.Square, [ybuf.b], [yq.b])
                        MM(p1.t[:, 0:T], ones32.t[:], ybuf.t[:, c, 0:T], c == 0, c == 7, [ones32.b, ybuf.b], [p1.b])
                        MM(p2.t[:, 0:T], ones32.t[:], yq.t[:, 0:T], c == 0, c == 7, [ones32.b, yq.b], [p2.b])
                    mean, ex2, var, rin = st4
                    ACT(mean.t[:, 0:T], p1.t[:, 0:T], AF.Copy, [p1.b], [mean.b], scale=1.0 / 1024)
                    ACT(ex2.t[:, 0:T], p2.t[:, 0:T], AF.Copy, [p2.b], [ex2.b], scale=1.0 / 1024)
                    TT("dve", var.t[:, 0:T], mean.t[:, 0:T], mean.t[:, 0:T], ALU.mult, [mean.b], [var.b])
                    TT("dve", var.t[:, 0:T], ex2.t[:, 0:T], var.t[:, 0:T], ALU.subtract, [ex2.b, var.b], [var.b])
                    ACT(rin.t[:, 0:T], var.t[:, 0:T], AF.Sqrt, [var.b], [rin.b], bias=EPS)
                    RECIP(rin.t[:, 0:T], rin.t[:, 0:T], [rin.b], [rin.b])
                    for c in range(8):
                        td = tdr.next()
                        TT("dve", td.t[:, 0:T], ybuf.t[:, c, 0:T], mean.t[:, 0:T], ALU.subtract, [ybuf.b, mean.b], [td.b])
                        TT("pool", td.t[:, 0:T], td.t[:, 0:T], rin.t[:, 0:T], ALU.mult, [td.b, rin.b], [td.b])
                        ACT(zT.t[:, c, 0:T], td.t[:, 0:T], AF.Silu, [td.b, cv3.b], [zT.b], bias=cv3.t[:, c, 2:3], scale=cv3.t[:, c, 1:2], partial=True)
                    cs_ = cst_.next()
                    for co in range(8):
                        pw_ = psn(6, 8)
                        for ci in range(8):
                            MM(pw_.t[:, 0:T], wpw.t[:, ci, co * 128:(co + 1) * 128], zT.t[:, ci, 0:T], ci == 0, ci == 7, [wpw.b, zT.b], [pw_.b])
                        TT("dve", cs_.t[:, co, 0:T], pw_.t[:, 0:T], cg.t[:, co, 0:T], ALU.mult, [pw_.b, cg.b], [cs_.b], partial=True)
                    DMA(cT[:, s0 + t0:s0 + t0 + T].rearrange("(c p) t -> p c t", p=128), cs_.t[:, :, 0:T], R=[cs_.b])

        with P.phase():
            wfm = Tl(P, "wfm", [128, 8, 256], F32)
            DMA(wfm.t[:], wfm_in[l].rearrange("g (k p) d -> p (g k) d", p=128), W=[wfm.b])
            CW = Tl(P, "CW", [128, 4, 2, 512], BF16)
            for g in range(4):
                for cs in range(2):
                    for mch in range(2):
                        pc = psn()
                        for k in range(2):
                            MM(pc.t[:, 0:256], dftc.t[:, cs, k, mch * 128:(mch + 1) * 128], wfm.t[:, g * 2 + k, :], k == 0, k == 1, [dftc.b, wfm.b], [pc.b])
                        ACT(CW.t[:, g, mch, cs * 256:(cs + 1) * 256], pc.t[:, 0:256], AF.Copy, [pc.b], [CW.b], partial=True)
            ufr = Rot(P, "uF", 2, [128, 8, 512], BF16)
            abr = Rot(P, "AB", 2, [128, 4, 512], BF16)
            gfb = Buf()
            for (tok0, T, isctx) in tiles():
                if isctx and l == 1:
                    continue
                uF = ufr.next()
                DMA(uF.t[:, :, 0:T], pT[F_OFF:F_OFF + 1024, tok0:tok0 + T].rearrange("(c p) t -> p c t", p=128), W=[uF.b])
                for s in range(T // 128):
                    ab = abr.next()
                    for g in range(4):
                        pa = psn()
                        for mch in range(2):
                            MM(pa.t[:, 0:512], uF.t[:, g * 2 + mch, s * 128:(s + 1) * 128], CW.t[:, g, mch, :], mch == 0, mch == 1, [uF.b, CW.b], [pa.b])
                        if g % 2 == 0:
                            ACT(ab.t[:, g, :], pa.t[:, 0:512], AF.Copy, [pa.b], [ab.b], partial=True)
                        else:
                            CP("dve", ab.t[:, g, :], pa.t[:, 0:512], [pa.b], [ab.b], partial=True)
                    if isctx:
                        DMA(gFc[s * 128:(s + 1) * 128, :], ab.t[:].rearrange("p g n -> p (g n)"), R=[ab.b])
                    else:
                        r0 = tok0 - CT + s * 128
                        DMA(gF_in[r0:r0 + 128, :], ab.t[:].rearrange("p g n -> p (g n)"), R=[ab.b], W=[gfb], partial=True)
            frc = min(LT, 256)
            for c in range(LT // frc):
                CC(gF_in[c * frc:(c + 1) * frc, :], gF_out[c * 4 * frc:(c + 1) * 4 * frc, :], RG, [gfb], [])

        with P.phase():
            gar = Rot(P, "ga", 3, [128, 2048], BF16)
            tcr = Rot(P, "tc", 3, [128, 512], BF16)
            tsr = Rot(P, "tsn", 3, [128, 512], BF16)
            fgr = Rot(P, "fg", 2, [128, 8, 512], BF16)
            fst = Rot(P, "fst", 2, [128, 8, 512], BF16)
            tcx = Tl(P, "tcx", [128, 2, 2, 256], F32)
            tcb = Tl(P, "tcb", [128, 2, 2, 256], BF16)
            DMA(tcx.t[:], dftctx_in.rearrange("c a p k -> p c a k"), W=[tcx.b])
            CP("dve", tcb.t[:], tcx.t[:], [tcx.b], [tcb.b])
            for (tok0, T, isctx) in tiles():
                if isctx and l == 1:
                    continue
                na = 2 if isctx else NA
                fg = fgr.next()
                DMA(fg.t[:, :, 0:T], pT[FG_OFF:FG_OFF + 1024, tok0:tok0 + T].rearrange("(c p) t -> p c t", p=128), W=[fg.b])
                for a in range(na):
                    ga = gar.next()
                    if isctx:
                        DMA(ga.t[:], gFc[a * 128:(a + 1) * 128, :], W=[ga.b])
                        tcap, tsap, tb1, tb2 = tcb.t[:, 0, a, :], tcb.t[:, 1, a, :], tcb.b, tcb.b
                    else:
                        t0 = tok0 - CT
                        DMA(ga.t[:], gF_out[a * 128:(a + 1) * 128, :], W=[ga.b])
                        tc_, ts_ = tcr.next(), tsr.next()
                        DMA(tc_.t[:, 0:T], tabC[a, :, t0:t0 + T], W=[tc_.b])
                        DMA(ts_.t[:, 0:T], tabS[a, :, t0:t0 + T], W=[ts_.b])
                        tcap, tsap, tb1, tb2 = tc_.t[:, 0:T], ts_.t[:, 0:T], tc_.b, ts_.b
                    for fc in range(8):
                        g, half = fc // 2, fc % 2
                        c0 = g * 512 + half * 128
                        MM(PS[fc].t[:, 0:T], ga.t[:, c0:c0 + 128], tcap, a == 0, False, [ga.b, tb1], [PS[fc].b])
                        MM(PS[fc].t[:, 0:T], ga.t[:, c0 + 256:c0 + 384], tsap, False, a == na - 1, [ga.b, tb2], [PS[fc].b])
                fs = fst.next()
                for fc in range(8):
                    TT("dve", fs.t[:, fc, 0:T], PS[fc].t[:, 0:T], fg.t[:, fc, 0:T], ALU.mult, [PS[fc].b, fg.b], [fs.b], partial=True)
                DMA(fT[:, tok0:tok0 + T].rearrange("(c p) t -> p c t", p=128), fs.t[:, :, 0:T], R=[fs.b])

        with P.phase():
            aTr = Rot(P, "aTt", 1, [128, 16, 512], BF16)
            fTr = Rot(P, "fTt", 1, [128, 8, 512], BF16)
            cTr = Rot(P, "cTt", 1, [128, 8, 512], BF16)
            war = Rot(P, "wa", 2, [128, 16, 512], BF16)
            wfr = Rot(P, "wf", 2, [128, 8, 512], BF16)
            wcr = Rot(P, "wc", 2, [128, 8, 512], BF16)
            mgr = Rot(P, "mg", 2, [128, 3, 4, 512], BF16)
            e1t = [Rot(P, "e1t%d" % i, 2, [128, 512], F32) for i in range(3)]
            mTt = Rot(P, "mTt", 2, [128, 4, 512], BF16)
            wau, wfu, wcu = W("wau"), W("wfu"), W("wcu")
            epool = "dve" if l == 0 else "pool"
            if l == 0:
                record_gathers(g_l1[n1:])
            for (tok0, T, isctx) in tiles():
                if isctx and l == 1:
                    continue
                at, ft, ct = aTr.next(), fTr.next(), cTr.next()
                DMA(at.t[:, :, 0:T], aT[:, tok0:tok0 + T].rearrange("(c p) t -> p c t", p=128), W=[at.b])
                DMA(ft.t[:, :, 0:T], fT[:, tok0:tok0 + T].rearrange("(c p) t -> p c t", p=128), W=[ft.b])
                DMA(ct.t[:, :, 0:T], cT[:, tok0:tok0 + T].rearrange("(c p) t -> p c t", p=128), W=[ct.b])
                for cb in range(D // 512):
                    cs = slice(cb * 512, (cb + 1) * 512)
                    wa, wf, wc, mg = war.next(), wfr.next(), wcr.next(), mgr.next()
                    DMA(wa.t[:].rearrange("p k n -> p (k n)"), wau[cb * 128:(cb + 1) * 128, :], W=[wa.b])
                    DMA(wf.t[:].rearrange("p k n -> p (k n)"), wfu[cb * 128:(cb + 1) * 128, :], W=[wf.b])
                    DMA(wc.t[:].rearrange("p k n -> p (k n)"), wcu[cb * 128:(cb + 1) * 128, :], W=[wc.b])
                    for br_ in range(3):
                        r0 = MG + br_ * D + cb * 512
                        DMA(mg.t[:, br_, :, 0:T], pT[r0:r0 + 512, tok0:tok0 + T].rearrange("(c p) t -> p c t", p=128), W=[mg.b], partial=(br_ > 0))
                    mt = mTt.next()
                    for dcl in range(4):
                        ds_ = slice(dcl * 128, (dcl + 1) * 128)
                        pa, pf, pc = psn(0, 3), psn(3, 6), psn(6, 8)
                        for k in range(16):
                            MM(pa.t[:, 0:T], wa.t[:, k, ds_], at.t[:, k, 0:T], k == 0, k == 15, [wa.b, at.b], [pa.b])
                        for k in range(8):
                            MM(pf.t[:, 0:T], wf.t[:, k, ds_], ft.t[:, k, 0:T], k == 0, k == 7, [wf.b, ft.b], [pf.b])
                        for k in range(8):
                            MM(pc.t[:, 0:T], wc.t[:, k, ds_], ct.t[:, k, 0:T], k == 0, k == 7, [wc.b, ct.b], [pc.b])
                        t0_, t1_, t2_ = e1t[0].next(), e1t[1].next(), e1t[2].next()
                        TT("dve", t0_.t[:, 0:T], pa.t[:, 0:T], mg.t[:, 0, dcl, 0:T], ALU.mult, [pa.b, mg.b], [t0_.b])
                        TT("dve", t1_.t[:, 0:T], pf.t[:, 0:T], mg.t[:, 1, dcl, 0:T], ALU.mult, [pf.b, mg.b], [t1_.b])
                        TT("dve", t2_.t[:, 0:T], pc.t[:, 0:T], mg.t[:, 2, dcl, 0:T], ALU.mult, [pc.b, mg.b], [t2_.b])
                        TT(epool, t0_.t[:, 0:T], t0_.t[:, 0:T], t1_.t[:, 0:T], ALU.add, [t0_.b, t1_.b], [t0_.b])
                        TT(epool, mt.t[:, dcl, 0:T], t0_.t[:, 0:T], t2_.t[:, 0:T], ALU.add, [t0_.b, t2_.b], [mt.b], partial=True)
                    DMA(mTd[cb * 512:(cb + 1) * 512, tok0:tok0 + T].rearrange("(c p) t -> p c t", p=128), mt.t[:, :, 0:T], R=[mt.b])

        with P.phase():
            mTr = Rot(P, "mT", 2, [128, KD, 512], BF16)
            wor = Rot(P, "wo", 2, [128, KD, 512], BF16)
            gtr = Rot(P, "gtb", 2, [128, 512], F32)
            xsr = Rot(P, "xs", 3, [128, 512], F32)
            e2t = Rot(P, "e2t", 2, [128, 512], F32)
            xor_ = Rot(P, "xo", 3, [128, 512], F32)
            wo = W("wo")
            epool = "pool"
            if l == 0:
                pass
            for (tok0, T, isctx) in tiles():
                if isctx and l == 1:
                    continue
                q = 2 * l + isctx
                mt = mTr.next()
                for k0 in range(0, KD, 8):
                    k1 = min(KD, k0 + 8)
                    DMA(mt.t[:, k0:k1, 0:T], mTd[k0 * 128:k1 * 128, tok0:tok0 + T].rearrange("(c p) t -> p c t", p=128), W=[mt.b], partial=(k0 > 0))
                for nb in range(D // 512):
                    cs = slice(nb * 512, (nb + 1) * 512)
                    wt = wor.next()
                    DMA(wt.t[:].rearrange("p k n -> p (k n)"), wo[nb * 128:(nb + 1) * 128, :], W=[wt.b])
                    gt = gtr.next()
                    DMA(gt.t[:], gtb_d[q, :, cs], W=[gt.b])
                    for s in range(T // 128):
                        xs = xsr.next()
                        DMA(xs.t[:], src_rows(l, tok0 + s * 128, 128)[:, cs], W=[xs.b])
                        po = psn()
                        for kc in range(KD):
                            MM(po.t[:, 0:512], mt.t[:, kc, s * 128:(s + 1) * 128], wt.t[:, kc, :], kc == 0, kc == KD - 1, [mt.b, wt.b], [po.b])
                        tt_ = e2t.next()
                        xo = xor_.next()
                        TT("dve", tt_.t[:], po.t[:, 0:512], gt.t[:], ALU.mult, [po.b, gt.b], [tt_.b])
                        TT(epool, xo.t[:], tt_.t[:], xs.t[:], ALU.add, [tt_.b, xs.b], [xo.b])
                        if l == 0:
                            DMA(x1[tok0 + s * 128:tok0 + (s + 1) * 128, cs], xo.t[:], R=[xo.b])
                        else:
                            r0 = tok0 - CT + s * 128
                            DMA(out[r0:r0 + 128, cs], xo.t[:], R=[xo.b])

    finish(P)
    P.top.close()
    return nc, P


def host_consts(cfg, r):
    LT, SEQ, NA = cfg.LT, cfg.SEQ, cfg.NA
    cst = np.zeros((128, 1040), np.float32)
    cst[:, 0:128] = np.eye(128, dtype=np.float32)
    R = np.zeros((128, 128), np.float32)
    for base in (0, 64):
        for i in range(32):
            R[base + i, base + i + 32] = -1.0
            R[base + i + 32, base + i] = 1.0
    cst[:, 128:256] = R.T
    jj = np.arange(128)[:, None]
    ii = np.arange(128)[None, :]
    cst[:, 256:384] = (ii <= jj).astype(np.float32)
    cst[:, 384:512] = (jj <= ii).astype(np.float32)
    if r > 0:
        cst[:, 512 + r - 1] = 1.0
        cst[:, 520] = 1.0
    if r < 3:
        cst[:, 516 + r + 1] = 1.0
        cst[:, 521] = 1.0
    for q in range(4):
        cst[q, 528 + q * 128:528 + (q + 1) * 128] = 1.0
    pos = np.arange(r * LT, (r + 1) * LT)
    row = (pos // 64).astype(np.float64)
    col = (pos % 64).astype(np.float64)
    inv = 10000.0 ** (-np.arange(0, 64, 2, dtype=np.float64) / 64)
    ar_, ac_ = row[:, None] * inv[None, :], col[:, None] * inv[None, :]
    ang = np.concatenate([ar_, ar_, ac_, ac_], -1).astype(np.float32)
    rope = np.stack([np.cos(ang).T, np.sin(ang).T]).astype(np.float32)
    k = pos.astype(np.float64)[None, :]
    p = np.arange(128, dtype=np.float64)[:, None]
    a = np.arange(NA, dtype=np.float64)[:, None]
    sc = 1.0 / np.sqrt(SEQ * 256.0)
    angB = 2 * np.pi * ((p * k) % SEQ) / SEQ
    frc = min(LT, 256)
    ai = np.arange(NA)
    m0 = ai * 128
    cc_, rr_, ii_ = m0 // (4 * frc), (m0 % (4 * frc)) // frc, m0 % frc
    l0 = rr_ * LT + cc_ * frc + ii_
    a = (l0 // 128).astype(np.float64)[:, None]
    angA = 2 * np.pi * ((128 * a * k) % SEQ) / SEQ
    dftB = np.stack([np.cos(angB), np.sin(angB)]).astype(np.float32)
    dftA = (np.stack([np.cos(angA), np.sin(angA)]) * sc).astype(np.float32)
    l_ = np.arange(256, dtype=np.float64)
    a256 = 2 * np.pi * np.outer(l_, l_) / 256
    scc = 1.0 / 256.0
    dftctx = np.stack([np.cos(a256) * scc, -np.sin(a256) * scc]).reshape(2, 2, 128, 256).astype(np.float32)
    dc = np.stack([np.cos(a256), np.sin(a256)])
    dftc = dc.reshape(2, 2, 128, 256).transpose(2, 0, 1, 3).astype(np.float32)
    return dict(cst=cst, rope=rope, dftB=dftB, dftA=dftA, dftctx=np.ascontiguousarray(dftctx), dftc=np.ascontiguousarray(dftc))


def make_in_maps(cfg, inp):
    D, LT, KD, MC = cfg.D, cfg.LT, cfg.KD, cfg.MC
    f = lambda a: np.ascontiguousarray(np.asarray(a, dtype=np.float32))
    x, c, ctx, c_ctx = f(inp["x"]), f(inp["c"]), f(inp["ctx"]), f(inp["c_ctx"])
    maps = []
    ngT = f(f(inp["norm_g"]).reshape(2, KD, 128).transpose(0, 2, 1))
    qg, kg = f(inp["q_norm_g"]), f(inp["k_norm_g"])
    qkg = f(np.stack([qg, kg], -1))
    qkrow = f(np.concatenate([qg, kg], -1).reshape(2, 1, 256))
    sink = f(f(inp["attn_sink"]).reshape(2, 1, 16))
    dww = f(f(inp["conv_dw_w"]).reshape(2, 31, 8, 128).transpose(0, 3, 2, 1))
    cv3 = f(np.stack([f(inp["conv_dw_b"]), f(inp["conv_ln_g"]), f(inp["conv_ln_b"])], -1).reshape(2, 8, 128, 3).transpose(0, 2, 1, 3))
    bmod = f(inp["b_mod"])
    wl = {"w_in": f(inp["w_in"]), "w_au": f(inp["w_attn_up"]), "w_fu": f(inp["w_fourier_up"]), "w_cu": f(inp["w_conv_up"]),
          "w_o": f(inp["w_out"]), "w_pw": f(inp["w_conv_pw"])}
    wl_g = {}
    for k_, w in wl.items():
        K_, N_ = w.shape[1], w.shape[2]
        if k_ == "w_pw":
            wl_g[k_] = w
        else:
            wl_g[k_] = np.ascontiguousarray(w.reshape(2, K_ // 128, 128, N_ // 512, 512).transpose(0, 3, 2, 1, 4)).reshape(2, N_ // 4, 4 * K_)
    wmod = f(inp["w_mod"])
    wfm = f(inp["w_fourier_mix"])
    for core in range(8):
        b, r = core // 4, core % 4
        m = host_consts(cfg, r)
        m["x"] = f(x[b, r * LT:(r + 1) * LT])
        m["ctx"] = f(ctx[b])
        m["cvT"] = f(np.stack([c[b], c_ctx], -1).reshape(KD, 128, 2).transpose(1, 0, 2))
        m["ngT"], m["qkg"], m["qkrow"], m["sink"], m["wfm"], m["dww"], m["cv3"] = ngT, qkg, qkrow, sink, wfm, dww, cv3
        for k_, w in wl.items():
            g = wl_g[k_]
            R_g, C_g = g.shape[1], g.shape[2]
            rc = wchunk(R_g, C_g)
            m[k_] = f(g.reshape(2, R_g // (4 * rc), 4, rc, C_g)[:, :, r].reshape(2, R_g // 4, C_g))
        m["w_mod"] = f(wmod[:, :, r * MC:(r + 1) * MC])
        m["b_mod"] = f(np.repeat(bmod[:, None, r * MC:(r + 1) * MC], 2, axis=1))
        maps.append(m)
    return maps


def kernel(**inputs):
    cfg = Cfg()
    nc, _ = build(cfg)
    maps = make_in_maps(cfg, inputs)
    res = run_bass_kernel_spmd(nc, maps, core_ids=list(range(8)))
    outp = np.empty((2, cfg.SEQ, cfg.D), np.float32)
    for core in range(8):
        b, r = core // 4, core % 4
        outp[b, r * cfg.LT:(r + 1) * cfg.LT] = res.results[core]["out"]
    return outp
```

```python
import contextlib
import numpy as np
import concourse.bass as bass
import concourse.mybir as mybir
from concourse.bass_utils import run_bass_kernel_spmd

F32 = mybir.dt.float32
BF16 = mybir.dt.bfloat16
AF = mybir.ActivationFunctionType
ALU = mybir.AluOpType
AX = mybir.AxisListType

ENGS = ("pe", "act", "dve", "pool", "sp")
DMAQ = ("sp", "act", "pool")
RING = 8
EPS = 1e-6


class Buf:
    __slots__ = ("writers", "readers")

    def __init__(self):
        self.writers = {}
        self.readers = {}


class Op:
    __slots__ = ("eng", "fn", "deps", "signal", "val", "dma", "slot", "key", "inc", "epoch")

    def __init__(self, eng, fn, dma=False):
        self.eng = eng
        self.fn = fn
        self.deps = []
        self.signal = False
        self.val = 0
        self.dma = dma
        self.slot = -1
        self.key = eng
        self.inc = 1
        self.epoch = 0


class Prog:
    def __init__(self, nc):
        self.nc = nc
        self.top = contextlib.ExitStack()
        self.scope = self.top
        st = self.top
        self.esem = {e: st.enter_context(nc.semaphore("s_" + e)) for e in ENGS}
        self.ring = {q: [st.enter_context(nc.semaphore("r_%s%d" % (q, i))) for i in range(RING)] for q in DMAQ}
        self.ccsem = st.enter_context(nc.semaphore("s_cc"))
        self.cnt = {e: 0 for e in ENGS}
        self.ndma = {q: 0 for q in DMAQ}
        self.ncc = 0
        self.epoch = 0
        self.ops = {e: [] for e in ENGS}
        self.waited = {e: {} for e in ENGS}
        self.nops = 0
        self.mk = self.sbuf("mk", [128, 8], F32)

    def sbuf(self, name, shape, dt):
        self.nalloc = getattr(self, "nalloc", 0) + 1
        return self.scope.enter_context(self.nc.sbuf_tensor("%s_%d" % (name, self.nalloc), list(shape), dt))

    def psum(self, name, shape, dt):
        return self.scope.enter_context(self.nc.psum_tensor(name, list(shape), dt))

    @contextlib.contextmanager
    def phase(self):
        old = self.scope
        with contextlib.ExitStack() as st:
            self.scope = st
            yield
            self.nphase = getattr(self, "nphase", 0) + 1
            import os as _os
            if self.nphase <= int(_os.environ.get("KSTOP", "999")):
                self.flush()
            else:
                self.ops = {e: [] for e in ENGS}
        self.scope = old

    def _add(self, op, reads, writes, partial):
        op.epoch = self.epoch
        deps = {}
        for b in reads:
            for w in b.writers.values():
                deps[id(w)] = w
        for b in writes:
            for w in b.readers.values():
                deps[id(w)] = w
            for w in b.writers.values():
                deps[id(w)] = w
        for d in deps.values():
            if d is op or d.epoch != self.epoch:
                continue
            if (not d.dma) and (not op.dma) and d.eng == op.eng and op.eng == "pe":
                continue
            d.signal = True
            op.deps.append(d)
        for b in reads:
            b.readers[op.key] = op
        for b in writes:
            if not partial:
                b.writers = {}
                b.readers = {}
            b.writers[op.key] = op
        self.ops[op.eng].append(op)
        self.nops += 1
        return op

    def op(self, eng, fn, reads=(), writes=(), partial=False):
        return self._add(Op(eng, fn), reads, writes, partial)

    def dma(self, q, fn, reads=(), writes=(), partial=False):
        o = Op(q, fn, dma=True)
        i = self.ndma[q]
        self.ndma[q] = i + 1
        o.slot = i % RING
        o.val = 16 * (i // RING + 1)
        o.inc = 16
        o.key = (q, o.slot)
        o.signal = True
        return self._add(o, reads, writes, partial)

    def cc(self, fn, reads=(), writes=()):
        o = Op("pool", fn, dma=True)
        self.ncc += 1
        o.slot = -2
        o.val = self.ncc
        o.inc = 1
        o.key = ("cc", 0)
        o.signal = True
        return self._add(o, reads, writes, False)

    def flush(self):
        nc = self.nc
        ops_snap = self.ops
        mval = {}
        for e in ENGS:
            c = self.cnt[e]
            for o in self.ops[e]:
                if not o.dma and o.signal:
                    c += 1
                    o.val = c
            mval[e] = c + 1
            self.cnt[e] = c + (0 if e == "pe" else 1)
        mk = self.mk

        def semof(o):
            if o.slot == -2:
                return self.ccsem
            if o.dma:
                return self.ring[o.eng][o.slot]
            return self.esem[o.eng]

        def run(e, eng):
            waited = self.waited[e]

            def wait(s, v):
                if waited.get(id(s), 0) < v:
                    eng.wait_ge(s, v)
                    waited[id(s)] = v

            last = {}
            for o in ops_snap[e]:
                for d in o.deps:
                    wait(semof(d), d.val)
                if o.slot == -2:
                    o.fn(eng).then_inc(self.ccsem, 1)
                    wait(self.ccsem, o.val)
                elif o.dma:
                    s = self.ring[e][o.slot]
                    if o.val > 16:
                        wait(s, o.val - 16)
                    o.fn(eng).then_inc(s, 16)
                    last[o.slot] = o
                else:
                    ins = o.fn(eng)
                    if o.signal:
                        ins.then_inc(self.esem[e], 1)
            for sl, o in last.items():
                wait(self.ring[e][sl], o.val)
            if e == "dve":
                m = eng.memset(mk[:, 0:1], 0.0)
            elif e == "pool":
                m = eng.memset(mk[:, 1:2], 0.0)
            elif e == "act":
                m = eng.memzero(mk[:, 2:3])
            elif e == "sp":
                m = eng.nop()
            else:
                m = None
            if m is not None:
                m.then_inc(self.esem[e], 1)
            for e2 in ENGS:
                if e2 != "pe":
                    wait(self.esem[e2], mval[e2])

        self.pending = getattr(self, 'pending', [])
        self.pending.append(run)

        self.epoch += 1
        self.ops = {e: [] for e in ENGS}


def finish(P):
    nc = P.nc
    with nc.Block() as block:
        @block.tensor
        def _(eng):
            for r in P.pending:
                r("pe", eng)

        @block.scalar
        def _(eng):
            for r in P.pending:
                r("act", eng)

        @block.vector
        def _(eng):
            for r in P.pending:
                r("dve", eng)

        @block.gpsimd
        def _(eng):
            for r in P.pending:
                r("pool", eng)

        @block.sync
        def _(eng):
            for r in P.pending:
                r("sp", eng)


class Tl:
    def __init__(self, P, name, shape, dt, psum=False):
        self.t = (P.psum if psum else P.sbuf)(name, shape, dt)
        self.b = Buf()


class Rot:
    def __init__(self, P, name, n, shape, dt):
        self.ts = [Tl(P, "%s%d" % (name, i), shape, dt) for i in range(n)]
        self.i = 0

    def next(self):
        t = self.ts[self.i % len(self.ts)]
        self.i += 1
        return t


class Cfg:
    def __init__(self, D=4096, SEQ=8192):
        self.D = D
        self.SEQ = SEQ
        self.KD = D // 128
        self.LT = SEQ // 4
        self.CT = 256
        self.NTOK = self.CT + self.LT
        self.MG = 10240
        self.INW = 10240 + 3 * D
        self.NA = SEQ // 128
        self.MC = 3 * D // 4
        self.TT = min(512, self.LT)


Q_OFF, K_OFF, V_OFF, AG_OFF, F_OFF, FG_OFF, CA_OFF, CB_OFF, CG_OFF = 0, 2048, 2560, 3072, 5120, 6144, 7168, 8192, 9216
VBLK = V_OFF // 512


def wchunk(K, N):
    n = K // 4
    for rc in range(n, 0, -1):
        if n % rc == 0 and rc * N * 2 <= (1 << 20):
            return rc


def gshape(nm, K, N):
    if nm == "wpw":
        return K, N
    return N // 4, 4 * K


def fam_func(blk):
    c = blk * 512
    if AG_OFF <= c < F_OFF or FG_OFF <= c < CA_OFF or CG_OFF <= c < 10240:
        return AF.Silu
    if CB_OFF <= c < CG_OFF or c >= 10240:
        return AF.Sigmoid
    return AF.Copy


def build(cfg, debug=False):
    D, KD, LT, CT, NTOK, INW, NA, MC, SEQ = cfg.D, cfg.KD, cfg.LT, cfg.CT, cfg.NTOK, cfg.INW, cfg.NA, cfg.MC, cfg.SEQ
    MG = cfg.MG
    NB = LT // 128
    nc = bass.Bass("TRN2", target_bir_lowering=False)

    def din(name, shape, dt=F32):
        return nc.dram_tensor(name, list(shape), dt, kind="ExternalInput").ap()

    def dscr(name, shape, dt=BF16, dbg=False):
        kind = "ExternalOutput" if (dbg and debug and dt == F32) else "Internal"
        return nc.dram_tensor(name, list(shape), dt, kind=kind).ap()

    x_in = din("x", [LT, D])
    ctx_in = din("ctx", [CT, D])
    cst_in = din("cst", [128, 1040])
    rope_in = din("rope", [2, 128, LT])
    dftB_in = din("dftB", [2, 128, LT])
    dftA_in = din("dftA", [2, NA, LT])
    dftctx_in = din("dftctx", [2, 2, 128, 256])
    dftc_in = din("dftc", [128, 2, 2, 256])
    cvT_in = din("cvT", [128, KD, 2])
    ngT_in = din("ngT", [2, 128, KD])
    qkg_in = din("qkg", [2, 128, 2])
    qkrow_in = din("qkrow", [2, 1, 256])
    sink_in = din("sink", [2, 1, 16])
    wfm_in = din("wfm", [2, 4, 256, 256])
    dww_in = din("dww", [2, 128, 8, 31])
    cv3_in = din("cv3", [2, 128, 8, 3])
    w_in_s = din("w_in", [2, INW // 16, 4 * D])
    w_au_s = din("w_au", [2, D // 16, 4 * 2048])
    w_fu_s = din("w_fu", [2, D // 16, 4 * 1024])
    w_cu_s = din("w_cu", [2, D // 16, 4 * 1024])
    w_o_s = din("w_o", [2, D // 16, 4 * D])
    w_pw_s = din("w_pw", [2, 256, 1024])
    w_mod_s = din("w_mod", [2, D, MC])
    b_mod_s = din("b_mod", [2, 2, MC])
    out = nc.dram_tensor("out", [LT, D], F32, kind="ExternalOutput").ap()

    wspec = [("win", D, INW, w_in_s), ("wau", 2048, D, w_au_s), ("wfu", 1024, D, w_fu_s),
             ("wcu", 1024, D, w_cu_s), ("wo", D, D, w_o_s), ("wpw", 1024, 1024, w_pw_s)]
    Wg = {}
    Wgin = {}
    for nm, K, N, _ in wspec:
        R_g, C_g = gshape(nm, K, N)
        for l in range(2):
            Wgin[(nm, l)] = dscr("gi_%s%d" % (nm, l), [R_g // 4, C_g])
            Wg[(nm, l)] = dscr("g_%s%d" % (nm, l), [R_g, C_g])
    pT = dscr("pT", [INW, NTOK], dbg=True)
    Vtm = dscr("Vtm", [NTOK, 512], dbg=True)
    kTn = dscr("kTn", [512, NTOK], dbg=True)
    gK_in = dscr("gK_in", [512, 256]); gK_out = dscr("gK_out", [4 * 512, 256])
    gV_in = dscr("gV_in", [256, 512]); gV_out = dscr("gV_out", [4 * 256, 512])
    gU_in = dscr("gU_in", [1024, 32]); gU_out = dscr("gU_out", [4 * 1024, 32])
    gF_in = dscr("gF_in", [LT, 2048]); gF_out = dscr("gF_out", [SEQ, 2048])
    gFc = dscr("gFc", [CT, 2048])
    tabC = dscr("tabC", [NA, 128, LT]); tabS = dscr("tabS", [NA, 128, LT])
    aT = dscr("aT", [2048, NTOK], dbg=True)
    fT = dscr("fT", [1024, NTOK], dbg=True)
    cT = dscr("cT", [1024, NTOK], dbg=True)
    mTd = dscr("mTd", [D, NTOK], dbg=True)
    x1 = dscr("x1", [NTOK, D], F32, dbg=True)
    gmod_in = dscr("gmod_in", [4, MC], F32); gmod_out = dscr("gmod_out", [16, MC], F32)
    gtb_d = dscr("gtb_d", [4, 128, D], F32)
    RG = [[0, 1, 2, 3], [4, 5, 6, 7]]
    RG8 = [list(range(8))]

    import os as _os2
    KSUB = int(_os2.environ.get('KSUB', '99'))
    P = Prog(nc)
    op, dma = P.op, P.dma

    def MM(o, lt, rh, start, stop, R, W):
        op("pe", lambda e: e.matmul(o, lhsT=lt, rhs=rh, start=start, stop=stop), reads=R, writes=W, partial=not start)

    def ACT(o, i, func, R, W, bias=None, scale=None, accum=None, partial=False):
        kw = {}
        if bias is not None:
            kw["bias"] = bias
        if scale is not None:
            kw["scale"] = scale
        if accum is not None:
            kw["accum_out"] = accum
        op("act", lambda e: e.activation(out=o, in_=i, func=func, **kw), reads=R, writes=W, partial=partial)

    def TT(eng, o, a, b, aop, R, W, partial=False):
        op(eng, lambda e: e.tensor_tensor(out=o, in0=a, in1=b, op=aop), reads=R, writes=W, partial=partial)

    def TS(eng, o, a, s1, s2, op0, op1, R, W, partial=False):
        if s2 is None:
            op(eng, lambda e: e.tensor_scalar(out=o, in0=a, scalar1=s1, scalar2=None, op0=op0), reads=R, writes=W, partial=partial)
        else:
            op(eng, lambda e: e.tensor_scalar(out=o, in0=a, scalar1=s1, scalar2=s2, op0=op0, op1=op1), reads=R, writes=W, partial=partial)

    def STT(eng, o, a, s, b, op0, op1, R, W, partial=False):
        op(eng, lambda e: e.scalar_tensor_tensor(out=o, in0=a, scalar=s, in1=b, op0=op0, op1=op1), reads=R, writes=W, partial=partial)

    def CP(eng, o, i, R, W, partial=False):
        op(eng, lambda e: e.tensor_copy(out=o, in_=i), reads=R, writes=W, partial=partial)

    def RECIP(o, i, R, W):
        op("dve", lambda e: e.reciprocal(out=o, in_=i), reads=R, writes=W)

    def DMA(o, i, R=(), W=(), q="sp", partial=False):
        if q == "sp" and str(o.space).endswith("DRAM") and not str(i.space).endswith("DRAM"):
            q = "act"
        dma(q, lambda e: e.dma_start(out=o, in_=i), reads=R, writes=W, partial=partial)

    def CC(i, o, groups, R, W):
        P.cc(lambda e: e.collective_compute("AllGather", ALU.bypass, replica_groups=groups, ins=[i], outs=[o]), reads=R, writes=W)

    cst = Tl(P, "cst", [128, 1040], F32)
    ident = cst.t[:, 0:128]
    rotT = cst.t[:, 128:256]
    selc = lambda i: cst.t[:, 512 + i:513 + i]
    selm = cst.t[0:4, 528:1040]
    ones32 = Tl(P, "ones32", [128, 128], F32)
    onesbf = Tl(P, "onesbf", [128, 128], BF16)
    identbf = Tl(P, "identbf", [128, 128], BF16)
    mskbf = Tl(P, "mskbf", [128, 4, 128], BF16)
    gsT = Tl(P, "gsT", [128, 4, KD], F32)
    shT = Tl(P, "shT", [128, 4, KD], F32)
    qkg = Tl(P, "qkg", [128, 2, 2], F32)
    negB = Tl(P, "negB", [128, 2], F32)
    sinkrow = Tl(P, "sinkrow", [1, 2, 2048], BF16)
    dftc = Tl(P, "dftc", [128, 2, 2, 256], F32)
    PS = [Tl(P, "ps%d" % i, [128, 512], F32, psum=True) for i in range(8)]
    psi = [0]

    def psn(lo=0, hi=8):
        t = PS[lo + psi[0] % (hi - lo)]
        psi[0] += 1
        return t

    with P.phase():
        DMA(cst.t[:], cst_in, W=[cst.b])
        DMA(qkg.t[:], qkg_in.rearrange("l p k -> p l k"), W=[qkg.b])
        DMA(dftc.t[:], dftc_in, W=[dftc.b])
        op("dve", lambda e: e.memset(ones32.t[:], 1.0), writes=[ones32.b])
        op("pool", lambda e: e.memset(onesbf.t[:], 1.0), writes=[onesbf.b])
        CP("dve", identbf.t[:], ident, [cst.b], [identbf.b])
        CP("dve", mskbf.t[:, 0, :], cst.t[:, 256:384], [cst.b], [mskbf.b], partial=True)
        CP("dve", mskbf.t[:, 1, :], cst.t[:, 384:512], [cst.b], [mskbf.b], partial=True)
        TS("dve", mskbf.t[:, 2, :], cst.t[:, 256:384], selc(8), None, ALU.mult, None, [cst.b], [mskbf.b], partial=True)
        TS("dve", mskbf.t[:, 3, :], cst.t[:, 384:512], selc(9), None, ALU.mult, None, [cst.b], [mskbf.b], partial=True)
        qkrow = Tl(P, "qkrow", [1, 2, 256], F32)
        sk = Tl(P, "sk", [1, 2, 16], F32)
        mx = Tl(P, "mx", [1, 8], F32)
        DMA(qkrow.t[:], qkrow_in.rearrange("l o k -> o l k"), W=[qkrow.b])
        DMA(sk.t[:], sink_in.rearrange("l o k -> o l k"), W=[sk.b])
        for l in range(2):
            for j in range(2):
                op("dve", lambda e, l=l, j=j: e.reduce_max(out=mx.t[0:1, 2 * l + j:2 * l + j + 1], in_=qkrow.t[0:1, l, j * 128:(j + 1) * 128],
                                                         axis=AX.X, apply_absolute_value=True), reads=[qkrow.b], writes=[mx.b], partial=True)
            TT("dve", mx.t[0:1, 4 + l:5 + l], mx.t[0:1, 2 * l:2 * l + 1], mx.t[0:1, 2 * l + 1:2 * l + 2], ALU.mult, [mx.b], [mx.b], partial=True)
            pb = psn()
            MM(pb.t[:, 0:1], ones32.t[0:1, :], mx.t[0:1, 4 + l:5 + l], True, True, [ones32.b, mx.b], [pb.b])
            ACT(negB.t[:, l:l + 1], pb.t[:, 0:1], AF.Copy, [pb.b], [negB.b], scale=-(128.0 ** 0.5), partial=True)
            es = Tl(P, "es%d" % l, [1, 16], F32)
            ACT(es.t[:], sk.t[0:1, l, :], AF.Exp, [sk.b, negB.b], [es.b], bias=negB.t[0:1, l:l + 1])
            for h in range(16):
                TS("dve", sinkrow.t[0:1, l, h * 128:(h + 1) * 128], ones32.t[0:1, :], es.t[0:1, h:h + 1], None, ALU.mult, None,
                   [ones32.b, es.b], [sinkrow.b], partial=True)

    with P.phase():
        gb = Buf()
        for nm, K, N, src in wspec:
            for l in range(2):
                DMA(Wgin[(nm, l)], src[l], W=[gb], q="pool", partial=True)
    chunkbuf = {}

    def gather_list():
        order = [("win", 0)] + [(nm, 0) for nm in ("wpw", "wau", "wfu", "wcu", "wo")] + [(nm, 1) for nm in ("win", "wpw", "wau", "wfu", "wcu", "wo")]
        dims = {nm: gshape(nm, K, N) for nm, K, N, _ in wspec}
        out_ = []
        for nm, l in order:
            R_g, C_g = dims[nm]
            rc = wchunk(R_g, C_g)
            for c in range((R_g // 4) // rc):
                out_.append((nm, l, c, rc))
        return out_

    glist = gather_list()
    g_l0 = [g for g in glist if g[1] == 0]
    g_l1 = [g for g in glist if g[1] == 1]
    n1 = len([g for g in g_l1 if g[0] == "win"]) * 10 // 11

    def record_gathers(items):
        for nm, l, c, rc in items:
            b = Buf()
            chunkbuf[(nm, l, c)] = b
            CC(Wgin[(nm, l)][c * rc:(c + 1) * rc, :], Wg[(nm, l)][c * 4 * rc:(c + 1) * 4 * rc, :], RG, [], [b])

    def wbufs(nm, l, r0, r1):
        K_, N_ = [(K, N) for n_, K, N, _ in wspec if n_ == nm][0]
        R_g, C_g = gshape(nm, K_, N_)
        rc4 = 4 * wchunk(R_g, C_g)
        return [chunkbuf[(nm, l, c)] for c in range(r0 // rc4, (r1 - 1) // rc4 + 1)]

    with P.phase():
        cvT = Tl(P, "cvT", [128, KD, 2], F32)
        scT = Tl(P, "scT", [128, KD, 2], F32)
        bm = Tl(P, "bm", [2, 2, MC], F32)
        mrow = Tl(P, "mrow", [2, 2, MC], F32)
        DMA(cvT.t[:], cvT_in, W=[cvT.b])
        DMA(bm.t[:], b_mod_s.rearrange("l q m -> q l m"), W=[bm.b])
        ACT(scT.t[:], cvT.t[:], AF.Silu, [cvT.b], [scT.b])
        wmr = Rot(P, "wm", 3, [128, MC], F32)
        nch = [(o, min(512, MC - o)) for o in range(0, MC, 512)]
        gmb = Buf()
        for l in range(2):
            pss = [PS[i] for i in range(len(nch))]
            for kc in range(KD):
                wm = wmr.next()
                DMA(wm.t[:], w_mod_s[l, kc * 128:(kc + 1) * 128, :], W=[wm.b])
                for i, (o, n) in enumerate(nch):
                    MM(pss[i].t[0:2, 0:n], scT.t[:, kc, :], wm.t[:, o:o + n], kc == 0, kc == KD - 1, [scT.b, wm.b], [pss[i].b])
            for i, (o, n) in enumerate(nch):
                TT("dve", mrow.t[0:2, l, o:o + n], pss[i].t[0:2, 0:n], bm.t[0:2, l, o:o + n], ALU.add, [pss[i].b, bm.b], [mrow.b], partial=True)
            DMA(gmod_in[2 * l:2 * l + 2, :], mrow.t[0:2, l, :], R=[mrow.b], W=[gmb], partial=True)
        gob = Buf()
        if KSUB >= 1:
            CC(gmod_in, gmod_out, RG, [gmb], [gob])
        R_ = Tl(P, "Rr", [4, 3 * D], F32)
        if KSUB >= 2:
          DMA(R_.t[:].rearrange("q (r j) -> q r j", r=4), gmod_out.rearrange("(r q) j -> q r j", q=4), R=[gob], W=[R_.b])
        modT = Tl(P, "modT", [128, 2 * KD, 4], F32)
        for c0 in (range(0, 2 * KD, 64) if KSUB >= 3 else []):
            pm = psn()
            n = min(64, 2 * KD - c0)
            for c in range(n):
                op("pe", lambda e, c=c, c0=c0, pm=pm: e.transpose(pm.t[:, c * 4:c * 4 + 4], R_.t[0:4, (c0 + c) * 128:(c0 + c + 1) * 128], cst.t[0:4, 0:4]),
                   reads=[R_.b, cst.b], writes=[pm.b], partial=(c > 0))
            CP("dve", modT.t[:, c0:c0 + n, :], pm.t[:, 0:4 * n].rearrange("p (c q) -> p c q", q=4), [pm.b], [modT.b], partial=True)
        ngT = Tl(P, "ngT", [128, 2, KD], F32)
        DMA(ngT.t[:], ngT_in.rearrange("l p k -> p l k"), W=[ngT.b])
        for q in (range(4) if KSUB >= 4 else []):
            STT("dve", gsT.t[:, q, :], modT.t[:, KD:2 * KD, q], 1.0, ngT.t[:, q // 2, :], ALU.add, ALU.mult, [modT.b, ngT.b], [gsT.b], partial=True)
            CP("dve", shT.t[:, q, :], modT.t[:, 0:KD, q], [modT.b], [shT.b], partial=True)
        gst = Rot(P, "gst", 2, [128, 512], F32)
        for q in (range(4) if KSUB >= 5 else []):
            for nb in range(D // 512):
                pg = psn()
                MM(pg.t[:, 0:512], selm[:, q * 128:(q + 1) * 128], R_.t[0:4, 2 * D + nb * 512:2 * D + (nb + 1) * 512], True, True, [cst.b, R_.b], [pg.b])
                g = gst.next()
                ACT(g.t[:], pg.t[:, 0:512], AF.Copy, [pg.b], [g.b])
                DMA(gtb_d[q, :, nb * 512:(nb + 1) * 512], g.t[:], R=[g.b])

    with P.phase():
        CB = Tl(P, "CB", [128, LT], F32)
        SB = Tl(P, "SB", [128, LT], F32)
        DMA(CB.t[:], dftB_in[0], W=[CB.b])
        DMA(SB.t[:], dftB_in[1], W=[SB.b])
        car = Rot(P, "ca", 2, [128, LT], F32)
        sar = Rot(P, "sa", 2, [128, LT], F32)
        t1r = Rot(P, "t1", 2, [128, LT], F32)
        t2r = Rot(P, "t2", 2, [128, LT], F32)
        ocr = Rot(P, "oc", 2, [128, LT], BF16)
        osr = Rot(P, "os", 2, [128, LT], BF16)
        g_w0 = [g for g in g_l0 if g[0] == "win"]
        record_gathers(g_w0)
        for a in range(NA):
            ca, sa, t1, t2, oc, os_ = car.next(), sar.next(), t1r.next(), t2r.next(), ocr.next(), osr.next()
            DMA(ca.t[:], dftA_in[0, a:a + 1, :].partition_broadcast(128), W=[ca.b])
            DMA(sa.t[:], dftA_in[1, a:a + 1, :].partition_broadcast(128), W=[sa.b])
            TT("dve", t1.t[:], ca.t[:], CB.t[:], ALU.mult, [ca.b, CB.b], [t1.b])
            TT("dve", t2.t[:], sa.t[:], SB.t[:], ALU.mult, [sa.b, SB.b], [t2.b])
            TT("dve", oc.t[:], t1.t[:], t2.t[:], ALU.subtract, [t1.b, t2.b], [oc.b])
            DMA(tabC[a], oc.t[:], R=[oc.b])
            TT("dve", t2.t[:], sa.t[:], CB.t[:], ALU.mult, [sa.b, CB.b], [t2.b])
            TT("dve", t1.t[:], ca.t[:], SB.t[:], ALU.mult, [ca.b, SB.b], [t1.b])
            STT("dve", os_.t[:], t2.t[:], -1.0, t1.t[:], ALU.mult, ALU.subtract, [t1.b, t2.b], [os_.b])
            DMA(tabS[a], os_.t[:], R=[os_.b])

    def src_rows(l, tok0, n):
        if l == 1:
            return x1[tok0:tok0 + n, :]
        if tok0 < CT:
            return ctx_in[tok0:tok0 + n, :]
        return x_in[tok0 - CT:tok0 - CT + n, :]

    def tiles():
        ts = [(0, CT, 1)]
        for t0 in range(0, LT, cfg.TT):
            ts.append((CT + t0, cfg.TT, 0))
        return ts

    def normrope(l, which, src, srcb, T, cos, sin, ropeb, outap, outb, tmp, pe_="pool"):
        sq, rs, kn, t1, t2 = [r_.next() for r_ in tmp]
        ACT(sq.t[:, 0:T], src, AF.Square, [srcb], [sq.b])
        p1 = psn(0, 4)
        MM(p1.t[:, 0:T], onesbf.t[:], sq.t[:, 0:T], True, True, [onesbf.b, sq.b], [p1.b])
        ACT(rs.t[:, 0:T], p1.t[:, 0:T], AF.Ln, [p1.b], [rs.b], bias=EPS, scale=1.0 / 128)
        ACT(rs.t[:, 0:T], rs.t[:, 0:T], AF.Exp, [rs.b], [rs.b], scale=-0.5)
        STT("dve", kn.t[:, 0:T], src, qkg.t[:, l, which:which + 1], rs.t[:, 0:T], ALU.mult, ALU.mult, [srcb, qkg.b, rs.b], [kn.b])
        if cos is None:
            CP(pe_, outap, kn.t[:, 0:T], [kn.b], [outb], partial=True)
            return
        p2 = psn(0, 4)
        MM(p2.t[:, 0:T], rotT, kn.t[:, 0:T], True, True, [cst.b, kn.b], [p2.b])
        TT(pe_, t1.t[:, 0:T], kn.t[:, 0:T], cos, ALU.mult, [kn.b, ropeb], [t1.b])
        TT("dve", t2.t[:, 0:T], p2.t[:, 0:T], sin, ALU.mult, [p2.b, ropeb], [t2.b])
        TT("dve", outap, t1.t[:, 0:T], t2.t[:, 0:T], ALU.add, [t1.b, t2.b], [outb], partial=True)

    for l in range(2):
        W = lambda nm: Wg[(nm, l)]
        with P.phase():
            xr = Rot(P, "xt", 2, [128, D], F32)
            junk = Tl(P, "junk", [128, D], BF16)
            ssr = Rot(P, "ss", 2, [128, 2], F32)
            hT = Tl(P, "hT", [128, KD, 512], BF16)
            wr = Rot(P, "wblk", 2, [128, KD, 512], BF16)
            stg = Rot(P, "stg", 3, [128, 512], BF16)
            win = W("win")
            if l == 0:
                record_gathers([g for g in g_l0 if g[0] != "win"])
            for (tok0, T, isctx) in tiles():
                q = 2 * l + isctx
                for s in range(T // 128):
                    xt = xr.next()
                    ss = ssr.next()
                    DMA(xt.t[:], src_rows(l, tok0 + s * 128, 128), W=[xt.b])
                    op("dve", lambda e, ss=ss: e.memset(ss.t[:], 0.0), writes=[ss.b])
                    ACT(junk.t[:], xt.t[:], AF.Square, [xt.b, ss.b], [junk.b, ss.b], accum=ss.t[:, 0:1])
                    ACT(ss.t[:, 1:2], ss.t[:, 0:1], AF.Sqrt, [ss.b], [ss.b], bias=EPS, scale=1.0 / D)
                    RECIP(ss.t[:, 1:2], ss.t[:, 1:2], [ss.b], [ss.b])
                    TS("dve", xt.t[:], xt.t[:], ss.t[:, 1:2], None, ALU.mult, None, [xt.b, ss.b], [xt.b])
                    for k0 in range(0, KD, 4):
                        pt = psn()
                        for j in range(4):
                            kc = k0 + j
                            op("pe", lambda e, pt=pt, j=j, kc=kc, xt=xt: e.transpose(pt.t[:, j * 128:(j + 1) * 128], xt.t[:, kc * 128:(kc + 1) * 128], ident),
                               reads=[xt.b, cst.b], writes=[pt.b], partial=(j > 0))
                        for j in range(4):
                            kc = k0 + j
                            if j % 2 == 0:
                                TS("dve", hT.t[:, kc, s * 128:(s + 1) * 128], pt.t[:, j * 128:(j + 1) * 128], gsT.t[:, q, kc:kc + 1], shT.t[:, q, kc:kc + 1],
                                   ALU.mult, ALU.add, [pt.b, gsT.b, shT.b], [hT.b], partial=True)
                            else:
                                ACT(hT.t[:, kc, s * 128:(s + 1) * 128], pt.t[:, j * 128:(j + 1) * 128], AF.Identity, [pt.b, gsT.b, shT.b], [hT.b],
                                    bias=shT.t[:, q, kc:kc + 1], scale=gsT.t[:, q, kc:kc + 1], partial=True)
                blks = list(range(INW // 512))
                if l == 1 and isctx:
                    blks = [K_OFF // 512, VBLK]
                for blk in blks:
                    wt = wr.next()
                    DMA(wt.t[:].rearrange("p k n -> p (k n)"), win[blk * 128:(blk + 1) * 128, :], R=wbufs("win", l, blk * 128, (blk + 1) * 128), W=[wt.b])
                    if blk == VBLK:
                        for s in range(T // 128):
                            pv = psn()
                            for kc in range(KD):
                                MM(pv.t[:, 0:512], hT.t[:, kc, s * 128:(s + 1) * 128], wt.t[:, kc, :], kc == 0, kc == KD - 1, [hT.b, wt.b], [pv.b])
                            sg = stg.next()
                            ACT(sg.t[:], pv.t[:, 0:512], AF.Copy, [pv.b], [sg.b])
                            DMA(Vtm[tok0 + s * 128:tok0 + (s + 1) * 128, :], sg.t[:], R=[sg.b])
                    else:
                        fn = fam_func(blk)
                        for c in range(4):
                            pv = psn()
                            for kc in range(KD):
                                MM(pv.t[:, 0:T], wt.t[:, kc, c * 128:(c + 1) * 128], hT.t[:, kc, 0:T], kc == 0, kc == KD - 1, [hT.b, wt.b], [pv.b])
                            sg = stg.next()
                            ACT(sg.t[:, 0:T], pv.t[:, 0:T], fn, [pv.b], [sg.b])
                            r0 = (blk * 4 + c) * 128
                            DMA(pT[r0:r0 + 128, tok0:tok0 + T], sg.t[:, 0:T], R=[sg.b])

        with P.phase():
            rope = Tl(P, "rope", [128, 2, LT], F32)
            DMA(rope.t[:], rope_in.rearrange("c p t -> p c t"), W=[rope.b])
            kr = Rot(P, "kraw", 2, [128, 512], BF16)
            ko = Rot(P, "kout", 2, [128, 512], BF16)
            tmp = [Rot(P, "nr0_", 2, [128, 512], BF16)] + [Rot(P, "nr%d_" % i, 2, [128, 512], F32) for i in range(1, 5)]
            gkb = Buf()
            for (tok0, T, isctx) in tiles():
                for g in range(4):
                    k = kr.next()
                    o = ko.next()
                    r0 = K_OFF + g * 128
                    DMA(k.t[:, 0:T], pT[r0:r0 + 128, tok0:tok0 + T], W=[k.b])
                    if isctx:
                        normrope(l, 1, k.t[:, 0:T], k.b, T, None, None, None, o.t[:, 0:T], o.b, tmp)
                    else:
                        t0 = tok0 - CT
                        normrope(l, 1, k.t[:, 0:T], k.b, T, rope.t[:, 0, t0:t0 + T], rope.t[:, 1, t0:t0 + T], rope.b, o.t[:, 0:T], o.b, tmp)
                    DMA(kTn[g * 128:(g + 1) * 128, tok0:tok0 + T], o.t[:, 0:T], R=[o.b])
                    if tok0 == CT:
                        DMA(gK_in[g * 128:(g + 1) * 128, 0:128], o.t[:, 0:128], R=[o.b], W=[gkb], partial=True)
                    if tok0 + T == NTOK:
                        DMA(gK_in[g * 128:(g + 1) * 128, 128:256], o.t[:, T - 128:T], R=[o.b], W=[gkb], partial=True)
            DMA(gV_in[0:128, :], Vtm[CT:CT + 128, :], W=[gkb], partial=True)
            DMA(gV_in[128:256, :], Vtm[NTOK - 128:NTOK, :], W=[gkb], partial=True)
            CC(gK_in, gK_out, RG, [gkb], [])
            CC(gV_in, gV_out, RG, [gkb], [])

        with P.phase():
            rope = Tl(P, "rope", [128, 2, LT], F32)
            DMA(rope.t[:], rope_in.rearrange("c p t -> p c t"), W=[rope.b])
            cpool = "dve" if l == 0 else "pool"
            if l == 0:
                record_gathers(g_l1[:n1])
            kTa = Tl(P, "kTa", [128, 4, LT + 256], BF16)
            kTc = Tl(P, "kTc", [128, 4, CT], BF16)
            Va = Tl(P, "Va", [128, NB + 4, 512], BF16)
            ek = Tl(P, "ek", [128, 4, 4, 256], BF16)
            ev = Tl(P, "ev", [128, 4, 2, 512], BF16)
            DMA(kTa.t[:, :, 128:128 + LT], kTn[:, CT:NTOK].rearrange("(g p) t -> p g t", p=128), W=[kTa.b])
            DMA(kTc.t[:], kTn[:, 0:CT].rearrange("(g p) t -> p g t", p=128), W=[kTc.b])
            DMA(Va.t[:, 1:NB + 1, :], Vtm[CT:NTOK, :].rearrange("(b p) n -> p b n", p=128), W=[Va.b])
            DMA(Va.t[:, NB + 2:NB + 4, :], Vtm[0:CT, :].rearrange("(b p) n -> p b n", p=128), W=[Va.b], partial=True)
            DMA(ek.t[:], gK_out.rearrange("(r g p) c -> p r g c", r=4, g=4), W=[ek.b])
            DMA(ev.t[:], gV_out.rearrange("(r e p) n -> p r e n", r=4, e=2), W=[ev.b])
            for side, dst_k, dst_v, ecol, erow, s0 in ((0, kTa.t[:, :, 0:128], Va.t[:, 0, :], slice(128, 256), 1, 0),
                                                       (1, kTa.t[:, :, 128 + LT:256 + LT], Va.t[:, NB + 1, :], slice(0, 128), 0, 4)):
                TS("dve", dst_k, ek.t[:, 0, :, ecol], selc(s0), None, ALU.mult, None, [ek.b, cst.b], [kTa.b], partial=True)
                TS(cpool, dst_v, ev.t[:, 0, erow, :], selc(s0), None, ALU.mult, None, [ev.b, cst.b], [Va.b], partial=True)
                for r in range(1, 4):
                    STT("dve", dst_k, ek.t[:, r, :, ecol], selc(s0 + r), dst_k, ALU.mult, ALU.add, [ek.b, cst.b, kTa.b], [kTa.b], partial=True)
                    STT("dve", dst_v, ev.t[:, r, erow, :], selc(s0 + r), dst_v, ALU.mult, ALU.add, [ev.b, cst.b, Va.b], [Va.b], partial=True)
            qraw = Tl(P, "qraw", [128, 16, 512], BF16)
            qb = [Buf() for _ in range(16)]
            qTn = qraw
            agT = Tl(P, "agT", [128, 16, 512], BF16)
            ogT = Tl(P, "ogT", [128, 16, 512], BF16)
            tmp = [Rot(P, "nr0_", 2, [128, 512], BF16)] + [Rot(P, "nr%d_" % i, 2, [128, 512], F32) for i in range(1, 5)]
            ptr = Rot(P, "PT", 3, [128, 512], BF16)
            rdr = Rot(P, "rden", 2, [128, 512], F32)
            onr = Rot(P, "on", 2, [128, 512], F32)
            scale = 128.0 ** -0.5
            for (tok0, T, isctx) in tiles():
                if isctx and l == 1:
                    continue
                DMA(qraw.t[:, :, 0:T], pT[0:2048, tok0:tok0 + T].rearrange("(h p) t -> p h t", p=128), W=qb)
                DMA(agT.t[:, :, 0:T], pT[AG_OFF:AG_OFF + 2048, tok0:tok0 + T].rearrange("(h p) t -> p h t", p=128), W=[agT.b])
                for h in range(16):
                    if isctx:
                        normrope(l, 0, qraw.t[:, h, 0:T], qb[h], T, None, None, None, qTn.t[:, h, 0:T], qb[h], tmp, cpool)
                    else:
                        t0 = tok0 - CT
                        normrope(l, 0, qraw.t[:, h, 0:T], qb[h], T, rope.t[:, 0, t0:t0 + T], rope.t[:, 1, t0:t0 + T], rope.b, qTn.t[:, h, 0:T], qb[h], tmp, cpool)
                for blk in range(T // 128):
                    qs = slice(blk * 128, (blk + 1) * 128)
                    for g in range(4):
                        chunks = []
                        if not isctx:
                            nb = (tok0 - CT) // 128 + blk
                            for d_, mi in ((0, 0), (1, None), (2, 1)):
                                m = mi
                                if d_ == 0 and nb == 0:
                                    m = 2
                                if d_ == 2 and nb == NB - 1:
                                    m = 3
                                cb = nb + d_
                                chunks.append((kTa.t[:, g, cb * 128:(cb + 1) * 128], kTa.b, Va.t[:, cb, g * 128:(g + 1) * 128], m))
                        for cc_ in range(2):
                            chunks.append((kTc.t[:, g, cc_ * 128:(cc_ + 1) * 128], kTc.b, Va.t[:, NB + 2 + cc_, g * 128:(g + 1) * 128], None))
                        pO = PS[4 + (psi[0] % 2)]
                        pD = PS[6 + (psi[0] % 2)]
                        psi[0] += 1
                        def fin(ci, pS, vap, m, last):
                            pt_ = ptr.next()
                            ACT(pt_.t[:], pS.t[:, 0:512], AF.Exp, [pS.b, negB.b], [pt_.b], bias=negB.t[:, l:l + 1], scale=scale)
                            if m is not None:
                                p3 = pt_.t[:].rearrange("p (h q) -> p h q", h=4)
                                TT("dve", p3, p3, mskbf.t[:, m, :].unsqueeze(1).to_broadcast([128, 4, 128]), ALU.mult, [pt_.b, mskbf.b], [pt_.b])
                            MM(pO.t[:, 0:512], vap, pt_.t[:], ci == 0, last, [Va.b, pt_.b], [pO.b])
                            MM(pD.t[:, 0:512], onesbf.t[:], pt_.t[:], ci == 0, False, [onesbf.b, pt_.b], [pD.b])

                        pend = None
                        for ci, (kap, kb_, vap, m) in enumerate(chunks):
                            pS = psn(0, 4)
                            MM(pS.t[:, 0:512].rearrange("p (h q) -> p h q", h=4), kap, qTn.t[:, 4 * g:4 * g + 4, qs], True, True, [kb_] + qb[4 * g:4 * g + 4], [pS.b])
                            if pend is not None:
                                fin(*pend)
                            pend = (ci, pS, vap, m, ci == len(chunks) - 1)
                        fin(*pend)
                        MM(pD.t[:, 0:512], onesbf.t[0:1, :], sinkrow.t[0:1, l, g * 512:(g + 1) * 512], False, True, [onesbf.b, sinkrow.b], [pD.b])
                        rd = rdr.next()
                        on = onr.next()
                        RECIP(rd.t[:], pD.t[:, 0:512], [pD.b], [rd.b])
                        TT("dve", on.t[:], pO.t[:, 0:512], rd.t[:], ALU.mult, [pO.b, rd.b], [on.b])
                        TT(cpool, ogT.t[:, 4 * g:4 * g + 4, qs], on.t[:].rearrange("p (h q) -> p h q", h=4), agT.t[:, 4 * g:4 * g + 4, qs], ALU.mult,
                           [on.b, agT.b], [ogT.b], partial=True)
                DMA(aT[:, tok0:tok0 + T].rearrange("(h p) t -> p h t", p=128), ogT.t[:, :, 0:T], R=[ogT.b])

        with P.phase():
            segs = [(CT, LT, 0)] + ([(0, CT, 1)] if l == 0 else [])
            wpw = Tl(P, "wpw", [128, 8, 1024], BF16)
            DMA(wpw.t[:], W("wpw").rearrange("(k p) n -> p k n", p=128), W=[wpw.b])
            dww = Tl(P, "dww", [128, 8, 31], F32)
            cv3 = Tl(P, "cv3", [128, 8, 3], F32)
            DMA(dww.t[:], dww_in[l], W=[dww.b])
            DMA(cv3.t[:], cv3_in[l], W=[cv3.b])
            ar = Rot(P, "cva", 1, [128, 8, 512], BF16)
            br = Rot(P, "cvb", 1, [128, 8, 512], BF16)
            cgr = Rot(P, "cvg", 1, [128, 8, 512], BF16)
            dgc = Rot(P, "dgc", 1, [128, 31, 128], BF16)
            ybuf = Tl(P, "ybuf", [128, 8, 512], F32)
            ysq = Rot(P, "ysq", 2, [128, 512], F32)
            st4 = [Tl(P, "st%d" % i, [128, 512], F32) for i in range(4)]
            tdr = Rot(P, "td", 2, [128, 512], F32)
            zT = Tl(P, "zT", [128, 8, 512], BF16)
            cst_ = Rot(P, "cstg", 1, [128, 8, 512], BF16)
            for (s0, SL, isctx) in segs:
                uT = Tl(P, "uT%d" % isctx, [128, 8, SL + 32], BF16)
                TTs = min(512, SL)
                op("pool", lambda e, uT=uT: e.memset(uT.t[:], 0.0), writes=[uT.b])
                for t0 in range(0, SL, TTs):
                    a_, b_ = ar.next(), br.next()
                    DMA(a_.t[:, :, 0:TTs], pT[CA_OFF:CA_OFF + 1024, s0 + t0:s0 + t0 + TTs].rearrange("(c p) t -> p c t", p=128), W=[a_.b])
                    DMA(b_.t[:, :, 0:TTs], pT[CB_OFF:CB_OFF + 1024, s0 + t0:s0 + t0 + TTs].rearrange("(c p) t -> p c t", p=128), W=[b_.b])
                    TT("dve", uT.t[:, :, 16 + t0:16 + t0 + TTs], a_.t[:, :, 0:TTs], b_.t[:, :, 0:TTs], ALU.mult, [a_.b, b_.b], [uT.b], partial=True)
                if not isctx:
                    gub = Buf()
                    gob2 = Buf()
                    DMA(gU_in[:, 0:16].rearrange("(c p) e -> p c e", p=128), uT.t[:, :, 16:32], R=[uT.b], W=[gub], partial=True)
                    DMA(gU_in[:, 16:32].rearrange("(c p) e -> p c e", p=128), uT.t[:, :, SL:SL + 16], R=[uT.b], W=[gub], partial=True)
                    CC(gU_in, gU_out, RG, [gub], [gob2])
                    eu = Tl(P, "eu", [128, 4, 8, 32], BF16)
                    DMA(eu.t[:], gU_out.rearrange("(r c p) e -> p r c e", r=4, c=8), R=[gob2], W=[eu.b])
                    for dst, ecol, sb in ((uT.t[:, :, 0:16], slice(16, 32), 0), (uT.t[:, :, 16 + SL:32 + SL], slice(0, 16), 4)):
                        TS("dve", dst, eu.t[:, 0, :, ecol], selc(sb), None, ALU.mult, None, [eu.b, cst.b], [uT.b], partial=True)
                        for r in range(1, 4):
                            STT("dve", dst, eu.t[:, r, :, ecol], selc(sb + r), dst, ALU.mult, ALU.add, [eu.b, cst.b, uT.b], [uT.b], partial=True)
                for t0 in range(0, SL, TTs):
                    T = TTs
                    cg = cgr.next()
                    DMA(cg.t[:, :, 0:T], pT[CG_OFF:CG_OFF + 1024, s0 + t0:s0 + t0 + T].rearrange("(c p) t -> p c t", p=128), W=[cg.b])
                    p1, p2 = PS[0], PS[1]
                    for c in range(8):
                        dg = dgc.next()
                        TT("pool", dg.t[:], identbf.t[:].unsqueeze(1).to_broadcast([128, 31, 128]),
                           dww.t[:, c, :].unsqueeze(2).to_broadcast([128, 31, 128]), ALU.mult, [identbf.b, dww.b], [dg.b])
                        pc = psn(2, 6)
                        for j in range(31):
                            MM(pc.t[:, 0:T], dg.t[:, j, :], uT.t[:, c, t0 + j + 1:t0 + j + 1 + T], j == 0, j == 30, [dg.b, uT.b], [pc.b])
                        ACT(ybuf.t[:, c, 0:T], pc.t[:, 0:T], AF.Identity, [pc.b, cv3.b], [ybuf.b], bias=cv3.t[:, c, 0:1], partial=True)
                        yq = ysq.next()
                        ACT(yq.t[:, 0:T], ybuf.t[:, c, 0:T], AF.Square, [ybuf.b], [yq.b])
                        MM(p1.t[:, 0:T], ones32.t[:], ybuf.t[:, c, 0:T], c == 0, c == 7, [ones32.b, ybuf.b], [p1.b])
                        MM(p2.t[:, 0:T], ones32.t[:], yq.t[:, 0:T], c == 0, c == 7, [ones32.b, yq.b], [p2.b])
                    mean, ex2, var, rin = st4
                    ACT(mean.t[:, 0:T], p1.t[:, 0:T], AF.Copy, [p1.b], [mean.b], scale=1.0 / 1024)
                    ACT(ex2.t[:, 0:T], p2.t[:, 0:T], AF.Copy, [p2.b], [ex2.b], scale=1.0 / 1024)
                    TT("dve", var.t[:, 0:T], mean.t[:, 0:T], mean.t[:, 0:T], ALU.mult, [mean.b], [var.b])
                    TT("dve", var.t[:, 0:T], ex2.t[:, 0:T], var.t[:, 0:T], ALU.subtract, [ex2.b, var.b], [var.b])
                    ACT(rin.t[:, 0:T], var.t[:, 0:T], AF.Sqrt, [var.b], [rin.b], bias=EPS)
                    RECIP(rin.t[:, 0:T], rin.t[:, 0:T], [rin.b], [rin.b])
                    for c in range(8):
                        td = tdr.next()
                        TT("dve", td.t[:, 0:T], ybuf.t[:, c, 0:T], mean.t[:, 0:T], ALU.subtract, [ybuf.b, mean.b], [td.b])
                        TT("pool", td.t[:, 0:T], td.t[:, 0:T], rin.t[:, 0:T], ALU.mult, [td.b, rin.b], [td.b])
                        ACT(zT.t[:, c, 0:T], td.t[:, 0:T], AF.Silu, [td.b, cv3.b], [zT.b], bias=cv3.t[:, c, 2:3], scale=cv3.t[:, c, 1:2], partial=True)
                    cs_ = cst_.next()
                    for co in range(8):
                        pw_ = psn(6, 8)
                        for ci in range(8):
                            MM(pw_.t[:, 0:T], wpw.t[:, ci, co * 128:(co + 1) * 128], zT.t[:, ci, 0:T], ci == 0, ci == 7, [wpw.b, zT.b], [pw_.b])
                        TT("dve", cs_.t[:, co, 0:T], pw_.t[:, 0:T], cg.t[:, co, 0:T], ALU.mult, [pw_.b, cg.b], [cs_.b], partial=True)
                    DMA(cT[:, s0 + t0:s0 + t0 + T].rearrange("(c p) t -> p c t", p=128), cs_.t[:, :, 0:T], R=[cs_.b])

        with P.phase():
            wfm = Tl(P, "wfm", [128, 8, 256], F32)
            DMA(wfm.t[:], wfm_in[l].rearrange("g (k p) d -> p (g k) d", p=128), W=[wfm.b])
            CW = Tl(P, "CW", [128, 4, 2, 512], BF16)
            for g in range(4):
                for cs in range(2):
                    for mch in range(2):
                        pc = psn()
                        for k in range(2):
                            MM(pc.t[:, 0:256], dftc.t[:, cs, k, mch * 128:(mch + 1) * 128], wfm.t[:, g * 2 + k, :], k == 0, k == 1, [dftc.b, wfm.b], [pc.b])
                        ACT(CW.t[:, g, mch, cs * 256:(cs + 1) * 256], pc.t[:, 0:256], AF.Copy, [pc.b], [CW.b], partial=True)
            ufr = Rot(P, "uF", 2, [128, 8, 512], BF16)
            abr = Rot(P, "AB", 2, [128, 4, 512], BF16)
            gfb = Buf()
            for (tok0, T, isctx) in tiles():
                if isctx and l == 1:
                    continue
                uF = ufr.next()
                DMA(uF.t[:, :, 0:T], pT[F_OFF:F_OFF + 1024, tok0:tok0 + T].rearrange("(c p) t -> p c t", p=128), W=[uF.b])
                for s in range(T // 128):
                    ab = abr.next()
                    for g in range(4):
                        pa = psn()
                        for mch in range(2):
                            MM(pa.t[:, 0:512], uF.t[:, g * 2 + mch, s * 128:(s + 1) * 128], CW.t[:, g, mch, :], mch == 0, mch == 1, [uF.b, CW.b], [pa.b])
                        if g % 2 == 0:
                            ACT(ab.t[:, g, :], pa.t[:, 0:512], AF.Copy, [pa.b], [ab.b], partial=True)
                        else:
                            CP("dve", ab.t[:, g, :], pa.t[:, 0:512], [pa.b], [ab.b], partial=True)
                    if isctx:
                        DMA(gFc[s * 128:(s + 1) * 128, :], ab.t[:].rearrange("p g n -> p (g n)"), R=[ab.b])
                    else:
                        r0 = tok0 - CT + s * 128
                        DMA(gF_in[r0:r0 + 128, :], ab.t[:].rearrange("p g n -> p (g n)"), R=[ab.b], W=[gfb], partial=True)
            frc = min(LT, 256)
            for c in range(LT // frc):
                CC(gF_in[c * frc:(c + 1) * frc, :], gF_out[c * 4 * frc:(c + 1) * 4 * frc, :], RG, [gfb], [])

        with P.phase():
            gar = Rot(P, "ga", 3, [128, 2048], BF16)
            tcr = Rot(P, "tc", 3, [128, 512], BF16)
            tsr = Rot(P, "tsn", 3, [128, 512], BF16)
            fgr = Rot(P, "fg", 2, [128, 8, 512], BF16)
            fst = Rot(P, "fst", 2, [128, 8, 512], BF16)
            tcx = Tl(P, "tcx", [128, 2, 2, 256], F32)
            tcb = Tl(P, "tcb", [128, 2, 2, 256], BF16)
            DMA(tcx.t[:], dftctx_in.rearrange("c a p k -> p c a k"), W=[tcx.b])
            CP("dve", tcb.t[:], tcx.t[:], [tcx.b], [tcb.b])
            for (tok0, T, isctx) in tiles():
                if isctx and l == 1:
                    continue
                na = 2 if isctx else NA
                fg = fgr.next()
                DMA(fg.t[:, :, 0:T], pT[FG_OFF:FG_OFF + 1024, tok0:tok0 + T].rearrange("(c p) t -> p c t", p=128), W=[fg.b])
                for a in range(na):
                    ga = gar.next()
                    if isctx:
                        DMA(ga.t[:], gFc[a * 128:(a + 1) * 128, :], W=[ga.b])
                        tcap, tsap, tb1, tb2 = tcb.t[:, 0, a, :], tcb.t[:, 1, a, :], tcb.b, tcb.b
                    else:
                        t0 = tok0 - CT
                        DMA(ga.t[:], gF_out[a * 128:(a + 1) * 128, :], W=[ga.b])
                        tc_, ts_ = tcr.next(), tsr.next()
                        DMA(tc_.t[:, 0:T], tabC[a, :, t0:t0 + T], W=[tc_.b])
                        DMA(ts_.t[:, 0:T], tabS[a, :, t0:t0 + T], W=[ts_.b])
                        tcap, tsap, tb1, tb2 = tc_.t[:, 0:T], ts_.t[:, 0:T], tc_.b, ts_.b
                    for fc in range(8):
                        g, half = fc // 2, fc % 2
                        c0 = g * 512 + half * 128
                        MM(PS[fc].t[:, 0:T], ga.t[:, c0:c0 + 128], tcap, a == 0, False, [ga.b, tb1], [PS[fc].b])
                        MM(PS[fc].t[:, 0:T], ga.t[:, c0 + 256:c0 + 384], tsap, False, a == na - 1, [ga.b, tb2], [PS[fc].b])
                fs = fst.next()
                for fc in range(8):
                    TT("dve", fs.t[:, fc, 0:T], PS[fc].t[:, 0:T], fg.t[:, fc, 0:T], ALU.mult, [PS[fc].b, fg.b], [fs.b], partial=True)
                DMA(fT[:, tok0:tok0 + T].rearrange("(c p) t -> p c t", p=128), fs.t[:, :, 0:T], R=[fs.b])

        with P.phase():
            aTr = Rot(P, "aTt", 1, [128, 16, 512], BF16)
            fTr = Rot(P, "fTt", 1, [128, 8, 512], BF16)
            cTr = Rot(P, "cTt", 1, [128, 8, 512], BF16)
            war = Rot(P, "wa", 2, [128, 16, 512], BF16)
            wfr = Rot(P, "wf", 2, [128, 8, 512], BF16)
            wcr = Rot(P, "wc", 2, [128, 8, 512], BF16)
            mgr = Rot(P, "mg", 2, [128, 3, 4, 512], BF16)
            e1t = [Rot(P, "e1t%d" % i, 2, [128, 512], F32) for i in range(3)]
            mTt = Rot(P, "mTt", 2, [128, 4, 512], BF16)
            wau, wfu, wcu = W("wau"), W("wfu"), W("wcu")
            epool = "dve" if l == 0 else "pool"
            n2 = n1 + (len(g_l1) - n1 + 1) // 2
            if l == 0:
                record_gathers(g_l1[n1:n2])
            for (tok0, T, isctx) in tiles():
                if isctx and l == 1:
                    continue
                at, ft, ct = aTr.next(), fTr.next(), cTr.next()
                DMA(at.t[:, :, 0:T], aT[:, tok0:tok0 + T].rearrange("(c p) t -> p c t", p=128), W=[at.b])
                DMA(ft.t[:, :, 0:T], fT[:, tok0:tok0 + T].rearrange("(c p) t -> p c t", p=128), W=[ft.b])
                DMA(ct.t[:, :, 0:T], cT[:, tok0:tok0 + T].rearrange("(c p) t -> p c t", p=128), W=[ct.b])
                for cb in range(D // 512):
                    cs = slice(cb * 512, (cb + 1) * 512)
                    wa, wf, wc, mg = war.next(), wfr.next(), wcr.next(), mgr.next()
                    DMA(wa.t[:].rearrange("p k n -> p (k n)"), wau[cb * 128:(cb + 1) * 128, :], W=[wa.b])
                    DMA(wf.t[:].rearrange("p k n -> p (k n)"), wfu[cb * 128:(cb + 1) * 128, :], W=[wf.b])
                    DMA(wc.t[:].rearrange("p k n -> p (k n)"), wcu[cb * 128:(cb + 1) * 128, :], W=[wc.b])
                    for br_ in range(3):
                        r0 = MG + br_ * D + cb * 512
                        DMA(mg.t[:, br_, :, 0:T], pT[r0:r0 + 512, tok0:tok0 + T].rearrange("(c p) t -> p c t", p=128), W=[mg.b], partial=(br_ > 0))
                    mt = mTt.next()
                    for dcl in range(4):
                        ds_ = slice(dcl * 128, (dcl + 1) * 128)
                        pa, pf, pc = psn(0, 3), psn(3, 6), psn(6, 8)
                        for k in range(16):
                            MM(pa.t[:, 0:T], wa.t[:, k, ds_], at.t[:, k, 0:T], k == 0, k == 15, [wa.b, at.b], [pa.b])
                        for k in range(8):
                            MM(pf.t[:, 0:T], wf.t[:, k, ds_], ft.t[:, k, 0:T], k == 0, k == 7, [wf.b, ft.b], [pf.b])
                        for k in range(8):
                            MM(pc.t[:, 0:T], wc.t[:, k, ds_], ct.t[:, k, 0:T], k == 0, k == 7, [wc.b, ct.b], [pc.b])
                        t0_, t1_, t2_ = e1t[0].next(), e1t[1].next(), e1t[2].next()
                        TT("dve", t0_.t[:, 0:T], pa.t[:, 0:T], mg.t[:, 0, dcl, 0:T], ALU.mult, [pa.b, mg.b], [t0_.b])
                        TT("dve", t1_.t[:, 0:T], pf.t[:, 0:T], mg.t[:, 1, dcl, 0:T], ALU.mult, [pf.b, mg.b], [t1_.b])
                        TT("dve", t2_.t[:, 0:T], pc.t[:, 0:T], mg.t[:, 2, dcl, 0:T], ALU.mult, [pc.b, mg.b], [t2_.b])
                        TT(epool, t0_.t[:, 0:T], t0_.t[:, 0:T], t1_.t[:, 0:T], ALU.add, [t0_.b, t1_.b], [t0_.b])
                        TT(epool, mt.t[:, dcl, 0:T], t0_.t[:, 0:T], t2_.t[:, 0:T], ALU.add, [t0_.b, t2_.b], [mt.b], partial=True)
                    DMA(mTd[cb * 512:(cb + 1) * 512, tok0:tok0 + T].rearrange("(c p) t -> p c t", p=128), mt.t[:, :, 0:T], R=[mt.b])

        with P.phase():
            mTr = Rot(P, "mT", 2, [128, KD, 512], BF16)
            wor = Rot(P, "wo", 2, [128, KD, 512], BF16)
            gtr = Rot(P, "gtb", 2, [128, 512], F32)
            xsr = Rot(P, "xs", 3, [128, 512], F32)
            e2t = Rot(P, "e2t", 2, [128, 512], F32)
            xor_ = Rot(P, "xo", 3, [128, 512], F32)
            wo = W("wo")
            epool = "dve" if l == 0 else "pool"
            if l == 0:
                record_gathers(g_l1[n1 + (len(g_l1) - n1 + 1) // 2:])
            for (tok0, T, isctx) in tiles():
                if isctx and l == 1:
                    continue
                q = 2 * l + isctx
                mt = mTr.next()
                for k0 in range(0, KD, 8):
                    k1 = min(KD, k0 + 8)
                    DMA(mt.t[:, k0:k1, 0:T], mTd[k0 * 128:k1 * 128, tok0:tok0 + T].rearrange("(c p) t -> p c t", p=128), W=[mt.b], partial=(k0 > 0))
                for nb in range(D // 512):
                    cs = slice(nb * 512, (nb + 1) * 512)
                    wt = wor.next()
                    DMA(wt.t[:].rearrange("p k n -> p (k n)"), wo[nb * 128:(nb + 1) * 128, :], W=[wt.b])
                    gt = gtr.next()
                    DMA(gt.t[:], gtb_d[q, :, cs], W=[gt.b])
                    for s in range(T // 128):
                        xs = xsr.next()
                        DMA(xs.t[:], src_rows(l, tok0 + s * 128, 128)[:, cs], W=[xs.b])
                        po = psn()
                        for kc in range(KD):
                            MM(po.t[:, 0:512], mt.t[:, kc, s * 128:(s + 1) * 128], wt.t[:, kc, :], kc == 0, kc == KD - 1, [mt.b, wt.b], [po.b])
                        tt_ = e2t.next()
                        xo = xor_.next()
                        TT("dve", tt_.t[:], po.t[:, 0:512], gt.t[:], ALU.mult, [po.b, gt.b], [tt_.b])
                        TT(epool, xo.t[:], tt_.t[:], xs.t[:], ALU.add, [tt_.b, xs.b], [xo.b])
                        if l == 0:
                            DMA(x1[tok0 + s * 128:tok0 + (s + 1) * 128, cs], xo.t[:], R=[xo.b])
                        else:
                            r0 = tok0 - CT + s * 128
                            DMA(out[r0:r0 + 128, cs], xo.t[:], R=[xo.b])

    finish(P)
    P.top.close()
    return nc, P


def host_consts(cfg, r):
    LT, SEQ, NA = cfg.LT, cfg.SEQ, cfg.NA
    cst = np.zeros((128, 1040), np.float32)
    cst[:, 0:128] = np.eye(128, dtype=np.float32)
    R = np.zeros((128, 128), np.float32)
    for base in (0, 64):
        for i in range(32):
            R[base + i, base + i + 32] = -1.0
            R[base + i + 32, base + i] = 1.0
    cst[:, 128:256] = R.T
    jj = np.arange(128)[:, None]
    ii = np.arange(128)[None, :]
    cst[:, 256:384] = (ii <= jj).astype(np.float32)
    cst[:, 384:512] = (jj <= ii).astype(np.float32)
    if r > 0:
        cst[:, 512 + r - 1] = 1.0
        cst[:, 520] = 1.0
    if r < 3:
        cst[:, 516 + r + 1] = 1.0
        cst[:, 521] = 1.0
    for q in range(4):
        cst[q, 528 + q * 128:528 + (q + 1) * 128] = 1.0
    pos = np.arange(r * LT, (r + 1) * LT)
    row = (pos // 64).astype(np.float64)
    col = (pos % 64).astype(np.float64)
    inv = 10000.0 ** (-np.arange(0, 64, 2, dtype=np.float64) / 64)
    ar_, ac_ = row[:, None] * inv[None, :], col[:, None] * inv[None, :]
    ang = np.concatenate([ar_, ar_, ac_, ac_], -1).astype(np.float32)
    rope = np.stack([np.cos(ang).T, np.sin(ang).T]).astype(np.float32)
    k = pos.astype(np.float64)[None, :]
    p = np.arange(128, dtype=np.float64)[:, None]
    a = np.arange(NA, dtype=np.float64)[:, None]
    sc = 1.0 / np.sqrt(SEQ * 256.0)
    angB = 2 * np.pi * ((p * k) % SEQ) / SEQ
    frc = min(LT, 256)
    ai = np.arange(NA)
    m0 = ai * 128
    cc_, rr_, ii_ = m0 // (4 * frc), (m0 % (4 * frc)) // frc, m0 % frc
    l0 = rr_ * LT + cc_ * frc + ii_
    a = (l0 // 128).astype(np.float64)[:, None]
    angA = 2 * np.pi * ((128 * a * k) % SEQ) / SEQ
    dftB = np.stack([np.cos(angB), np.sin(angB)]).astype(np.float32)
    dftA = (np.stack([np.cos(angA), np.sin(angA)]) * sc).astype(np.float32)
    l_ = np.arange(256, dtype=np.float64)
    a256 = 2 * np.pi * np.outer(l_, l_) / 256
    scc = 1.0 / 256.0
    dftctx = np.stack([np.cos(a256) * scc, -np.sin(a256) * scc]).reshape(2, 2, 128, 256).astype(np.float32)
    dc = np.stack([np.cos(a256), np.sin(a256)])
    dftc = dc.reshape(2, 2, 128, 256).transpose(2, 0, 1, 3).astype(np.float32)
    return dict(cst=cst, rope=rope, dftB=dftB, dftA=dftA, dftctx=np.ascontiguousarray(dftctx), dftc=np.ascontiguousarray(dftc))


def make_in_maps(cfg, inp):
    D, LT, KD, MC = cfg.D, cfg.LT, cfg.KD, cfg.MC
    f = lambda a: np.ascontiguousarray(np.asarray(a, dtype=np.float32))
    x, c, ctx, c_ctx = f(inp["x"]), f(inp["c"]), f(inp["ctx"]), f(inp["c_ctx"])
    maps = []
    ngT = f(f(inp["norm_g"]).reshape(2, KD, 128).transpose(0, 2, 1))
    qg, kg = f(inp["q_norm_g"]), f(inp["k_norm_g"])
    qkg = f(np.stack([qg, kg], -1))
    qkrow = f(np.concatenate([qg, kg], -1).reshape(2, 1, 256))
    sink = f(f(inp["attn_sink"]).reshape(2, 1, 16))
    dww = f(f(inp["conv_dw_w"]).reshape(2, 31, 8, 128).transpose(0, 3, 2, 1))
    cv3 = f(np.stack([f(inp["conv_dw_b"]), f(inp["conv_ln_g"]), f(inp["conv_ln_b"])], -1).reshape(2, 8, 128, 3).transpose(0, 2, 1, 3))
    bmod = f(inp["b_mod"])
    wl = {"w_in": f(inp["w_in"]), "w_au": f(inp["w_attn_up"]), "w_fu": f(inp["w_fourier_up"]), "w_cu": f(inp["w_conv_up"]),
          "w_o": f(inp["w_out"]), "w_pw": f(inp["w_conv_pw"])}
    wl_g = {}
    for k_, w in wl.items():
        K_, N_ = w.shape[1], w.shape[2]
        if k_ == "w_pw":
            wl_g[k_] = w
        else:
            wl_g[k_] = np.ascontiguousarray(w.reshape(2, K_ // 128, 128, N_ // 512, 512).transpose(0, 3, 2, 1, 4)).reshape(2, N_ // 4, 4 * K_)
    wmod = f(inp["w_mod"])
    wfm = f(inp["w_fourier_mix"])
    for core in range(8):
        b, r = core // 4, core % 4
        m = host_consts(cfg, r)
        m["x"] = f(x[b, r * LT:(r + 1) * LT])
        m["ctx"] = f(ctx[b])
        m["cvT"] = f(np.stack([c[b], c_ctx], -1).reshape(KD, 128, 2).transpose(1, 0, 2))
        m["ngT"], m["qkg"], m["qkrow"], m["sink"], m["wfm"], m["dww"], m["cv3"] = ngT, qkg, qkrow, sink, wfm, dww, cv3
        for k_, w in wl.items():
            g = wl_g[k_]
            R_g, C_g = g.shape[1], g.shape[2]
            rc = wchunk(R_g, C_g)
            m[k_] = f(g.reshape(2, R_g // (4 * rc), 4, rc, C_g)[:, :, r].reshape(2, R_g // 4, C_g))
        m["w_mod"] = f(wmod[:, :, r * MC:(r + 1) * MC])
        m["b_mod"] = f(np.repeat(bmod[:, None, r * MC:(r + 1) * MC], 2, axis=1))
        maps.append(m)
    return maps


def kernel(**inputs):
    cfg = Cfg()
    nc, _ = build(cfg)
    maps = make_in_maps(cfg, inputs)
    res = run_bass_kernel_spmd(nc, maps, core_ids=list(range(8)))
    outp = np.empty((2, cfg.SEQ, cfg.D), np.float32)
    for core in range(8):
        b, r = core // 4, core % 4
        outp[b, r * cfg.LT:(r + 1) * cfg.LT] = res.results[core]["out"]
    return outp
```

```python
import contextlib
import numpy as np
import concourse.bass as bass
import concourse.mybir as mybir
from concourse.bass_utils import run_bass_kernel_spmd

F32 = mybir.dt.float32
BF16 = mybir.dt.bfloat16
AF = mybir.ActivationFunctionType
ALU = mybir.AluOpType
AX = mybir.AxisListType

ENGS = ("pe", "act", "dve", "pool", "sp")
DMAQ = ("sp", "act", "pool")
RING = 8
EPS = 1e-6


class Buf:
    __slots__ = ("writers", "readers")

    def __init__(self):
        self.writers = {}
        self.readers = {}


class Op:
    __slots__ = ("eng", "fn", "deps", "signal", "val", "dma", "slot", "key", "inc", "epoch")

    def __init__(self, eng, fn, dma=False):
        self.eng = eng
        self.fn = fn
        self.deps = []
        self.signal = False
        self.val = 0
        self.dma = dma
        self.slot = -1
        self.key = eng
        self.inc = 1
        self.epoch = 0


class Prog:
    def __init__(self, nc):
        self.nc = nc
        self.top = contextlib.ExitStack()
        self.scope = self.top
        st = self.top
        self.esem = {e: st.enter_context(nc.semaphore("s_" + e)) for e in ENGS}
        self.ring = {q: [st.enter_context(nc.semaphore("r_%s%d" % (q, i))) for i in range(RING)] for q in DMAQ}
        self.ccsem = st.enter_context(nc.semaphore("s_cc"))
        self.cnt = {e: 0 for e in ENGS}
        self.ndma = {q: 0 for q in DMAQ}
        self.ncc = 0
        self.epoch = 0
        self.ops = {e: [] for e in ENGS}
        self.waited = {e: {} for e in ENGS}
        self.nops = 0
        self.mk = self.sbuf("mk", [128, 8], F32)

    def sbuf(self, name, shape, dt):
        self.nalloc = getattr(self, "nalloc", 0) + 1
        return self.scope.enter_context(self.nc.sbuf_tensor("%s_%d" % (name, self.nalloc), list(shape), dt))

    def psum(self, name, shape, dt):
        return self.scope.enter_context(self.nc.psum_tensor(name, list(shape), dt))

    @contextlib.contextmanager
    def phase(self):
        old = self.scope
        with contextlib.ExitStack() as st:
            self.scope = st
            yield
            self.nphase = getattr(self, "nphase", 0) + 1
            import os as _os
            if self.nphase <= int(_os.environ.get("KSTOP", "999")):
                self.flush()
            else:
                self.ops = {e: [] for e in ENGS}
        self.scope = old

    def _add(self, op, reads, writes, partial):
        op.epoch = self.epoch
        deps = {}
        for b in reads:
            for w in b.writers.values():
                deps[id(w)] = w
        for b in writes:
            for w in b.readers.values():
                deps[id(w)] = w
            for w in b.writers.values():
                deps[id(w)] = w
        for d in deps.values():
            if d is op or d.epoch != self.epoch:
                continue
            if (not d.dma) and (not op.dma) and d.eng == op.eng and op.eng == "pe":
                continue
            d.signal = True
            op.deps.append(d)
        for b in reads:
            b.readers[op.key] = op
        for b in writes:
            if not partial:
                b.writers = {}
                b.readers = {}
            b.writers[op.key] = op
        self.ops[op.eng].append(op)
        self.nops += 1
        return op

    def op(self, eng, fn, reads=(), writes=(), partial=False):
        return self._add(Op(eng, fn), reads, writes, partial)

    def dma(self, q, fn, reads=(), writes=(), partial=False):
        o = Op(q, fn, dma=True)
        i = self.ndma[q]
        self.ndma[q] = i + 1
        o.slot = i % RING
        o.val = 16 * (i // RING + 1)
        o.inc = 16
        o.key = (q, o.slot)
        o.signal = True
        return self._add(o, reads, writes, partial)

    def cc(self, fn, reads=(), writes=()):
        o = Op("pool", fn, dma=True)
        self.ncc += 1
        o.slot = -2
        o.val = self.ncc
        o.inc = 1
        o.key = ("cc", 0)
        o.signal = True
        return self._add(o, reads, writes, False)

    def flush(self):
        nc = self.nc
        ops_snap = self.ops
        mval = {}
        for e in ENGS:
            c = self.cnt[e]
            for o in self.ops[e]:
                if not o.dma and o.signal:
                    c += 1
                    o.val = c
            mval[e] = c + 1
            self.cnt[e] = c + (0 if e == "pe" else 1)
        mk = self.mk

        def semof(o):
            if o.slot == -2:
                return self.ccsem
            if o.dma:
                return self.ring[o.eng][o.slot]
            return self.esem[o.eng]

        def run(e, eng):
            waited = self.waited[e]

            def wait(s, v):
                if waited.get(id(s), 0) < v:
                    eng.wait_ge(s, v)
                    waited[id(s)] = v

            last = {}
            for o in ops_snap[e]:
                for d in o.deps:
                    wait(semof(d), d.val)
                if o.slot == -2:
                    o.fn(eng).then_inc(self.ccsem, 1)
                    wait(self.ccsem, o.val)
                elif o.dma:
                    s = self.ring[e][o.slot]
                    if o.val > 16:
                        wait(s, o.val - 16)
                    o.fn(eng).then_inc(s, 16)
                    last[o.slot] = o
                else:
                    ins = o.fn(eng)
                    if o.signal:
                        ins.then_inc(self.esem[e], 1)
            for sl, o in last.items():
                wait(self.ring[e][sl], o.val)
            if e == "dve":
                m = eng.memset(mk[:, 0:1], 0.0)
            elif e == "pool":
                m = eng.memset(mk[:, 1:2], 0.0)
            elif e == "act":
                m = eng.memzero(mk[:, 2:3])
            elif e == "sp":
                m = eng.nop()
            else:
                m = None
            if m is not None:
                m.then_inc(self.esem[e], 1)
            for e2 in ENGS:
                if e2 != "pe":
                    wait(self.esem[e2], mval[e2])

        self.pending = getattr(self, 'pending', [])
        self.pending.append(run)

        self.epoch += 1
        self.ops = {e: [] for e in ENGS}


def finish(P):
    nc = P.nc
    with nc.Block() as block:
        @block.tensor
        def _(eng):
            for r in P.pending:
                r("pe", eng)

        @block.scalar
        def _(eng):
            for r in P.pending:
                r("act", eng)

        @block.vector
        def _(eng):
            for r in P.pending:
                r("dve", eng)

        @block.gpsimd
        def _(eng):
            for r in P.pending:
                r("pool", eng)

        @block.sync
        def _(eng):
            for r in P.pending:
                r("sp", eng)


class Tl:
    def __init__(self, P, name, shape, dt, psum=False):
        self.t = (P.psum if psum else P.sbuf)(name, shape, dt)
        self.b = Buf()


class Rot:
    def __init__(self, P, name, n, shape, dt):
        self.ts = [Tl(P, "%s%d" % (name, i), shape, dt) for i in range(n)]
        self.i = 0

    def next(self):
        t = self.ts[self.i % len(self.ts)]
        self.i += 1
        return t


class Cfg:
    def __init__(self, D=4096, SEQ=8192):
        self.D = D
        self.SEQ = SEQ
        self.KD = D // 128
        self.LT = SEQ // 4
        self.CT = 256
        self.NTOK = self.CT + self.LT
        self.MG = 10240
        self.INW = 10240 + 3 * D
        self.NA = SEQ // 128
        self.MC = 3 * D // 4
        self.TT = min(512, self.LT)


Q_OFF, K_OFF, V_OFF, AG_OFF, F_OFF, FG_OFF, CA_OFF, CB_OFF, CG_OFF = 0, 2048, 2560, 3072, 5120, 6144, 7168, 8192, 9216
VBLK = V_OFF // 512


def wchunk(K, N):
    n = K // 4
    for rc in range(n, 0, -1):
        if n % rc == 0 and rc * N * 2 <= (1 << 20):
            return rc


def gshape(nm, K, N):
    if nm == "wpw":
        return K, N
    return N // 4, 4 * K


def fam_func(blk):
    c = blk * 512
    if AG_OFF <= c < F_OFF or FG_OFF <= c < CA_OFF or CG_OFF <= c < 10240:
        return AF.Silu
    if CB_OFF <= c < CG_OFF or c >= 10240:
        return AF.Sigmoid
    return AF.Copy


def build(cfg, debug=False):
    D, KD, LT, CT, NTOK, INW, NA, MC, SEQ = cfg.D, cfg.KD, cfg.LT, cfg.CT, cfg.NTOK, cfg.INW, cfg.NA, cfg.MC, cfg.SEQ
    MG = cfg.MG
    NB = LT // 128
    nc = bass.Bass("TRN2", target_bir_lowering=False)

    def din(name, shape, dt=F32):
        return nc.dram_tensor(name, list(shape), dt, kind="ExternalInput").ap()

    def dscr(name, shape, dt=BF16, dbg=False):
        kind = "ExternalOutput" if (dbg and debug and dt == F32) else "Internal"
        return nc.dram_tensor(name, list(shape), dt, kind=kind).ap()

    x_in = din("x", [LT, D])
    ctx_in = din("ctx", [CT, D])
    cst_in = din("cst", [128, 1040])
    rope_in = din("rope", [2, 128, LT])
    dftB_in = din("dftB", [2, 128, LT])
    dftA_in = din("dftA", [2, NA, LT])
    dftctx_in = din("dftctx", [2, 2, 128, 256])
    dftc_in = din("dftc", [128, 2, 2, 256])
    cvT_in = din("cvT", [128, KD, 2])
    ngT_in = din("ngT", [2, 128, KD])
    qkg_in = din("qkg", [2, 128, 2])
    qkrow_in = din("qkrow", [2, 1, 256])
    sink_in = din("sink", [2, 1, 16])
    wfm_in = din("wfm", [2, 4, 256, 256])
    dww_in = din("dww", [2, 128, 8, 31])
    cv3_in = din("cv3", [2, 128, 8, 3])
    w_in_s = din("w_in", [2, INW // 16, 4 * D])
    w_au_s = din("w_au", [2, D // 16, 4 * 2048])
    w_fu_s = din("w_fu", [2, D // 16, 4 * 1024])
    w_cu_s = din("w_cu", [2, D // 16, 4 * 1024])
    w_o_s = din("w_o", [2, D // 16, 4 * D])
    w_pw_s = din("w_pw", [2, 256, 1024])
    w_mod_s = din("w_mod", [2, D, MC])
    b_mod_s = din("b_mod", [2, 2, MC])
    out = nc.dram_tensor("out", [LT, D], F32, kind="ExternalOutput").ap()

    wspec = [("win", D, INW, w_in_s), ("wau", 2048, D, w_au_s), ("wfu", 1024, D, w_fu_s),
             ("wcu", 1024, D, w_cu_s), ("wo", D, D, w_o_s), ("wpw", 1024, 1024, w_pw_s)]
    Wg = {}
    Wgin = {}
    for nm, K, N, _ in wspec:
        R_g, C_g = gshape(nm, K, N)
        for l in range(2):
            Wgin[(nm, l)] = dscr("gi_%s%d" % (nm, l), [R_g // 4, C_g])
            Wg[(nm, l)] = dscr("g_%s%d" % (nm, l), [R_g, C_g])
    pT = dscr("pT", [INW, NTOK], dbg=True)
    Vtm = dscr("Vtm", [NTOK, 512], dbg=True)
    kTn = dscr("kTn", [512, NTOK], dbg=True)
    gK_in = dscr("gK_in", [512, 256]); gK_out = dscr("gK_out", [4 * 512, 256])
    gV_in = dscr("gV_in", [256, 512]); gV_out = dscr("gV_out", [4 * 256, 512])
    gU_in = dscr("gU_in", [1024, 32]); gU_out = dscr("gU_out", [4 * 1024, 32])
    gF_in = dscr("gF_in", [LT, 2048]); gF_out = dscr("gF_out", [SEQ, 2048])
    gFc = dscr("gFc", [CT, 2048])
    tabC = dscr("tabC", [NA, 128, LT]); tabS = dscr("tabS", [NA, 128, LT])
    aT = dscr("aT", [2048, NTOK], dbg=True)
    fT = dscr("fT", [1024, NTOK], dbg=True)
    cT = dscr("cT", [1024, NTOK], dbg=True)
    mTd = dscr("mTd", [D, NTOK], dbg=True)
    x1 = dscr("x1", [NTOK, D], F32, dbg=True)
    gmod_in = dscr("gmod_in", [4, MC], F32); gmod_out = dscr("gmod_out", [16, MC], F32)
    gtb_d = dscr("gtb_d", [4, 128, D], F32)
    RG = [[0, 1, 2, 3], [4, 5, 6, 7]]
    RG8 = [list(range(8))]

    import os as _os2
    KSUB = int(_os2.environ.get('KSUB', '99'))
    P = Prog(nc)
    op, dma = P.op, P.dma

    def MM(o, lt, rh, start, stop, R, W):
        op("pe", lambda e: e.matmul(o, lhsT=lt, rhs=rh, start=start, stop=stop), reads=R, writes=W, partial=not start)

    def ACT(o, i, func, R, W, bias=None, scale=None, accum=None, partial=False):
        kw = {}
        if bias is not None:
            kw["bias"] = bias
        if scale is not None:
            kw["scale"] = scale
        if accum is not None:
            kw["accum_out"] = accum
        op("act", lambda e: e.activation(out=o, in_=i, func=func, **kw), reads=R, writes=W, partial=partial)

    def TT(eng, o, a, b, aop, R, W, partial=False):
        op(eng, lambda e: e.tensor_tensor(out=o, in0=a, in1=b, op=aop), reads=R, writes=W, partial=partial)

    def TS(eng, o, a, s1, s2, op0, op1, R, W, partial=False):
        if s2 is None:
            op(eng, lambda e: e.tensor_scalar(out=o, in0=a, scalar1=s1, scalar2=None, op0=op0), reads=R, writes=W, partial=partial)
        else:
            op(eng, lambda e: e.tensor_scalar(out=o, in0=a, scalar1=s1, scalar2=s2, op0=op0, op1=op1), reads=R, writes=W, partial=partial)

    def STT(eng, o, a, s, b, op0, op1, R, W, partial=False):
        op(eng, lambda e: e.scalar_tensor_tensor(out=o, in0=a, scalar=s, in1=b, op0=op0, op1=op1), reads=R, writes=W, partial=partial)

    def CP(eng, o, i, R, W, partial=False):
        op(eng, lambda e: e.tensor_copy(out=o, in_=i), reads=R, writes=W, partial=partial)

    def RECIP(o, i, R, W):
        op("dve", lambda e: e.reciprocal(out=o, in_=i), reads=R, writes=W)

    def DMA(o, i, R=(), W=(), q="sp", partial=False):
        if q == "sp" and str(o.space).endswith("DRAM") and not str(i.space).endswith("DRAM"):
            q = "act"
        dma(q, lambda e: e.dma_start(out=o, in_=i), reads=R, writes=W, partial=partial)

    def CC(i, o, groups, R, W):
        P.cc(lambda e: e.collective_compute("AllGather", ALU.bypass, replica_groups=groups, ins=[i], outs=[o]), reads=R, writes=W)

    cst = Tl(P, "cst", [128, 1040], F32)
    ident = cst.t[:, 0:128]
    rotT = cst.t[:, 128:256]
    selc = lambda i: cst.t[:, 512 + i:513 + i]
    selm = cst.t[0:4, 528:1040]
    ones32 = Tl(P, "ones32", [128, 128], F32)
    onesbf = Tl(P, "onesbf", [128, 128], BF16)
    identbf = Tl(P, "identbf", [128, 128], BF16)
    mskbf = Tl(P, "mskbf", [128, 4, 128], BF16)
    gsT = Tl(P, "gsT", [128, 4, KD], F32)
    shT = Tl(P, "shT", [128, 4, KD], F32)
    qkg = Tl(P, "qkg", [128, 2, 2], F32)
    negB = Tl(P, "negB", [128, 2], F32)
    sinkrow = Tl(P, "sinkrow", [1, 2, 2048], BF16)
    dftc = Tl(P, "dftc", [128, 2, 2, 256], F32)
    PS = [Tl(P, "ps%d" % i, [128, 512], F32, psum=True) for i in range(8)]
    psi = [0]

    def psn(lo=0, hi=8):
        t = PS[lo + psi[0] % (hi - lo)]
        psi[0] += 1
        return t

    with P.phase():
        DMA(cst.t[:], cst_in, W=[cst.b])
        DMA(qkg.t[:], qkg_in.rearrange("l p k -> p l k"), W=[qkg.b])
        DMA(dftc.t[:], dftc_in, W=[dftc.b])
        op("dve", lambda e: e.memset(ones32.t[:], 1.0), writes=[ones32.b])
        op("pool", lambda e: e.memset(onesbf.t[:], 1.0), writes=[onesbf.b])
        CP("dve", identbf.t[:], ident, [cst.b], [identbf.b])
        CP("dve", mskbf.t[:, 0, :], cst.t[:, 256:384], [cst.b], [mskbf.b], partial=True)
        CP("dve", mskbf.t[:, 1, :], cst.t[:, 384:512], [cst.b], [mskbf.b], partial=True)
        TS("dve", mskbf.t[:, 2, :], cst.t[:, 256:384], selc(8), None, ALU.mult, None, [cst.b], [mskbf.b], partial=True)
        TS("dve", mskbf.t[:, 3, :], cst.t[:, 384:512], selc(9), None, ALU.mult, None, [cst.b], [mskbf.b], partial=True)
        qkrow = Tl(P, "qkrow", [1, 2, 256], F32)
        sk = Tl(P, "sk", [1, 2, 16], F32)
        mx = Tl(P, "mx", [1, 8], F32)
        DMA(qkrow.t[:], qkrow_in.rearrange("l o k -> o l k"), W=[qkrow.b])
        DMA(sk.t[:], sink_in.rearrange("l o k -> o l k"), W=[sk.b])
        for l in range(2):
            for j in range(2):
                op("dve", lambda e, l=l, j=j: e.reduce_max(out=mx.t[0:1, 2 * l + j:2 * l + j + 1], in_=qkrow.t[0:1, l, j * 128:(j + 1) * 128],
                                                         axis=AX.X, apply_absolute_value=True), reads=[qkrow.b], writes=[mx.b], partial=True)
            TT("dve", mx.t[0:1, 4 + l:5 + l], mx.t[0:1, 2 * l:2 * l + 1], mx.t[0:1, 2 * l + 1:2 * l + 2], ALU.mult, [mx.b], [mx.b], partial=True)
            pb = psn()
            MM(pb.t[:, 0:1], ones32.t[0:1, :], mx.t[0:1, 4 + l:5 + l], True, True, [ones32.b, mx.b], [pb.b])
            ACT(negB.t[:, l:l + 1], pb.t[:, 0:1], AF.Copy, [pb.b], [negB.b], scale=-(128.0 ** 0.5), partial=True)
            es = Tl(P, "es%d" % l, [1, 16], F32)
            ACT(es.t[:], sk.t[0:1, l, :], AF.Exp, [sk.b, negB.b], [es.b], bias=negB.t[0:1, l:l + 1])
            for h in range(16):
                TS("dve", sinkrow.t[0:1, l, h * 128:(h + 1) * 128], ones32.t[0:1, :], es.t[0:1, h:h + 1], None, ALU.mult, None,
                   [ones32.b, es.b], [sinkrow.b], partial=True)

    with P.phase():
        gb = Buf()
        for nm, K, N, src in wspec:
            for l in range(2):
                DMA(Wgin[(nm, l)], src[l], W=[gb], q="pool", partial=True)
    chunkbuf = {}

    def gather_list():
        order = [("win", 0)] + [(nm, 0) for nm in ("wpw", "wau", "wfu", "wcu", "wo")] + [(nm, 1) for nm in ("win", "wpw", "wau", "wfu", "wcu", "wo")]
        dims = {nm: gshape(nm, K, N) for nm, K, N, _ in wspec}
        out_ = []
        for nm, l in order:
            R_g, C_g = dims[nm]
            rc = wchunk(R_g, C_g)
            for c in range((R_g // 4) // rc):
                out_.append((nm, l, c, rc))
        return out_

    glist = gather_list()
    g_l0 = [g for g in glist if g[1] == 0]
    g_l1 = [g for g in glist if g[1] == 1]
    n1 = len([g for g in g_l1 if g[0] == "win"]) * 10 // 11

    def record_gathers(items):
        for nm, l, c, rc in items:
            b = Buf()
            chunkbuf[(nm, l, c)] = b
            CC(Wgin[(nm, l)][c * rc:(c + 1) * rc, :], Wg[(nm, l)][c * 4 * rc:(c + 1) * 4 * rc, :], RG, [], [b])

    def wbufs(nm, l, r0, r1):
        K_, N_ = [(K, N) for n_, K, N, _ in wspec if n_ == nm][0]
        R_g, C_g = gshape(nm, K_, N_)
        rc4 = 4 * wchunk(R_g, C_g)
        return [chunkbuf[(nm, l, c)] for c in range(r0 // rc4, (r1 - 1) // rc4 + 1)]

    with P.phase():
        cvT = Tl(P, "cvT", [128, KD, 2], F32)
        scT = Tl(P, "scT", [128, KD, 2], F32)
        bm = Tl(P, "bm", [2, 2, MC], F32)
        mrow = Tl(P, "mrow", [2, 2, MC], F32)
        DMA(cvT.t[:], cvT_in, W=[cvT.b])
        DMA(bm.t[:], b_mod_s.rearrange("l q m -> q l m"), W=[bm.b])
        ACT(scT.t[:], cvT.t[:], AF.Silu, [cvT.b], [scT.b])
        wmr = Rot(P, "wm", 3, [128, MC], F32)
        nch = [(o, min(512, MC - o)) for o in range(0, MC, 512)]
        gmb = Buf()
        for l in range(2):
            pss = [PS[i] for i in range(len(nch))]
            for kc in range(KD):
                wm = wmr.next()
                DMA(wm.t[:], w_mod_s[l, kc * 128:(kc + 1) * 128, :], W=[wm.b])
                for i, (o, n) in enumerate(nch):
                    MM(pss[i].t[0:2, 0:n], scT.t[:, kc, :], wm.t[:, o:o + n], kc == 0, kc == KD - 1, [scT.b, wm.b], [pss[i].b])
            for i, (o, n) in enumerate(nch):
                TT("dve", mrow.t[0:2, l, o:o + n], pss[i].t[0:2, 0:n], bm.t[0:2, l, o:o + n], ALU.add, [pss[i].b, bm.b], [mrow.b], partial=True)
            DMA(gmod_in[2 * l:2 * l + 2, :], mrow.t[0:2, l, :], R=[mrow.b], W=[gmb], partial=True)
        gob = Buf()
        if KSUB >= 1:
            CC(gmod_in, gmod_out, RG, [gmb], [gob])
        R_ = Tl(P, "Rr", [4, 3 * D], F32)
        if KSUB >= 2:
          DMA(R_.t[:].rearrange("q (r j) -> q r j", r=4), gmod_out.rearrange("(r q) j -> q r j", q=4), R=[gob], W=[R_.b])
        modT = Tl(P, "modT", [128, 2 * KD, 4], F32)
        for c0 in (range(0, 2 * KD, 64) if KSUB >= 3 else []):
            pm = psn()
            n = min(64, 2 * KD - c0)
            for c in range(n):
                op("pe", lambda e, c=c, c0=c0, pm=pm: e.transpose(pm.t[:, c * 4:c * 4 + 4], R_.t[0:4, (c0 + c) * 128:(c0 + c + 1) * 128], cst.t[0:4, 0:4]),
                   reads=[R_.b, cst.b], writes=[pm.b], partial=(c > 0))
            CP("dve", modT.t[:, c0:c0 + n, :], pm.t[:, 0:4 * n].rearrange("p (c q) -> p c q", q=4), [pm.b], [modT.b], partial=True)
        ngT = Tl(P, "ngT", [128, 2, KD], F32)
        DMA(ngT.t[:], ngT_in.rearrange("l p k -> p l k"), W=[ngT.b])
        for q in (range(4) if KSUB >= 4 else []):
            STT("dve", gsT.t[:, q, :], modT.t[:, KD:2 * KD, q], 1.0, ngT.t[:, q // 2, :], ALU.add, ALU.mult, [modT.b, ngT.b], [gsT.b], partial=True)
            CP("dve", shT.t[:, q, :], modT.t[:, 0:KD, q], [modT.b], [shT.b], partial=True)
        gst = Rot(P, "gst", 2, [128, 512], F32)
        for q in (range(4) if KSUB >= 5 else []):
            for nb in range(D // 512):
                pg = psn()
                MM(pg.t[:, 0:512], selm[:, q * 128:(q + 1) * 128], R_.t[0:4, 2 * D + nb * 512:2 * D + (nb + 1) * 512], True, True, [cst.b, R_.b], [pg.b])
                g = gst.next()
                ACT(g.t[:], pg.t[:, 0:512], AF.Copy, [pg.b], [g.b])
                DMA(gtb_d[q, :, nb * 512:(nb + 1) * 512], g.t[:], R=[g.b])

    with P.phase():
        CB = Tl(P, "CB", [128, LT], F32)
        SB = Tl(P, "SB", [128, LT], F32)
        DMA(CB.t[:], dftB_in[0], W=[CB.b])
        DMA(SB.t[:], dftB_in[1], W=[SB.b])
        car = Rot(P, "ca", 2, [128, LT], F32)
        sar = Rot(P, "sa", 2, [128, LT], F32)
        t1r = Rot(P, "t1", 2, [128, LT], F32)
        t2r = Rot(P, "t2", 2, [128, LT], F32)
        ocr = Rot(P, "oc", 2, [128, LT], BF16)
        osr = Rot(P, "os", 2, [128, LT], BF16)
        g_w0 = [g for g in g_l0 if g[0] == "win"]
        record_gathers(g_w0)
        for a in range(NA):
            ca, sa, t1, t2, oc, os_ = car.next(), sar.next(), t1r.next(), t2r.next(), ocr.next(), osr.next()
            DMA(ca.t[:], dftA_in[0, a:a + 1, :].partition_broadcast(128), W=[ca.b])
            DMA(sa.t[:], dftA_in[1, a:a + 1, :].partition_broadcast(128), W=[sa.b])
            TT("dve", t1.t[:], ca.t[:], CB.t[:], ALU.mult, [ca.b, CB.b], [t1.b])
            TT("dve", t2.t[:], sa.t[:], SB.t[:], ALU.mult, [sa.b, SB.b], [t2.b])
            TT("dve", oc.t[:], t1.t[:], t2.t[:], ALU.subtract, [t1.b, t2.b], [oc.b])
            DMA(tabC[a], oc.t[:], R=[oc.b])
            TT("dve", t2.t[:], sa.t[:], CB.t[:], ALU.mult, [sa.b, CB.b], [t2.b])
            TT("dve", t1.t[:], ca.t[:], SB.t[:], ALU.mult, [ca.b, SB.b], [t1.b])
            STT("dve", os_.t[:], t2.t[:], -1.0, t1.t[:], ALU.mult, ALU.subtract, [t1.b, t2.b], [os_.b])
            DMA(tabS[a], os_.t[:], R=[os_.b])

    def src_rows(l, tok0, n):
        if l == 1:
            return x1[tok0:tok0 + n, :]
        if tok0 < CT:
            return ctx_in[tok0:tok0 + n, :]
        return x_in[tok0 - CT:tok0 - CT + n, :]

    def tiles():
        ts = [(0, CT, 1)]
        for t0 in range(0, LT, cfg.TT):
            ts.append((CT + t0, cfg.TT, 0))
        return ts

    def normrope(l, which, src, srcb, T, cos, sin, ropeb, outap, outb, tmp, pe_="pool"):
        sq, rs, kn, t1, t2 = [r_.next() for r_ in tmp]
        ACT(sq.t[:, 0:T], src, AF.Square, [srcb], [sq.b])
        p1 = psn(0, 4)
        MM(p1.t[:, 0:T], onesbf.t[:], sq.t[:, 0:T], True, True, [onesbf.b, sq.b], [p1.b])
        ACT(rs.t[:, 0:T], p1.t[:, 0:T], AF.Ln, [p1.b], [rs.b], bias=EPS, scale=1.0 / 128)
        ACT(rs.t[:, 0:T], rs.t[:, 0:T], AF.Exp, [rs.b], [rs.b], scale=-0.5)
        STT("dve", kn.t[:, 0:T], src, qkg.t[:, l, which:which + 1], rs.t[:, 0:T], ALU.mult, ALU.mult, [srcb, qkg.b, rs.b], [kn.b])
        if cos is None:
            CP(pe_, outap, kn.t[:, 0:T], [kn.b], [outb], partial=True)
            return
        p2 = psn(0, 4)
        MM(p2.t[:, 0:T], rotT, kn.t[:, 0:T], True, True, [cst.b, kn.b], [p2.b])
        TT(pe_, t1.t[:, 0:T], kn.t[:, 0:T], cos, ALU.mult, [kn.b, ropeb], [t1.b])
        TT("dve", t2.t[:, 0:T], p2.t[:, 0:T], sin, ALU.mult, [p2.b, ropeb], [t2.b])
        TT("dve", outap, t1.t[:, 0:T], t2.t[:, 0:T], ALU.add, [t1.b, t2.b], [outb], partial=True)

    for l in range(2):
        W = lambda nm: Wg[(nm, l)]
        with P.phase():
            xr = Rot(P, "xt", 2, [128, D], F32)
            junk = Tl(P, "junk", [128, D], BF16)
            ssr = Rot(P, "ss", 2, [128, 2], F32)
            hT = Tl(P, "hT", [128, KD, 512], BF16)
            wr = Rot(P, "wblk", 2, [128, KD, 512], BF16)
            stg = Rot(P, "stg", 3, [128, 512], BF16)
            win = W("win")
            if l == 0:
                record_gathers([g for g in g_l0 if g[0] != "win"])
            for (tok0, T, isctx) in tiles():
                q = 2 * l + isctx
                for s in range(T // 128):
                    xt = xr.next()
                    ss = ssr.next()
                    DMA(xt.t[:], src_rows(l, tok0 + s * 128, 128), W=[xt.b])
                    op("dve", lambda e, ss=ss: e.memset(ss.t[:], 0.0), writes=[ss.b])
                    ACT(junk.t[:], xt.t[:], AF.Square, [xt.b, ss.b], [junk.b, ss.b], accum=ss.t[:, 0:1])
                    ACT(ss.t[:, 1:2], ss.t[:, 0:1], AF.Sqrt, [ss.b], [ss.b], bias=EPS, scale=1.0 / D)
                    RECIP(ss.t[:, 1:2], ss.t[:, 1:2], [ss.b], [ss.b])
                    TS("dve", xt.t[:], xt.t[:], ss.t[:, 1:2], None, ALU.mult, None, [xt.b, ss.b], [xt.b])
                    for k0 in range(0, KD, 4):
                        pt = psn()
                        for j in range(4):
                            kc = k0 + j
                            op("pe", lambda e, pt=pt, j=j, kc=kc, xt=xt: e.transpose(pt.t[:, j * 128:(j + 1) * 128], xt.t[:, kc * 128:(kc + 1) * 128], ident),
                               reads=[xt.b, cst.b], writes=[pt.b], partial=(j > 0))
                        for j in range(4):
                            kc = k0 + j
                            if j % 2 == 0:
                                TS("dve", hT.t[:, kc, s * 128:(s + 1) * 128], pt.t[:, j * 128:(j + 1) * 128], gsT.t[:, q, kc:kc + 1], shT.t[:, q, kc:kc + 1],
                                   ALU.mult, ALU.add, [pt.b, gsT.b, shT.b], [hT.b], partial=True)
                            else:
                                ACT(hT.t[:, kc, s * 128:(s + 1) * 128], pt.t[:, j * 128:(j + 1) * 128], AF.Identity, [pt.b, gsT.b, shT.b], [hT.b],
                                    bias=shT.t[:, q, kc:kc + 1], scale=gsT.t[:, q, kc:kc + 1], partial=True)
                blks = list(range(INW // 512))
                if l == 1 and isctx:
                    blks = [K_OFF // 512, VBLK]
                for blk in blks:
                    wt = wr.next()
                    DMA(wt.t[:].rearrange("p k n -> p (k n)"), win[blk * 128:(blk + 1) * 128, :], R=wbufs("win", l, blk * 128, (blk + 1) * 128), W=[wt.b])
                    if blk == VBLK:
                        for s in range(T // 128):
                            pv = psn()
                            for kc in range(KD):
                                MM(pv.t[:, 0:512], hT.t[:, kc, s * 128:(s + 1) * 128], wt.t[:, kc, :], kc == 0, kc == KD - 1, [hT.b, wt.b], [pv.b])
                            sg = stg.next()
                            ACT(sg.t[:], pv.t[:, 0:512], AF.Copy, [pv.b], [sg.b])
                            DMA(Vtm[tok0 + s * 128:tok0 + (s + 1) * 128, :], sg.t[:], R=[sg.b])
                    else:
                        fn = fam_func(blk)
                        for c in range(4):
                            pv = psn()
                            for kc in range(KD):
                                MM(pv.t[:, 0:T], wt.t[:, kc, c * 128:(c + 1) * 128], hT.t[:, kc, 0:T], kc == 0, kc == KD - 1, [hT.b, wt.b], [pv.b])
                            sg = stg.next()
                            ACT(sg.t[:, 0:T], pv.t[:, 0:T], fn, [pv.b], [sg.b])
                            r0 = (blk * 4 + c) * 128
                            DMA(pT[r0:r0 + 128, tok0:tok0 + T], sg.t[:, 0:T], R=[sg.b])

        with P.phase():
            rope = Tl(P, "rope", [128, 2, LT], F32)
            DMA(rope.t[:], rope_in.rearrange("c p t -> p c t"), W=[rope.b])
            kr = Rot(P, "kraw", 2, [128, 512], BF16)
            ko = Rot(P, "kout", 2, [128, 512], BF16)
            tmp = [Rot(P, "nr0_", 2, [128, 512], BF16)] + [Rot(P, "nr%d_" % i, 2, [128, 512], F32) for i in range(1, 5)]
            gkb = Buf()
            for (tok0, T, isctx) in tiles():
                for g in range(4):
                    k = kr.next()
                    o = ko.next()
                    r0 = K_OFF + g * 128
                    DMA(k.t[:, 0:T], pT[r0:r0 + 128, tok0:tok0 + T], W=[k.b])
                    if isctx:
                        normrope(l, 1, k.t[:, 0:T], k.b, T, None, None, None, o.t[:, 0:T], o.b, tmp)
                    else:
                        t0 = tok0 - CT
                        normrope(l, 1, k.t[:, 0:T], k.b, T, rope.t[:, 0, t0:t0 + T], rope.t[:, 1, t0:t0 + T], rope.b, o.t[:, 0:T], o.b, tmp)
                    DMA(kTn[g * 128:(g + 1) * 128, tok0:tok0 + T], o.t[:, 0:T], R=[o.b])
                    if tok0 == CT:
                        DMA(gK_in[g * 128:(g + 1) * 128, 0:128], o.t[:, 0:128], R=[o.b], W=[gkb], partial=True)
                    if tok0 + T == NTOK:
                        DMA(gK_in[g * 128:(g + 1) * 128, 128:256], o.t[:, T - 128:T], R=[o.b], W=[gkb], partial=True)
            DMA(gV_in[0:128, :], Vtm[CT:CT + 128, :], W=[gkb], partial=True)
            DMA(gV_in[128:256, :], Vtm[NTOK - 128:NTOK, :], W=[gkb], partial=True)
            CC(gK_in, gK_out, RG, [gkb], [])
            CC(gV_in, gV_out, RG, [gkb], [])

        with P.phase():
            rope = Tl(P, "rope", [128, 2, LT], F32)
            DMA(rope.t[:], rope_in.rearrange("c p t -> p c t"), W=[rope.b])
            cpool = "dve" if l == 0 else "pool"
            if l == 0:
                record_gathers(g_l1[:n1])
            kTa = Tl(P, "kTa", [128, 4, LT + 256], BF16)
            kTc = Tl(P, "kTc", [128, 4, CT], BF16)
            Va = Tl(P, "Va", [128, NB + 4, 512], BF16)
            ek = Tl(P, "ek", [128, 4, 4, 256], BF16)
            ev = Tl(P, "ev", [128, 4, 2, 512], BF16)
            DMA(kTa.t[:, :, 128:128 + LT], kTn[:, CT:NTOK].rearrange("(g p) t -> p g t", p=128), W=[kTa.b])
            DMA(kTc.t[:], kTn[:, 0:CT].rearrange("(g p) t -> p g t", p=128), W=[kTc.b])
            DMA(Va.t[:, 1:NB + 1, :], Vtm[CT:NTOK, :].rearrange("(b p) n -> p b n", p=128), W=[Va.b])
            DMA(Va.t[:, NB + 2:NB + 4, :], Vtm[0:CT, :].rearrange("(b p) n -> p b n", p=128), W=[Va.b], partial=True)
            DMA(ek.t[:], gK_out.rearrange("(r g p) c -> p r g c", r=4, g=4), W=[ek.b])
            DMA(ev.t[:], gV_out.rearrange("(r e p) n -> p r e n", r=4, e=2), W=[ev.b])
            for side, dst_k, dst_v, ecol, erow, s0 in ((0, kTa.t[:, :, 0:128], Va.t[:, 0, :], slice(128, 256), 1, 0),
                                                       (1, kTa.t[:, :, 128 + LT:256 + LT], Va.t[:, NB + 1, :], slice(0, 128), 0, 4)):
                TS("dve", dst_k, ek.t[:, 0, :, ecol], selc(s0), None, ALU.mult, None, [ek.b, cst.b], [kTa.b], partial=True)
                TS(cpool, dst_v, ev.t[:, 0, erow, :], selc(s0), None, ALU.mult, None, [ev.b, cst.b], [Va.b], partial=True)
                for r in range(1, 4):
                    STT("dve", dst_k, ek.t[:, r, :, ecol], selc(s0 + r), dst_k, ALU.mult, ALU.add, [ek.b, cst.b, kTa.b], [kTa.b], partial=True)
                    STT("dve", dst_v, ev.t[:, r, erow, :], selc(s0 + r), dst_v, ALU.mult, ALU.add, [ev.b, cst.b, Va.b], [Va.b], partial=True)
            qraw = Tl(P, "qraw", [128, 16, 512], BF16)
            qb = [Buf() for _ in range(16)]
            qTn = qraw
            agT = Tl(P, "agT", [128, 16, 512], BF16)
            ogT = Tl(P, "ogT", [128, 16, 512], BF16)
            tmp = [Rot(P, "nr0_", 2, [128, 512], BF16)] + [Rot(P, "nr%d_" % i, 2, [128, 512], F32) for i in range(1, 5)]
            ptr = Rot(P, "PT", 3, [128, 512], BF16)
            rdr = Rot(P, "rden", 2, [128, 512], F32)
            onr = Rot(P, "on", 2, [128, 512], F32)
            scale = 128.0 ** -0.5
            for (tok0, T, isctx) in tiles():
                if isctx and l == 1:
                    continue
                DMA(qraw.t[:, :, 0:T], pT[0:2048, tok0:tok0 + T].rearrange("(h p) t -> p h t", p=128), W=qb)
                DMA(agT.t[:, :, 0:T], pT[AG_OFF:AG_OFF + 2048, tok0:tok0 + T].rearrange("(h p) t -> p h t", p=128), W=[agT.b])
                for h in range(16):
                    if isctx:
                        normrope(l, 0, qraw.t[:, h, 0:T], qb[h], T, None, None, None, qTn.t[:, h, 0:T], qb[h], tmp, cpool)
                    else:
                        t0 = tok0 - CT
                        normrope(l, 0, qraw.t[:, h, 0:T], qb[h], T, rope.t[:, 0, t0:t0 + T], rope.t[:, 1, t0:t0 + T], rope.b, qTn.t[:, h, 0:T], qb[h], tmp, cpool)
                for blk in range(T // 128):
                    qs = slice(blk * 128, (blk + 1) * 128)
                    for g in range(4):
                        chunks = []
                        if not isctx:
                            nb = (tok0 - CT) // 128 + blk
                            for d_, mi in ((0, 0), (1, None), (2, 1)):
                                m = mi
                                if d_ == 0 and nb == 0:
                                    m = 2
                                if d_ == 2 and nb == NB - 1:
                                    m = 3
                                cb = nb + d_
                                chunks.append((kTa.t[:, g, cb * 128:(cb + 1) * 128], kTa.b, Va.t[:, cb, g * 128:(g + 1) * 128], m))
                        for cc_ in range(2):
                            chunks.append((kTc.t[:, g, cc_ * 128:(cc_ + 1) * 128], kTc.b, Va.t[:, NB + 2 + cc_, g * 128:(g + 1) * 128], None))
                        pO = PS[4 + (psi[0] % 2)]
                        pD = PS[6 + (psi[0] % 2)]
                        psi[0] += 1
                        def fin(ci, pS, vap, m, last):
                            pt_ = ptr.next()
                            ACT(pt_.t[:], pS.t[:, 0:512], AF.Exp, [pS.b, negB.b], [pt_.b], bias=negB.t[:, l:l + 1], scale=scale)
                            if m is not None:
                                p3 = pt_.t[:].rearrange("p (h q) -> p h q", h=4)
                                TT("dve", p3, p3, mskbf.t[:, m, :].unsqueeze(1).to_broadcast([128, 4, 128]), ALU.mult, [pt_.b, mskbf.b], [pt_.b])
                            MM(pO.t[:, 0:512], vap, pt_.t[:], ci == 0, last, [Va.b, pt_.b], [pO.b])
                            MM(pD.t[:, 0:512], onesbf.t[:], pt_.t[:], ci == 0, False, [onesbf.b, pt_.b], [pD.b])

                        pend = None
                        for ci, (kap, kb_, vap, m) in enumerate(chunks):
                            pS = psn(0, 4)
                            MM(pS.t[:, 0:512].rearrange("p (h q) -> p h q", h=4), kap, qTn.t[:, 4 * g:4 * g + 4, qs], True, True, [kb_] + qb[4 * g:4 * g + 4], [pS.b])
                            if pend is not None:
                                fin(*pend)
                            pend = (ci, pS, vap, m, ci == len(chunks) - 1)
                        fin(*pend)
                        MM(pD.t[:, 0:512], onesbf.t[0:1, :], sinkrow.t[0:1, l, g * 512:(g + 1) * 512], False, True, [onesbf.b, sinkrow.b], [pD.b])
                        rd = rdr.next()
                        on = onr.next()
                        RECIP(rd.t[:], pD.t[:, 0:512], [pD.b], [rd.b])
                        TT("dve", on.t[:], pO.t[:, 0:512], rd.t[:], ALU.mult, [pO.b, rd.b], [on.b])
                        TT(cpool, ogT.t[:, 4 * g:4 * g + 4, qs], on.t[:].rearrange("p (h q) -> p h q", h=4), agT.t[:, 4 * g:4 * g + 4, qs], ALU.mult,
                           [on.b, agT.b], [ogT.b], partial=True)
                DMA(aT[:, tok0:tok0 + T].rearrange("(h p) t -> p h t", p=128), ogT.t[:, :, 0:T], R=[ogT.b])

        with P.phase():
            wfm = Tl(P, "wfm", [128, 8, 256], F32)
            DMA(wfm.t[:], wfm_in[l].rearrange("g (k p) d -> p (g k) d", p=128), W=[wfm.b])
            CW = Tl(P, "CW", [128, 4, 2, 512], BF16)
            for g in range(4):
                for cs in range(2):
                    for mch in range(2):
                        pc = psn()
                        for k in range(2):
                            MM(pc.t[:, 0:256], dftc.t[:, cs, k, mch * 128:(mch + 1) * 128], wfm.t[:, g * 2 + k, :], k == 0, k == 1, [dftc.b, wfm.b], [pc.b])
                        ACT(CW.t[:, g, mch, cs * 256:(cs + 1) * 256], pc.t[:, 0:256], AF.Copy, [pc.b], [CW.b], partial=True)
            ufr = Rot(P, "uF", 2, [128, 8, 512], BF16)
            abr = Rot(P, "AB", 2, [128, 4, 512], BF16)
            gfb = Buf()
            for (tok0, T, isctx) in tiles():
                if isctx and l == 1:
                    continue
                uF = ufr.next()
                DMA(uF.t[:, :, 0:T], pT[F_OFF:F_OFF + 1024, tok0:tok0 + T].rearrange("(c p) t -> p c t", p=128), W=[uF.b])
                for s in range(T // 128):
                    ab = abr.next()
                    for g in range(4):
                        pa = psn()
                        for mch in range(2):
                            MM(pa.t[:, 0:512], uF.t[:, g * 2 + mch, s * 128:(s + 1) * 128], CW.t[:, g, mch, :], mch == 0, mch == 1, [uF.b, CW.b], [pa.b])
                        if g % 2 == 0:
                            ACT(ab.t[:, g, :], pa.t[:, 0:512], AF.Copy, [pa.b], [ab.b], partial=True)
                        else:
                            CP("dve", ab.t[:, g, :], pa.t[:, 0:512], [pa.b], [ab.b], partial=True)
                    if isctx:
                        DMA(gFc[s * 128:(s + 1) * 128, :], ab.t[:].rearrange("p g n -> p (g n)"), R=[ab.b])
                    else:
                        r0 = tok0 - CT + s * 128
                        DMA(gF_in[r0:r0 + 128, :], ab.t[:].rearrange("p g n -> p (g n)"), R=[ab.b], W=[gfb], partial=True)

        with P.phase():
            segs = [(CT, LT, 0)] + ([(0, CT, 1)] if l == 0 else [])
            wpw = Tl(P, "wpw", [128, 8, 1024], BF16)
            DMA(wpw.t[:], W("wpw").rearrange("(k p) n -> p k n", p=128), W=[wpw.b])
            dww = Tl(P, "dww", [128, 8, 31], F32)
            cv3 = Tl(P, "cv3", [128, 8, 3], F32)
            DMA(dww.t[:], dww_in[l], W=[dww.b])
            DMA(cv3.t[:], cv3_in[l], W=[cv3.b])
            ar = Rot(P, "cva", 1, [128, 8, 512], BF16)
            br = Rot(P, "cvb", 1, [128, 8, 512], BF16)
            cgr = Rot(P, "cvg", 1, [128, 8, 512], BF16)
            dgc = Rot(P, "dgc", 1, [128, 31, 128], BF16)
            ybuf = Tl(P, "ybuf", [128, 8, 512], F32)
            ysq = Rot(P, "ysq", 2, [128, 512], F32)
            st4 = [Tl(P, "st%d" % i, [128, 512], F32) for i in range(4)]
            tdr = Rot(P, "td", 2, [128, 512], F32)
            zT = Tl(P, "zT", [128, 8, 512], BF16)
            cst_ = Rot(P, "cstg", 1, [128, 8, 512], BF16)
            for (s0, SL, isctx) in segs:
                uT = Tl(P, "uT%d" % isctx, [128, 8, SL + 32], BF16)
                TTs = min(512, SL)
                op("dve", lambda e, uT=uT: e.memset(uT.t[:], 0.0), writes=[uT.b])
                for t0 in range(0, SL, TTs):
                    a_, b_ = ar.next(), br.next()
                    DMA(a_.t[:, :, 0:TTs], pT[CA_OFF:CA_OFF + 1024, s0 + t0:s0 + t0 + TTs].rearrange("(c p) t -> p c t", p=128), W=[a_.b])
                    DMA(b_.t[:, :, 0:TTs], pT[CB_OFF:CB_OFF + 1024, s0 + t0:s0 + t0 + TTs].rearrange("(c p) t -> p c t", p=128), W=[b_.b])
                    TT("dve", uT.t[:, :, 16 + t0:16 + t0 + TTs], a_.t[:, :, 0:TTs], b_.t[:, :, 0:TTs], ALU.mult, [a_.b, b_.b], [uT.b], partial=True)
                if not isctx:
                    gub = Buf()
                    gob2 = Buf()
                    DMA(gU_in[:, 0:16].rearrange("(c p) e -> p c e", p=128), uT.t[:, :, 16:32], R=[uT.b], W=[gub], partial=True)
                    DMA(gU_in[:, 16:32].rearrange("(c p) e -> p c e", p=128), uT.t[:, :, SL:SL + 16], R=[uT.b], W=[gub], partial=True)
                    CC(gU_in, gU_out, RG, [gub], [gob2])
                    frc = min(LT, 256)
                    for c in range(LT // frc):
                        CC(gF_in[c * frc:(c + 1) * frc, :], gF_out[c * 4 * frc:(c + 1) * 4 * frc, :], RG, [], [])
                    eu = Tl(P, "eu", [128, 4, 8, 32], BF16)
                    DMA(eu.t[:], gU_out.rearrange("(r c p) e -> p r c e", r=4, c=8), R=[gob2], W=[eu.b])
                    for dst, ecol, sb in ((uT.t[:, :, 0:16], slice(16, 32), 0), (uT.t[:, :, 16 + SL:32 + SL], slice(0, 16), 4)):
                        TS("dve", dst, eu.t[:, 0, :, ecol], selc(sb), None, ALU.mult, None, [eu.b, cst.b], [uT.b], partial=True)
                        for r in range(1, 4):
                            STT("dve", dst, eu.t[:, r, :, ecol], selc(sb + r), dst, ALU.mult, ALU.add, [eu.b, cst.b, uT.b], [uT.b], partial=True)
                for t0 in range(0, SL, TTs):
                    T = TTs
                    cg = cgr.next()
                    DMA(cg.t[:, :, 0:T], pT[CG_OFF:CG_OFF + 1024, s0 + t0:s0 + t0 + T].rearrange("(c p) t -> p c t", p=128), W=[cg.b])
                    p1, p2 = PS[0], PS[1]
                    for c in range(8):
                        dg = dgc.next()
                        TT("dve", dg.t[:], identbf.t[:].unsqueeze(1).to_broadcast([128, 31, 128]),
                           dww.t[:, c, :].unsqueeze(2).to_broadcast([128, 31, 128]), ALU.mult, [identbf.b, dww.b], [dg.b])
                        pc = psn(2, 6)
                        for j in range(31):
                            MM(pc.t[:, 0:T], dg.t[:, j, :], uT.t[:, c, t0 + j + 1:t0 + j + 1 + T], j == 0, j == 30, [dg.b, uT.b], [pc.b])
                        ACT(ybuf.t[:, c, 0:T], pc.t[:, 0:T], AF.Identity, [pc.b, cv3.b], [ybuf.b], bias=cv3.t[:, c, 0:1], partial=True)
                        yq = ysq.next()
                        ACT(yq.t[:, 0:T], ybuf.t[:, c, 0:T], AF.Square, [ybuf.b], [yq.b])
                        MM(p1.t[:, 0:T], ones32.t[:], ybuf.t[:, c, 0:T], c == 0, c == 7, [ones32.b, ybuf.b], [p1.b])
                        MM(p2.t[:, 0:T], ones32.t[:], yq.t[:, 0:T], c == 0, c == 7, [ones32.b, yq.b], [p2.b])
                    mean, ex2, var, rin = st4
                    ACT(mean.t[:, 0:T], p1.t[:, 0:T], AF.Copy, [p1.b], [mean.b], scale=1.0 / 1024)
                    ACT(ex2.t[:, 0:T], p2.t[:, 0:T], AF.Copy, [p2.b], [ex2.b], scale=1.0 / 1024)
                    TT("dve", var.t[:, 0:T], mean.t[:, 0:T], mean.t[:, 0:T], ALU.mult, [mean.b], [var.b])
                    TT("dve", var.t[:, 0:T], ex2.t[:, 0:T], var.t[:, 0:T], ALU.subtract, [ex2.b, var.b], [var.b])
                    ACT(rin.t[:, 0:T], var.t[:, 0:T], AF.Sqrt, [var.b], [rin.b], bias=EPS)
                    RECIP(rin.t[:, 0:T], rin.t[:, 0:T], [rin.b], [rin.b])
                    for c in range(8):
                        td = tdr.next()
                        TT("dve", td.t[:, 0:T], ybuf.t[:, c, 0:T], mean.t[:, 0:T], ALU.subtract, [ybuf.b, mean.b], [td.b])
                        TT("dve", td.t[:, 0:T], td.t[:, 0:T], rin.t[:, 0:T], ALU.mult, [td.b, rin.b], [td.b])
                        ACT(zT.t[:, c, 0:T], td.t[:, 0:T], AF.Silu, [td.b, cv3.b], [zT.b], bias=cv3.t[:, c, 2:3], scale=cv3.t[:, c, 1:2], partial=True)
                    cs_ = cst_.next()
                    for co in range(8):
                        pw_ = psn(6, 8)
                        for ci in range(8):
                            MM(pw_.t[:, 0:T], wpw.t[:, ci, co * 128:(co + 1) * 128], zT.t[:, ci, 0:T], ci == 0, ci == 7, [wpw.b, zT.b], [pw_.b])
                        TT("dve", cs_.t[:, co, 0:T], pw_.t[:, 0:T], cg.t[:, co, 0:T], ALU.mult, [pw_.b, cg.b], [cs_.b], partial=True)
                    DMA(cT[:, s0 + t0:s0 + t0 + T].rearrange("(c p) t -> p c t", p=128), cs_.t[:, :, 0:T], R=[cs_.b])

        with P.phase():
            gar = Rot(P, "ga", 3, [128, 2048], BF16)
            tcr = Rot(P, "tc", 3, [128, 512], BF16)
            tsr = Rot(P, "tsn", 3, [128, 512], BF16)
            fgr = Rot(P, "fg", 2, [128, 8, 512], BF16)
            fst = Rot(P, "fst", 2, [128, 8, 512], BF16)
            tcx = Tl(P, "tcx", [128, 2, 2, 256], F32)
            tcb = Tl(P, "tcb", [128, 2, 2, 256], BF16)
            DMA(tcx.t[:], dftctx_in.rearrange("c a p k -> p c a k"), W=[tcx.b])
            CP("dve", tcb.t[:], tcx.t[:], [tcx.b], [tcb.b])
            for (tok0, T, isctx) in tiles():
                if isctx and l == 1:
                    continue
                na = 2 if isctx else NA
                fg = fgr.next()
                DMA(fg.t[:, :, 0:T], pT[FG_OFF:FG_OFF + 1024, tok0:tok0 + T].rearrange("(c p) t -> p c t", p=128), W=[fg.b])
                for a in range(na):
                    ga = gar.next()
                    if isctx:
                        DMA(ga.t[:], gFc[a * 128:(a + 1) * 128, :], W=[ga.b])
                        tcap, tsap, tb1, tb2 = tcb.t[:, 0, a, :], tcb.t[:, 1, a, :], tcb.b, tcb.b
                    else:
                        t0 = tok0 - CT
                        DMA(ga.t[:], gF_out[a * 128:(a + 1) * 128, :], W=[ga.b])
                        tc_, ts_ = tcr.next(), tsr.next()
                        DMA(tc_.t[:, 0:T], tabC[a, :, t0:t0 + T], W=[tc_.b])
                        DMA(ts_.t[:, 0:T], tabS[a, :, t0:t0 + T], W=[ts_.b])
                        tcap, tsap, tb1, tb2 = tc_.t[:, 0:T], ts_.t[:, 0:T], tc_.b, ts_.b
                    for fc in range(8):
                        g, half = fc // 2, fc % 2
                        c0 = g * 512 + half * 128
                        MM(PS[fc].t[:, 0:T], ga.t[:, c0:c0 + 128], tcap, a == 0, False, [ga.b, tb1], [PS[fc].b])
                        MM(PS[fc].t[:, 0:T], ga.t[:, c0 + 256:c0 + 384], tsap, False, a == na - 1, [ga.b, tb2], [PS[fc].b])
                fs = fst.next()
                for fc in range(8):
                    TT("dve", fs.t[:, fc, 0:T], PS[fc].t[:, 0:T], fg.t[:, fc, 0:T], ALU.mult, [PS[fc].b, fg.b], [fs.b], partial=True)
                DMA(fT[:, tok0:tok0 + T].rearrange("(c p) t -> p c t", p=128), fs.t[:, :, 0:T], R=[fs.b])

        with P.phase():
            aTr = Rot(P, "aTt", 1, [128, 16, 512], BF16)
            fTr = Rot(P, "fTt", 1, [128, 8, 512], BF16)
            cTr = Rot(P, "cTt", 1, [128, 8, 512], BF16)
            war = Rot(P, "wa", 2, [128, 16, 512], BF16)
            wfr = Rot(P, "wf", 2, [128, 8, 512], BF16)
            wcr = Rot(P, "wc", 2, [128, 8, 512], BF16)
            mgr = Rot(P, "mg", 2, [128, 3, 4, 512], BF16)
            e1t = [Rot(P, "e1t%d" % i, 2, [128, 512], F32) for i in range(3)]
            mTt = Rot(P, "mTt", 2, [128, 4, 512], BF16)
            wau, wfu, wcu = W("wau"), W("wfu"), W("wcu")
            epool = "dve" if l == 0 else "pool"
            if l == 0:
                record_gathers(g_l1[n1:])
            for (tok0, T, isctx) in tiles():
                if isctx and l == 1:
                    continue
                at, ft, ct = aTr.next(), fTr.next(), cTr.next()
                DMA(at.t[:, :, 0:T], aT[:, tok0:tok0 + T].rearrange("(c p) t -> p c t", p=128), W=[at.b])
                DMA(ft.t[:, :, 0:T], fT[:, tok0:tok0 + T].rearrange("(c p) t -> p c t", p=128), W=[ft.b])
                DMA(ct.t[:, :, 0:T], cT[:, tok0:tok0 + T].rearrange("(c p) t -> p c t", p=128), W=[ct.b])
                for cb in range(D // 512):
                    cs = slice(cb * 512, (cb + 1) * 512)
                    wa, wf, wc, mg = war.next(), wfr.next(), wcr.next(), mgr.next()
                    DMA(wa.t[:].rearrange("p k n -> p (k n)"), wau[cb * 128:(cb + 1) * 128, :], W=[wa.b])
                    DMA(wf.t[:].rearrange("p k n -> p (k n)"), wfu[cb * 128:(cb + 1) * 128, :], W=[wf.b])
                    DMA(wc.t[:].rearrange("p k n -> p (k n)"), wcu[cb * 128:(cb + 1) * 128, :], W=[wc.b])
                    for br_ in range(3):
                        r0 = MG + br_ * D + cb * 512
                        DMA(mg.t[:, br_, :, 0:T], pT[r0:r0 + 512, tok0:tok0 + T].rearrange("(c p) t -> p c t", p=128), W=[mg.b], partial=(br_ > 0))
                    mt = mTt.next()
                    for dcl in range(4):
                        ds_ = slice(dcl * 128, (dcl + 1) * 128)
                        pa, pf, pc = psn(0, 3), psn(3, 6), psn(6, 8)
                        for k in range(16):
                            MM(pa.t[:, 0:T], wa.t[:, k, ds_], at.t[:, k, 0:T], k == 0, k == 15, [wa.b, at.b], [pa.b])
                        for k in range(8):
                            MM(pf.t[:, 0:T], wf.t[:, k, ds_], ft.t[:, k, 0:T], k == 0, k == 7, [wf.b, ft.b], [pf.b])
                        for k in range(8):
                            MM(pc.t[:, 0:T], wc.t[:, k, ds_], ct.t[:, k, 0:T], k == 0, k == 7, [wc.b, ct.b], [pc.b])
                        t0_, t1_, t2_ = e1t[0].next(), e1t[1].next(), e1t[2].next()
                        TT("dve", t0_.t[:, 0:T], pa.t[:, 0:T], mg.t[:, 0, dcl, 0:T], ALU.mult, [pa.b, mg.b], [t0_.b])
                        TT("dve", t1_.t[:, 0:T], pf.t[:, 0:T], mg.t[:, 1, dcl, 0:T], ALU.mult, [pf.b, mg.b], [t1_.b])
                        TT("dve", t2_.t[:, 0:T], pc.t[:, 0:T], mg.t[:, 2, dcl, 0:T], ALU.mult, [pc.b, mg.b], [t2_.b])
                        TT(epool, t0_.t[:, 0:T], t0_.t[:, 0:T], t1_.t[:, 0:T], ALU.add, [t0_.b, t1_.b], [t0_.b])
                        TT(epool, mt.t[:, dcl, 0:T], t0_.t[:, 0:T], t2_.t[:, 0:T], ALU.add, [t0_.b, t2_.b], [mt.b], partial=True)
                    DMA(mTd[cb * 512:(cb + 1) * 512, tok0:tok0 + T].rearrange("(c p) t -> p c t", p=128), mt.t[:, :, 0:T], R=[mt.b])

        with P.phase():
            mTr = Rot(P, "mT", 2, [128, KD, 512], BF16)
            wor = Rot(P, "wo", 2, [128, KD, 512], BF16)
            gtr = Rot(P, "gtb", 2, [128, 512], F32)
            xsr = Rot(P, "xs", 3, [128, 512], F32)
            e2t = Rot(P, "e2t", 2, [128, 512], F32)
            xor_ = Rot(P, "xo", 3, [128, 512], F32)
            wo = W("wo")
            epool = "pool"
            if l == 0:
                pass
            for (tok0, T, isctx) in tiles():
                if isctx and l == 1:
                    continue
                q = 2 * l + isctx
                mt = mTr.next()
                for k0 in range(0, KD, 8):
                    k1 = min(KD, k0 + 8)
                    DMA(mt.t[:, k0:k1, 0:T], mTd[k0 * 128:k1 * 128, tok0:tok0 + T].rearrange("(c p) t -> p c t", p=128), W=[mt.b], partial=(k0 > 0))
                for nb in range(D // 512):
                    cs = slice(nb * 512, (nb + 1) * 512)
                    wt = wor.next()
                    DMA(wt.t[:].rearrange("p k n -> p (k n)"), wo[nb * 128:(nb + 1) * 128, :], W=[wt.b])
                    gt = gtr.next()
                    DMA(gt.t[:], gtb_d[q, :, cs], W=[gt.b])
                    for s in range(T // 128):
                        xs = xsr.next()
                        DMA(xs.t[:], src_rows(l, tok0 + s * 128, 128)[:, cs], W=[xs.b])
                        po = psn()
                        for kc in range(KD):
                            MM(po.t[:, 0:512], mt.t[:, kc, s * 128:(s + 1) * 128], wt.t[:, kc, :], kc == 0, kc == KD - 1, [mt.b, wt.b], [po.b])
                        tt_ = e2t.next()
                        xo = xor_.next()
                        TT("dve", tt_.t[:], po.t[:, 0:512], gt.t[:], ALU.mult, [po.b, gt.b], [tt_.b])
                        TT(epool, xo.t[:], tt_.t[:], xs.t[:], ALU.add, [tt_.b, xs.b], [xo.b])
                        if l == 0:
                            DMA(x1[tok0 + s * 128:tok0 + (s + 1) * 128, cs], xo.t[:], R=[xo.b])
                        else:
                            r0 = tok0 - CT + s * 128
                            DMA(out[r0:r0 + 128, cs], xo.t[:], R=[xo.b])

    finish(P)
    P.top.close()
    return nc, P


def host_consts(cfg, r):
    LT, SEQ, NA = cfg.LT, cfg.SEQ, cfg.NA
    cst = np.zeros((128, 1040), np.float32)
    cst[:, 0:128] = np.eye(128, dtype=np.float32)
    R = np.zeros((128, 128), np.float32)
    for base in (0, 64):
        for i in range(32):
            R[base + i, base + i + 32] = -1.0
            R[base + i + 32, base + i] = 1.0
    cst[:, 128:256] = R.T
    jj = np.arange(128)[:, None]
    ii = np.arange(128)[None, :]
    cst[:, 256:384] = (ii <= jj).astype(np.float32)
    cst[:, 384:512] = (jj <= ii).astype(np.float32)
    if r > 0:
        cst[:, 512 + r - 1] = 1.0
        cst[:, 520] = 1.0
    if r < 3:
        cst[:, 516 + r + 1] = 1.0
        cst[:, 521] = 1.0
    for q in range(4):
        cst[q, 528 + q * 128:528 + (q + 1) * 128] = 1.0
    pos = np.arange(r * LT, (r + 1) * LT)
    row = (pos // 64).astype(np.float64)
    col = (pos % 64).astype(np.float64)
    inv = 10000.0 ** (-np.arange(0, 64, 2, dtype=np.float64) / 64)
    ar_, ac_ = row[:, None] * inv[None, :], col[:, None] * inv[None, :]
    ang = np.concatenate([ar_, ar_, ac_, ac_], -1).astype(np.float32)
    rope = np.stack([np.cos(ang).T, np.sin(ang).T]).astype(np.float32)
    k = pos.astype(np.float64)[None, :]
    p = np.arange(128, dtype=np.float64)[:, None]
    a = np.arange(NA, dtype=np.float64)[:, None]
    sc = 1.0 / np.sqrt(SEQ * 256.0)
    angB = 2 * np.pi * ((p * k) % SEQ) / SEQ
    frc = min(LT, 256)
    ai = np.arange(NA)
    m0 = ai * 128
    cc_, rr_, ii_ = m0 // (4 * frc), (m0 % (4 * frc)) // frc, m0 % frc
    l0 = rr_ * LT + cc_ * frc + ii_
    a = (l0 // 128).astype(np.float64)[:, None]
    angA = 2 * np.pi * ((128 * a * k) % SEQ) / SEQ
    dftB = np.stack([np.cos(angB), np.sin(angB)]).astype(np.float32)
    dftA = (np.stack([np.cos(angA), np.sin(angA)]) * sc).astype(np.float32)
    l_ = np.arange(256, dtype=np.float64)
    a256 = 2 * np.pi * np.outer(l_, l_) / 256
    scc = 1.0 / 256.0
    dftctx = np.stack([np.cos(a256) * scc, -np.sin(a256) * scc]).reshape(2, 2, 128, 256).astype(np.float32)
    dc = np.stack([np.cos(a256), np.sin(a256)])
    dftc = dc.reshape(2, 2, 128, 256).transpose(2, 0, 1, 3).astype(np.float32)
    return dict(cst=cst, rope=rope, dftB=dftB, dftA=dftA, dftctx=np.ascontiguousarray(dftctx), dftc=np.ascontiguousarray(dftc))


def make_in_maps(cfg, inp):
    D, LT, KD, MC = cfg.D, cfg.LT, cfg.KD, cfg.MC
    f = lambda a: np.ascontiguousarray(np.asarray(a, dtype=np.float32))
    x, c, ctx, c_ctx = f(inp["x"]), f(inp["c"]), f(inp["ctx"]), f(inp["c_ctx"])
    maps = []
    ngT = f(f(inp["norm_g"]).reshape(2, KD, 128).transpose(0, 2, 1))
    qg, kg = f(inp["q_norm_g"]), f(inp["k_norm_g"])
    qkg = f(np.stack([qg, kg], -1))
    qkrow = f(np.concatenate([qg, kg], -1).reshape(2, 1, 256))
    sink = f(f(inp["attn_sink"]).reshape(2, 1, 16))
    dww = f(f(inp["conv_dw_w"]).reshape(2, 31, 8, 128).transpose(0, 3, 2, 1))
    cv3 = f(np.stack([f(inp["conv_dw_b"]), f(inp["conv_ln_g"]), f(inp["conv_ln_b"])], -1).reshape(2, 8, 128, 3).transpose(0, 2, 1, 3))
    bmod = f(inp["b_mod"])
    wl = {"w_in": f(inp["w_in"]), "w_au": f(inp["w_attn_up"]), "w_fu": f(inp["w_fourier_up"]), "w_cu": f(inp["w_conv_up"]),
          "w_o": f(inp["w_out"]), "w_pw": f(inp["w_conv_pw"])}
    wl_g = {}
    for k_, w in wl.items():
        K_, N_ = w.shape[1], w.shape[2]
        if k_ == "w_pw":
            wl_g[k_] = w
        else:
            wl_g[k_] = np.ascontiguousarray(w.reshape(2, K_ // 128, 128, N_ // 512, 512).transpose(0, 3, 2, 1, 4)).reshape(2, N_ // 4, 4 * K_)
    wmod = f(inp["w_mod"])
    wfm = f(inp["w_fourier_mix"])
    for core in range(8):
        b, r = core // 4, core % 4
        m = host_consts(cfg, r)
        m["x"] = f(x[b, r * LT:(r + 1) * LT])
        m["ctx"] = f(ctx[b])
        m["cvT"] = f(np.stack([c[b], c_ctx], -1).reshape(KD, 128, 2).transpose(1, 0, 2))
        m["ngT"], m["qkg"], m["qkrow"], m["sink"], m["wfm"], m["dww"], m["cv3"] = ngT, qkg, qkrow, sink, wfm, dww, cv3
        for k_, w in wl.items():
            g = wl_g[k_]
            R_g, C_g = g.shape[1], g.shape[2]
            rc = wchunk(R_g, C_g)
            m[k_] = f(g.reshape(2, R_g // (4 * rc), 4, rc, C_g)[:, :, r].reshape(2, R_g // 4, C_g))
        m["w_mod"] = f(wmod[:, :, r * MC:(r + 1) * MC])
        m["b_mod"] = f(np.repeat(bmod[:, None, r * MC:(r + 1) * MC], 2, axis=1))
        maps.append(m)
    return maps


def kernel(**inputs):
    cfg = Cfg()
    nc, _ = build(cfg)
    maps = make_in_maps(cfg, inputs)
    res = run_bass_kernel_spmd(nc, maps, core_ids=list(range(8)))
    outp = np.empty((2, cfg.SEQ, cfg.D), np.float32)
    for core in range(8):
        b, r = core // 4, core % 4
        outp[b, r * cfg.LT:(r + 1) * cfg.LT] = res.results[core]["out"]
    return outp
```

```python
import contextlib
import numpy as np
import concourse.bass as bass
import concourse.mybir as mybir
from concourse.bass_utils import run_bass_kernel_spmd

F32 = mybir.dt.float32
BF16 = mybir.dt.bfloat16
AF = mybir.ActivationFunctionType
ALU = mybir.AluOpType
AX = mybir.AxisListType

ENGS = ("pe", "act", "dve", "pool", "sp")
DMAQ = ("sp", "act", "pool")
RING = 8
EPS = 1e-6


class Buf:
    __slots__ = ("writers", "readers")

    def __init__(self):
        self.writers = {}
        self.readers = {}


class Op:
    __slots__ = ("eng", "fn", "deps", "signal", "val", "dma", "slot", "key", "inc", "epoch")

    def __init__(self, eng, fn, dma=False):
        self.eng = eng
        self.fn = fn
        self.deps = []
        self.signal = False
        self.val = 0
        self.dma = dma
        self.slot = -1
        self.key = eng
        self.inc = 1
        self.epoch = 0


class Prog:
    def __init__(self, nc):
        self.nc = nc
        self.top = contextlib.ExitStack()
        self.scope = self.top
        st = self.top
        self.esem = {e: st.enter_context(nc.semaphore("s_" + e)) for e in ENGS}
        self.ring = {q: [st.enter_context(nc.semaphore("r_%s%d" % (q, i))) for i in range(RING)] for q in DMAQ}
        self.ccsem = st.enter_context(nc.semaphore("s_cc"))
        self.cnt = {e: 0 for e in ENGS}
        self.ndma = {q: 0 for q in DMAQ}
        self.ncc = 0
        self.epoch = 0
        self.ops = {e: [] for e in ENGS}
        self.waited = {e: {} for e in ENGS}
        self.nops = 0
        self.mk = self.sbuf("mk", [128, 8], F32)

    def sbuf(self, name, shape, dt):
        self.nalloc = getattr(self, "nalloc", 0) + 1
        return self.scope.enter_context(self.nc.sbuf_tensor("%s_%d" % (name, self.nalloc), list(shape), dt))

    def psum(self, name, shape, dt):
        return self.scope.enter_context(self.nc.psum_tensor(name, list(shape), dt))

    @contextlib.contextmanager
    def phase(self):
        old = self.scope
        with contextlib.ExitStack() as st:
            self.scope = st
            yield
            self.nphase = getattr(self, "nphase", 0) + 1
            import os as _os
            if self.nphase <= int(_os.environ.get("KSTOP", "999")):
                self.flush()
            else:
                self.ops = {e: [] for e in ENGS}
        self.scope = old

    def _add(self, op, reads, writes, partial):
        op.epoch = self.epoch
        deps = {}
        for b in reads:
            for w in b.writers.values():
                deps[id(w)] = w
        for b in writes:
            for w in b.readers.values():
                deps[id(w)] = w
            for w in b.writers.values():
                deps[id(w)] = w
        for d in deps.values():
            if d is op or d.epoch != self.epoch:
                continue
            if (not d.dma) and (not op.dma) and d.eng == op.eng and op.eng == "pe":
                continue
            d.signal = True
            op.deps.append(d)
        for b in reads:
            b.readers[op.key] = op
        for b in writes:
            if not partial:
                b.writers = {}
                b.readers = {}
            b.writers[op.key] = op
        self.ops[op.eng].append(op)
        self.nops += 1
        return op

    def op(self, eng, fn, reads=(), writes=(), partial=False):
        return self._add(Op(eng, fn), reads, writes, partial)

    def dma(self, q, fn, reads=(), writes=(), partial=False):
        o = Op(q, fn, dma=True)
        i = self.ndma[q]
        self.ndma[q] = i + 1
        o.slot = i % RING
        o.val = 16 * (i // RING + 1)
        o.inc = 16
        o.key = (q, o.slot)
        o.signal = True
        return self._add(o, reads, writes, partial)

    def cc(self, fn, reads=(), writes=()):
        o = Op("pool", fn, dma=True)
        self.ncc += 1
        o.slot = -2
        o.val = self.ncc
        o.inc = 1
        o.key = ("cc", 0)
        o.signal = True
        return self._add(o, reads, writes, False)

    def flush(self):
        nc = self.nc
        ops_snap = self.ops
        mval = {}
        for e in ENGS:
            c = self.cnt[e]
            for o in self.ops[e]:
                if not o.dma and o.signal:
                    c += 1
                    o.val = c
            mval[e] = c + 1
            self.cnt[e] = c + (0 if e == "pe" else 1)
        mk = self.mk

        def semof(o):
            if o.slot == -2:
                return self.ccsem
            if o.dma:
                return self.ring[o.eng][o.slot]
            return self.esem[o.eng]

        def run(e, eng):
            waited = self.waited[e]

            def wait(s, v):
                if waited.get(id(s), 0) < v:
                    eng.wait_ge(s, v)
                    waited[id(s)] = v

            last = {}
            for o in ops_snap[e]:
                for d in o.deps:
                    wait(semof(d), d.val)
                if o.slot == -2:
                    o.fn(eng).then_inc(self.ccsem, 1)
                    wait(self.ccsem, o.val)
                elif o.dma:
                    s = self.ring[e][o.slot]
                    if o.val > 16:
                        wait(s, o.val - 16)
                    o.fn(eng).then_inc(s, 16)
                    last[o.slot] = o
                else:
                    ins = o.fn(eng)
                    if o.signal:
                        ins.then_inc(self.esem[e], 1)
            for sl, o in last.items():
                wait(self.ring[e][sl], o.val)
            if e == "dve":
                m = eng.memset(mk[:, 0:1], 0.0)
            elif e == "pool":
                m = eng.memset(mk[:, 1:2], 0.0)
            elif e == "act":
                m = eng.memzero(mk[:, 2:3])
            elif e == "sp":
                m = eng.nop()
            else:
                m = None
            if m is not None:
                m.then_inc(self.esem[e], 1)
            for e2 in ENGS:
                if e2 != "pe":
                    wait(self.esem[e2], mval[e2])

        self.pending = getattr(self, 'pending', [])
        self.pending.append(run)

        self.epoch += 1
        self.ops = {e: [] for e in ENGS}


def finish(P):
    nc = P.nc
    with nc.Block() as block:
        @block.tensor
        def _(eng):
            for r in P.pending:
                r("pe", eng)

        @block.scalar
        def _(eng):
            for r in P.pending:
                r("act", eng)

        @block.vector
        def _(eng):
            for r in P.pending:
                r("dve", eng)

        @block.gpsimd
        def _(eng):
            for r in P.pending:
                r("pool", eng)

        @block.sync
        def _(eng):
            for r in P.pending:
                r("sp", eng)


class Tl:
    def __init__(self, P, name, shape, dt, psum=False):
        self.t = (P.psum if psum else P.sbuf)(name, shape, dt)
        self.b = Buf()


class Rot:
    def __init__(self, P, name, n, shape, dt):
        self.ts = [Tl(P, "%s%d" % (name, i), shape, dt) for i in range(n)]
        self.i = 0

    def next(self):
        t = self.ts[self.i % len(self.ts)]
        self.i += 1
        return t


class Cfg:
    def __init__(self, D=4096, SEQ=8192):
        self.D = D
        self.SEQ = SEQ
        self.KD = D // 128
        self.LT = SEQ // 4
        self.CT = 256
        self.NTOK = self.CT + self.LT
        self.MG = 10240
        self.INW = 10240 + 3 * D
        self.NA = SEQ // 128
        self.MC = 3 * D // 4
        self.TT = min(512, self.LT)


Q_OFF, K_OFF, V_OFF, AG_OFF, F_OFF, FG_OFF, CA_OFF, CB_OFF, CG_OFF = 0, 2048, 2560, 3072, 5120, 6144, 7168, 8192, 9216
VBLK = V_OFF // 512


def wchunk(K, N):
    n = K // 4
    for rc in range(n, 0, -1):
        if n % rc == 0 and rc * N * 2 <= (1 << 20):
            return rc


def gshape(nm, K, N):
    if nm == "wpw":
        return K, N
    return N // 4, 4 * K


def fam_func(blk):
    c = blk * 512
    if AG_OFF <= c < F_OFF or FG_OFF <= c < CA_OFF or CG_OFF <= c < 10240:
        return AF.Silu
    if CB_OFF <= c < CG_OFF or c >= 10240:
        return AF.Sigmoid
    return AF.Copy


def build(cfg, debug=False):
    D, KD, LT, CT, NTOK, INW, NA, MC, SEQ = cfg.D, cfg.KD, cfg.LT, cfg.CT, cfg.NTOK, cfg.INW, cfg.NA, cfg.MC, cfg.SEQ
    MG = cfg.MG
    NB = LT // 128
    nc = bass.Bass("TRN2", target_bir_lowering=False)

    def din(name, shape, dt=F32):
        return nc.dram_tensor(name, list(shape), dt, kind="ExternalInput").ap()

    def dscr(name, shape, dt=BF16, dbg=False):
        kind = "ExternalOutput" if (dbg and debug and dt == F32) else "Internal"
        return nc.dram_tensor(name, list(shape), dt, kind=kind).ap()

    x_in = din("x", [LT, D])
    ctx_in = din("ctx", [CT, D])
    cst_in = din("cst", [128, 1040])
    rope_in = din("rope", [2, 128, LT])
    dftB_in = din("dftB", [2, 128, LT])
    dftA_in = din("dftA", [2, NA, LT])
    dftctx_in = din("dftctx", [2, 2, 128, 256])
    dftc_in = din("dftc", [128, 2, 2, 256])
    cvT_in = din("cvT", [128, KD, 2])
    ngT_in = din("ngT", [2, 128, KD])
    qkg_in = din("qkg", [2, 128, 2])
    qkrow_in = din("qkrow", [2, 1, 256])
    sink_in = din("sink", [2, 1, 16])
    wfm_in = din("wfm", [2, 4, 256, 256])
    dww_in = din("dww", [2, 128, 8, 31])
    cv3_in = din("cv3", [2, 128, 8, 3])
    w_in_s = din("w_in", [2, INW // 16, 4 * D])
    w_au_s = din("w_au", [2, D // 16, 4 * 2048])
    w_fu_s = din("w_fu", [2, D // 16, 4 * 1024])
    w_cu_s = din("w_cu", [2, D // 16, 4 * 1024])
    w_o_s = din("w_o", [2, D // 16, 4 * D])
    w_pw_s = din("w_pw", [2, 256, 1024])
    w_mod_s = din("w_mod", [2, D, MC])
    b_mod_s = din("b_mod", [2, 2, MC])
    out = nc.dram_tensor("out", [LT, D], F32, kind="ExternalOutput").ap()

    wspec = [("win", D, INW, w_in_s), ("wau", 2048, D, w_au_s), ("wfu", 1024, D, w_fu_s),
             ("wcu", 1024, D, w_cu_s), ("wo", D, D, w_o_s), ("wpw", 1024, 1024, w_pw_s)]
    Wg = {}
    Wgin = {}
    for nm, K, N, _ in wspec:
        R_g, C_g = gshape(nm, K, N)
        for l in range(2):
            Wgin[(nm, l)] = dscr("gi_%s%d" % (nm, l), [R_g // 4, C_g])
            Wg[(nm, l)] = dscr("g_%s%d" % (nm, l), [R_g, C_g])
    pT = dscr("pT", [INW, NTOK], dbg=True)
    Vtm = dscr("Vtm", [NTOK, 512], dbg=True)
    kTn = dscr("kTn", [512, NTOK], dbg=True)
    gK_in = dscr("gK_in", [512, 256]); gK_out = dscr("gK_out", [4 * 512, 256])
    gV_in = dscr("gV_in", [256, 512]); gV_out = dscr("gV_out", [4 * 256, 512])
    gU_in = dscr("gU_in", [1024, 32]); gU_out = dscr("gU_out", [4 * 1024, 32])
    gF_in = dscr("gF_in", [LT, 2048]); gF_out = dscr("gF_out", [SEQ, 2048])
    gFc = dscr("gFc", [CT, 2048])
    tabC = dscr("tabC", [NA, 128, LT]); tabS = dscr("tabS", [NA, 128, LT])
    aT = dscr("aT", [2048, NTOK], dbg=True)
    fT = dscr("fT", [1024, NTOK], dbg=True)
    cT = dscr("cT", [1024, NTOK], dbg=True)
    mTd = dscr("mTd", [D, NTOK], dbg=True)
    x1 = dscr("x1", [NTOK, D], F32, dbg=True)
    gmod_in = dscr("gmod_in", [4, MC], F32); gmod_out = dscr("gmod_out", [16, MC], F32)
    gtb_d = dscr("gtb_d", [4, 128, D], F32)
    RG = [[0, 1, 2, 3], [4, 5, 6, 7]]
    RG8 = [list(range(8))]

    import os as _os2
    KSUB = int(_os2.environ.get('KSUB', '99'))
    P = Prog(nc)
    op, dma = P.op, P.dma

    def MM(o, lt, rh, start, stop, R, W):
        op("pe", lambda e: e.matmul(o, lhsT=lt, rhs=rh, start=start, stop=stop), reads=R, writes=W, partial=not start)

    def ACT(o, i, func, R, W, bias=None, scale=None, accum=None, partial=False):
        kw = {}
        if bias is not None:
            kw["bias"] = bias
        if scale is not None:
            kw["scale"] = scale
        if accum is not None:
            kw["accum_out"] = accum
        op("act", lambda e: e.activation(out=o, in_=i, func=func, **kw), reads=R, writes=W, partial=partial)

    def TT(eng, o, a, b, aop, R, W, partial=False):
        op(eng, lambda e: e.tensor_tensor(out=o, in0=a, in1=b, op=aop), reads=R, writes=W, partial=partial)

    def TS(eng, o, a, s1, s2, op0, op1, R, W, partial=False):
        if s2 is None:
            op(eng, lambda e: e.tensor_scalar(out=o, in0=a, scalar1=s1, scalar2=None, op0=op0), reads=R, writes=W, partial=partial)
        else:
            op(eng, lambda e: e.tensor_scalar(out=o, in0=a, scalar1=s1, scalar2=s2, op0=op0, op1=op1), reads=R, writes=W, partial=partial)

    def STT(eng, o, a, s, b, op0, op1, R, W, partial=False):
        op(eng, lambda e: e.scalar_tensor_tensor(out=o, in0=a, scalar=s, in1=b, op0=op0, op1=op1), reads=R, writes=W, partial=partial)

    def CP(eng, o, i, R, W, partial=False):
        op(eng, lambda e: e.tensor_copy(out=o, in_=i), reads=R, writes=W, partial=partial)

    def RECIP(o, i, R, W):
        op("dve", lambda e: e.reciprocal(out=o, in_=i), reads=R, writes=W)

    def DMA(o, i, R=(), W=(), q="sp", partial=False):
        if q == "sp" and str(o.space).endswith("DRAM") and not str(i.space).endswith("DRAM"):
            q = "act"
        dma(q, lambda e: e.dma_start(out=o, in_=i), reads=R, writes=W, partial=partial)

    def CC(i, o, groups, R, W):
        P.cc(lambda e: e.collective_compute("AllGather", ALU.bypass, replica_groups=groups, ins=[i], outs=[o]), reads=R, writes=W)

    cst = Tl(P, "cst", [128, 1040], F32)
    ident = cst.t[:, 0:128]
    rotT = cst.t[:, 128:256]
    selc = lambda i: cst.t[:, 512 + i:513 + i]
    selm = cst.t[0:4, 528:1040]
    ones32 = Tl(P, "ones32", [128, 128], F32)
    onesbf = Tl(P, "onesbf", [128, 128], BF16)
    identbf = Tl(P, "identbf", [128, 128], BF16)
    mskbf = Tl(P, "mskbf", [128, 4, 128], BF16)
    gsT = Tl(P, "gsT", [128, 4, KD], F32)
    shT = Tl(P, "shT", [128, 4, KD], F32)
    qkg = Tl(P, "qkg", [128, 2, 2], F32)
    negB = Tl(P, "negB", [128, 2], F32)
    sinkrow = Tl(P, "sinkrow", [1, 2, 2048], BF16)
    dftc = Tl(P, "dftc", [128, 2, 2, 256], F32)
    PS = [Tl(P, "ps%d" % i, [128, 512], F32, psum=True) for i in range(8)]
    psi = [0]

    def psn(lo=0, hi=8):
        t = PS[lo + psi[0] % (hi - lo)]
        psi[0] += 1
        return t

    with P.phase():
        DMA(cst.t[:], cst_in, W=[cst.b])
        DMA(qkg.t[:], qkg_in.rearrange("l p k -> p l k"), W=[qkg.b])
        DMA(dftc.t[:], dftc_in, W=[dftc.b])
        op("dve", lambda e: e.memset(ones32.t[:], 1.0), writes=[ones32.b])
        op("pool", lambda e: e.memset(onesbf.t[:], 1.0), writes=[onesbf.b])
        CP("dve", identbf.t[:], ident, [cst.b], [identbf.b])
        CP("dve", mskbf.t[:, 0, :], cst.t[:, 256:384], [cst.b], [mskbf.b], partial=True)
        CP("dve", mskbf.t[:, 1, :], cst.t[:, 384:512], [cst.b], [mskbf.b], partial=True)
        TS("dve", mskbf.t[:, 2, :], cst.t[:, 256:384], selc(8), None, ALU.mult, None, [cst.b], [mskbf.b], partial=True)
        TS("dve", mskbf.t[:, 3, :], cst.t[:, 384:512], selc(9), None, ALU.mult, None, [cst.b], [mskbf.b], partial=True)
        qkrow = Tl(P, "qkrow", [1, 2, 256], F32)
        sk = Tl(P, "sk", [1, 2, 16], F32)
        mx = Tl(P, "mx", [1, 8], F32)
        DMA(qkrow.t[:], qkrow_in.rearrange("l o k -> o l k"), W=[qkrow.b])
        DMA(sk.t[:], sink_in.rearrange("l o k -> o l k"), W=[sk.b])
        for l in range(2):
            for j in range(2):
                op("dve", lambda e, l=l, j=j: e.reduce_max(out=mx.t[0:1, 2 * l + j:2 * l + j + 1], in_=qkrow.t[0:1, l, j * 128:(j + 1) * 128],
                                                         axis=AX.X, apply_absolute_value=True), reads=[qkrow.b], writes=[mx.b], partial=True)
            TT("dve", mx.t[0:1, 4 + l:5 + l], mx.t[0:1, 2 * l:2 * l + 1], mx.t[0:1, 2 * l + 1:2 * l + 2], ALU.mult, [mx.b], [mx.b], partial=True)
            pb = psn()
            MM(pb.t[:, 0:1], ones32.t[0:1, :], mx.t[0:1, 4 + l:5 + l], True, True, [ones32.b, mx.b], [pb.b])
            ACT(negB.t[:, l:l + 1], pb.t[:, 0:1], AF.Copy, [pb.b], [negB.b], scale=-(128.0 ** 0.5), partial=True)
            es = Tl(P, "es%d" % l, [1, 16], F32)
            ACT(es.t[:], sk.t[0:1, l, :], AF.Exp, [sk.b, negB.b], [es.b], bias=negB.t[0:1, l:l + 1])
            for h in range(16):
                TS("dve", sinkrow.t[0:1, l, h * 128:(h + 1) * 128], ones32.t[0:1, :], es.t[0:1, h:h + 1], None, ALU.mult, None,
                   [ones32.b, es.b], [sinkrow.b], partial=True)

    with P.phase():
        gb = Buf()
        for nm, K, N, src in wspec:
            for l in range(2):
                DMA(Wgin[(nm, l)], src[l], W=[gb], q="pool", partial=True)
    chunkbuf = {}

    def gather_list():
        order = [("win", 0)] + [(nm, 0) for nm in ("wpw", "wau", "wfu", "wcu", "wo")] + [(nm, 1) for nm in ("win", "wpw", "wau", "wfu", "wcu", "wo")]
        dims = {nm: gshape(nm, K, N) for nm, K, N, _ in wspec}
        out_ = []
        for nm, l in order:
            R_g, C_g = dims[nm]
            rc = wchunk(R_g, C_g)
            for c in range((R_g // 4) // rc):
                out_.append((nm, l, c, rc))
        return out_

    glist = gather_list()
    g_l0 = [g for g in glist if g[1] == 0]
    g_l1 = [g for g in glist if g[1] == 1]
    n1 = len([g for g in g_l1 if g[0] == "win"]) * 10 // 11

    def record_gathers(items):
        for nm, l, c, rc in items:
            b = Buf()
            chunkbuf[(nm, l, c)] = b
            CC(Wgin[(nm, l)][c * rc:(c + 1) * rc, :], Wg[(nm, l)][c * 4 * rc:(c + 1) * 4 * rc, :], RG, [], [b])

    def wbufs(nm, l, r0, r1):
        K_, N_ = [(K, N) for n_, K, N, _ in wspec if n_ == nm][0]
        R_g, C_g = gshape(nm, K_, N_)
        rc4 = 4 * wchunk(R_g, C_g)
        return [chunkbuf[(nm, l, c)] for c in range(r0 // rc4, (r1 - 1) // rc4 + 1)]

    with P.phase():
        cvT = Tl(P, "cvT", [128, KD, 2], F32)
        scT = Tl(P, "scT", [128, KD, 2], F32)
        bm = Tl(P, "bm", [2, 2, MC], F32)
        mrow = Tl(P, "mrow", [2, 2, MC], F32)
        DMA(cvT.t[:], cvT_in, W=[cvT.b])
        DMA(bm.t[:], b_mod_s.rearrange("l q m -> q l m"), W=[bm.b])
        ACT(scT.t[:], cvT.t[:], AF.Silu, [cvT.b], [scT.b])
        wmr = Rot(P, "wm", 3, [128, MC], F32)
        nch = [(o, min(512, MC - o)) for o in range(0, MC, 512)]
        gmb = Buf()
        for l in range(2):
            pss = [PS[i] for i in range(len(nch))]
            for kc in range(KD):
                wm = wmr.next()
                DMA(wm.t[:], w_mod_s[l, kc * 128:(kc + 1) * 128, :], W=[wm.b])
                for i, (o, n) in enumerate(nch):
                    MM(pss[i].t[0:2, 0:n], scT.t[:, kc, :], wm.t[:, o:o + n], kc == 0, kc == KD - 1, [scT.b, wm.b], [pss[i].b])
            for i, (o, n) in enumerate(nch):
                TT("dve", mrow.t[0:2, l, o:o + n], pss[i].t[0:2, 0:n], bm.t[0:2, l, o:o + n], ALU.add, [pss[i].b, bm.b], [mrow.b], partial=True)
            DMA(gmod_in[2 * l:2 * l + 2, :], mrow.t[0:2, l, :], R=[mrow.b], W=[gmb], partial=True)
        gob = Buf()
        if KSUB >= 1:
            CC(gmod_in, gmod_out, RG, [gmb], [gob])
        R_ = Tl(P, "Rr", [4, 3 * D], F32)
        if KSUB >= 2:
          DMA(R_.t[:].rearrange("q (r j) -> q r j", r=4), gmod_out.rearrange("(r q) j -> q r j", q=4), R=[gob], W=[R_.b])
        modT = Tl(P, "modT", [128, 2 * KD, 4], F32)
        for c0 in (range(0, 2 * KD, 64) if KSUB >= 3 else []):
            pm = psn()
            n = min(64, 2 * KD - c0)
            for c in range(n):
                op("pe", lambda e, c=c, c0=c0, pm=pm: e.transpose(pm.t[:, c * 4:c * 4 + 4], R_.t[0:4, (c0 + c) * 128:(c0 + c + 1) * 128], cst.t[0:4, 0:4]),
                   reads=[R_.b, cst.b], writes=[pm.b], partial=(c > 0))
            CP("dve", modT.t[:, c0:c0 + n, :], pm.t[:, 0:4 * n].rearrange("p (c q) -> p c q", q=4), [pm.b], [modT.b], partial=True)
        ngT = Tl(P, "ngT", [128, 2, KD], F32)
        DMA(ngT.t[:], ngT_in.rearrange("l p k -> p l k"), W=[ngT.b])
        for q in (range(4) if KSUB >= 4 else []):
            STT("dve", gsT.t[:, q, :], modT.t[:, KD:2 * KD, q], 1.0, ngT.t[:, q // 2, :], ALU.add, ALU.mult, [modT.b, ngT.b], [gsT.b], partial=True)
            CP("dve", shT.t[:, q, :], modT.t[:, 0:KD, q], [modT.b], [shT.b], partial=True)
        gst = Rot(P, "gst", 2, [128, 512], F32)
        for q in (range(4) if KSUB >= 5 else []):
            for nb in range(D // 512):
                pg = psn()
                MM(pg.t[:, 0:512], selm[:, q * 128:(q + 1) * 128], R_.t[0:4, 2 * D + nb * 512:2 * D + (nb + 1) * 512], True, True, [cst.b, R_.b], [pg.b])
                g = gst.next()
                ACT(g.t[:], pg.t[:, 0:512], AF.Copy, [pg.b], [g.b])
                DMA(gtb_d[q, :, nb * 512:(nb + 1) * 512], g.t[:], R=[g.b])

    with P.phase():
        CB = Tl(P, "CB", [128, LT], F32)
        SB = Tl(P, "SB", [128, LT], F32)
        DMA(CB.t[:], dftB_in[0], W=[CB.b])
        DMA(SB.t[:], dftB_in[1], W=[SB.b])
        car = Rot(P, "ca", 2, [128, LT], F32)
        sar = Rot(P, "sa", 2, [128, LT], F32)
        t1r = Rot(P, "t1", 2, [128, LT], F32)
        t2r = Rot(P, "t2", 2, [128, LT], F32)
        ocr = Rot(P, "oc", 2, [128, LT], BF16)
        osr = Rot(P, "os", 2, [128, LT], BF16)
        g_w0 = [g for g in g_l0 if g[0] == "win"]
        record_gathers(g_w0)
        for a in range(NA):
            ca, sa, t1, t2, oc, os_ = car.next(), sar.next(), t1r.next(), t2r.next(), ocr.next(), osr.next()
            DMA(ca.t[:], dftA_in[0, a:a + 1, :].partition_broadcast(128), W=[ca.b])
            DMA(sa.t[:], dftA_in[1, a:a + 1, :].partition_broadcast(128), W=[sa.b])
            TT("dve", t1.t[:], ca.t[:], CB.t[:], ALU.mult, [ca.b, CB.b], [t1.b])
            TT("dve", t2.t[:], sa.t[:], SB.t[:], ALU.mult, [sa.b, SB.b], [t2.b])
            TT("dve", oc.t[:], t1.t[:], t2.t[:], ALU.subtract, [t1.b, t2.b], [oc.b])
            DMA(tabC[a], oc.t[:], R=[oc.b])
            TT("dve", t2.t[:], sa.t[:], CB.t[:], ALU.mult, [sa.b, CB.b], [t2.b])
            TT("dve", t1.t[:], ca.t[:], SB.t[:], ALU.mult, [ca.b, SB.b], [t1.b])
            STT("dve", os_.t[:], t2.t[:], -1.0, t1.t[:], ALU.mult, ALU.subtract, [t1.b, t2.b], [os_.b])
            DMA(tabS[a], os_.t[:], R=[os_.b])

    def src_rows(l, tok0, n):
        if l == 1:
            return x1[tok0:tok0 + n, :]
        if tok0 < CT:
            return ctx_in[tok0:tok0 + n, :]
        return x_in[tok0 - CT:tok0 - CT + n, :]

    def tiles():
        ts = [(0, CT, 1)]
        for t0 in range(0, LT, cfg.TT):
            ts.append((CT + t0, cfg.TT, 0))
        return ts

    def normrope(l, which, src, srcb, T, cos, sin, ropeb, outap, outb, tmp, pe_="pool"):
        sq, rs, kn, t1, t2 = [r_.next() for r_ in tmp]
        ACT(sq.t[:, 0:T], src, AF.Square, [srcb], [sq.b])
        p1 = psn(0, 4)
        MM(p1.t[:, 0:T], onesbf.t[:], sq.t[:, 0:T], True, True, [onesbf.b, sq.b], [p1.b])
        ACT(rs.t[:, 0:T], p1.t[:, 0:T], AF.Ln, [p1.b], [rs.b], bias=EPS, scale=1.0 / 128)
        ACT(rs.t[:, 0:T], rs.t[:, 0:T], AF.Exp, [rs.b], [rs.b], scale=-0.5)
        STT("dve", kn.t[:, 0:T], src, qkg.t[:, l, which:which + 1], rs.t[:, 0:T], ALU.mult, ALU.mult, [srcb, qkg.b, rs.b], [kn.b])
        if cos is None:
            CP(pe_, outap, kn.t[:, 0:T], [kn.b], [outb], partial=True)
            return
        p2 = psn(0, 4)
        MM(p2.t[:, 0:T], rotT, kn.t[:, 0:T], True, True, [cst.b, kn.b], [p2.b])
        TT(pe_, t1.t[:, 0:T], kn.t[:, 0:T], cos, ALU.mult, [kn.b, ropeb], [t1.b])
        TT("dve", t2.t[:, 0:T], p2.t[:, 0:T], sin, ALU.mult, [p2.b, ropeb], [t2.b])
        TT("dve", outap, t1.t[:, 0:T], t2.t[:, 0:T], ALU.add, [t1.b, t2.b], [outb], partial=True)

    for l in range(2):
        W = lambda nm: Wg[(nm, l)]
        with P.phase():
            xr = Rot(P, "xt", 2, [128, D], F32)
            junk = Tl(P, "junk", [128, D], BF16)
            ssr = Rot(P, "ss", 2, [128, 2], F32)
            hT = Tl(P, "hT", [128, KD, 512], BF16)
            wr = Rot(P, "wblk", 2, [128, KD, 512], BF16)
            stg = Rot(P, "stg", 3, [128, 512], BF16)
            win = W("win")
            if l == 0:
                record_gathers([g for g in g_l0 if g[0] != "win"])
            for (tok0, T, isctx) in tiles():
                q = 2 * l + isctx
                for s in range(T // 128):
                    xt = xr.next()
                    ss = ssr.next()
                    DMA(xt.t[:], src_rows(l, tok0 + s * 128, 128), W=[xt.b])
                    op("dve", lambda e, ss=ss: e.memset(ss.t[:], 0.0), writes=[ss.b])
                    ACT(junk.t[:], xt.t[:], AF.Square, [xt.b, ss.b], [junk.b, ss.b], accum=ss.t[:, 0:1])
                    ACT(ss.t[:, 1:2], ss.t[:, 0:1], AF.Sqrt, [ss.b], [ss.b], bias=EPS, scale=1.0 / D)
                    RECIP(ss.t[:, 1:2], ss.t[:, 1:2], [ss.b], [ss.b])
                    TS("dve", xt.t[:], xt.t[:], ss.t[:, 1:2], None, ALU.mult, None, [xt.b, ss.b], [xt.b])
                    for k0 in range(0, KD, 4):
                        pt = psn()
                        for j in range(4):
                            kc = k0 + j
                            op("pe", lambda e, pt=pt, j=j, kc=kc, xt=xt: e.transpose(pt.t[:, j * 128:(j + 1) * 128], xt.t[:, kc * 128:(kc + 1) * 128], ident),
                               reads=[xt.b, cst.b], writes=[pt.b], partial=(j > 0))
                        for j in range(4):
                            kc = k0 + j
                            if j % 2 == 0:
                                TS("dve", hT.t[:, kc, s * 128:(s + 1) * 128], pt.t[:, j * 128:(j + 1) * 128], gsT.t[:, q, kc:kc + 1], shT.t[:, q, kc:kc + 1],
                                   ALU.mult, ALU.add, [pt.b, gsT.b, shT.b], [hT.b], partial=True)
                            else:
                                ACT(hT.t[:, kc, s * 128:(s + 1) * 128], pt.t[:, j * 128:(j + 1) * 128], AF.Identity, [pt.b, gsT.b, shT.b], [hT.b],
                                    bias=shT.t[:, q, kc:kc + 1], scale=gsT.t[:, q, kc:kc + 1], partial=True)
                blks = list(range(INW // 512))
                if l == 1 and isctx:
                    blks = [K_OFF // 512, VBLK]
                for blk in blks:
                    wt = wr.next()
                    DMA(wt.t[:].rearrange("p k n -> p (k n)"), win[blk * 128:(blk + 1) * 128, :], R=wbufs("win", l, blk * 128, (blk + 1) * 128), W=[wt.b])
                    if blk == VBLK:
                        for s in range(T // 128):
                            pv = psn()
                            for kc in range(KD):
                                MM(pv.t[:, 0:512], hT.t[:, kc, s * 128:(s + 1) * 128], wt.t[:, kc, :], kc == 0, kc == KD - 1, [hT.b, wt.b], [pv.b])
                            sg = stg.next()
                            ACT(sg.t[:], pv.t[:, 0:512], AF.Copy, [pv.b], [sg.b])
                            DMA(Vtm[tok0 + s * 128:tok0 + (s + 1) * 128, :], sg.t[:], R=[sg.b])
                    else:
                        fn = fam_func(blk)
                        for c in range(4):
                            pv = psn()
                            for kc in range(KD):
                                MM(pv.t[:, 0:T], wt.t[:, kc, c * 128:(c + 1) * 128], hT.t[:, kc, 0:T], kc == 0, kc == KD - 1, [hT.b, wt.b], [pv.b])
                            sg = stg.next()
                            ACT(sg.t[:, 0:T], pv.t[:, 0:T], fn, [pv.b], [sg.b])
                            r0 = (blk * 4 + c) * 128
                            DMA(pT[r0:r0 + 128, tok0:tok0 + T], sg.t[:, 0:T], R=[sg.b])

        with P.phase():
            rope = Tl(P, "rope", [128, 2, LT], F32)
            DMA(rope.t[:], rope_in.rearrange("c p t -> p c t"), W=[rope.b])
            kr = Rot(P, "kraw", 2, [128, 512], BF16)
            ko = Rot(P, "kout", 2, [128, 512], BF16)
            tmp = [Rot(P, "nr0_", 2, [128, 512], BF16)] + [Rot(P, "nr%d_" % i, 2, [128, 512], F32) for i in range(1, 5)]
            gkb = Buf()
            for (tok0, T, isctx) in tiles():
                for g in range(4):
                    k = kr.next()
                    o = ko.next()
                    r0 = K_OFF + g * 128
                    DMA(k.t[:, 0:T], pT[r0:r0 + 128, tok0:tok0 + T], W=[k.b])
                    if isctx:
                        normrope(l, 1, k.t[:, 0:T], k.b, T, None, None, None, o.t[:, 0:T], o.b, tmp)
                    else:
                        t0 = tok0 - CT
                        normrope(l, 1, k.t[:, 0:T], k.b, T, rope.t[:, 0, t0:t0 + T], rope.t[:, 1, t0:t0 + T], rope.b, o.t[:, 0:T], o.b, tmp)
                    DMA(kTn[g * 128:(g + 1) * 128, tok0:tok0 + T], o.t[:, 0:T], R=[o.b])
                    if tok0 == CT:
                        DMA(gK_in[g * 128:(g + 1) * 128, 0:128], o.t[:, 0:128], R=[o.b], W=[gkb], partial=True)
                    if tok0 + T == NTOK:
                        DMA(gK_in[g * 128:(g + 1) * 128, 128:256], o.t[:, T - 128:T], R=[o.b], W=[gkb], partial=True)
            DMA(gV_in[0:128, :], Vtm[CT:CT + 128, :], W=[gkb], partial=True)
            DMA(gV_in[128:256, :], Vtm[NTOK - 128:NTOK, :], W=[gkb], partial=True)
            CC(gK_in, gK_out, RG, [gkb], [])
            CC(gV_in, gV_out, RG, [gkb], [])

        with P.phase():
            rope = Tl(P, "rope", [128, 2, LT], F32)
            DMA(rope.t[:], rope_in.rearrange("c p t -> p c t"), W=[rope.b])
            cpool = "dve" if l == 0 else "pool"
            if l == 0:
                record_gathers(g_l1[:n1])
            kTa = Tl(P, "kTa", [128, 4, LT + 256], BF16)
            kTc = Tl(P, "kTc", [128, 4, CT], BF16)
            Va = Tl(P, "Va", [128, NB + 4, 512], BF16)
            ek = Tl(P, "ek", [128, 4, 4, 256], BF16)
            ev = Tl(P, "ev", [128, 4, 2, 512], BF16)
            DMA(kTa.t[:, :, 128:128 + LT], kTn[:, CT:NTOK].rearrange("(g p) t -> p g t", p=128), W=[kTa.b])
            DMA(kTc.t[:], kTn[:, 0:CT].rearrange("(g p) t -> p g t", p=128), W=[kTc.b])
            DMA(Va.t[:, 1:NB + 1, :], Vtm[CT:NTOK, :].rearrange("(b p) n -> p b n", p=128), W=[Va.b])
            DMA(Va.t[:, NB + 2:NB + 4, :], Vtm[0:CT, :].rearrange("(b p) n -> p b n", p=128), W=[Va.b], partial=True)
            DMA(ek.t[:], gK_out.rearrange("(r g p) c -> p r g c", r=4, g=4), W=[ek.b])
            DMA(ev.t[:], gV_out.rearrange("(r e p) n -> p r e n", r=4, e=2), W=[ev.b])
            for side, dst_k, dst_v, ecol, erow, s0 in ((0, kTa.t[:, :, 0:128], Va.t[:, 0, :], slice(128, 256), 1, 0),
                                                       (1, kTa.t[:, :, 128 + LT:256 + LT], Va.t[:, NB + 1, :], slice(0, 128), 0, 4)):
                TS("dve", dst_k, ek.t[:, 0, :, ecol], selc(s0), None, ALU.mult, None, [ek.b, cst.b], [kTa.b], partial=True)
                TS(cpool, dst_v, ev.t[:, 0, erow, :], selc(s0), None, ALU.mult, None, [ev.b, cst.b], [Va.b], partial=True)
                for r in range(1, 4):
                    STT("dve", dst_k, ek.t[:, r, :, ecol], selc(s0 + r), dst_k, ALU.mult, ALU.add, [ek.b, cst.b, kTa.b], [kTa.b], partial=True)
                    STT("dve", dst_v, ev.t[:, r, erow, :], selc(s0 + r), dst_v, ALU.mult, ALU.add, [ev.b, cst.b, Va.b], [Va.b], partial=True)
            qraw = Tl(P, "qraw", [128, 16, 512], BF16)
            qb = [Buf() for _ in range(16)]
            qTn = qraw
            agT = Tl(P, "agT", [128, 16, 512], BF16)
            ogT = Tl(P, "ogT", [128, 16, 512], BF16)
            tmp = [Rot(P, "nr0_", 2, [128, 512], BF16)] + [Rot(P, "nr%d_" % i, 2, [128, 512], F32) for i in range(1, 5)]
            ptr = Rot(P, "PT", 3, [128, 512], BF16)
            rdr = Rot(P, "rden", 2, [128, 512], F32)
            onr = Rot(P, "on", 2, [128, 512], F32)
            scale = 128.0 ** -0.5
            for (tok0, T, isctx) in tiles():
                if isctx and l == 1:
                    continue
                DMA(qraw.t[:, :, 0:T], pT[0:2048, tok0:tok0 + T].rearrange("(h p) t -> p h t", p=128), W=qb)
                DMA(agT.t[:, :, 0:T], pT[AG_OFF:AG_OFF + 2048, tok0:tok0 + T].rearrange("(h p) t -> p h t", p=128), W=[agT.b])
                for h in range(16):
                    if isctx:
                        normrope(l, 0, qraw.t[:, h, 0:T], qb[h], T, None, None, None, qTn.t[:, h, 0:T], qb[h], tmp, cpool)
                    else:
                        t0 = tok0 - CT
                        normrope(l, 0, qraw.t[:, h, 0:T], qb[h], T, rope.t[:, 0, t0:t0 + T], rope.t[:, 1, t0:t0 + T], rope.b, qTn.t[:, h, 0:T], qb[h], tmp, cpool)
                for blk in range(T // 128):
                    qs = slice(blk * 128, (blk + 1) * 128)
                    for g in range(4):
                        chunks = []
                        if not isctx:
                            nb = (tok0 - CT) // 128 + blk
                            for d_, mi in ((0, 0), (1, None), (2, 1)):
                                m = mi
                                if d_ == 0 and nb == 0:
                                    m = 2
                                if d_ == 2 and nb == NB - 1:
                                    m = 3
                                cb = nb + d_
                                chunks.append((kTa.t[:, g, cb * 128:(cb + 1) * 128], kTa.b, Va.t[:, cb, g * 128:(g + 1) * 128], m))
                        for cc_ in range(2):
                            chunks.append((kTc.t[:, g, cc_ * 128:(cc_ + 1) * 128], kTc.b, Va.t[:, NB + 2 + cc_, g * 128:(g + 1) * 128], None))
                        pO = PS[4 + (psi[0] % 2)]
                        pD = PS[6 + (psi[0] % 2)]
                        psi[0] += 1
                        def fin(ci, pS, vap, m, last):
                            pt_ = ptr.next()
                            ACT(pt_.t[:], pS.t[:, 0:512], AF.Exp, [pS.b, negB.b], [pt_.b], bias=negB.t[:, l:l + 1], scale=scale)
                            if m is not None:
                                p3 = pt_.t[:].rearrange("p (h q) -> p h q", h=4)
                                TT("dve", p3, p3, mskbf.t[:, m, :].unsqueeze(1).to_broadcast([128, 4, 128]), ALU.mult, [pt_.b, mskbf.b], [pt_.b])
                            MM(pO.t[:, 0:512], vap, pt_.t[:], ci == 0, last, [Va.b, pt_.b], [pO.b])
                            MM(pD.t[:, 0:512], onesbf.t[:], pt_.t[:], ci == 0, False, [onesbf.b, pt_.b], [pD.b])

                        pend = None
                        for ci, (kap, kb_, vap, m) in enumerate(chunks):
                            pS = psn(0, 4)
                            MM(pS.t[:, 0:512].rearrange("p (h q) -> p h q", h=4), kap, qTn.t[:, 4 * g:4 * g + 4, qs], True, True, [kb_] + qb[4 * g:4 * g + 4], [pS.b])
                            if pend is not None:
                                fin(*pend)
                            pend = (ci, pS, vap, m, ci == len(chunks) - 1)
                        fin(*pend)
                        MM(pD.t[:, 0:512], onesbf.t[0:1, :], sinkrow.t[0:1, l, g * 512:(g + 1) * 512], False, True, [onesbf.b, sinkrow.b], [pD.b])
                        rd = rdr.next()
                        on = onr.next()
                        ACT(rd.t[:], pD.t[:, 0:512], AF.Ln, [pD.b], [rd.b])
                        ACT(rd.t[:], rd.t[:], AF.Exp, [rd.b], [rd.b], scale=-1.0)
                        TT("dve", on.t[:], pO.t[:, 0:512], rd.t[:], ALU.mult, [pO.b, rd.b], [on.b])
                        TT(cpool, ogT.t[:, 4 * g:4 * g + 4, qs], on.t[:].rearrange("p (h q) -> p h q", h=4), agT.t[:, 4 * g:4 * g + 4, qs], ALU.mult,
                           [on.b, agT.b], [ogT.b], partial=True)
                DMA(aT[:, tok0:tok0 + T].rearrange("(h p) t -> p h t", p=128), ogT.t[:, :, 0:T], R=[ogT.b])

        with P.phase():
            wfm = Tl(P, "wfm", [128, 8, 256], F32)
            DMA(wfm.t[:], wfm_in[l].rearrange("g (k p) d -> p (g k) d", p=128), W=[wfm.b])
            CW = Tl(P, "CW", [128, 4, 2, 512], BF16)
            for g in range(4):
                for cs in range(2):
                    for mch in range(2):
                        pc = psn()
                        for k in range(2):
                            MM(pc.t[:, 0:256], dftc.t[:, cs, k, mch * 128:(mch + 1) * 128], wfm.t[:, g * 2 + k, :], k == 0, k == 1, [dftc.b, wfm.b], [pc.b])
                        ACT(CW.t[:, g, mch, cs * 256:(cs + 1) * 256], pc.t[:, 0:256], AF.Copy, [pc.b], [CW.b], partial=True)
            ufr = Rot(P, "uF", 2, [128, 8, 512], BF16)
            abr = Rot(P, "AB", 2, [128, 4, 512], BF16)
            gfb = Buf()
            for (tok0, T, isctx) in tiles():
                if isctx and l == 1:
                    continue
                uF = ufr.next()
                DMA(uF.t[:, :, 0:T], pT[F_OFF:F_OFF + 1024, tok0:tok0 + T].rearrange("(c p) t -> p c t", p=128), W=[uF.b])
                for s in range(T // 128):
                    ab = abr.next()
                    for g in range(4):
                        pa = psn()
                        for mch in range(2):
                            MM(pa.t[:, 0:512], uF.t[:, g * 2 + mch, s * 128:(s + 1) * 128], CW.t[:, g, mch, :], mch == 0, mch == 1, [uF.b, CW.b], [pa.b])
                        if g % 2 == 0:
                            ACT(ab.t[:, g, :], pa.t[:, 0:512], AF.Copy, [pa.b], [ab.b], partial=True)
                        else:
                            CP("dve", ab.t[:, g, :], pa.t[:, 0:512], [pa.b], [ab.b], partial=True)
                    if isctx:
                        DMA(gFc[s * 128:(s + 1) * 128, :], ab.t[:].rearrange("p g n -> p (g n)"), R=[ab.b])
                    else:
                        r0 = tok0 - CT + s * 128
                        DMA(gF_in[r0:r0 + 128, :], ab.t[:].rearrange("p g n -> p (g n)"), R=[ab.b], W=[gfb], partial=True)

        with P.phase():
            segs = [(CT, LT, 0)] + ([(0, CT, 1)] if l == 0 else [])
            wpw = Tl(P, "wpw", [128, 8, 1024], BF16)
            DMA(wpw.t[:], W("wpw").rearrange("(k p) n -> p k n", p=128), W=[wpw.b])
            dww = Tl(P, "dww", [128, 8, 31], F32)
            cv3 = Tl(P, "cv3", [128, 8, 3], F32)
            DMA(dww.t[:], dww_in[l], W=[dww.b])
            DMA(cv3.t[:], cv3_in[l], W=[cv3.b])
            ar = Rot(P, "cva", 1, [128, 8, 512], BF16)
            br = Rot(P, "cvb", 1, [128, 8, 512], BF16)
            cgr = Rot(P, "cvg", 1, [128, 8, 512], BF16)
            dgc = Rot(P, "dgc", 1, [128, 31, 128], BF16)
            ybuf = Tl(P, "ybuf", [128, 8, 512], F32)
            ysq = Rot(P, "ysq", 2, [128, 512], F32)
            st4 = [Tl(P, "st%d" % i, [128, 512], F32) for i in range(4)]
            tdr = Rot(P, "td", 2, [128, 512], F32)
            zT = Tl(P, "zT", [128, 8, 512], BF16)
            cst_ = Rot(P, "cstg", 1, [128, 8, 512], BF16)
            for (s0, SL, isctx) in segs:
                uT = Tl(P, "uT%d" % isctx, [128, 8, SL + 32], BF16)
                TTs = min(512, SL)
                op("dve", lambda e, uT=uT: e.memset(uT.t[:], 0.0), writes=[uT.b])
                for t0 in range(0, SL, TTs):
                    a_, b_ = ar.next(), br.next()
                    DMA(a_.t[:, :, 0:TTs], pT[CA_OFF:CA_OFF + 1024, s0 + t0:s0 + t0 + TTs].rearrange("(c p) t -> p c t", p=128), W=[a_.b])
                    DMA(b_.t[:, :, 0:TTs], pT[CB_OFF:CB_OFF + 1024, s0 + t0:s0 + t0 + TTs].rearrange("(c p) t -> p c t", p=128), W=[b_.b])
                    TT("dve", uT.t[:, :, 16 + t0:16 + t0 + TTs], a_.t[:, :, 0:TTs], b_.t[:, :, 0:TTs], ALU.mult, [a_.b, b_.b], [uT.b], partial=True)
                if not isctx:
                    gub = Buf()
                    gob2 = Buf()
                    DMA(gU_in[:, 0:16].rearrange("(c p) e -> p c e", p=128), uT.t[:, :, 16:32], R=[uT.b], W=[gub], partial=True)
                    DMA(gU_in[:, 16:32].rearrange("(c p) e -> p c e", p=128), uT.t[:, :, SL:SL + 16], R=[uT.b], W=[gub], partial=True)
                    CC(gU_in, gU_out, RG, [gub], [gob2])
                    frc = min(LT, 256)
                    for c in range(LT // frc):
                        CC(gF_in[c * frc:(c + 1) * frc, :], gF_out[c * 4 * frc:(c + 1) * 4 * frc, :], RG, [], [])
                    eu = Tl(P, "eu", [128, 4, 8, 32], BF16)
                    DMA(eu.t[:], gU_out.rearrange("(r c p) e -> p r c e", r=4, c=8), R=[gob2], W=[eu.b])
                    for dst, ecol, sb in ((uT.t[:, :, 0:16], slice(16, 32), 0), (uT.t[:, :, 16 + SL:32 + SL], slice(0, 16), 4)):
                        TS("dve", dst, eu.t[:, 0, :, ecol], selc(sb), None, ALU.mult, None, [eu.b, cst.b], [uT.b], partial=True)
                        for r in range(1, 4):
                            STT("dve", dst, eu.t[:, r, :, ecol], selc(sb + r), dst, ALU.mult, ALU.add, [eu.b, cst.b, uT.b], [uT.b], partial=True)
                for t0 in range(0, SL, TTs):
                    T = TTs
                    cg = cgr.next()
                    DMA(cg.t[:, :, 0:T], pT[CG_OFF:CG_OFF + 1024, s0 + t0:s0 + t0 + T].rearrange("(c p) t -> p c t", p=128), W=[cg.b])
                    p1, p2 = PS[0], PS[1]
                    for c in range(8):
                        dg = dgc.next()
                        TT("dve", dg.t[:], identbf.t[:].unsqueeze(1).to_broadcast([128, 31, 128]),
                           dww.t[:, c, :].unsqueeze(2).to_broadcast([128, 31, 128]), ALU.mult, [identbf.b, dww.b], [dg.b])
                        pc = psn(2, 6)
                        for j in range(31):
                            MM(pc.t[:, 0:T], dg.t[:, j, :], uT.t[:, c, t0 + j + 1:t0 + j + 1 + T], j == 0, j == 30, [dg.b, uT.b], [pc.b])
                        ACT(ybuf.t[:, c, 0:T], pc.t[:, 0:T], AF.Identity, [pc.b, cv3.b], [ybuf.b], bias=cv3.t[:, c, 0:1], partial=True)
                        yq = ysq.next()
                        ACT(yq.t[:, 0:T], ybuf.t[:, c, 0:T], AF.Square, [ybuf.b], [yq.b])
                        MM(p1.t[:, 0:T], ones32.t[:], ybuf.t[:, c, 0:T], c == 0, c == 7, [ones32.b, ybuf.b], [p1.b])
                        MM(p2.t[:, 0:T], ones32.t[:], yq.t[:, 0:T], c == 0, c == 7, [ones32.b, yq.b], [p2.b])
                    mean, ex2, var, rin = st4
                    ACT(mean.t[:, 0:T], p1.t[:, 0:T], AF.Copy, [p1.b], [mean.b], scale=1.0 / 1024)
                    ACT(ex2.t[:, 0:T], p2.t[:, 0:T], AF.Copy, [p2.b], [ex2.b], scale=1.0 / 1024)
                    TT("dve", var.t[:, 0:T], mean.t[:, 0:T], mean.t[:, 0:T], ALU.mult, [mean.b], [var.b])
                    TT("dve", var.t[:, 0:T], ex2.t[:, 0:T], var.t[:, 0:T], ALU.subtract, [ex2.b, var.b], [var.b])
                    ACT(rin.t[:, 0:T], var.t[:, 0:T], AF.Sqrt, [var.b], [rin.b], bias=EPS)
                    RECIP(rin.t[:, 0:T], rin.t[:, 0:T], [rin.b], [rin.b])
                    for c in range(8):
                        td = tdr.next()
                        TT("dve", td.t[:, 0:T], ybuf.t[:, c, 0:T], mean.t[:, 0:T], ALU.subtract, [ybuf.b, mean.b], [td.b])
                        TT("dve", td.t[:, 0:T], td.t[:, 0:T], rin.t[:, 0:T], ALU.mult, [td.b, rin.b], [td.b])
                        ACT(zT.t[:, c, 0:T], td.t[:, 0:T], AF.Silu, [td.b, cv3.b], [zT.b], bias=cv3.t[:, c, 2:3], scale=cv3.t[:, c, 1:2], partial=True)
                    cs_ = cst_.next()
                    for co in range(8):
                        pw_ = psn(6, 8)
                        for ci in range(8):
                            MM(pw_.t[:, 0:T], wpw.t[:, ci, co * 128:(co + 1) * 128], zT.t[:, ci, 0:T], ci == 0, ci == 7, [wpw.b, zT.b], [pw_.b])
                        TT("dve", cs_.t[:, co, 0:T], pw_.t[:, 0:T], cg.t[:, co, 0:T], ALU.mult, [pw_.b, cg.b], [cs_.b], partial=True)
                    DMA(cT[:, s0 + t0:s0 + t0 + T].rearrange("(c p) t -> p c t", p=128), cs_.t[:, :, 0:T], R=[cs_.b])

        with P.phase():
            gar = Rot(P, "ga", 3, [128, 2048], BF16)
            tcr = Rot(P, "tc", 3, [128, 512], BF16)
            tsr = Rot(P, "tsn", 3, [128, 512], BF16)
            fgr = Rot(P, "fg", 2, [128, 8, 512], BF16)
            fst = Rot(P, "fst", 2, [128, 8, 512], BF16)
            tcx = Tl(P, "tcx", [128, 2, 2, 256], F32)
            tcb = Tl(P, "tcb", [128, 2, 2, 256], BF16)
            DMA(tcx.t[:], dftctx_in.rearrange("c a p k -> p c a k"), W=[tcx.b])
            CP("dve", tcb.t[:], tcx.t[:], [tcx.b], [tcb.b])
            for (tok0, T, isctx) in tiles():
                if isctx and l == 1:
                    continue
                na = 2 if isctx else NA
                fg = fgr.next()
                DMA(fg.t[:, :, 0:T], pT[FG_OFF:FG_OFF + 1024, tok0:tok0 + T].rearrange("(c p) t -> p c t", p=128), W=[fg.b])
                for a in range(na):
                    ga = gar.next()
                    if isctx:
                        DMA(ga.t[:], gFc[a * 128:(a + 1) * 128, :], W=[ga.b])
                        tcap, tsap, tb1, tb2 = tcb.t[:, 0, a, :], tcb.t[:, 1, a, :], tcb.b, tcb.b
                    else:
                        t0 = tok0 - CT
                        DMA(ga.t[:], gF_out[a * 128:(a + 1) * 128, :], W=[ga.b])
                        tc_, ts_ = tcr.next(), tsr.next()
                        DMA(tc_.t[:, 0:T], tabC[a, :, t0:t0 + T], W=[tc_.b])
                        DMA(ts_.t[:, 0:T], tabS[a, :, t0:t0 + T], W=[ts_.b])
                        tcap, tsap, tb1, tb2 = tc_.t[:, 0:T], ts_.t[:, 0:T], tc_.b, ts_.b
                    for fc in range(8):
                        g, half = fc // 2, fc % 2
                        c0 = g * 512 + half * 128
                        MM(PS[fc].t[:, 0:T], ga.t[:, c0:c0 + 128], tcap, a == 0, False, [ga.b, tb1], [PS[fc].b])
                        MM(PS[fc].t[:, 0:T], ga.t[:, c0 + 256:c0 + 384], tsap, False, a == na - 1, [ga.b, tb2], [PS[fc].b])
                fs = fst.next()
                for fc in range(8):
                    TT("dve", fs.t[:, fc, 0:T], PS[fc].t[:, 0:T], fg.t[:, fc, 0:T], ALU.mult, [PS[fc].b, fg.b], [fs.b], partial=True)
                DMA(fT[:, tok0:tok0 + T].rearrange("(c p) t -> p c t", p=128), fs.t[:, :, 0:T], R=[fs.b])

        with P.phase():
            aTr = Rot(P, "aTt", 1, [128, 16, 512], BF16)
            fTr = Rot(P, "fTt", 1, [128, 8, 512], BF16)
            cTr = Rot(P, "cTt", 1, [128, 8, 512], BF16)
            war = Rot(P, "wa", 2, [128, 16, 512], BF16)
            wfr = Rot(P, "wf", 2, [128, 8, 512], BF16)
            wcr = Rot(P, "wc", 2, [128, 8, 512], BF16)
            mgr = Rot(P, "mg", 2, [128, 3, 4, 512], BF16)
            e1t = [Rot(P, "e1t%d" % i, 2, [128, 512], F32) for i in range(3)]
            mTt = Rot(P, "mTt", 2, [128, 4, 512], BF16)
            wau, wfu, wcu = W("wau"), W("wfu"), W("wcu")
            epool = "dve" if l == 0 else "pool"
            if l == 0:
                record_gathers(g_l1[n1:])
            for (tok0, T, isctx) in tiles():
                if isctx and l == 1:
                    continue
                at, ft, ct = aTr.next(), fTr.next(), cTr.next()
                DMA(at.t[:, :, 0:T], aT[:, tok0:tok0 + T].rearrange("(c p) t -> p c t", p=128), W=[at.b])
                DMA(ft.t[:, :, 0:T], fT[:, tok0:tok0 + T].rearrange("(c p) t -> p c t", p=128), W=[ft.b])
                DMA(ct.t[:, :, 0:T], cT[:, tok0:tok0 + T].rearrange("(c p) t -> p c t", p=128), W=[ct.b])
                for cb in range(D // 512):
                    cs = slice(cb * 512, (cb + 1) * 512)
                    wa, wf, wc, mg = war.next(), wfr.next(), wcr.next(), mgr.next()
                    DMA(wa.t[:].rearrange("p k n -> p (k n)"), wau[cb * 128:(cb + 1) * 128, :], W=[wa.b])
                    DMA(wf.t[:].rearrange("p k n -> p (k n)"), wfu[cb * 128:(cb + 1) * 128, :], W=[wf.b])
                    DMA(wc.t[:].rearrange("p k n -> p (k n)"), wcu[cb * 128:(cb + 1) * 128, :], W=[wc.b])
                    for br_ in range(3):
                        r0 = MG + br_ * D + cb * 512
                        DMA(mg.t[:, br_, :, 0:T], pT[r0:r0 + 512, tok0:tok0 + T].rearrange("(c p) t -> p c t", p=128), W=[mg.b], partial=(br_ > 0))
                    mt = mTt.next()
                    for dcl in range(4):
                        ds_ = slice(dcl * 128, (dcl + 1) * 128)
                        pa, pf, pc = psn(0, 3), psn(3, 6), psn(6, 8)
                        for k in range(16):
                            MM(pa.t[:, 0:T], wa.t[:, k, ds_], at.t[:, k, 0:T], k == 0, k == 15, [wa.b, at.b], [pa.b])
                        for k in range(8):
                            MM(pf.t[:, 0:T], wf.t[:, k, ds_], ft.t[:, k, 0:T], k == 0, k == 7, [wf.b, ft.b], [pf.b])
                        for k in range(8):
                            MM(pc.t[:, 0:T], wc.t[:, k, ds_], ct.t[:, k, 0:T], k == 0, k == 7, [wc.b, ct.b], [pc.b])
                        t0_, t1_, t2_ = e1t[0].next(), e1t[1].next(), e1t[2].next()
                        TT("dve", t0_.t[:, 0:T], pa.t[:, 0:T], mg.t[:, 0, dcl, 0:T], ALU.mult, [pa.b, mg.b], [t0_.b])
                        TT("dve", t1_.t[:, 0:T], pf.t[:, 0:T], mg.t[:, 1, dcl, 0:T], ALU.mult, [pf.b, mg.b], [t1_.b])
                        TT("dve", t2_.t[:, 0:T], pc.t[:, 0:T], mg.t[:, 2, dcl, 0:T], ALU.mult, [pc.b, mg.b], [t2_.b])
                        TT(epool, t0_.t[:, 0:T], t0_.t[:, 0:T], t1_.t[:, 0:T], ALU.add, [t0_.b, t1_.b], [t0_.b])
                        TT(epool, mt.t[:, dcl, 0:T], t0_.t[:, 0:T], t2_.t[:, 0:T], ALU.add, [t0_.b, t2_.b], [mt.b], partial=True)
                    DMA(mTd[cb * 512:(cb + 1) * 512, tok0:tok0 + T].rearrange("(c p) t -> p c t", p=128), mt.t[:, :, 0:T], R=[mt.b])

        with P.phase():
            mTr = Rot(P, "mT", 2, [128, KD, 512], BF16)
            wor = Rot(P, "wo", 2, [128, KD, 512], BF16)
            gtr = Rot(P, "gtb", 2, [128, 512], F32)
            xsr = Rot(P, "xs", 3, [128, 512], F32)
            e2t = Rot(P, "e2t", 2, [128, 512], F32)
            xor_ = Rot(P, "xo", 3, [128, 512], F32)
            wo = W("wo")
            epool = "pool"
            if l == 0:
                pass
            for (tok0, T, isctx) in tiles():
                if isctx and l == 1:
                    continue
                q = 2 * l + isctx
                mt = mTr.next()
                for k0 in range(0, KD, 8):
                    k1 = min(KD, k0 + 8)
                    DMA(mt.t[:, k0:k1, 0:T], mTd[k0 * 128:k1 * 128, tok0:tok0 + T].rearrange("(c p) t -> p c t", p=128), W=[mt.b], partial=(k0 > 0))
                for nb in range(D // 512):
                    cs = slice(nb * 512, (nb + 1) * 512)
                    wt = wor.next()
                    DMA(wt.t[:].rearrange("p k n -> p (k n)"), wo[nb * 128:(nb + 1) * 128, :], W=[wt.b])
                    gt = gtr.next()
                    DMA(gt.t[:], gtb_d[q, :, cs], W=[gt.b])
                    for s in range(T // 128):
                        xs = xsr.next()
                        DMA(xs.t[:], src_rows(l, tok0 + s * 128, 128)[:, cs], W=[xs.b])
                        po = psn()
                        for kc in range(KD):
                            MM(po.t[:, 0:512], mt.t[:, kc, s * 128:(s + 1) * 128], wt.t[:, kc, :], kc == 0, kc == KD - 1, [mt.b, wt.b], [po.b])
                        tt_ = e2t.next()
                        xo = xor_.next()
                        TT("dve", tt_.t[:], po.t[:, 0:512], gt.t[:], ALU.mult, [po.b, gt.b], [tt_.b])
                        TT(epool, xo.t[:], tt_.t[:], xs.t[:], ALU.add, [tt_.b, xs.b], [xo.b])
                        if l == 0:
                            DMA(x1[tok0 + s * 128:tok0 + (s + 1) * 128, cs], xo.t[:], R=[xo.b])
                        else:
                            r0 = tok0 - CT + s * 128
                            DMA(out[r0:r0 + 128, cs], xo.t[:], R=[xo.b])

    finish(P)
    P.top.close()
    return nc, P


def host_consts(cfg, r):
    LT, SEQ, NA = cfg.LT, cfg.SEQ, cfg.NA
    cst = np.zeros((128, 1040), np.float32)
    cst[:, 0:128] = np.eye(128, dtype=np.float32)
    R = np.zeros((128, 128), np.float32)
    for base in (0, 64):
        for i in range(32):
            R[base + i, base + i + 32] = -1.0
            R[base + i + 32, base + i] = 1.0
    cst[:, 128:256] = R.T
    jj = np.arange(128)[:, None]
    ii = np.arange(128)[None, :]
    cst[:, 256:384] = (ii <= jj).astype(np.float32)
    cst[:, 384:512] = (jj <= ii).astype(np.float32)
    if r > 0:
        cst[:, 512 + r - 1] = 1.0
        cst[:, 520] = 1.0
    if r < 3:
        cst[:, 516 + r + 1] = 1.0
        cst[:, 521] = 1.0
    for q in range(4):
        cst[q, 528 + q * 128:528 + (q + 1) * 128] = 1.0
    pos = np.arange(r * LT, (r + 1) * LT)
    row = (pos // 64).astype(np.float64)
    col = (pos % 64).astype(np.float64)
    inv = 10000.0 ** (-np.arange(0, 64, 2, dtype=np.float64) / 64)
    ar_, ac_ = row[:, None] * inv[None, :], col[:, None] * inv[None, :]
    ang = np.concatenate([ar_, ar_, ac_, ac_], -1).astype(np.float32)
    rope = np.stack([np.cos(ang).T, np.sin(ang).T]).astype(np.float32)
    k = pos.astype(np.float64)[None, :]
    p = np.arange(128, dtype=np.float64)[:, None]
    a = np.arange(NA, dtype=np.float64)[:, None]
    sc = 1.0 / np.sqrt(SEQ * 256.0)
    angB = 2 * np.pi * ((p * k) % SEQ) / SEQ
    frc = min(LT, 256)
    ai = np.arange(NA)
    m0 = ai * 128
    cc_, rr_, ii_ = m0 // (4 * frc), (m0 % (4 * frc)) // frc, m0 % frc
    l0 = rr_ * LT + cc_ * frc + ii_
    a = (l0 // 128).astype(np.float64)[:, None]
    angA = 2 * np.pi * ((128 * a * k) % SEQ) / SEQ
    dftB = np.stack([np.cos(angB), np.sin(angB)]).astype(np.float32)
    dftA = (np.stack([np.cos(angA), np.sin(angA)]) * sc).astype(np.float32)
    l_ = np.arange(256, dtype=np.float64)
    a256 = 2 * np.pi * np.outer(l_, l_) / 256
    scc = 1.0 / 256.0
    dftctx = np.stack([np.cos(a256) * scc, -np.sin(a256) * scc]).reshape(2, 2, 128, 256).astype(np.float32)
    dc = np.stack([np.cos(a256), np.sin(a256)])
    dftc = dc.reshape(2, 2, 128, 256).transpose(2, 0, 1, 3).astype(np.float32)
    return dict(cst=cst, rope=rope, dftB=dftB, dftA=dftA, dftctx=np.ascontiguousarray(dftctx), dftc=np.ascontiguousarray(dftc))


def make_in_maps(cfg, inp):
    D, LT, KD, MC = cfg.D, cfg.LT, cfg.KD, cfg.MC
    f = lambda a: np.ascontiguousarray(np.asarray(a, dtype=np.float32))
    x, c, ctx, c_ctx = f(inp["x"]), f(inp["c"]), f(inp["ctx"]), f(inp["c_ctx"])
    maps = []
    ngT = f(f(inp["norm_g"]).reshape(2, KD, 128).transpose(0, 2, 1))
    qg, kg = f(inp["q_norm_g"]), f(inp["k_norm_g"])
    qkg = f(np.stack([qg, kg], -1))
    qkrow = f(np.concatenate([qg, kg], -1).reshape(2, 1, 256))
    sink = f(f(inp["attn_sink"]).reshape(2, 1, 16))
    dww = f(f(inp["conv_dw_w"]).reshape(2, 31, 8, 128).transpose(0, 3, 2, 1))
    cv3 = f(np.stack([f(inp["conv_dw_b"]), f(inp["conv_ln_g"]), f(inp["conv_ln_b"])], -1).reshape(2, 8, 128, 3).transpose(0, 2, 1, 3))
    bmod = f(inp["b_mod"])
    wl = {"w_in": f(inp["w_in"]), "w_au": f(inp["w_attn_up"]), "w_fu": f(inp["w_fourier_up"]), "w_cu": f(inp["w_conv_up"]),
          "w_o": f(inp["w_out"]), "w_pw": f(inp["w_conv_pw"])}
    wl_g = {}
    for k_, w in wl.items():
        K_, N_ = w.shape[1], w.shape[2]
        if k_ == "w_pw":
            wl_g[k_] = w
        else:
            wl_g[k_] = np.ascontiguousarray(w.reshape(2, K_ // 128, 128, N_ // 512, 512).transpose(0, 3, 2, 1, 4)).reshape(2, N_ // 4, 4 * K_)
    wmod = f(inp["w_mod"])
    wfm = f(inp["w_fourier_mix"])
    for core in range(8):
        b, r = core // 4, core % 4
        m = host_consts(cfg, r)
        m["x"] = f(x[b, r * LT:(r + 1) * LT])
        m["ctx"] = f(ctx[b])
        m["cvT"] = f(np.stack([c[b], c_ctx], -1).reshape(KD, 128, 2).transpose(1, 0, 2))
        m["ngT"], m["qkg"], m["qkrow"], m["sink"], m["wfm"], m["dww"], m["cv3"] = ngT, qkg, qkrow, sink, wfm, dww, cv3
        for k_, w in wl.items():
            g = wl_g[k_]
            R_g, C_g = g.shape[1], g.shape[2]
            rc = wchunk(R_g, C_g)
            m[k_] = f(g.reshape(2, R_g // (4 * rc), 4, rc, C_g)[:, :, r].reshape(2, R_g // 4, C_g))
        m["w_mod"] = f(wmod[:, :, r * MC:(r + 1) * MC])
        m["b_mod"] = f(np.repeat(bmod[:, None, r * MC:(r + 1) * MC], 2, axis=1))
        maps.append(m)
    return maps


def kernel(**inputs):
    cfg = Cfg()
    nc, _ = build(cfg)
    maps = make_in_maps(cfg, inputs)
    res = run_bass_kernel_spmd(nc, maps, core_ids=list(range(8)))
    outp = np.empty((2, cfg.SEQ, cfg.D), np.float32)
    for core in range(8):
        b, r = core // 4, core % 4
        outp[b, r * cfg.LT:(r + 1) * cfg.LT] = res.results[core]["out"]
    return outp
```
